# Optimizing a Trainium2 kernel written in Bass

```python
import math
import jax, jax.numpy as jnp
from jax import lax
import numpy as np

D_MODEL = 1024
BATCH = 8
SEQ = 2048
DEPTH = 1
DEC_BATCH = 128
DEC_SEQ = 1
PAST_LEN = 8192
PAGE_SIZE = 128

GDN_HEADS = 8
GDN_DK = 128
GDN_DV = 128
GDN_CONV = 4
GDN_CHUNK = 64
GDN_QK = GDN_HEADS * GDN_DK
GDN_V = GDN_HEADS * GDN_DV
GDN_CONV_CH = 2 * GDN_QK + GDN_V
SWA_Q_HEADS = 16
SWA_KV_HEADS = 4
SWA_GROUP = SWA_Q_HEADS // SWA_KV_HEADS
SWA_HD = 64
SWA_Q = SWA_Q_HEADS * SWA_HD
SWA_KV = SWA_KV_HEADS * SWA_HD
WINDOW = 128
D_FF = 2816
FFN_CONV = 3
IN_WIDTHS = (GDN_CONV_CH, GDN_V, GDN_HEADS, GDN_HEADS, SWA_Q, SWA_KV, SWA_KV, D_MODEL, D_MODEL)
IN_WIDTH = GDN_CONV_CH + GDN_V + 2 * GDN_HEADS + SWA_Q + 2 * SWA_KV + 2 * D_MODEL
EPS = 1e-6

kernel_name = "hybrid_gdn_swa_convffn_adaln_step"


def _rmsnorm(x, w):
    xf = x.astype(jnp.float32)
    y = xf * lax.rsqrt(jnp.mean(xf * xf, axis=-1, keepdims=True) + EPS)
    return (y * w.astype(jnp.float32)).astype(x.dtype)


def _l2norm(x):
    return x * lax.rsqrt(jnp.sum(x * x, axis=-1, keepdims=True) + EPS)


def _adaln(c, w_mod, b_mod):
    m = jax.nn.silu(c) @ w_mod + b_mod
    return jnp.split(m[:, None, :], 6, axis=-1)


def _split_in(z):
    parts, o = [], 0
    for w_ in IN_WIDTHS:
        parts.append(z[..., o:o + w_])
        o += w_
    return parts


def _causal_dwconv(x_ext, w, n_out):
    width = w.shape[0]
    return sum(w[j] * x_ext[:, j:j + n_out] for j in range(width))


def _gdn_features(qkv_conv, beta_raw, a_raw, a_log, dt_bias):
    f32 = jnp.float32
    B_, L = qkv_conv.shape[:2]
    act = jax.nn.silu(qkv_conv).astype(f32)
    q = act[..., :GDN_QK].reshape(B_, L, GDN_HEADS, GDN_DK)
    k = act[..., GDN_QK:2 * GDN_QK].reshape(B_, L, GDN_HEADS, GDN_DK)
    v = act[..., 2 * GDN_QK:].reshape(B_, L, GDN_HEADS, GDN_DV)
    q = _l2norm(q) * (GDN_DK ** -0.5)
    k = _l2norm(k)
    beta = jax.nn.sigmoid(beta_raw.astype(f32))
    g = -jnp.exp(a_log.astype(f32)) * jax.nn.softplus(a_raw.astype(f32) + dt_bias.astype(f32))
    return q, k, v, beta, g


def _gdn_chunked(q, k, v, beta, g):
    f32 = jnp.float32
    B_, L = q.shape[:2]
    C = GDN_CHUNK
    N = L // C

    def blk(t):
        return jnp.moveaxis(t.reshape((B_, N, C) + t.shape[2:]), 3, 1)

    q, k, v, beta, g = blk(q), blk(k), blk(v), blk(beta), blk(g)
    decay = jnp.cumsum(g, axis=-1)
    idx = jnp.arange(C)
    tril = idx[:, None] >= idx[None, :]
    strict = idx[:, None] > idx[None, :]
    diff = decay[..., :, None] - decay[..., None, :]
    gam = jnp.where(tril, jnp.exp(jnp.where(tril, diff, 0.0)), 0.0)
    kb = k * beta[..., None]
    m = jnp.where(strict, jnp.einsum('bhnik,bhnjk->bhnij', kb, k) * gam, 0.0)
    eye = jnp.eye(C, dtype=f32)
    t_inv = lax.linalg.triangular_solve(eye + m, jnp.broadcast_to(eye, m.shape),
                                        left_side=True, lower=True, unit_diagonal=True)
    u = jnp.einsum('bhnij,bhnjv->bhniv', t_inv, v * beta[..., None])
    w = jnp.einsum('bhnij,bhnjk->bhnik', t_inv, kb * jnp.exp(decay)[..., None])
    a_intra = jnp.where(tril, jnp.einsum('bhnik,bhnjk->bhnij', q, k) * gam, 0.0)
    q_dec = q * jnp.exp(decay)[..., None]
    k_dec = k * jnp.exp(decay[..., -1:] - decay)[..., None]
    last = jnp.exp(decay[..., -1])
    xs = tuple(jnp.moveaxis(t, 2, 0) for t in (u, w, a_intra, q_dec, k_dec, last))

    def step(S, inp):
        u_n, w_n, a_n, qd_n, kd_n, last_n = inp
        v_new = u_n - jnp.einsum('bhck,bhkv->bhcv', w_n, S)
        o = jnp.einsum('bhck,bhkv->bhcv', qd_n, S) + jnp.einsum('bhij,bhjv->bhiv', a_n, v_new)
        S = S * last_n[..., None, None] + jnp.einsum('bhck,bhcv->bhkv', kd_n, v_new)
        return S, o

    S0 = jnp.zeros((B_, GDN_HEADS, GDN_DK, GDN_DV), f32)
    S, o = lax.scan(step, S0, xs)
    o = jnp.transpose(o, (1, 0, 3, 2, 4)).reshape(B_, L, GDN_HEADS, GDN_DV)
    return o, S


def _gdn_recurrent(q, k, v, beta, g, S0):
    xs = tuple(jnp.moveaxis(t, 1, 0) for t in (q, k, v, beta, g))

    def step(S, inp):
        q_t, k_t, v_t, b_t, g_t = inp
        S = S * jnp.exp(g_t)[..., None, None]
        delta = (v_t - jnp.einsum('bhk,bhkv->bhv', k_t, S)) * b_t[..., None]
        S = S + jnp.einsum('bhk,bhv->bhkv', k_t, delta)
        return S, jnp.einsum('bhk,bhkv->bhv', q_t, S)

    S, o = lax.scan(step, S0.astype(jnp.float32), xs)
    return jnp.moveaxis(o, 0, 1), S


def _gdn_out(o, gate, w):
    B_, L = o.shape[:2]
    on = o * lax.rsqrt(jnp.mean(o * o, axis=-1, keepdims=True) + EPS) * w.astype(jnp.float32)
    gf = jax.nn.silu(gate.astype(jnp.float32)).reshape(B_, L, GDN_HEADS, GDN_DV)
    return (on * gf).reshape(B_, L, GDN_V).astype(gate.dtype)


def _sink_attention(qg, kb, vb, mask, sinks):
    s = jnp.einsum('...qhgd,...khd->...hgqk', qg, kb).astype(jnp.float32) * (SWA_HD ** -0.5)
    s = jnp.where(mask, s, -jnp.inf)
    sink = jnp.broadcast_to(sinks.astype(jnp.float32).reshape(SWA_KV_HEADS, SWA_GROUP, 1, 1),
                            s.shape[:-1] + (1,))
    p = jax.nn.softmax(jnp.concatenate([s, sink], axis=-1), axis=-1)[..., :-1]
    return jnp.einsum('...hgqk,...khd->...qhgd', p.astype(vb.dtype), vb)


def _swa_banded(qg, k, v, sinks):
    B_, L = qg.shape[:2]
    nb = L // WINDOW
    qb = qg.reshape(B_, nb, WINDOW, SWA_KV_HEADS, SWA_GROUP, SWA_HD)

    def band(t):
        tp = jnp.concatenate([jnp.zeros_like(t[:, :WINDOW]), t], axis=1)
        tp = tp.reshape(B_, nb + 1, WINDOW, SWA_KV_HEADS, SWA_HD)
        return jnp.concatenate([tp[:, :-1], tp[:, 1:]], axis=2)

    blocks = jnp.arange(nb)[:, None]
    qabs = blocks * WINDOW + jnp.arange(WINDOW)[None, :]
    kabs = (blocks - 1) * WINDOW + jnp.arange(2 * WINDOW)[None, :]
    d = qabs[:, :, None] - kabs[:, None, :]
    mask = (d >= 0) & (d < WINDOW) & (kabs[:, None, :] >= 0)
    o = _sink_attention(qb, band(k), band(v), mask[None, :, None, None], sinks)
    return o.reshape(B_, L, SWA_Q)


def _layer(x, c, lp, st):
    B_, L, _ = x.shape
    sh1, sc1, gt1, sh2, sc2, gt2 = _adaln(c, lp['w_mod'], lp['b_mod'])
    h = _rmsnorm(x, lp['norm1_w']) * (1 + sc1) + sh1
    qkv, gdn_gate, beta_raw, a_raw, sq, sk, sv, ga, gb = _split_in(h @ lp['w_in'])

    if st is None:
        conv_prev = jnp.zeros((B_, GDN_CONV - 1, GDN_CONV_CH), qkv.dtype)
    else:
        conv_prev = st['gdn_conv'].astype(qkv.dtype)
    qkv_ext = jnp.concatenate([conv_prev, qkv], axis=1)
    new_gdn_conv = qkv_ext[:, -(GDN_CONV - 1):]
    q, k, v, beta, g = _gdn_features(_causal_dwconv(qkv_ext, lp['gdn_conv_w'], L),
                                     beta_raw, a_raw, lp['gdn_a_log'], lp['gdn_dt_bias'])
    if st is None:
        o_a, S = _gdn_chunked(q, k, v, beta, g)
    else:
        o_a, S = _gdn_recurrent(q, k, v, beta, g, st['gdn_S'])
    y_a = _gdn_out(o_a, gdn_gate, lp['gdn_onorm_w']) @ lp['w_gdn_out']

    qg = sq.reshape(B_, L, SWA_KV_HEADS, SWA_GROUP, SWA_HD)
    kk = sk.reshape(B_, L, SWA_KV_HEADS, SWA_HD)
    vv = sv.reshape(B_, L, SWA_KV_HEADS, SWA_HD)
    if st is None:
        o_b = _swa_banded(qg, kk, vv, lp['swa_sinks'])
        new_k, new_v = kk[:, -WINDOW:], vv[:, -WINDOW:]
    else:
        kc = jnp.concatenate([st['k'].astype(kk.dtype), kk], axis=1)
        vc = jnp.concatenate([st['v'].astype(vv.dtype), vv], axis=1)
        qpos = WINDOW + jnp.arange(L)
        kpos = jnp.arange(WINDOW + L)
        d = qpos[:, None] - kpos[None, :]
        mask = (d >= 0) & (d < WINDOW)
        o_b = _sink_attention(qg, kc, vc, mask[None, None, None], lp['swa_sinks']).reshape(B_, L, SWA_Q)
        new_k, new_v = kc[:, -WINDOW:], vc[:, -WINDOW:]
    y_b = o_b @ lp['w_swa_out']

    mix = (jax.nn.sigmoid(ga) * y_a + jax.nn.sigmoid(gb) * y_b) @ lp['w_o']
    x = x + gt1 * mix

    h2 = _rmsnorm(x, lp['norm2_w']) * (1 + sc2) + sh2
    gate = h2 @ lp['w_ffn_gate']
    up = h2 @ lp['w_ffn_up']
    if st is None:
        ffn_prev = jnp.zeros((B_, FFN_CONV - 1, D_FF), gate.dtype)
    else:
        ffn_prev = st['ffn_conv'].astype(gate.dtype)
    gate_ext = jnp.concatenate([ffn_prev, gate], axis=1)
    new_ffn_conv = gate_ext[:, -(FFN_CONV - 1):]
    gc = _causal_dwconv(gate_ext, lp['ffn_conv_w'], L) + lp['ffn_conv_b']
    x = x + gt2 * ((jax.nn.silu(gc) * up) @ lp['w_ffn_down'])
    return x, (S.astype(x.dtype), new_gdn_conv, new_k, new_v, new_ffn_conv)


def setup_inputs(seed: int = 0) -> dict:
    key = jax.random.key(seed)
    ks = jax.random.split(key, 32)
    f32 = jnp.float32
    D = D_MODEL

    def nrm(k, shape, s=1.0):
        return jax.random.normal(k, shape, f32) * s

    dt = jnp.exp(jax.random.uniform(ks[12], (DEPTH, GDN_HEADS), f32, math.log(1e-3), math.log(1e-1)))
    return {
        'x_prompt': nrm(ks[0], (BATCH, SEQ, D)),
        'x_sample': nrm(ks[1], (DEC_BATCH, DEC_SEQ, D)),
        'c_prompt': nrm(ks[2], (BATCH, D)),
        'c_sample': nrm(ks[3], (DEC_BATCH, D)),
        'state_gdn_S': nrm(ks[4], (DEPTH, DEC_BATCH, GDN_HEADS, GDN_DK, GDN_DV), 0.1),
        'state_gdn_conv': nrm(ks[5], (DEPTH, DEC_BATCH, GDN_CONV - 1, GDN_CONV_CH)),
        'cache_swa_k': nrm(ks[6], (DEPTH, DEC_BATCH, WINDOW, SWA_KV_HEADS, SWA_HD)),
        'cache_swa_v': nrm(ks[7], (DEPTH, DEC_BATCH, WINDOW, SWA_KV_HEADS, SWA_HD)),
        'state_ffn_conv': nrm(ks[8], (DEPTH, DEC_BATCH, FFN_CONV - 1, D_FF)),
        'w_mod': nrm(ks[9], (DEPTH, D, 6 * D), 0.5 * D ** -0.5),
        'b_mod': nrm(ks[10], (DEPTH, 6 * D), 0.02),
        'norm1_w': 1.0 + nrm(ks[11], (DEPTH, D), 0.05),
        'norm2_w': 1.0 + nrm(ks[13], (DEPTH, D), 0.05),
        'w_in': nrm(ks[14], (DEPTH, D, IN_WIDTH), D ** -0.5),
        'gdn_conv_w': nrm(ks[15], (DEPTH, GDN_CONV, GDN_CONV_CH), GDN_CONV ** -0.5),
        'gdn_a_log': jnp.log(jax.random.uniform(ks[16], (DEPTH, GDN_HEADS), f32, 1.0, 16.0)),
        'gdn_dt_bias': dt + jnp.log(-jnp.expm1(-dt)),
        'gdn_onorm_w': 1.0 + nrm(ks[17], (DEPTH, GDN_DV), 0.05),
        'w_gdn_out': nrm(ks[18], (DEPTH, GDN_V, D), GDN_V ** -0.5),
        'swa_sinks': nrm(ks[19], (DEPTH, SWA_Q_HEADS)),
        'w_swa_out': nrm(ks[20], (DEPTH, SWA_Q, D), SWA_Q ** -0.5),
        'w_o': nrm(ks[21], (DEPTH, D, D), D ** -0.5),
        'w_ffn_gate': nrm(ks[22], (DEPTH, D, D_FF), D ** -0.5),
        'w_ffn_up': nrm(ks[23], (DEPTH, D, D_FF), D ** -0.5),
        'ffn_conv_w': nrm(ks[24], (DEPTH, FFN_CONV, D_FF), FFN_CONV ** -0.5),
        'ffn_conv_b': nrm(ks[25], (DEPTH, D_FF), 0.02),
        'w_ffn_down': nrm(ks[26], (DEPTH, D_FF, D), D_FF ** -0.5),
        'final_norm_w': 1.0 + nrm(ks[27], (D,), 0.05),
    }


def reference(x_prompt, x_sample, c_prompt, c_sample, state_gdn_S, state_gdn_conv, cache_swa_k,
              cache_swa_v, state_ffn_conv, w_mod, b_mod, norm1_w, norm2_w, w_in, gdn_conv_w,
              gdn_a_log, gdn_dt_bias, gdn_onorm_w, w_gdn_out, swa_sinks, w_swa_out, w_o,
              w_ffn_gate, w_ffn_up, ffn_conv_w, ffn_conv_b, w_ffn_down, final_norm_w):
    xp, xs = x_prompt, x_sample
    new_p = [[] for _ in range(5)]
    new_s = [[] for _ in range(5)]
    for l in range(DEPTH):
        lp = dict(w_mod=w_mod[l], b_mod=b_mod[l], norm1_w=norm1_w[l], norm2_w=norm2_w[l],
                  w_in=w_in[l], gdn_conv_w=gdn_conv_w[l], gdn_a_log=gdn_a_log[l],
                  gdn_dt_bias=gdn_dt_bias[l], gdn_onorm_w=gdn_onorm_w[l], w_gdn_out=w_gdn_out[l],
                  swa_sinks=swa_sinks[l], w_swa_out=w_swa_out[l], w_o=w_o[l],
                  w_ffn_gate=w_ffn_gate[l], w_ffn_up=w_ffn_up[l], ffn_conv_w=ffn_conv_w[l],
                  ffn_conv_b=ffn_conv_b[l], w_ffn_down=w_ffn_down[l])
        st = dict(gdn_S=state_gdn_S[l], gdn_conv=state_gdn_conv[l], k=cache_swa_k[l],
                  v=cache_swa_v[l], ffn_conv=state_ffn_conv[l])
        xp, sp = _layer(xp, c_prompt, lp, None)
        xs, ss = _layer(xs, c_sample, lp, st)
        for i in range(5):
            new_p[i].append(sp[i])
            new_s[i].append(ss[i])
    y_prompt = _rmsnorm(xp, final_norm_w)
    y_sample = _rmsnorm(xs, final_norm_w)
    gS_p, gc_p, k_p, v_p, f_p = [jnp.stack(a) for a in new_p]
    gS_s, gc_s, k_s, v_s, f_s = [jnp.stack(a) for a in new_s]
    return (y_prompt, y_sample, gS_p, gS_s, gc_p, gc_s, k_p, k_s, v_p, v_s, f_p, f_s)
```

```python
import os
import numpy as np
import concourse.bass as bass
import concourse.mybir as mybir
from concourse.bass_utils import run_bass_kernel_spmd
from contextlib import ExitStack

F32 = mybir.dt.float32
BF16 = mybir.dt.bfloat16
AF = mybir.ActivationFunctionType
ALU = mybir.AluOpType
AX = mybir.AxisListType

D = 1024
SEQ = 2048
TB = 512
NBLK = SEQ // TB
NS = 16
DFF = 2816
NFC = DFF // 128
OFF_QKV, OFF_GATE, OFF_BETA, OFF_A, OFF_SQ, OFF_SK, OFF_SV, OFF_GA, OFF_GB = 0, 3072, 4096, 4104, 4112, 5136, 5392, 5648, 6672
INW = 7696
EPS = 1e-6
NEG = -30000.0


class Tile:
    def __init__(self, t, name):
        self.t = t
        self.name = name
        self.lw = None
        self.rd = {}
        self.dkey = None
        self.dcnt = 0

    def __getitem__(self, k):
        return self.t[k]


class K:
    def __init__(self, nc, es):
        self.nc = nc
        self.es = es
        self.eng = {'pe': nc.tensor, 'dve': nc.vector, 'act': nc.scalar, 'pool': nc.gpsimd, 'sp': nc.sync}
        self.sem = {}
        for e in self.eng:
            self.sem[e] = es.enter_context(nc.semaphore('s_' + e))
        self.cnt = {e: 0 for e in self.eng}
        self.seen = {e: {} for e in self.eng}
        self.ndsem = 0
        self.tiles = []
        self.free_banks = []
        self.dbg = []
        self.phase_off = {}

    def sb(self, name, shape, dt=F32, es=None):
        t = (es or self.es).enter_context(self.nc.sbuf_tensor(name, list(shape), dt))
        T = Tile(t, name)
        self.tiles.append(T)
        return T

    def view(self, ap, name):
        T = Tile(ap, name)
        self.tiles.append(T)
        return T

    def init_psum(self):
        self.psum = self.es.enter_context(self.nc.psum_tensor("psum", [128, 4096], F32))
        self.banks = []
        for i in range(8):
            T = Tile(self.psum[:, i * 512:(i + 1) * 512], "bank%d" % i)
            T.excl = True
            self.tiles.append(T)
            self.banks.append(T)
        self.free_banks = list(self.banks)

    def bank(self):
        assert self.free_banks, "out of PSUM banks"
        return self.free_banks.pop(0)

    def rel(self, *bs):
        for b in bs:
            assert b not in self.free_banks
            self.free_banks.append(b)

    def _deps(self, e, reads, writes, skip=None):
        deps = {}

        def add(kv):
            k_, v = kv
            if deps.get(k_, 0) < v:
                deps[k_] = v
        for t in reads:
            if t.lw:
                add(t.lw)
            if getattr(t, 'excl', False):
                for kv in t.rd.items():
                    if kv[0] != e:
                        add(kv)
        for t in writes:
            if t.lw:
                add(t.lw)
            for kv in t.rd.items():
                add(kv)
        for k_, v in deps.items():
            if k_ == e and e == 'pe':
                continue
            if skip is not None and k_ == skip:
                continue
            if self.seen[e].get(k_, 0) >= v:
                continue
            self.eng[e].wait_ge(self.sem[k_], v)
            self.seen[e][k_] = v

    def op(self, e, fn, reads=(), writes=()):
        if e == 'pool' and getattr(self, 'pool_to', None):
            e = self.pool_to
        self._deps(e, reads, writes)
        ins = fn(self.eng[e])
        ins.then_inc(self.sem[e], 1)
        self.cnt[e] += 1
        c = self.cnt[e]
        for t in writes:
            t.lw = (e, c)
            t.rd = {}
        for t in reads:
            if t not in writes:
                t.rd[e] = c

    def dma(self, q, out, in_, reads=(), writes=(), semtile=None, indep=False, **kw):
        T = semtile if semtile is not None else (writes[0] if writes else reads[0])
        self._deps(q, reads, writes, skip=(T.dkey if indep else None))
        if T.dkey is None:
            T.dkey = 'd%d' % self.ndsem
            self.ndsem += 1
            self.sem[T.dkey] = self.es.enter_context(self.nc.semaphore(T.dkey))
        self.eng[q].dma_start(out=out, in_=in_, **kw).then_inc(self.sem[T.dkey], 16)
        T.dcnt += 16
        for t in writes:
            t.lw = (T.dkey, T.dcnt)
            t.rd = {}
        for t in reads:
            t.rd[T.dkey] = T.dcnt

    def init_arena(self, nbytes):
        self.arena = self.es.enter_context(self.nc.sbuf_tensor("arena", [128, nbytes // 4], F32))
        self.phase_tiles = {}

    def carve(self, phase, name, shape, dt=F32):
        off = self.phase_off.get(phase, 0)
        n = 1
        for d_ in shape[1:]:
            n *= d_
        nb = n * (2 if dt == BF16 else 4)
        nb = (nb + 63) // 64 * 64
        assert off + nb <= self.arena.shape[1] * 4, "arena overflow in phase %s at %s: %d" % (phase, name, off + nb)
        ap = self.arena[0:shape[0], off // 4:(off + nb) // 4]
        if dt == BF16:
            ap = ap.bitcast(BF16)
        ap = ap[:, 0:n]
        if len(shape) == 3:
            ap = ap.rearrange("p (a b) -> p a b", b=shape[2])
        elif len(shape) == 4:
            ap = ap.rearrange("p (a b c) -> p a b c", b=shape[2], c=shape[3])
        self.phase_off[phase] = off + nb
        T = Tile(ap, name)
        self.tiles.append(T)
        self.phase_tiles.setdefault(phase, []).append(T)
        return T

    def switch(self, frm, to):
        acc = {}
        for F_ in self.phase_tiles.get(frm, []):
            if F_.lw:
                acc[F_.lw[0]] = max(acc.get(F_.lw[0], 0), F_.lw[1])
            for k_, v in F_.rd.items():
                acc[k_] = max(acc.get(k_, 0), v)
        for T in self.phase_tiles.get(to, []):
            for k_, v in acc.items():
                T.rd[k_] = max(T.rd.get(k_, 0), v)

    def barrier(self):
        for e in self.eng:
            for T in self.tiles:
                if T.dkey is not None and self.seen[e].get(T.dkey, 0) < T.dcnt:
                    self.eng[e].wait_ge(self.sem[T.dkey], T.dcnt)
                    self.seen[e][T.dkey] = T.dcnt
            for k_ in self.eng:
                if k_ != e and self.cnt[k_] > 0 and self.seen[e].get(k_, 0) < self.cnt[k_]:
                    self.eng[e].wait_ge(self.sem[k_], self.cnt[k_])
                    self.seen[e][k_] = self.cnt[k_]

    def finish(self, e='sp'):
        for T in self.tiles:
            if T.dkey is not None and self.seen[e].get(T.dkey, 0) < T.dcnt:
                self.eng[e].wait_ge(self.sem[T.dkey], T.dcnt)
                self.seen[e][T.dkey] = T.dcnt
        for k_ in self.eng:
            if k_ != e and self.cnt[k_] > 0 and self.seen[e].get(k_, 0) < self.cnt[k_]:
                self.eng[e].wait_ge(self.sem[k_], self.cnt[k_])
                self.seen[e][k_] = self.cnt[k_]

    def mm(self, out, lhsT, rhs, r, w, start=True, stop=True):
        import traceback
        ln = traceback.extract_stack(limit=2)[0].lineno
        n = 1
        for d_ in out.shape[1:]:
            n *= d_
        cyc = n * (4 if rhs.dtype == F32 else 1)
        st = self.__dict__.setdefault('mmstat', {})
        st[ln] = st.get(ln, 0) + max(cyc, 64)
        self.op('pe', lambda e: e.matmul(out, lhsT=lhsT, rhs=rhs, start=start, stop=stop), reads=r, writes=w)

    def tr(self, out, in_, ident, r, w):
        import traceback
        ln = traceback.extract_stack(limit=2)[0].lineno
        n = 1
        for d_ in out.shape[1:]:
            n *= d_
        cyc = n * (2 if in_.dtype == F32 else 1)
        st = self.__dict__.setdefault('mmstat', {})
        st[ln] = st.get(ln, 0) + max(cyc, 64)
        self.op('pe', lambda e: e.transpose(out=out, in_=in_, identity=ident), reads=r, writes=w)

    def act(self, out, in_, func, r, w, bias=None, scale=None, accum=None, eng='act'):
        kw = {}
        if bias is not None:
            kw['bias'] = bias
        if scale is not None:
            kw['scale'] = scale
        if accum is not None:
            kw['accum_out'] = accum
        self.op('act', lambda e: e.activation(out=out, in_=in_, func=func, **kw), reads=r, writes=w)

    def tt(self, out, in0, in1, op, r, w, eng='dve'):
        self.op(eng, lambda e: e.tensor_tensor(out=out, in0=in0, in1=in1, op=op), reads=r, writes=w)

    def ts(self, out, in0, s1, s2, op0, op1, r, w, eng='dve', accum=None):
        if op1 is None:
            self.op(eng, lambda e: e.tensor_scalar(out=out, in0=in0, scalar1=s1, scalar2=None, op0=op0), reads=r, writes=w)
        else:
            self.op(eng, lambda e: e.tensor_scalar(out=out, in0=in0, scalar1=s1, scalar2=s2, op0=op0, op1=op1), reads=r, writes=w)

    def stt(self, out, in0, scalar, in1, op0, op1, r, w, accum=None):
        if accum is None:
            self.op('dve', lambda e: e.scalar_tensor_tensor(out=out, in0=in0, scalar=scalar, in1=in1, op0=op0, op1=op1), reads=r, writes=w)
        else:
            self.op('dve', lambda e: e.scalar_tensor_tensor(out=out, in0=in0, scalar=scalar, in1=in1, op0=op0, op1=op1, accum_out=accum), reads=r, writes=w)

    def cp(self, out, in_, r, w, eng='dve'):
        if eng == 'act':
            self.op('act', lambda e: e.activation(out=out, in_=in_, func=AF.Copy), reads=r, writes=w)
        else:
            self.op(eng, lambda e: e.tensor_copy(out=out, in_=in_), reads=r, writes=w)

    def memset(self, out, val, w, eng='pool'):
        self.op(eng, lambda e: e.memset(out, val), writes=w)

    def asel(self, out, in_, pattern, cmp, fill, base, cm, r, w):
        self.op('pool', lambda e: e.affine_select(out=out, in_=in_, pattern=pattern, compare_op=cmp, fill=fill,
                                                  base=base, channel_multiplier=cm), reads=r, writes=w)


def bc(ap, axis, n):
    a = ap.unsqueeze(axis)
    shp = list(a.shape)
    shp[axis] = n
    return a.broadcast_to(shp)


class _Stop(Exception):
    pass


def build_nc(debug=False, nblk=NBLK, do_sample=True, stop=None):
    nc = bass.Bass("TRN2", target_bir_lowering=False)

    def din(name, shape):
        return nc.dram_tensor(name, list(shape), F32, kind="ExternalInput").ap()

    def dout(name, shape):
        return nc.dram_tensor(name, list(shape), F32, kind="ExternalOutput").ap()

    x_p = din("x_p", [SEQ, D])
    x_s = din("x_s", [NS, D])
    c17 = din("c17", [NS + 1, D])
    st_S = din("st_S", [NS, 128, 8, 128])
    st_conv = din("st_conv", [NS * 3, 3072])
    st_k = din("st_k", [128, NS, 256])
    st_v = din("st_v", [128, NS, 256])
    st_ffn = din("st_ffn", [NS * 2, DFF])
    w_mod = din("w_mod", [D, 6 * D])
    b_mod = din("b_mod", [1, 6 * D])
    norm1_w = din("norm1_w", [1, D])
    norm2_w = din("norm2_w", [1, D])
    w_in = din("w_in", [D, INW])
    gdn_conv_w = din("gdn_conv_w", [4, 3072])
    gdn_a_log = din("gdn_a_log", [8])
    gdn_dt_bias = din("gdn_dt_bias", [8])
    gdn_onorm_w = din("gdn_onorm_w", [1, 128])
    w_gdn_out = din("w_gdn_out", [D, D])
    swa_sinks = din("swa_sinks", [16])
    w_swa_out = din("w_swa_out", [D, D])
    w_o = din("w_o", [D, D])
    w_ffn_gate = din("w_ffn_gate", [D, DFF])
    w_ffn_up = din("w_ffn_up", [D, DFF])
    ffn_conv_w = din("ffn_conv_w", [3, DFF])
    ffn_conv_b = din("ffn_conv_b", [1, DFF])
    w_ffn_down = din("w_ffn_down", [DFF, D])
    final_norm_w = din("final_norm_w", [D])

    y_p = dout("y_p", [SEQ, D])
    y_s = dout("y_s", [NS, D])
    o_S_p = dout("o_S_p", [128, 8, 128])
    o_S_s = dout("o_S_s", [NS, 128, 8, 128])
    o_conv_p = dout("o_conv_p", [3, 3072])
    o_conv_s = dout("o_conv_s", [NS, 3, 3072])
    o_k_p = dout("o_k_p", [128, 256])
    o_k_s = dout("o_k_s", [128, NS, 256])
    o_v_p = dout("o_v_p", [128, 256])
    o_v_s = dout("o_v_s", [128, NS, 256])
    o_ffn_p = dout("o_ffn_p", [2, DFF])
    o_ffn_s = dout("o_ffn_s", [NS, 2, DFF])

    with ExitStack() as es:
        k = K(nc, es)
        k.init_psum()
        PS = k.psum

        def dump(name, ap, tiles):
            if not debug:
                return
            o = nc.dram_tensor("dbg_" + name, list(ap.shape), ap.dtype, kind="ExternalOutput").ap()
            dt_ = Tile(None, 'dbg_' + name)
            k.tiles.append(dt_)
            k.dma('sp', o, ap, reads=tiles, semtile=dt_)

        dbgsem = k.sb("dbgsem", [1, 1])

        def ck(name):
            if stop == name:
                raise _Stop()

        try:
            identf = k.sb("identf", [128, 128])
            identb = k.sb("identb", [128, 128], BF16)
            onesf = k.sb("onesf", [128, 128])
            onesb = k.sb("onesb", [128, 128], BF16)
            Um = k.sb("Um", [64, 64])
            SLm = k.sb("SLm", [64, 64])
            SUm = k.sb("SUm", [64, 64])
            nSL = k.sb("nSL", [64, 64])
            nSU = k.sb("nSU", [64, 64])
            maskA = k.sb("maskA", [128, 256])
            maskB = k.sb("maskB", [128, 256])
            E16 = k.sb("E16", [NS + 1, 128])
            Esel = k.sb("Esel", [NS, NS, 128])
            epsc = k.sb("epsc", [128, 1])
            onec = k.sb("onec", [128, 1])

            k.memset(identf[:], 0.0, [identf])
            k.asel(identf[:], identf[:], [[-1, 128]], ALU.not_equal, 1.0, 0, 1, [identf], [identf])
            k.cp(identb[:], identf[:], [identf], [identb])
            k.memset(onesf[:], 1.0, [onesf])
            k.memset(onesb[:], 1.0, [onesb])
            k.memset(epsc[:], EPS, [epsc])
            k.memset(onec[:], 1.0, [onec])
            k.memset(Um[:], 1.0, [Um])
            k.asel(Um[:], Um[:], [[1, 64]], ALU.is_ge, 0.0, 0, -1, [Um], [Um])
            k.memset(SLm[:], 1.0, [SLm])
            k.asel(SLm[:], SLm[:], [[-1, 64]], ALU.is_ge, 0.0, -1, 1, [SLm], [SLm])
            k.memset(SUm[:], 1.0, [SUm])
            k.asel(SUm[:], SUm[:], [[1, 64]], ALU.is_ge, 0.0, -1, -1, [SUm], [SUm])
            k.ts(nSL[:], SLm[:], -1.0, None, ALU.mult, None, [SLm], [nSL])
            k.ts(nSU[:], SUm[:], -1.0, None, ALU.mult, None, [SUm], [nSU])
            k.memset(maskA[:], 0.0, [maskA])
            k.asel(maskA[:], maskA[:], [[1, 256]], ALU.is_ge, NEG, -1, -1, [maskA], [maskA])
            k.asel(maskA[:], maskA[:], [[-1, 256]], ALU.is_ge, NEG, 128, 1, [maskA], [maskA])
            k.asel(maskB[:], maskA[:], [[1, 256]], ALU.is_ge, NEG, -128, 0, [maskA], [maskB])
            k.memset(E16[:], 0.0, [E16])
            k.asel(E16[:], E16[:], [[0, 128]], ALU.not_equal, 1.0, -NS, 1, [E16], [E16])
            k.memset(Esel[:], 0.0, [Esel])
            k.asel(Esel[:], Esel[:], [[-1, NS], [0, 128]], ALU.not_equal, 1.0, 0, 1, [Esel], [Esel])

            k.pool_to = 'dve'
            modT = k.sb("modT", [128, 48, NS + 1])
            n1w = k.sb("n1w", [128, 8])
            n2w = k.sb("n2w", [128, 8])
            a1 = k.sb("a1", [128, 8])
            a2 = k.sb("a2", [128, 8])
            A1s = k.sb("A1s", [128, 8, NS])
            A2s = k.sb("A2s", [128, 8, NS])
            cwT = k.sb("cwT", [128, 24, 4])
            fcwT = k.sb("fcwT", [128, NFC, 3])
            fcbT = k.sb("fcbT", [128, NFC])
            onwT = k.sb("onwT", [128, 1])
            fnw_bc = k.sb("fnw_bc", [128, D])
            g1bc = k.sb("g1bc", [128, D])
            g2bc = k.sb("g2bc", [128, D])
            gtok1 = k.sb("gtok1", [NS + 1, D])
            gtok2 = k.sb("gtok2", [NS + 1, D])
            negA = k.sb("negA", [64, 8])
            dtb = k.sb("dtb", [64, 8])
            sinks = k.sb("sinks", [128, 16])

            W8 = [k.sb("W8_%d" % i, [128, 8, 512], BF16) for i in range(3)]
            ring = {'w8': 0, 'wd': 0}

            wlist = {'cur': W8}

            def nextw():
                wl = wlist['cur']
                t_ = wl[ring['w8'] % len(wl)]
                ring['w8'] += 1
                return t_

            def wload(src, ncols, c0=0, tile=None):
                piece = tile is not None
                if tile is None:
                    tile = nextw()
                k.dma('pool', tile[:, :, c0:c0 + ncols], src.rearrange("(c p) n -> p c n", p=128), writes=[tile], indep=piece)
                return tile

            with ExitStack() as es2:
                stage = k.sb("stage", [8, 6 * D], F32, es=es2)
                c17t = k.sb("c17t", [NS + 1, D], F32, es=es2)
                scT = k.sb("scT", [128, 8, NS + 1], BF16, es=es2)
                bmT = k.sb("bmT", [128, 48], F32, es=es2)

                def featmajor(src, r, C, dst_ap, dst_tile):
                    k.dma('sp', stage[0:r, 0:C], src, writes=[stage])
                    nchunk = C // 128
                    c = 0
                    while c < nchunk:
                        n = min(nchunk - c, 512 // r)
                        b = k.bank()
                        for j in range(n):
                            k.tr(b[:, j * r:(j + 1) * r], stage[0:r, (c + j) * 128:(c + j + 1) * 128], identf[0:r, 0:r],
                                 [stage, identf], [b])
                        if r == 1:
                            k.cp(dst_ap[:, c:c + n], b[:, 0:n], [b], [dst_tile])
                        else:
                            k.cp(dst_ap[:, c:c + n, :], b[:, 0:n * r].rearrange("p (c r) -> p c r", r=r), [b], [dst_tile])
                        k.rel(b)
                        c += n

                ck('consts')
                featmajor(b_mod, 1, 6 * D, bmT, bmT)
                featmajor(norm1_w, 1, D, n1w, n1w)
                featmajor(norm2_w, 1, D, n2w, n2w)
                featmajor(gdn_conv_w, 4, 3072, cwT, cwT)
                featmajor(ffn_conv_w, 3, DFF, fcwT, fcwT)
                featmajor(ffn_conv_b, 1, DFF, fcbT, fcbT)
                featmajor(gdn_onorm_w, 1, 128, onwT, onwT)
                ck('fm')
                k.dma('sp', fnw_bc[:], final_norm_w.partition_broadcast(128), writes=[fnw_bc])
                k.dma('sp', negA[:], gdn_a_log.partition_broadcast(64), writes=[negA])
                k.dma('sp', dtb[:], gdn_dt_bias.partition_broadcast(64), writes=[dtb])
                k.dma('sp', sinks[:], swa_sinks.partition_broadcast(128), writes=[sinks])
                k.act(negA[:], negA[:], AF.Exp, [negA], [negA])
                k.ts(negA[:], negA[:], -1.0, None, ALU.mult, None, [negA], [negA])

                ck('bcast')
                k.dma('sp', c17t[:], c17, writes=[c17t])
                k.act(c17t[:], c17t[:], AF.Silu, [c17t], [c17t])
                b = k.bank()
                for kk in range(8):
                    k.tr(b[:, kk * 17:(kk + 1) * 17], c17t[:, kk * 128:(kk + 1) * 128], identf[0:17, 0:17], [c17t, identf], [b])
                k.cp(scT[:], b[:, 0:8 * 17].rearrange("p (c r) -> p c r", r=17), [b], [scT])
                k.rel(b)
                ck('silu')
                for half in range(2):
                    b = k.bank()
                    for g in range(6):
                        wt = wload(w_mod[:, (half * 6 + g) * 512:(half * 6 + g + 1) * 512], 512)
                        for j in range(4):
                            jj = g * 4 + j
                            for kk in range(8):
                                k.mm(b[:, jj * 17:(jj + 1) * 17], wt[:, kk, j * 128:(j + 1) * 128], scT[:, kk, :], [wt, scT], [b],
                                     start=(kk == 0), stop=(kk == 7))
                    k.tt(modT[:, half * 24:(half + 1) * 24, :], b[:, 0:24 * 17].rearrange("p (c r) -> p c r", r=17),
                         bc(bmT[:, half * 24:(half + 1) * 24], 2, 17), ALU.add, [b, bmT], [modT])
                    k.rel(b)
                ck('modT')
                k.barrier()
            dump("modT", modT[:], [modT])

            k.ts(a1[:], modT[:, 8:16, NS], 1.0, None, ALU.add, None, [modT], [a1])
            k.tt(a1[:], a1[:], n1w[:], ALU.mult, [a1, n1w], [a1])
            k.ts(a2[:], modT[:, 32:40, NS], 1.0, None, ALU.add, None, [modT], [a2])
            k.tt(a2[:], a2[:], n2w[:], ALU.mult, [a2, n2w], [a2])
            k.ts(A1s[:], modT[:, 8:16, 0:NS], 1.0, None, ALU.add, None, [modT], [A1s])
            k.tt(A1s[:], A1s[:], bc(n1w[:], 2, NS), ALU.mult, [A1s, n1w], [A1s])
            k.ts(A2s[:], modT[:, 32:40, 0:NS], 1.0, None, ALU.add, None, [modT], [A2s])
            k.tt(A2s[:], A2s[:], bc(n2w[:], 2, NS), ALU.mult, [A2s, n2w], [A2s])
            for (c0, gtok, gbc) in ((16, gtok1, g1bc), (40, gtok2, g2bc)):
                b0, b1 = k.bank(), k.bank()
                for j in range(8):
                    bb = b0 if j < 4 else b1
                    k.tr(bb[0:17, (j % 4) * 128:(j % 4 + 1) * 128], modT[:, c0 + j, :], identf[:], [modT, identf], [bb])
                k.cp(gtok[:, 0:512], b0[0:17, :], [b0], [gtok])
                k.cp(gtok[:, 512:1024], b1[0:17, :], [b1], [gtok])
                for hf, bb in ((0, b0), (1, b1)):
                    k.mm(bb[:, :], E16[:], gtok[:, hf * 512:(hf + 1) * 512], [E16, gtok], [bb])
                    k.cp(gbc[:, hf * 512:(hf + 1) * 512], bb[:, :], [bb], [gbc])
                k.rel(b0, b1)
            dump("g1bc", g1bc[:], [g1bc])

            ck('derived')
            S_all = k.sb("S_all", [128, 8, 128])
            halo = k.sb("halo", [128, 24, 3])
            fhalo = k.sb("fhalo", [128, NFC, 2])
            KTl = k.sb("KTl", [128, 4, 128 + TB], BF16)
            KTh = k.sb("KTh", [128, 4, 128 + TB], BF16)
            Vtok = k.sb("Vtok", [128, 5, 512], BF16)
            k.memset(S_all[:], 0.0, [S_all])
            k.memset(halo[:], 0.0, [halo])
            k.memset(fhalo[:], 0.0, [fhalo])
            k.memset(KTl[:], 0.0, [KTl])
            k.memset(KTh[:], 0.0, [KTh])
            k.memset(Vtok[:], 0.0, [Vtok])

            xres = [k.sb("xres%d" % i, [128, D]) for i in range(4)]
            x1 = xres
            xn = k.sb("xn", [128, D], BF16)
            ss1 = k.sb("ss1", [128, 1])
            rs1 = k.sb("rs1", [128, 1])
            ss2, rs2 = ss1, rs1
            hT = k.sb("hT", [128, 8, TB], BF16)
            h2T = hT
            onT = k.sb("onT", [128, 8, TB], BF16)
            obT = k.sb("obT", [128, 8, TB], BF16)
            mixT = k.sb("mixT", [128, 8, TB], BF16)
            QT = mixT
            halo_out = k.sb("halo_out", [128, 24, 3])

            k.init_arena(74240)
            cG = lambda n, shp, dt=F32: k.carve('G', n, shp, dt)
            cA = lambda n, shp, dt=F32: k.carve('A', n, shp, dt)
            cF = lambda n, shp, dt=F32: k.carve('F', n, shp, dt)
            ba = cG("ba", [64, 8, 16])
            beta = cG("beta", [64, 8, 8])
            gg = cG("gg", [64, 8, 8])
            t64a = cG("t64a", [64, 8, 8])
            t64b = cG("t64b", [64, 8, 8])
            dd = cG("dd", [64, 64])
            ed = cG("ed", [64, 64])
            ekd = cG("ekd", [64, 64])
            bed = cG("bed", [64, 64])
            elast = cG("elast", [128, 64])
            pre = cG("pre", [128, 3 + TB])
            cv = cG("cv", [128, 3, TB])
            rqk = cG("rqk", [128, 2, TB])
            Qd2 = [cG("Qd%d" % i, [128, TB], BF16) for i in range(2)]
            Qtb = cG("Qtb", [128, TB], BF16)
            Ktb = cG("Ktb", [128, TB], BF16)
            Vtb = cG("Vtb", [128, TB], BF16)
            Sb = cG("Sb", [128, 128], BF16)
            Gp2 = [cG("Gp%d" % i, [128, TB]) for i in range(2)]
            SA = cG("SA", [64, 8, 64])
            SB = cG("SB", [64, 8, 64])
            Wm = cG("Wm", [64, 8, 64])
            Zm = cG("Zm", [64, 8, 64])
            Am2 = [cG("Am%d" % i, [64, 8, 64]) for i in range(2)]
            Amb2 = [cG("Amb%d" % i, [64, 8, 64], BF16) for i in range(2)]
            NEU = F32 if os.environ.get('NEU32', '1') == '1' else BF16
            WXH = [cG("WXH%d" % i, [64, 4, 2, 64], BF16) for i in range(2)]
            ZYH = [cG("ZYH%d" % i, [64, 4, 2, 64], BF16) for i in range(2)]
            Y32 = [cG("Y32_%d" % i, [64, 4, 64]) for i in range(2)]
            Kbd = cG("Kbd", [64, 8, 128], NEU)
            Kdec2 = [cG("Kdec%d" % i, [64, 8, 128], F32 if os.environ.get('SUPD32', '0') == '1' else BF16) for i in range(2)]
            Vb = cG("Vb", [64, 8, 128], NEU)
            osb = cG("osb", [64, 8, 128])
            uu2 = [cG("uu%d" % i, [64, 8, 128]) for i in range(2)]
            wT2 = [cG("wT%d" % i, [128, 8, 64], BF16) for i in range(2)]
            vnew = cG("vnew", [64, 128], F32 if os.environ.get('SUPD32', '0') == '1' else BF16)
            oss = cG("oss", [64, 8])
            ors = cG("ors", [64, 8])
            on1 = cG("on1", [64, 8, 128], BF16)
            Qt_ap, Kt_ap = cv[:, 0, :], cv[:, 1, :]
            Ktok = cA("Ktok", [128, 512], BF16)
            kvout = cA("kvout", [128, 512])
            sc2 = [cA("sc%d" % i, [128, 4, 256]) for i in range(2)]
            pb2 = [cA("pb%d" % i, [128, 4, 256], BF16) for i in range(2)]
            PT2 = [cA("PT%d" % i, [128, 4, 2, 128], BF16) for i in range(2)]
            mx2 = [cA("mx%d" % i, [128, 4]) for i in range(2)]
            nmx2 = [cA("nmx%d" % i, [128, 4]) for i in range(2)]
            rsum2 = [cA("rsum%d" % i, [128, 4]) for i in range(2)]
            esk2 = [cA("esk%d" % i, [128, 4]) for i in range(2)]
            actT = cF("actT", [128, NFC, TB], BF16)
            sga = cF("sga", [128, TB])
            sgb = cF("sgb", [128, TB])
            gpre = cF("gpre", [128, 2 + TB])
            gcv = cF("gcv", [128, TB])
            yt = cF("yt", [128, D])
            k.carve('FW', 'fwpad', [128, k.phase_off['F'] // 4])
            FW = [k.carve('FW', 'FW%d' % i, [128, 8, 512], BF16) for i in range(4)]
            k.phase_tiles['FW'] = k.phase_tiles['FW'][1:]
            print("arena use", k.phase_off)

            def rms_to_T(xt, xt_tile, dstT, dst_tile, acol, bcol, t):
                k.act(xn[:], xt, AF.Square, [xt_tile], [xn, ss1], accum=ss1[:])
                k.act(rs1[:], ss1[:], AF.Ln, [ss1, epsc], [rs1], bias=epsc[:], scale=1.0 / D)
                k.act(rs1[:], rs1[:], AF.Exp, [rs1], [rs1], scale=-0.5)
                k.ts(xn[:], xt, rs1[:], None, ALU.mult, None, [xt_tile, rs1], [xn])
                b = k.bank()
                bv = b[:, :].bitcast(BF16)
                for kk in range(8):
                    k.tr(bv[:, kk * 128:(kk + 1) * 128], xn[:, kk * 128:(kk + 1) * 128], identb[:], [xn, identb], [b])
                for kk in range(8):
                    k.act(dstT[:, kk, t * 128:(t + 1) * 128], bv[:, kk * 128:(kk + 1) * 128], AF.Identity,
                          [b, acol[1], bcol[1]], [dst_tile], bias=bcol[0][:, kk:kk + 1], scale=acol[0][:, kk:kk + 1])
                k.rel(b)

            hoist = {'p1': False}

            def emit_p1(t0_):
                for t in range(4):
                    k.dma('sp', xres[t][:], x_p[t0_ + t * 128:t0_ + (t + 1) * 128, :], writes=[xres[t]])
                for t in range(4):
                    rms_to_T(xres[t][:], xres[t], hT, hT, (a1, a1), (modT[:, 0:8, NS], modT), t)

            def sample_phase():
                c1 = lambda n, shp, dt=F32: k.carve('S1', n, shp, dt)
                c2 = lambda n, shp, dt=F32: k.carve('S2', n, shp, dt)
                c3 = lambda n, shp, dt=F32: k.carve('S3', n, shp, dt)
                xs = c1("xs", [NS, D])
                xs_2 = c2("xs_2", [NS, D])
                xs_3 = c3("xs_3", [NS, D])
                hTs = k.sb("hTs", [128, 8, NS], BF16)
                onTs = k.sb("onTs", [128, 8, NS], BF16)
                obTs = k.sb("obTs", [128, 8, NS], BF16)
                OH = k.sb("OH", [128, NS, NS])
                sinkcol = k.sb("sinkcol", [128, 1])
                onw_bc = k.sb("onw_bc", [NS, 128])
                hsc = k.sb("hsc", [128, 8, NS])
                k.pool_to = None
                k.memset(OH[:], 0.0, [OH])
                k.asel(OH[:], OH[:], [[1, NS], [-1, NS]], ALU.not_equal, 1.0, 0, 0, [OH], [OH])
                k.pool_to = 'dve'
                for a_ in range(8):
                    k.dma('sp', sinkcol[a_ * 16:(a_ + 1) * 16, :], swa_sinks.rearrange("(h o) -> h o", o=1), writes=[sinkcol])
                k.dma('sp', onw_bc[:], gdn_onorm_w[0].partition_broadcast(NS), writes=[onw_bc])

                def rms_T_s(src, src_tile, dst, A_, B_ap, B_tile):
                    k.act(xn[0:NS, :], src, AF.Square, [src_tile], [xn, ss1], accum=ss1[0:NS, :])
                    k.act(rs1[0:NS, :], ss1[0:NS, :], AF.Ln, [ss1, epsc], [rs1], bias=epsc[0:NS, :], scale=1.0 / D)
                    k.act(rs1[0:NS, :], rs1[0:NS, :], AF.Exp, [rs1], [rs1], scale=-0.5)
                    k.ts(xn[0:NS, :], src, rs1[0:NS, :], None, ALU.mult, None, [src_tile, rs1], [xn])
                    b = k.bank()
                    bv = b[:, :].bitcast(BF16)
                    for kk in range(8):
                        k.tr(bv[:, kk * NS:(kk + 1) * NS], xn[0:NS, kk * 128:(kk + 1) * 128], identb[0:NS, 0:NS], [xn, identb], [b])
                    pv = bv[:, 0:8 * NS].rearrange("p (c s) -> p c s", s=NS)
                    k.tt(hsc[:], pv, A_[:], ALU.mult, [b, A_], [hsc])
                    k.rel(b)
                    k.tt(dst[:], hsc[:], B_ap, ALU.add, [hsc, B_tile], [dst])

                def tok_mm(srcT, wt, c0, n, dst_ap, dst_tile, scale=None, func=None):
                    b = k.bank()
                    for kk in range(8):
                        k.mm(b[0:NS, 0:n], srcT[:, kk, :], wt[:, kk, c0:c0 + n], [srcT, wt], [b], start=(kk == 0), stop=(kk == 7))
                    if func is not None:
                        k.act(dst_ap, b[0:NS, 0:n], func, [b], [dst_tile])
                    elif scale is not None:
                        k.act(dst_ap, b[0:NS, 0:n], AF.Copy, [b], [dst_tile], scale=scale)
                    else:
                        k.cp(dst_ap, b[0:NS, 0:n], [b], [dst_tile])
                    k.rel(b)

                def to_T(src_ap, src_tile, nch, dst, dst_tile, rows=NS):
                    c = 0
                    per = 512 // rows
                    while c < nch:
                        n = min(per, nch - c)
                        b = k.bank()
                        for j in range(n):
                            k.tr(b[:, j * rows:(j + 1) * rows], src_ap[:, (c + j) * 128:(c + j + 1) * 128], identf[0:rows, 0:rows], [src_tile, identf], [b])
                        k.cp(dst[:, c:c + n, :], b[:, 0:n * rows].rearrange("p (c s) -> p c s", s=rows), [b], [dst_tile])
                        k.rel(b)
                        c += n

                def to_tok(srcT, src_tile, nch, dst_ap, dst_tile):
                    c = 0
                    while c < nch:
                        n = min(4, nch - c)
                        b = k.bank()
                        for j in range(n):
                            k.tr(b[0:NS, j * 128:(j + 1) * 128], srcT[:, c + j, :], identf[:], [src_tile, identf], [b])
                        k.cp(dst_ap[:, c * 128:(c + n) * 128], b[0:NS, 0:n * 128], [b], [dst_tile])
                        k.rel(b)
                        c += n

                qkv_p = [xres[0], xres[1], xres[2]]
                gate_s = xres[3]
                ba_s = c1("ba_s", [NS, 16])
                stc = c1("stc", [NS * 3, 3072])
                stT = c1("stT", [128, 24, NS * 3])
                newT = c1("newT", [128, 24, NS])
                cvT = c1("cvT", [128, 24, NS])
                tmpT = c1("tmpT", [128, 24, NS])
                rq_s = c1("rq_s", [128, 16, NS])
                qkp = c1("qkp", [128, 8, NS])
                beta_s = c1("beta_s", [NS, 8])
                alpha_s = c1("alpha_s", [NS, 8])
                t16a = c1("t16a", [NS, 8])
                t16b = c1("t16b", [NS, 8])
                qk_s = c1("qk_s", [NS, 8])
                v_tok = c1("v_tok", [NS, 8, 128])
                d_tok = c1("d_tok", [NS, 8, 128])
                o_tok = c1("o_tok", [NS, 8, 128])
                t_tok = c1("t_tok", [NS, 8, 128])
                oss_s = c1("oss_s", [NS, 8])
                Ss = [c1("Ss%d" % i, [128, 8, 128]) for i in range(2)]
                pK = c1("pK", [128, 8, 128])
                pQ = c1("pQ", [128, 8, 128])
                abc = c1("abc", [128, 8])

                k.dma('sp', xs[:], x_s, writes=[xs])
                rms_T_s(xs[:], xs, hTs, A1s, modT[:, 0:8, 0:NS], modT)
                for g_ in range(6):
                    wt = wload(w_in[:, OFF_QKV + g_ * 512:OFF_QKV + (g_ + 1) * 512], 512)
                    tok_mm(hTs, wt, 0, 512, qkv_p[g_ // 2][0:NS, (g_ % 2) * 512:(g_ % 2 + 1) * 512], qkv_p[g_ // 2])
                for g_ in range(2):
                    wt = wload(w_in[:, OFF_GATE + g_ * 512:OFF_GATE + (g_ + 1) * 512], 512)
                    tok_mm(hTs, wt, 0, 512, gate_s[0:NS, g_ * 512:(g_ + 1) * 512], gate_s, func=AF.Silu)
                wt = wload(w_in[:, OFF_BETA:OFF_BETA + 16], 16)
                tok_mm(hTs, wt, 0, 16, ba_s[:], ba_s)
                k.dma('sp', stc[:], st_conv, writes=[stc])
                st3 = stc[:].rearrange("(s j) c -> s j c", j=3) if False else None
                for s_ in range(NS):
                    k.dma('sp', o_conv_s[s_, 0:2, :], stc[s_ * 3 + 1:s_ * 3 + 3, :], reads=[stc])
                for p_ in range(3):
                    k.dma('sp', o_conv_s[:, 2, p_ * 1024:(p_ + 1) * 1024], qkv_p[p_][0:NS, :], reads=[qkv_p[p_]])
                to_T(stc[:], stc, 24, stT, stT, rows=NS * 3)
                for p_ in range(3):
                    to_T(qkv_p[p_][0:NS, :], qkv_p[p_], 8, newT[:, p_ * 8:(p_ + 1) * 8, :], newT)
                st4 = stT[:].rearrange("p c (s j) -> p c s j", j=3)
                k.tt(cvT[:], newT[:], bc(cwT[:, :, 3], 2, NS), ALU.mult, [newT, cwT], [cvT])
                for j_ in range(3):
                    k.tt(tmpT[:], st4[:, :, :, j_], bc(cwT[:, :, j_], 2, NS), ALU.mult, [stT, cwT], [tmpT])
                    k.tt(cvT[:], cvT[:], tmpT[:], ALU.add, [cvT, tmpT], [cvT])
                k.act(cvT[:], cvT[:], AF.Silu, [cvT], [cvT])
                k.tt(tmpT[:, 0:16, :], cvT[:, 0:16, :], cvT[:, 0:16, :], ALU.mult, [cvT], [tmpT])
                b = k.bank()
                k.mm(b[:, 0:256], onesf[:], tmpT[:, 0:16, :].rearrange("p c s -> p (c s)"), [onesf, tmpT], [b])
                k.act(rq_s[:].rearrange("p c s -> p (c s)"), b[:, 0:256], AF.Ln, [b, epsc], [rq_s], bias=epsc[:])
                k.rel(b)
                k.act(rq_s[:], rq_s[:], AF.Exp, [rq_s], [rq_s], scale=-0.5)
                k.stt(cvT[:, 0:8, :], cvT[:, 0:8, :], 128.0 ** -0.5, rq_s[:, 0:8, :], ALU.mult, ALU.mult, [cvT, rq_s], [cvT])
                k.tt(cvT[:, 8:16, :], cvT[:, 8:16, :], rq_s[:, 8:16, :], ALU.mult, [cvT, rq_s], [cvT])
                qsT, ksT, vsT = cvT[:, 0:8, :], cvT[:, 8:16, :], cvT[:, 16:24, :]
                k.act(beta_s[:], ba_s[:, 0:8], AF.Exp, [ba_s], [beta_s], scale=-1.0)
                k.ts(beta_s[:], beta_s[:], 1.0, None, ALU.add, None, [beta_s], [beta_s])
                k.op('dve', lambda e: e.reciprocal(out=beta_s[:], in_=beta_s[:]), reads=[beta_s], writes=[beta_s])
                k.tt(t16a[:], ba_s[:, 8:16], dtb[0:NS, :], ALU.add, [ba_s, dtb], [t16a])
                k.act(t16b[:], t16a[:], AF.Abs, [t16a], [t16b])
                k.act(t16b[:], t16b[:], AF.Exp, [t16b], [t16b], scale=-1.0)
                k.act(t16b[:], t16b[:], AF.Ln, [t16b, onec], [t16b], bias=onec[0:NS, :])
                k.stt(t16a[:], t16a[:], 0.0, t16b[:], ALU.max, ALU.add, [t16a, t16b], [t16a])
                k.tt(t16a[:], t16a[:], negA[0:NS, :], ALU.mult, [t16a, negA], [t16a])
                k.act(alpha_s[:], t16a[:], AF.Exp, [t16a], [alpha_s])
                to_tok(cvT[:, 16:24, :], cvT, 8, v_tok[:].rearrange("s h d -> s (h d)"), v_tok)
                k.tt(qkp[:], qsT, ksT, ALU.mult, [cvT], [qkp])
                b = k.bank()
                for h in range(8):
                    k.mm(b[0:NS, h:h + 1], qkp[:, h, :], onesf[:, 0:1], [qkp, onesf], [b])
                k.cp(qk_s[:], b[0:NS, 0:8], [b], [qk_s])
                k.rel(b)
                bks = [k.bank() for _ in range(4)]
                for s_ in range(NS):
                    S_ = Ss[s_ % 2]
                    k.dma('sp', S_[:], st_S[s_], writes=[S_])
                    k.tt(pK[:], S_[:], bc(cvT[:, 8:16, s_], 2, 128), ALU.mult, [S_, cvT], [pK], eng='pool')
                    k.tt(pQ[:], S_[:], bc(cvT[:, 0:8, s_], 2, 128), ALU.mult, [S_, cvT], [pQ])
                    for hf in range(2):
                        k.mm(bks[hf][0:NS, :], OH[:, s_, :], pK[:, hf * 4:(hf + 1) * 4, :].rearrange("p h d -> p (h d)"), [OH, pK], [bks[hf]],
                             start=(s_ == 0), stop=(s_ == NS - 1))
                        k.mm(bks[2 + hf][0:NS, :], OH[:, s_, :], pQ[:, hf * 4:(hf + 1) * 4, :].rearrange("p h d -> p (h d)"), [OH, pQ], [bks[2 + hf]],
                             start=(s_ == 0), stop=(s_ == NS - 1))
                for hf in range(2):
                    hs = slice(hf * 4, hf * 4 + 4)
                    kS = bks[hf][0:NS, :].rearrange("s (h d) -> s h d", d=128)
                    qS = bks[2 + hf][0:NS, :].rearrange("s (h d) -> s h d", d=128)
                    k.tt(t_tok[:, hs, :], kS, bc(alpha_s[:, hs], 2, 128), ALU.mult, [bks[hf], alpha_s], [t_tok])
                    k.tt(t_tok[:, hs, :], v_tok[:, hs, :], t_tok[:, hs, :], ALU.subtract, [v_tok, t_tok], [t_tok])
                    k.tt(d_tok[:, hs, :], t_tok[:, hs, :], bc(beta_s[:, hs], 2, 128), ALU.mult, [t_tok, beta_s], [d_tok])
                    k.tt(o_tok[:, hs, :], qS, bc(alpha_s[:, hs], 2, 128), ALU.mult, [bks[2 + hf], alpha_s], [o_tok])
                    k.tt(t_tok[:, hs, :], d_tok[:, hs, :], bc(qk_s[:, hs], 2, 128), ALU.mult, [d_tok, qk_s], [t_tok])
                    k.tt(o_tok[:, hs, :], o_tok[:, hs, :], t_tok[:, hs, :], ALU.add, [o_tok, t_tok], [o_tok])
                k.rel(*bks)
                k.tt(t_tok[:], o_tok[:], o_tok[:], ALU.mult, [o_tok], [t_tok])
                k.op('dve', lambda e: e.tensor_reduce(out=oss_s[:], in_=t_tok[:], axis=AX.X, op=ALU.add), reads=[t_tok], writes=[oss_s])
                k.act(oss_s[:], oss_s[:], AF.Ln, [oss_s, epsc], [oss_s], bias=epsc[0:NS, :], scale=1.0 / 128)
                k.act(oss_s[:], oss_s[:], AF.Exp, [oss_s], [oss_s], scale=-0.5)
                k.tt(o_tok[:], o_tok[:], bc(oss_s[:], 2, 128), ALU.mult, [o_tok, oss_s], [o_tok])
                k.tt(o_tok[:], o_tok[:], bc(onw_bc[:], 1, 8), ALU.mult, [o_tok, onw_bc], [o_tok])
                k.tt(o_tok[:], o_tok[:], gate_s[0:NS, :].rearrange("s (h d) -> s h d", d=128), ALU.mult, [o_tok, gate_s], [o_tok])
                to_T(o_tok[:].rearrange("s h d -> s (h d)"), o_tok, 8, newT[:, 0:8, :], newT)
                k.cp(onTs[:], newT[:, 0:8, :], [newT], [onTs])
                k.dma('sp', Ss[0][:], st_S[0], writes=[Ss[0]])
                for s_ in range(NS):
                    S_ = Ss[s_ % 2]
                    if s_ + 1 < NS:
                        k.dma('sp', Ss[(s_ + 1) % 2][:], st_S[s_ + 1], writes=[Ss[(s_ + 1) % 2]])
                    b0, b1, b2 = k.bank(), k.bank(), k.bank()
                    k.mm(b2[:, 0:8], Esel[:, s_, :], alpha_s[:], [Esel, alpha_s], [b2])
                    k.cp(abc[:], b2[:, 0:8], [b2], [abc])
                    k.mm(b0[:, :], Esel[:, s_, :], d_tok[:, 0:4, :].rearrange("s h d -> s (h d)"), [Esel, d_tok], [b0])
                    k.mm(b1[:, :], Esel[:, s_, :], d_tok[:, 4:8, :].rearrange("s h d -> s (h d)"), [Esel, d_tok], [b1])
                    k.tt(pK[:], S_[:], bc(abc[:], 2, 128), ALU.mult, [S_, abc], [pK], eng='pool')
                    k.tt(pQ[:, 0:4, :], b0[:, :].rearrange("p (h d) -> p h d", d=128), bc(cvT[:, 8:12, s_], 2, 128), ALU.mult, [b0, cvT], [pQ])
                    k.tt(pQ[:, 4:8, :], b1[:, :].rearrange("p (h d) -> p h d", d=128), bc(cvT[:, 12:16, s_], 2, 128), ALU.mult, [b1, cvT], [pQ])
                    k.rel(b0, b1, b2)
                    k.tt(S_[:], pK[:], pQ[:], ALU.add, [pK, pQ], [S_], eng='pool')
                    k.dma('sp', o_S_s[s_], S_[:], reads=[S_])

                q_s = c2("q_s", [NS, 1024])
                kv_s = c2("kv_s", [NS, 512])
                KCs = [c2("KC%d" % i, [128, NS // 2, 256]) for i in range(2)]
                VCs = [c2("VC%d" % i, [128, NS // 2, 256]) for i in range(2)]
                prd = c2("prd", [128, 16, 64])
                scT = c2("scT", [128, NS, 16])
                Pm = c2("Pm", [128, 2, 128])
                PTa = c2("PTa", [128, 2, 128])
                mx_s = c2("mx_s", [128, 2])
                nmx_s = c2("nmx_s", [128, 2])
                rs_s = c2("rs_s", [128, 2])
                es_s = c2("es_s", [128, 2])
                ob_tok = c2("ob_tok", [NS, 1024])
                obTf = c2("obTf", [128, 8, NS])
                W2x = []
                for i_ in range(3):
                    try:
                        W2x.append(c2("W2x%d" % i_, [128, 8, 512], BF16))
                    except AssertionError:
                        break
                k.switch('S1', 'S2')
                wlist['cur'] = W8 + W2x
                for g_ in range(2):
                    wt = wload(w_in[:, OFF_SQ + g_ * 512:OFF_SQ + (g_ + 1) * 512], 512)
                    tok_mm(hTs, wt, 0, 512, q_s[:, g_ * 512:(g_ + 1) * 512], q_s, scale=0.125)
                wt = wload(w_in[:, OFF_SK:OFF_SK + 512], 512)
                tok_mm(hTs, wt, 0, 512, kv_s[:], kv_s)
                for i_, q_ in ((0, 'sp'), (1, 'act')):
                    k.dma(q_, KCs[i_][0:127, :, :], st_k[1:128, i_ * 8:(i_ + 1) * 8, :], writes=[KCs[i_]])
                for i_ in range(2):
                    k.dma('pool', VCs[i_][0:127, :, :], st_v[1:128, i_ * 8:(i_ + 1) * 8, :], writes=[VCs[i_]])
                for s_ in range(NS):
                    KC_, VC_ = KCs[s_ // 8], VCs[s_ // 8]
                    k.dma('sp' if s_ < 8 else 'act', KC_[127:128, s_ % 8, :], kv_s[s_:s_ + 1, 0:256], reads=[kv_s], writes=[KC_], indep=True)
                    k.dma('pool', VC_[127:128, s_ % 8, :], kv_s[s_:s_ + 1, 256:512], reads=[kv_s], writes=[VC_], indep=True)
                for i_, q_ in ((0, 'sp'), (1, 'act')):
                    k.dma(q_, o_k_s[:, i_ * 8:(i_ + 1) * 8, :], KCs[i_][:], reads=[KCs[i_]])
                for i_ in range(2):
                    k.dma('pool', o_v_s[:, i_ * 8:(i_ + 1) * 8, :], VCs[i_][:], reads=[VCs[i_]])
                for s_ in range(NS):
                    b0, b1 = k.bank(), k.bank()
                    k.mm(b0[:, :], Esel[:, s_, :], q_s[:, 0:512], [Esel, q_s], [b0])
                    k.mm(b1[:, :], Esel[:, s_, :], q_s[:, 512:1024], [Esel, q_s], [b1])
                    for hf, bb in ((0, b0), (1, b1)):
                        KC = KCs[s_ // 8]
                        kc = KC[:, s_ % 8, hf * 128:(hf + 1) * 128].rearrange("p (g d) -> p g d", d=64)
                        k.tt(prd[:, hf * 8:(hf + 1) * 8, :].rearrange("p (g i) d -> p g i d", i=4),
                             bb[:, :].rearrange("p (g i d) -> p g i d", i=4, d=64), bc(kc, 2, 4), ALU.mult, [bb, KC], [prd])
                    k.rel(b0, b1)
                    k.op('dve', lambda e: e.tensor_reduce(out=scT[:, s_, :], in_=prd[:], axis=AX.X, op=ALU.add), reads=[prd], writes=[scT])
                b = k.bank()
                for a_ in range(2):
                    k.tr(b[:, a_ * 128:(a_ + 1) * 128], scT[:, a_ * 8:(a_ + 1) * 8, :].rearrange("p s h -> p (s h)"), identf[:], [scT, identf], [b])
                k.op('dve', lambda e: e.tensor_reduce(out=mx_s[:], in_=b[:, 0:256].rearrange("p (a q) -> p a q", q=128), axis=AX.X, op=ALU.max),
                     reads=[b], writes=[mx_s])
                k.ts(mx_s[:], mx_s[:], sinkcol[:, 0:1], None, ALU.max, None, [mx_s, sinkcol], [mx_s])
                k.ts(nmx_s[:], mx_s[:], -1.0, None, ALU.mult, None, [mx_s], [nmx_s])
                for a_ in range(2):
                    k.act(Pm[:, a_, :], b[:, a_ * 128:(a_ + 1) * 128], AF.Exp, [b, nmx_s], [Pm, rs_s], bias=nmx_s[:, a_:a_ + 1], accum=rs_s[:, a_:a_ + 1])
                k.rel(b)
                k.act(es_s[:], nmx_s[:], AF.Exp, [nmx_s, sinkcol], [es_s], bias=sinkcol[:, 0:1])
                k.tt(rs_s[:], rs_s[:], es_s[:], ALU.add, [rs_s, es_s], [rs_s])
                k.op('dve', lambda e: e.reciprocal(out=rs_s[:], in_=rs_s[:]), reads=[rs_s], writes=[rs_s])
                k.tt(Pm[:], Pm[:], bc(rs_s[:], 2, 128), ALU.mult, [Pm, rs_s], [Pm])
                b = k.bank()
                for a_ in range(2):
                    k.tr(b[:, a_ * 128:(a_ + 1) * 128], Pm[:, a_, :], identf[:], [Pm, identf], [b])
                k.cp(PTa[:].rearrange("p a q -> p (a q)"), b[:, 0:256], [b], [PTa])
                k.rel(b)
                PT3 = PTa[:].rearrange("p a (s h) -> p (a s) h", h=16)
                b0, b1 = k.bank(), k.bank()
                for s_ in range(NS):
                    for hf in range(2):
                        VC = VCs[s_ // 8]
                        vc = VC[:, s_ % 8, hf * 128:(hf + 1) * 128].rearrange("p (g d) -> p g d", d=64)
                        pt_ = PT3[:, s_, hf * 8:(hf + 1) * 8].rearrange("p (g i) -> p g i", i=4)
                        k.tt(prd[:, hf * 8:(hf + 1) * 8, :].rearrange("p (g i) d -> p g i d", i=4), bc(vc, 2, 4), bc(pt_, 3, 64), ALU.mult,
                             [VC, PTa], [prd], eng='pool')
                    k.mm(b0[0:NS, :], OH[:, s_, :], prd[:, 0:8, :].rearrange("p h d -> p (h d)"), [OH, prd], [b0], start=(s_ == 0), stop=(s_ == NS - 1))
                    k.mm(b1[0:NS, :], OH[:, s_, :], prd[:, 8:16, :].rearrange("p h d -> p (h d)"), [OH, prd], [b1], start=(s_ == 0), stop=(s_ == NS - 1))
                k.cp(ob_tok[:, 0:512], b0[0:NS, :], [b0], [ob_tok])
                k.cp(ob_tok[:, 512:1024], b1[0:NS, :], [b1], [ob_tok])
                k.rel(b0, b1)
                to_T(ob_tok[:], ob_tok, 8, obTf, obTf)
                k.cp(obTs[:], obTf[:], [obTf], [obTs])

                if nblk > 0:
                    emit_p1(0)
                    hoist['p1'] = True
                gab = c3("gab", [NS, 2048])
                yab = c3("yab", [NS, 2048])
                mix_s = c3("mix_s", [NS, 1024])
                mixTs = c3("mixTs", [128, 8, NS], BF16)
                mixTf = c3("mixTf", [128, 8, NS])
                h2Ts = c3("h2Ts", [128, 8, NS], BF16)
                gtok = c3("gtok", [NS, DFF])
                stf = c3("stf", [NS * 2, DFF])
                stfT = c3("stfT", [128, NFC, NS * 2])
                gT = c3("gT", [128, NFC, NS])
                uT = c3("uT", [128, NFC, NS])
                tT = c3("tT", [128, NFC, NS])
                aTs = c3("aTs", [128, NFC, NS], BF16)
                ys = c3("ys", [NS, 1024])
                W3x = []
                for i_ in range(3):
                    try:
                        W3x.append(c3("W3x%d" % i_, [128, 8, 512], BF16))
                    except AssertionError:
                        break
                k.switch('S2', 'S3')
                wlist['cur'] = W8 + W3x
                for g_ in range(4):
                    wt = wload(w_in[:, OFF_GA + g_ * 512:OFF_GA + (g_ + 1) * 512], 512)
                    tok_mm(hTs, wt, 0, 512, gab[:, g_ * 512:(g_ + 1) * 512], gab, func=AF.Sigmoid)
                for g_ in range(2):
                    wt = wload(w_gdn_out[:, g_ * 512:(g_ + 1) * 512], 512)
                    tok_mm(onTs, wt, 0, 512, yab[:, g_ * 512:(g_ + 1) * 512], yab)
                for g_ in range(2):
                    wt = wload(w_swa_out[:, g_ * 512:(g_ + 1) * 512], 512)
                    tok_mm(obTs, wt, 0, 512, yab[:, 1024 + g_ * 512:1024 + (g_ + 1) * 512], yab)
                k.tt(yab[:], yab[:], gab[:], ALU.mult, [yab, gab], [yab])
                k.tt(mix_s[:], yab[:, 0:1024], yab[:, 1024:2048], ALU.add, [yab], [mix_s])
                to_T(mix_s[:], mix_s, 8, mixTf, mixTf)
                k.cp(mixTs[:], mixTf[:], [mixTf], [mixTs])
                for g_ in range(2):
                    wt = wload(w_o[:, g_ * 512:(g_ + 1) * 512], 512)
                    tok_mm(mixTs, wt, 0, 512, mix_s[:, g_ * 512:(g_ + 1) * 512], mix_s)
                k.tt(mix_s[:], mix_s[:], gtok1[0:NS, :], ALU.mult, [mix_s, gtok1], [mix_s])
                k.tt(xs_3[:], xs_3[:], mix_s[:], ALU.add, [xs_3, mix_s], [xs_3])
                rms_T_s(xs_3[:], xs_3, h2Ts, A2s, modT[:, 24:32, 0:NS], modT)
                k.dma('sp', stf[:], st_ffn, writes=[stf])
                for s_ in range(NS):
                    k.dma('sp', o_ffn_s[s_, 0:1, :], stf[s_ * 2 + 1:s_ * 2 + 2, :], reads=[stf])
                to_T(stf[:], stf, NFC, stfT, stfT, rows=NS * 2)
                for (wsrc, dstT, is_gate) in ((w_ffn_gate, gT, True), (w_ffn_up, uT, False)):
                    for g_ in range(6):
                        n = 512 if g_ < 5 else DFF - 5 * 512
                        wt = wload(wsrc[:, g_ * 512:g_ * 512 + n], n)
                        if is_gate:
                            tok_mm(h2Ts, wt, 0, n, gtok[:, g_ * 512:g_ * 512 + n], gtok)
                        b = k.bank()
                        for j in range(n // 128):
                            for kk in range(8):
                                k.mm(b[:, j * NS:(j + 1) * NS], wt[:, kk, j * 128:(j + 1) * 128], h2Ts[:, kk, :], [wt, h2Ts], [b], start=(kk == 0), stop=(kk == 7))
                        k.cp(dstT[:, g_ * 4:g_ * 4 + n // 128, :], b[:, 0:(n // 128) * NS].rearrange("p (c s) -> p c s", s=NS), [b], [dstT])
                        k.rel(b)
                k.dma('sp', o_ffn_s[:, 1, :], gtok[:], reads=[gtok])
                sf4 = stfT[:].rearrange("p c (s j) -> p c s j", j=2)
                k.tt(tT[:], gT[:], bc(fcwT[:, :, 2], 2, NS), ALU.mult, [gT, fcwT], [tT])
                for j_ in range(2):
                    k.tt(gT[:], sf4[:, :, :, j_], bc(fcwT[:, :, j_], 2, NS), ALU.mult, [stfT, fcwT], [gT])
                    k.tt(tT[:], tT[:], gT[:], ALU.add, [tT, gT], [tT])
                k.tt(tT[:], tT[:], bc(fcbT[:], 2, NS), ALU.add, [tT, fcbT], [tT])
                k.act(tT[:], tT[:], AF.Silu, [tT], [tT])
                k.tt(aTs[:], tT[:], uT[:], ALU.mult, [tT, uT], [aTs])
                for hf in range(2):
                    b = k.bank()
                    for kg in range(3):
                        nk = 8 if kg < 2 else NFC - 16
                        wt = nextw()
                        k.dma('pool', wt[:, 0:nk, :], w_ffn_down[kg * 1024:kg * 1024 + nk * 128, hf * 512:(hf + 1) * 512].rearrange("(c p) n -> p c n", p=128),
                              writes=[wt])
                        for kk in range(nk):
                            kf = kg * 8 + kk
                            k.mm(b[0:NS, :], aTs[:, kf, :], wt[:, kk, :], [aTs, wt], [b], start=(kf == 0), stop=(kf == NFC - 1))
                    k.tt(mix_s[:, hf * 512:(hf + 1) * 512], b[0:NS, :], gtok2[0:NS, hf * 512:(hf + 1) * 512], ALU.mult, [b, gtok2], [mix_s])
                    k.rel(b)
                k.tt(xs_3[:], xs_3[:], mix_s[:], ALU.add, [xs_3, mix_s], [xs_3])
                k.act(ys[:], xs_3[:], AF.Square, [xs_3], [ys, ss1], accum=ss1[0:NS, :])
                k.act(rs1[0:NS, :], ss1[0:NS, :], AF.Ln, [ss1, epsc], [rs1], bias=epsc[0:NS, :], scale=1.0 / D)
                k.act(rs1[0:NS, :], rs1[0:NS, :], AF.Exp, [rs1], [rs1], scale=-0.5)
                k.stt(ys[:], xs_3[:], rs1[0:NS, :], fnw_bc[0:NS, :], ALU.mult, ALU.mult, [xs_3, rs1, fnw_bc], [ys])
                k.dma('sp', y_s, ys[:], reads=[ys])
                k.switch('S3', 'G')
                wlist['cur'] = W8

            if do_sample:
                sample_phase()

            for blk in range(nblk):
                t0 = blk * TB
                last = (blk == NBLK - 1)
                if not (blk == 0 and hoist['p1']):
                    emit_p1(t0)
                if blk == 0:
                    dump("hT", hT[:], [hT])
                if blk > 0:
                    k.switch('F', 'G')
                    k.switch('FW', 'G')
                wlist['cur'] = W8

                ck('p1')
                wt = wload(w_in[:, OFF_BETA:OFF_BETA + 16], 16)
                b = k.bank()
                for c in range(8):
                    for kk in range(8):
                        k.mm(b[0:64, c * 16:(c + 1) * 16], hT[:, kk, c * 64:(c + 1) * 64], wt[:, kk, 0:16], [hT, wt], [b],
                             start=(kk == 0), stop=(kk == 7))
                k.cp(ba[:], b[0:64, 0:128].rearrange("p (c r) -> p c r", r=16), [b], [ba])
                k.rel(b)
                k.act(beta[:], ba[:, :, 0:8], AF.Exp, [ba], [beta], scale=-1.0)
                k.ts(beta[:], beta[:], 1.0, None, ALU.add, None, [beta], [beta])
                k.op('dve', lambda e: e.reciprocal(out=beta[:], in_=beta[:]), reads=[beta], writes=[beta])
                k.tt(t64a[:], ba[:, :, 8:16], bc(dtb[:], 1, 8), ALU.add, [ba, dtb], [t64a])
                k.act(t64b[:], t64a[:], AF.Abs, [t64a], [t64b])
                k.act(t64b[:], t64b[:], AF.Exp, [t64b], [t64b], scale=-1.0)
                k.act(t64b[:], t64b[:], AF.Ln, [t64b, onec], [t64b], bias=onec[0:64, :])
                k.stt(t64a[:], t64a[:], 0.0, t64b[:], ALU.max, ALU.add, [t64a, t64b], [t64a])
                k.tt(gg[:], t64a[:], bc(negA[:], 1, 8), ALU.mult, [t64a, negA], [gg])
                ggf = gg[:].rearrange("p c h -> p (c h)")
                b = k.bank()
                k.mm(b[0:64, 0:64], Um[:], ggf, [Um, gg], [b])
                k.mm(b[:, 64:128], onesf[0:64, :], ggf, [onesf, gg], [b])
                k.cp(dd[:], b[0:64, 0:64], [b], [dd])
                k.act(ed[:], dd[:], AF.Exp, [dd], [ed])
                k.tt(ekd[:], b[0:64, 64:128], dd[:], ALU.subtract, [b, dd], [ekd])
                k.act(ekd[:], ekd[:], AF.Exp, [ekd], [ekd])
                k.act(elast[:], b[:, 64:128], AF.Exp, [b], [elast])
                k.rel(b)
                k.tt(bed[:], ed[:], beta[:].rearrange("p c h -> p (c h)"), ALU.mult, [ed, beta], [bed])
                if blk == 0:
                    dump("gg", gg[:], [gg])
                    dump("beta", beta[:], [beta])

                ck('p2')
                def gdn_front(h, hb):
                    wT, uu, Qd, Gp, Kdec, Am, Amb = wT2[hb], uu2[hb], Qd2[hb], Gp2[hb], Kdec2[hb], Am2[hb], Amb2[hb]
                    wt = nextw()
                    for part in range(3):
                        wload(w_in[:, OFF_QKV + part * 1024 + h * 128:OFF_QKV + part * 1024 + (h + 1) * 128], 128, c0=part * 128, tile=wt)
                    wload(w_in[:, OFF_GATE + h * 128:OFF_GATE + (h + 1) * 128], 128, c0=384, tile=wt)
                    for part in range(3):
                        b = k.bank()
                        for kk in range(8):
                            k.mm(b[:, :], wt[:, kk, part * 128:(part + 1) * 128], hT[:, kk, :], [wt, hT], [b], start=(kk == 0), stop=(kk == 7))
                        j = part * 8 + h
                        k.cp(pre[:, 0:3], halo[:, j, :], [halo], [pre], eng='pool')
                        k.cp(pre[:, 3:3 + TB], b[:, :], [b], [pre], eng='act')
                        k.rel(b)
                        yield
                        k.cp(halo[:, j, :], pre[:, TB:TB + 3], [pre], [halo], eng='pool')
                        k.ts(cv[:, part, :], pre[:, 0:TB], cwT[:, j, 0:1], None, ALU.mult, None, [pre, cwT], [cv])
                        for tap in range(1, 4):
                            k.stt(cv[:, part, :], pre[:, tap:tap + TB], cwT[:, j, tap:tap + 1], cv[:, part, :], ALU.mult, ALU.add,
                                  [pre, cwT, cv], [cv])
                    b = k.bank()
                    for kk in range(8):
                        k.mm(b[:, :], wt[:, kk, 384:512], hT[:, kk, :], [wt, hT], [b], start=(kk == 0), stop=(kk == 7))
                    k.act(Gp[:], b[:, :], AF.Silu, [b], [Gp])
                    k.rel(b)
                    yield
                    k.act(cv[:], cv[:], AF.Silu, [cv], [cv])
                    for qk in range(2):
                        prebf = pre[:, 0:TB // 2].bitcast(BF16)
                        k.tt(prebf, cv[:, qk, :], cv[:, qk, :], ALU.mult, [cv], [pre], eng='pool')
                        b = k.bank()
                        k.mm(b[:, :], onesb[:], prebf, [onesb, pre], [b])
                        k.act(rqk[:, qk, :], b[:, :], AF.Ln, [b, epsc], [rqk], bias=epsc[:])
                        k.rel(b)
                        yield
                    k.act(rqk[:], rqk[:], AF.Exp, [rqk], [rqk], scale=-0.5)
                    k.stt(Qt_ap, cv[:, 0, :], 128.0 ** -0.5, rqk[:, 0, :], ALU.mult, ALU.mult, [cv, rqk], [cv])
                    k.tt(Kt_ap, cv[:, 1, :], rqk[:, 1, :], ALU.mult, [cv, rqk], [cv])
                    k.cp(Qtb[:], Qt_ap, [cv], [Qtb], eng='pool')
                    k.cp(Ktb[:], Kt_ap, [cv], [Ktb], eng='pool')
                    k.cp(Vtb[:], cv[:, 2, :], [cv], [Vtb], eng='pool')
                    if blk == 0 and h == 0:
                        dump("Qt", Qt_ap, [cv])
                        dump("Kt", Kt_ap, [cv])
                        dump("Vt", cv[:, 2, :], [cv])
                    ck('gdn_a')
                    gh = gg[:, :, h]
                    k.tt(SA[:], bc(gh, 2, 64), bc(SLm[:], 1, 8), ALU.mult, [gg, SLm], [SA], eng='pool')
                    k.tt(SB[:], bc(gh, 2, 64), bc(Um[:], 1, 8), ALU.mult, [gg, Um], [SB], eng='pool')
                    b = k.bank()
                    k.mm(b[0:64, :], Um[:], SA[:].rearrange("p c j -> p (c j)"), [Um, SA], [b])
                    k.act(Wm[:].rearrange("p c j -> p (c j)"), b[0:64, :], AF.Exp, [b], [Wm])
                    k.rel(b)
                    yield
                    k.tt(Wm[:], Wm[:], bc(nSL[:], 1, 8), ALU.mult, [Wm, nSL], [Wm])
                    k.tt(Wm[:], Wm[:], bc(beta[:, :, h], 2, 64), ALU.mult, [Wm, beta], [Wm])
                    k.tt(SA[:], bc(beta[:, :, h], 2, 64), bc(identf[0:64, 0:64], 1, 8), ALU.mult, [beta, identf], [SA], eng='pool')
                    b = k.bank()
                    k.mm(b[0:64, :], SLm[:], SB[:].rearrange("p c j -> p (c j)"), [SLm, SB], [b])
                    k.act(Zm[:].rearrange("p c j -> p (c j)"), b[0:64, :], AF.Exp, [b], [Zm])
                    k.rel(b)
                    yield
                    k.tt(Am[:], Zm[:], bc(Um[:], 1, 8), ALU.mult, [Zm, Um], [Am])
                    k.tt(Zm[:], Zm[:], bc(nSU[:], 1, 8), ALU.mult, [Zm, nSU], [Zm])
                    b = k.bank()
                    k.mm(b[0:64, :], onesf[0:64, 0:64], SA[:].rearrange("p c j -> p (c j)"), [onesf, SA], [b])
                    k.tt(Zm[:].rearrange("p c j -> p (c j)"), Zm[:].rearrange("p c j -> p (c j)"), b[0:64, :], ALU.mult, [Zm, b], [Zm])
                    k.rel(b)
                    yield
                    k.cp(Qd[:], Qt_ap, [cv], [Qd])
                    ck('gdn_b')
                    bK0, bV0 = k.bank(), k.bank()
                    bkv = bK0[:, :].bitcast(BF16)
                    bvv = bV0[:, :].bitcast(BF16)
                    for c in range(8):
                        k.tr(bkv[0:64, c * 128:(c + 1) * 128], Ktb[:, c * 64:(c + 1) * 64], identb[:], [Ktb, identb], [bK0])
                        k.tr(bvv[0:64, c * 128:(c + 1) * 128], Vtb[:, c * 64:(c + 1) * 64], identb[:], [Vtb, identb], [bV0])
                    kin = bkv[0:64, :].rearrange("p (c d) -> p c d", d=128)
                    vin = bvv[0:64, :].rearrange("p (c d) -> p c d", d=128)
                    k.tt(Kbd[:], kin, bc(bed[:].rearrange("p (c h) -> p c h", h=8)[:, :, h], 2, 128), ALU.mult, [bK0, bed], [Kbd])
                    k.tt(Kdec[:], kin, bc(ekd[:].rearrange("p (c h) -> p c h", h=8)[:, :, h], 2, 128), ALU.mult, [bK0, ekd], [Kdec])
                    k.tt(Vb[:], vin, bc(beta[:, :, h], 2, 128), ALU.mult, [bV0, beta], [Vb])
                    k.rel(bK0, bV0)
                    yield
                    bA, bB = k.bank(), k.bank()
                    for c in range(8):
                        k.mm(bA[0:64, c * 64:(c + 1) * 64], Ktb[:, c * 64:(c + 1) * 64], Ktb[:, c * 64:(c + 1) * 64], [Ktb], [bA])
                        k.mm(bB[0:64, c * 64:(c + 1) * 64], Ktb[:, c * 64:(c + 1) * 64], Qtb[:, c * 64:(c + 1) * 64], [Ktb, Qtb], [bB])
                    A3 = bA[0:64, :].rearrange("p (c j) -> p c j", j=64)
                    B3 = bB[0:64, :].rearrange("p (c j) -> p c j", j=64)
                    k.tt(Wm[:], A3, Wm[:], ALU.mult, [bA, Wm], [Wm])
                    k.tt(Zm[:], A3, Zm[:], ALU.mult, [bA, Zm], [Zm])
                    Amb_ = Am if os.environ.get('SUPD32', '0') == '1' else Amb
                    k.tt(Amb_[:], B3, Am[:], ALU.mult, [bB, Am], [Amb_])
                    k.rel(bA, bB)
                    yield
                    I4 = bc(identf[0:64, 0:64], 1, 4)
                    for hf in range(2):
                        cs = slice(hf * 4, hf * 4 + 4)
                        k.cp(ZYH[hf][:, :, 0, :], Zm[:, cs, :], [Zm], [ZYH[hf]], eng='act')
                        k.tt(ZYH[hf][:, :, 1, :], Zm[:, cs, :], I4, ALU.add, [Zm, identf], [ZYH[hf]])
                        k.cp(WXH[hf][:, :, 0, :], Wm[:, cs, :], [Wm], [WXH[hf]], eng='act')
                        k.tt(WXH[hf][:, :, 1, :], Wm[:, cs, :], I4, ALU.add, [Wm, identf], [WXH[hf]])
                    ck('gdn_c')
                    for lev in range(6):
                        for hf in range(2):
                            zy, wx = ZYH[hf], WXH[hf]
                            bZ, bW = k.bank(), k.bank()
                            for cc in range(4):
                                if lev == 0:
                                    k.mm(bZ[0:64, cc * 128:cc * 128 + 64], wx[:, cc, 0, :], zy[:, cc, 0, :], [wx, zy], [bZ])
                                    k.mm(bW[0:64, cc * 128:cc * 128 + 64], zy[:, cc, 0, :], wx[:, cc, 0, :], [wx, zy], [bW])
                                elif lev < 5:
                                    k.mm(bZ[0:64, cc * 128:(cc + 1) * 128], wx[:, cc, 0, :], zy[:, cc, :, :].rearrange("p a j -> p (a j)"), [wx, zy], [bZ])
                                    k.mm(bW[0:64, cc * 128:(cc + 1) * 128], zy[:, cc, 0, :], wx[:, cc, :, :].rearrange("p a j -> p (a j)"), [wx, zy], [bW])
                                else:
                                    k.mm(bZ[0:64, cc * 128 + 64:(cc + 1) * 128], wx[:, cc, 0, :], zy[:, cc, 1, :], [wx, zy], [bZ])
                                    k.mm(bW[0:64, cc * 128 + 64:(cc + 1) * 128], zy[:, cc, 0, :], wx[:, cc, 1, :], [wx, zy], [bW])
                            cs = slice(hf * 4, hf * 4 + 4)
                            Z4 = bZ[0:64, :].rearrange("p (c a j) -> p c a j", a=2, j=64)
                            W4 = bW[0:64, :].rearrange("p (c a j) -> p c a j", a=2, j=64)
                            if lev == 0:
                                k.cp(zy[:, :, 0, :], Z4[:, :, 0, :], [bZ], [zy], eng='act')
                                k.cp(wx[:, :, 0, :], W4[:, :, 0, :], [bW], [wx], eng='act')
                            elif lev < 5:
                                k.tt(zy[:, :, 1, :], Z4[:, :, 1, :], zy[:, :, 1, :], ALU.add, [bZ, zy], [zy])
                                k.cp(zy[:, :, 0, :], Z4[:, :, 0, :], [bZ], [zy], eng='act')
                                k.tt(wx[:, :, 1, :], W4[:, :, 1, :], wx[:, :, 1, :], ALU.add, [bW, wx], [wx])
                                k.cp(wx[:, :, 0, :], W4[:, :, 0, :], [bW], [wx], eng='act')
                            else:
                                k.tt(Y32[hf][:], Z4[:, :, 1, :], zy[:, :, 1, :], ALU.add, [bZ, zy], [Y32[hf]])
                                k.tt(SB[:, cs, :], W4[:, :, 1, :], wx[:, :, 1, :], ALU.add, [bW, wx], [SB])
                            k.rel(bZ, bW)
                            yield
                    for hf in range(2):
                        cs = slice(hf * 4, hf * 4 + 4)
                        bR = k.bank()
                        for cc in range(4):
                            k.mm(bR[0:64, cc * 64:(cc + 1) * 64], Wm[:, hf * 4 + cc, :], Y32[hf][:, cc, :], [Wm, Y32[hf]], [bR])
                        R3 = bR[0:64, 0:256].rearrange("p (c j) -> p c j", j=64)
                        k.tt(SA[:, cs, :], R3, Y32[hf][:], ALU.subtract, [bR, Y32[hf]], [SA])
                        k.rel(bR)
                        k.tt(SA[:, cs, :], SA[:, cs, :], I4, ALU.add, [SA, identf], [SA])
                        bF = k.bank()
                        for cc in range(4):
                            k.mm(bF[0:64, cc * 64:(cc + 1) * 64], SB[:, hf * 4 + cc, :], SA[:, hf * 4 + cc, :], [SB, SA], [bF])
                        k.tt(Y32[hf][:], bF[0:64, 0:256].rearrange("p (c j) -> p c j", j=64), Y32[hf][:], ALU.add, [bF, Y32[hf]], [Y32[hf]])
                        k.rel(bF)
                        yield
                    if blk == 0 and h == 0:
                        dump("Yf", Y32[0][:], [Y32[0]])
                    bU0, bU1, bWt = k.bank(), k.bank(), k.bank()
                    for c in range(8):
                        bu_ = bU0 if c < 4 else bU1
                        Yc = Y32[c // 4][:, c % 4, :]
                        k.mm(bu_[0:64, (c % 4) * 128:(c % 4 + 1) * 128], Yc, Vb[:, c, :], [Y32[c // 4], Vb], [bu_])
                        k.mm(bWt[:, c * 64:(c + 1) * 64], Kbd[:, c, :], Yc, [Kbd, Y32[c // 4]], [bWt])
                    k.cp(uu[:, 0:4, :], bU0[0:64, :].rearrange("p (c d) -> p c d", d=128), [bU0], [uu], eng='act')
                    k.cp(uu[:, 4:8, :], bU1[0:64, :].rearrange("p (c d) -> p c d", d=128), [bU1], [uu], eng='act')
                    k.cp(wT[:], bWt[:, :].rearrange("p (c j) -> p c j", j=64), [bWt], [wT])
                    k.rel(bU0, bU1, bWt)
                    yield
                    yield

                def gdn_back(h, hb):
                    wT, uu, Qd, Gp, Kdec, Am, Amb = wT2[hb], uu2[hb], Qd2[hb], Gp2[hb], Kdec2[hb], Am2[hb], Amb2[hb]
                    Amb_ = Am if os.environ.get('SUPD32', '0') == '1' else Amb
                    Sh = S_all[:, h, :]
                    k.cp(Sb[:], Sh, [S_all], [Sb], eng='pool')
                    for c in range(8):
                        col = c * 8 + h
                        b1_, b2_, b3_ = k.bank(), k.bank(), k.bank()
                        k.mm(b1_[0:64, 0:128], wT[:, c, :], Sb[:], [wT, Sb], [b1_])
                        k.mm(b2_[0:64, 0:128], Qd[:, c * 64:(c + 1) * 64], Sb[:], [Qd, Sb], [b2_])
                        k.tt(vnew[:], uu[:, c, :], b1_[0:64, 0:128], ALU.subtract, [uu, b1_], [vnew])
                        yield
                        k.mm(b2_[0:64, 128:256], Amb_[:, c, :], vnew[:], [Amb_, vnew], [b2_])
                        k.mm(b3_[:, 0:128], Kdec[:, c, :], vnew[:], [Kdec, vnew], [b3_])
                        k.stt(Sh, Sh, elast[:, col:col + 1], b3_[:, 0:128], ALU.mult, ALU.add, [S_all, elast, b3_], [S_all])
                        if c < 7:
                            k.cp(Sb[:], Sh, [S_all], [Sb], eng='pool')
                        k.act(osb[:, c, :], b2_[0:64, 0:128], AF.Copy, [b2_, ed], [osb], scale=ed[:, col:col + 1])
                        k.tt(osb[:, c, :], osb[:, c, :], b2_[0:64, 128:256], ALU.add, [osb, b2_], [osb])
                        k.rel(b1_, b2_, b3_)
                        yield
                    if blk == 0 and h == 0:
                        dump("osb", osb[:], [osb])
                    ck('gdn_e')
                    k.tt(uu[:], osb[:], osb[:], ALU.mult, [osb], [uu], eng='pool')
                    k.op('dve', lambda e: e.tensor_reduce(out=oss[:], in_=uu[:], axis=AX.X, op=ALU.add), reads=[uu], writes=[oss])
                    k.act(ors[:], oss[:], AF.Ln, [oss, epsc], [ors], bias=epsc[0:64, :], scale=1.0 / 128)
                    k.act(ors[:], ors[:], AF.Exp, [ors], [ors], scale=-0.5)
                    k.tt(on1[:], osb[:], bc(ors[:], 2, 128), ALU.mult, [osb, ors], [on1])
                    b = k.bank()
                    bv = b[:, :].bitcast(BF16)
                    for c in range(8):
                        k.tr(bv[:, c * 64:(c + 1) * 64], on1[:, c, :], identb[0:64, 0:64], [on1, identb], [b])
                    k.stt(onT[:, h, :], bv[:, 0:TB], onwT[:, 0:1], Gp[:], ALU.mult, ALU.mult, [b, Gp, onwT], [onT])
                    k.rel(b)
                    yield
                    yield

                def _drain(gl):
                    gl = [[g_, w_] for g_, w_ in gl]
                    while gl:
                        for it in list(gl):
                            for _ in range(it[1]):
                                try:
                                    next(it[0])
                                except StopIteration:
                                    gl.remove(it)
                                    break

                _drain([(gdn_front(0, 0), 1)])
                for h in range(8):
                    gl = [(gdn_back(h, h % 2), 1)]
                    if h < 7:
                        gl.append((gdn_front(h + 1, (h + 1) % 2), 3))
                    _drain(gl)
                if blk == 0:
                    dump("onT", onT[:], [onT])
                    dump("S0", S_all[:], [S_all])
                if last:
                    k.dma('sp', o_S_p, S_all[:], reads=[S_all])
                    k.cp(halo_out[:], halo[:], [halo], [halo_out], eng='pool')
                    for j_ in range(3):
                        k.dma('sp', o_conv_p[j_].rearrange("(c p) -> p c", p=128), halo_out[:, :, j_], reads=[halo_out],
                              allow_slow_non_contiguous=True)

                ck('gdn')
                k.switch('G', 'A')
                k.switch('G', 'FW')
                wt = wload(w_in[:, OFF_SK:OFF_SK + 512], 512)
                for t in range(4):
                    b = k.bank()
                    for kk in range(8):
                        k.mm(b[:, :], hT[:, kk, t * 128:(t + 1) * 128], wt[:, kk, :], [hT, wt], [b], start=(kk == 0), stop=(kk == 7))
                    ck('swa_a0')
                    kin = b[:, 0:256].rearrange("p (g d) -> p g d", d=64)
                    vin = b[:, 256:512].rearrange("p (g d) -> p g d", d=64)
                    Kt4 = Ktok[:].rearrange("p (g a d) -> p g a d", a=2, d=64)
                    Vt4 = Vtok[:, 1 + t, :].rearrange("p (g a d) -> p g a d", a=2, d=64)
                    for a_ in range(2):
                        _v = os.environ.get('SWA_VAR', '')
                        ke, ve = {'': ('act', 'dve'), 'konly': ('act', None), 'vonly': (None, 'dve'), 'kdve': ('dve', None),
                                  'both_dve': ('dve', 'dve'), 'both_act': ('act', 'act'), 'swap': ('dve', 'act')}[_v]
                        if ke:
                            k.cp(Kt4[:, :, a_, :], kin, [b], [Ktok], eng=ke)
                        if ve:
                            k.cp(Vt4[:, :, a_, :], vin, [b], [Vtok], eng=ve)
                    ck('swa_a1')
                    if last and t == 3:
                        k.cp(kvout[:], b[:, :], [b], [kvout])
                        k.dma('sp', o_k_p, kvout[:, 0:256], reads=[kvout])
                        k.dma('sp', o_v_p, kvout[:, 256:512], reads=[kvout])
                    k.rel(b)
                    b = k.bank()
                    bv = b[:, :].bitcast(BF16)
                    for g in range(4):
                        k.tr(bv[:, g * 128:(g + 1) * 128], Ktok[:, g * 128:(g + 1) * 128], identb[:], [Ktok, identb], [b])
                    ck('swa_a2')
                    k.cp(KTl[0:64, :, 128 + t * 128:128 + (t + 1) * 128], bv[0:64, 0:512].rearrange("p (g q) -> p g q", q=128), [b], [KTl])
                    k.cp(KTh[64:128, :, 128 + t * 128:128 + (t + 1) * 128], bv[64:128, 0:512].rearrange("p (g q) -> p g q", q=128), [b], [KTh])
                    k.rel(b)
                ck('swa_a')
                for half in range(2):
                    wt = wload(w_in[:, OFF_SQ + half * 512:OFF_SQ + (half + 1) * 512], 512)
                    for j in range(4):
                        b = k.bank()
                        for kk in range(8):
                            k.mm(b[:, :], wt[:, kk, j * 128:(j + 1) * 128], hT[:, kk, :], [wt, hT], [b], start=(kk == 0), stop=(kk == 7))
                        k.act(QT[:, half * 4 + j, :], b[:, :], AF.Copy, [b], [QT], scale=0.125)
                        k.rel(b)
                ck('swa_b')
                def swa_iter(t, g, sb_):
                    msk = maskB if (blk == 0 and t == 0) else maskA
                    sc, pb, PT, mx, nmx, rsum, esk = sc2[sb_], pb2[sb_], PT2[sb_], mx2[sb_], nmx2[sb_], rsum2[sb_], esk2[sb_]
                    b0, b1 = k.bank(), k.bank()
                    for i in range(4):
                        hq = g * 4 + i
                        ch, hf = hq // 2, hq % 2
                        if os.environ.get('HF0'):
                            hf = 0
                        bb = b0 if i < 2 else b1
                        KTx = KTh if hf else KTl
                        k.mm(bb[:, (i % 2) * 256:(i % 2 + 1) * 256], QT[:, ch, t * 128:(t + 1) * 128],
                             KTx[:, g, t * 128:t * 128 + 256], [QT, KTx], [bb])
                    k.tt(sc[:, 0:2, :], b0[:, :].rearrange("p (i q) -> p i q", q=256), bc(msk[:], 1, 2), ALU.add, [b0, msk], [sc])
                    k.tt(sc[:, 2:4, :], b1[:, :].rearrange("p (i q) -> p i q", q=256), bc(msk[:], 1, 2), ALU.add, [b1, msk], [sc])
                    k.rel(b0, b1)
                    yield
                    ck('swa_c')
                    k.op('dve', lambda e: e.tensor_reduce(out=mx[:], in_=sc[:], axis=AX.X, op=ALU.max), reads=[sc], writes=[mx])
                    k.tt(mx[:], mx[:], sinks[:, g * 4:(g + 1) * 4], ALU.max, [mx, sinks], [mx])
                    k.ts(nmx[:], mx[:], -1.0, None, ALU.mult, None, [mx], [nmx])
                    for i in range(4):
                        k.act(pb[:, i, :], sc[:, i, :], AF.Exp, [sc, nmx], [pb, rsum], bias=nmx[:, i:i + 1], accum=rsum[:, i:i + 1])
                    k.tt(esk[:], sinks[:, g * 4:(g + 1) * 4], mx[:], ALU.subtract, [sinks, mx], [esk])
                    k.act(esk[:], esk[:], AF.Exp, [esk], [esk])
                    k.tt(rsum[:], rsum[:], esk[:], ALU.add, [rsum, esk], [rsum])
                    k.op('dve', lambda e: e.reciprocal(out=rsum[:], in_=rsum[:]), reads=[rsum], writes=[rsum])
                    ck('swa_d')
                    k.tt(pb[:], pb[:], bc(rsum[:], 2, 256), ALU.mult, [pb, rsum], [pb])
                    yield
                    b = k.bank()
                    bv = b[:, :].bitcast(BF16)
                    for i in range(4):
                        for kt in range(2):
                            k.tr(bv[:, (i * 2 + kt) * 128:(i * 2 + kt + 1) * 128], pb[:, i, kt * 128:(kt + 1) * 128], identb[:], [pb, identb], [b])
                    ck('swa_e')
                    k.cp(PT[:].rearrange("p i a q -> p (i a q)"), bv[:, 0:1024], [b], [PT], eng='act')
                    k.rel(b)
                    yield
                    b = k.bank()
                    for i in range(4):
                        for kt in range(2):
                            k.mm(b[:, i * 128:(i + 1) * 128], Vtok[:, t + kt, g * 128:(g + 1) * 128], PT[:, i, kt, :], [Vtok, PT], [b],
                                 start=(kt == 0), stop=(kt == 1))
                    for i in range(4):
                        hq = g * 4 + i
                        ch, hf = hq // 2, hq % 2
                        k.cp(obT[hf * 64:(hf + 1) * 64, ch, t * 128:(t + 1) * 128], b[hf * 64:(hf + 1) * 64, i * 128:(i + 1) * 128], [b], [obT],
                             eng=('act' if i % 2 else 'dve'))
                    k.rel(b)
                    yield

                def swa_stream(its, sb_):
                    for (t_, g_) in its:
                        yield from swa_iter(t_, g_, sb_)

                def _drain2(gl):
                    gl = list(gl)
                    while gl:
                        for it in list(gl):
                            try:
                                next(it)
                            except StopIteration:
                                gl.remove(it)

                its_ = [(t_, g_) for t_ in range(4) for g_ in range(4)]
                _drain2([swa_stream(its_[0::2], 0), swa_stream(its_[1::2], 1)])
                ck('swa_f')
                k.cp(KTl[0:64, :, 0:128], KTl[0:64, :, TB:TB + 128], [KTl], [KTl], eng='pool')
                k.cp(KTh[64:128, :, 0:128], KTh[64:128, :, TB:TB + 128], [KTh], [KTh], eng='pool')
                k.cp(Vtok[:, 0, :], Vtok[:, 4, :], [Vtok], [Vtok], eng='pool')
                if blk == 0:
                    dump("obT", obT[:], [obT])

                ck('swa')
                k.switch('A', 'F')
                wlist['cur'] = W8 + FW
                for j in range(8):
                    wt = nextw()
                    wload(w_in[:, OFF_GA + j * 128:OFF_GA + (j + 1) * 128], 128, c0=0, tile=wt)
                    wload(w_in[:, OFF_GB + j * 128:OFF_GB + (j + 1) * 128], 128, c0=128, tile=wt)
                    wload(w_gdn_out[:, j * 128:(j + 1) * 128], 128, c0=256, tile=wt)
                    wload(w_swa_out[:, j * 128:(j + 1) * 128], 128, c0=384, tile=wt)
                    bs = [k.bank() for _ in range(4)]
                    srcs = [hT, hT, onT, obT]
                    for q in range(4):
                        for kk in range(8):
                            k.mm(bs[q][:, :], wt[:, kk, q * 128:(q + 1) * 128], srcs[q][:, kk, :], [wt, srcs[q]], [bs[q]], start=(kk == 0), stop=(kk == 7))
                    k.act(sga[:], bs[0][:, :], AF.Sigmoid, [bs[0]], [sga])
                    k.act(sgb[:], bs[1][:, :], AF.Sigmoid, [bs[1]], [sgb])
                    k.tt(sga[:], sga[:], bs[2][:, :], ALU.mult, [sga, bs[2]], [sga])
                    k.tt(sgb[:], sgb[:], bs[3][:, :], ALU.mult, [sgb, bs[3]], [sgb])
                    k.tt(mixT[:, j, :], sga[:], sgb[:], ALU.add, [sga, sgb], [mixT])
                    k.rel(*bs)
                ck('merge')
                wts = [wload(w_o[:, hf * 512:(hf + 1) * 512], 512) for hf in range(2)]
                for t in range(4):
                    for hf in range(2):
                        b = k.bank()
                        for kk in range(8):
                            k.mm(b[:, :], mixT[:, kk, t * 128:(t + 1) * 128], wts[hf][:, kk, :], [mixT, wts[hf]], [b], start=(kk == 0), stop=(kk == 7))
                        k.tt(sga[:], b[:, :], g1bc[:, hf * 512:(hf + 1) * 512], ALU.mult, [b, g1bc], [sga])
                        k.rel(b)
                        k.tt(x1[t][:, hf * 512:(hf + 1) * 512], x1[t][:, hf * 512:(hf + 1) * 512], sga[:], ALU.add, [x1[t], sga], [x1[t]], eng='pool')
                    rms_to_T(x1[t][:], x1[t], h2T, h2T, (a2, a2), (modT[:, 24:32, NS], modT), t)
                if blk == 0:
                    dump("x1", x1[0][:], [x1[0]])

                ck('wo')
                for jp in range(NFC // 2):
                    wt = nextw()
                    wload(w_ffn_gate[:, jp * 256:(jp + 1) * 256], 256, c0=0, tile=wt)
                    wload(w_ffn_up[:, jp * 256:(jp + 1) * 256], 256, c0=256, tile=wt)
                    for jj in range(2):
                        j = jp * 2 + jj
                        bg, bu = k.bank(), k.bank()
                        for kk in range(8):
                            k.mm(bg[:, :], wt[:, kk, jj * 128:(jj + 1) * 128], h2T[:, kk, :], [wt, h2T], [bg], start=(kk == 0), stop=(kk == 7))
                        for kk in range(8):
                            k.mm(bu[:, :], wt[:, kk, 256 + jj * 128:256 + (jj + 1) * 128], h2T[:, kk, :], [wt, h2T], [bu], start=(kk == 0), stop=(kk == 7))
                        k.cp(gpre[:, 0:2], fhalo[:, j, :], [fhalo], [gpre], eng='pool')
                        k.cp(gpre[:, 2:2 + TB], bg[:, :], [bg], [gpre], eng='act')
                        k.cp(fhalo[:, j, :], gpre[:, TB:TB + 2], [gpre], [fhalo], eng='pool')
                        k.ts(gcv[:], gpre[:, 0:TB], fcwT[:, j, 0:1], fcbT[:, j:j + 1], ALU.mult, ALU.add, [gpre, fcwT, fcbT], [gcv])
                        for tap in range(1, 3):
                            k.stt(gcv[:], gpre[:, tap:tap + TB], fcwT[:, j, tap:tap + 1], gcv[:], ALU.mult, ALU.add, [gpre, fcwT, gcv], [gcv])
                        k.act(gcv[:], gcv[:], AF.Silu, [gcv], [gcv])
                        k.tt(actT[:, j, :], gcv[:], bu[:, :], ALU.mult, [gcv, bu], [actT])
                        k.rel(bg, bu)
                if last:
                    for j_ in range(2):
                        k.dma('sp', o_ffn_p[j_].rearrange("(c p) -> p c", p=128), fhalo[:, :, j_], reads=[fhalo], allow_slow_non_contiguous=True)
                for hf in range(2):
                    bs = [k.bank() for _ in range(4)]
                    for kg in range(3):
                        nk = 8 if kg < 2 else NFC - 16
                        wt = nextw()
                        k.dma('pool', wt[:, 0:nk, :], w_ffn_down[kg * 1024:kg * 1024 + nk * 128, hf * 512:(hf + 1) * 512].rearrange("(c p) n -> p c n", p=128),
                              writes=[wt])
                        for kk in range(nk):
                            kf = kg * 8 + kk
                            for t in range(4):
                                k.mm(bs[t][:, :], actT[:, kf, t * 128:(t + 1) * 128], wt[:, kk, :], [actT, wt], [bs[t]], start=(kf == 0), stop=(kf == NFC - 1))
                    for t in range(4):
                        k.tt(sga[:], bs[t][:, :], g2bc[:, hf * 512:(hf + 1) * 512], ALU.mult, [bs[t], g2bc], [sga])
                        k.tt(x1[t][:, hf * 512:(hf + 1) * 512], x1[t][:, hf * 512:(hf + 1) * 512], sga[:], ALU.add, [x1[t], sga], [x1[t]], eng='pool')
                    k.rel(*bs)
                ck('ffn')
                for t in range(4):
                    k.act(yt[:], x1[t][:], AF.Square, [x1[t]], [yt, ss2], accum=ss2[:])
                    k.act(rs2[:], ss2[:], AF.Ln, [ss2, epsc], [rs2], bias=epsc[:], scale=1.0 / D)
                    k.act(rs2[:], rs2[:], AF.Exp, [rs2], [rs2], scale=-0.5)
                    k.stt(yt[:], x1[t][:], rs2[:], fnw_bc[:], ALU.mult, ALU.mult, [x1[t], rs2, fnw_bc], [yt])
                    k.dma('sp', y_p[t0 + t * 128:t0 + (t + 1) * 128, :], yt[:], reads=[yt])
        except _Stop:
            pass
        k.finish('sp')
        print("instr counts", k.cnt, "dma sems", k.ndsem)
        if os.environ.get('MMSTAT'):
            tot = sum(k.mmstat.values())
            for ln, c in sorted(k.mmstat.items(), key=lambda kv: -kv[1])[:40]:
                print("  mm line %d: %.1f us (%.1f%%)" % (ln, c / 2400.0, 100.0 * c / tot))
            print("  total est %.1f us" % (tot / 2400.0))
    return nc


OUT_NAMES = ["y_p", "y_s", "o_S_p", "o_S_s", "o_conv_p", "o_conv_s", "o_k_p", "o_k_s", "o_v_p", "o_v_s", "o_ffn_p", "o_ffn_s"]


def make_in_maps(inp, cores):
    f = lambda a: np.ascontiguousarray(a, dtype=np.float32)
    shared = {
        "w_mod": f(inp["w_mod"][0]), "b_mod": f(inp["b_mod"][0][None]), "norm1_w": f(inp["norm1_w"][0][None]),
        "norm2_w": f(inp["norm2_w"][0][None]), "w_in": f(inp["w_in"][0]), "gdn_conv_w": f(inp["gdn_conv_w"][0]),
        "gdn_a_log": f(inp["gdn_a_log"][0]), "gdn_dt_bias": f(inp["gdn_dt_bias"][0]),
        "gdn_onorm_w": f(inp["gdn_onorm_w"][0][None]), "w_gdn_out": f(inp["w_gdn_out"][0]),
        "swa_sinks": f(inp["swa_sinks"][0]), "w_swa_out": f(inp["w_swa_out"][0]), "w_o": f(inp["w_o"][0]),
        "w_ffn_gate": f(inp["w_ffn_gate"][0]), "w_ffn_up": f(inp["w_ffn_up"][0]), "ffn_conv_w": f(inp["ffn_conv_w"][0]),
        "ffn_conv_b": f(inp["ffn_conv_b"][0][None]), "w_ffn_down": f(inp["w_ffn_down"][0]),
        "final_norm_w": f(inp["final_norm_w"]),
    }
    maps = []
    for b in cores:
        s = slice(b * NS, (b + 1) * NS)
        m = dict(shared)
        m["x_p"] = f(inp["x_prompt"][b])
        m["x_s"] = f(inp["x_sample"][s, 0])
        m["c17"] = f(np.concatenate([inp["c_sample"][s], inp["c_prompt"][b:b + 1]], axis=0))
        m["st_S"] = f(np.transpose(inp["state_gdn_S"][0, s], (0, 2, 1, 3)))
        m["st_conv"] = f(inp["state_gdn_conv"][0, s].reshape(NS * 3, 3072))
        m["st_k"] = f(np.transpose(inp["cache_swa_k"][0, s].reshape(NS, 128, 256), (1, 0, 2)))
        m["st_v"] = f(np.transpose(inp["cache_swa_v"][0, s].reshape(NS, 128, 256), (1, 0, 2)))
        m["st_ffn"] = f(inp["state_ffn_conv"][0, s].reshape(NS * 2, DFF))
        maps.append(m)
    return maps


def kernel(**inp):
    nc = build_nc()
    cores = list(range(8))
    res = run_bass_kernel_spmd(nc, make_in_maps(inp, cores), core_ids=cores)
    r = res.results
    cat = lambda n: np.concatenate([r[i][n] for i in range(8)], axis=0)
    stack = lambda n: np.stack([r[i][n] for i in range(8)], axis=0)
    y_prompt = stack("y_p")
    y_sample = cat("y_s").reshape(128, 1, D)
    gS_p = np.ascontiguousarray(np.transpose(stack("o_S_p"), (0, 2, 1, 3)))[None]
    gS_s = np.ascontiguousarray(np.transpose(cat("o_S_s"), (0, 2, 1, 3)))[None]
    gc_p = stack("o_conv_p")[None]
    gc_s = cat("o_conv_s")[None]
    k_p = stack("o_k_p").reshape(1, 8, 128, 4, 64)
    k_s = np.ascontiguousarray(np.concatenate([np.transpose(r[i]["o_k_s"], (1, 0, 2)) for i in range(8)], axis=0)).reshape(1, 128, 128, 4, 64)
    v_p = stack("o_v_p").reshape(1, 8, 128, 4, 64)
    v_s = np.ascontiguousarray(np.concatenate([np.transpose(r[i]["o_v_s"], (1, 0, 2)) for i in range(8)], axis=0)).reshape(1, 128, 128, 4, 64)
    f_p = stack("o_ffn_p")[None]
    f_s = cat("o_ffn_s")[None]
    return (y_prompt, y_sample, gS_p, gS_s, gc_p, gc_s, k_p, k_s, v_p, v_s, f_p, f_s)
```

```python
import os
import numpy as np
import concourse.bass as bass
import concourse.mybir as mybir
from concourse.bass_utils import run_bass_kernel_spmd
from contextlib import ExitStack

F32 = mybir.dt.float32
BF16 = mybir.dt.bfloat16
AF = mybir.ActivationFunctionType
ALU = mybir.AluOpType
AX = mybir.AxisListType

D = 1024
SEQ = 2048
TB = 512
NBLK = SEQ // TB
NS = 16
DFF = 2816
NFC = DFF // 128
OFF_QKV, OFF_GATE, OFF_BETA, OFF_A, OFF_SQ, OFF_SK, OFF_SV, OFF_GA, OFF_GB = 0, 3072, 4096, 4104, 4112, 5136, 5392, 5648, 6672
INW = 7696
EPS = 1e-6
NEG = -30000.0


class Tile:
    def __init__(self, t, name):
        self.t = t
        self.name = name
        self.lw = None
        self.rd = {}
        self.dkey = None
        self.dcnt = 0

    def __getitem__(self, k):
        return self.t[k]


class K:
    def __init__(self, nc, es):
        self.nc = nc
        self.es = es
        self.eng = {'pe': nc.tensor, 'dve': nc.vector, 'act': nc.scalar, 'pool': nc.gpsimd, 'sp': nc.sync}
        self.sem = {}
        for e in self.eng:
            self.sem[e] = es.enter_context(nc.semaphore('s_' + e))
        self.cnt = {e: 0 for e in self.eng}
        self.seen = {e: {} for e in self.eng}
        self.ndsem = 0
        self.tiles = []
        self.free_banks = []
        self.dbg = []
        self.phase_off = {}

    def sb(self, name, shape, dt=F32, es=None):
        t = (es or self.es).enter_context(self.nc.sbuf_tensor(name, list(shape), dt))
        T = Tile(t, name)
        self.tiles.append(T)
        return T

    def view(self, ap, name):
        T = Tile(ap, name)
        self.tiles.append(T)
        return T

    def init_psum(self):
        self.psum = self.es.enter_context(self.nc.psum_tensor("psum", [128, 4096], F32))
        self.banks = []
        for i in range(8):
            T = Tile(self.psum[:, i * 512:(i + 1) * 512], "bank%d" % i)
            T.excl = True
            self.tiles.append(T)
            self.banks.append(T)
        self.free_banks = list(self.banks)

    def bank(self):
        assert self.free_banks, "out of PSUM banks"
        return self.free_banks.pop(0)

    def rel(self, *bs):
        for b in bs:
            assert b not in self.free_banks
            self.free_banks.append(b)

    def _deps(self, e, reads, writes, skip=None):
        deps = {}

        def add(kv):
            k_, v = kv
            if deps.get(k_, 0) < v:
                deps[k_] = v
        for t in reads:
            if t.lw:
                add(t.lw)
            if getattr(t, 'excl', False):
                for kv in t.rd.items():
                    if kv[0] != e:
                        add(kv)
        for t in writes:
            if t.lw:
                add(t.lw)
            for kv in t.rd.items():
                add(kv)
        for k_, v in deps.items():
            if k_ == e and e == 'pe':
                continue
            if skip is not None and k_ == skip:
                continue
            if self.seen[e].get(k_, 0) >= v:
                continue
            self.eng[e].wait_ge(self.sem[k_], v)
            self.seen[e][k_] = v

    def op(self, e, fn, reads=(), writes=()):
        if e == 'pool' and getattr(self, 'pool_to', None):
            e = self.pool_to
        self._deps(e, reads, writes)
        ins = fn(self.eng[e])
        ins.then_inc(self.sem[e], 1)
        self.cnt[e] += 1
        c = self.cnt[e]
        for t in writes:
            t.lw = (e, c)
            t.rd = {}
        for t in reads:
            if t not in writes:
                t.rd[e] = c

    def dma(self, q, out, in_, reads=(), writes=(), semtile=None, indep=False, **kw):
        T = semtile if semtile is not None else (writes[0] if writes else reads[0])
        self._deps(q, reads, writes, skip=(T.dkey if indep else None))
        if T.dkey is None:
            T.dkey = 'd%d' % self.ndsem
            self.ndsem += 1
            self.sem[T.dkey] = self.es.enter_context(self.nc.semaphore(T.dkey))
        self.eng[q].dma_start(out=out, in_=in_, **kw).then_inc(self.sem[T.dkey], 16)
        T.dcnt += 16
        for t in writes:
            t.lw = (T.dkey, T.dcnt)
            t.rd = {}
        for t in reads:
            t.rd[T.dkey] = T.dcnt

    def init_arena(self, nbytes):
        self.arena = self.es.enter_context(self.nc.sbuf_tensor("arena", [128, nbytes // 4], F32))
        self.phase_tiles = {}

    def carve(self, phase, name, shape, dt=F32):
        off = self.phase_off.get(phase, 0)
        n = 1
        for d_ in shape[1:]:
            n *= d_
        nb = n * (2 if dt == BF16 else 4)
        nb = (nb + 63) // 64 * 64
        assert off + nb <= self.arena.shape[1] * 4, "arena overflow in phase %s at %s: %d" % (phase, name, off + nb)
        ap = self.arena[0:shape[0], off // 4:(off + nb) // 4]
        if dt == BF16:
            ap = ap.bitcast(BF16)
        ap = ap[:, 0:n]
        if len(shape) == 3:
            ap = ap.rearrange("p (a b) -> p a b", b=shape[2])
        elif len(shape) == 4:
            ap = ap.rearrange("p (a b c) -> p a b c", b=shape[2], c=shape[3])
        self.phase_off[phase] = off + nb
        T = Tile(ap, name)
        self.tiles.append(T)
        self.phase_tiles.setdefault(phase, []).append(T)
        return T

    def switch(self, frm, to):
        acc = {}
        for F_ in self.phase_tiles.get(frm, []):
            if F_.lw:
                acc[F_.lw[0]] = max(acc.get(F_.lw[0], 0), F_.lw[1])
            for k_, v in F_.rd.items():
                acc[k_] = max(acc.get(k_, 0), v)
        for T in self.phase_tiles.get(to, []):
            for k_, v in acc.items():
                T.rd[k_] = max(T.rd.get(k_, 0), v)

    def barrier(self):
        for e in self.eng:
            for T in self.tiles:
                if T.dkey is not None and self.seen[e].get(T.dkey, 0) < T.dcnt:
                    self.eng[e].wait_ge(self.sem[T.dkey], T.dcnt)
                    self.seen[e][T.dkey] = T.dcnt
            for k_ in self.eng:
                if k_ != e and self.cnt[k_] > 0 and self.seen[e].get(k_, 0) < self.cnt[k_]:
                    self.eng[e].wait_ge(self.sem[k_], self.cnt[k_])
                    self.seen[e][k_] = self.cnt[k_]

    def finish(self, e='sp'):
        for T in self.tiles:
            if T.dkey is not None and self.seen[e].get(T.dkey, 0) < T.dcnt:
                self.eng[e].wait_ge(self.sem[T.dkey], T.dcnt)
                self.seen[e][T.dkey] = T.dcnt
        for k_ in self.eng:
            if k_ != e and self.cnt[k_] > 0 and self.seen[e].get(k_, 0) < self.cnt[k_]:
                self.eng[e].wait_ge(self.sem[k_], self.cnt[k_])
                self.seen[e][k_] = self.cnt[k_]

    def mm(self, out, lhsT, rhs, r, w, start=True, stop=True):
        import traceback
        ln = traceback.extract_stack(limit=2)[0].lineno
        n = 1
        for d_ in out.shape[1:]:
            n *= d_
        cyc = n * (4 if rhs.dtype == F32 else 1)
        st = self.__dict__.setdefault('mmstat', {})
        st[ln] = st.get(ln, 0) + max(cyc, 64)
        self.op('pe', lambda e: e.matmul(out, lhsT=lhsT, rhs=rhs, start=start, stop=stop), reads=r, writes=w)

    def tr(self, out, in_, ident, r, w):
        import traceback
        ln = traceback.extract_stack(limit=2)[0].lineno
        n = 1
        for d_ in out.shape[1:]:
            n *= d_
        cyc = n * (2 if in_.dtype == F32 else 1)
        st = self.__dict__.setdefault('mmstat', {})
        st[ln] = st.get(ln, 0) + max(cyc, 64)
        self.op('pe', lambda e: e.transpose(out=out, in_=in_, identity=ident), reads=r, writes=w)

    def act(self, out, in_, func, r, w, bias=None, scale=None, accum=None, eng='act'):
        kw = {}
        if bias is not None:
            kw['bias'] = bias
        if scale is not None:
            kw['scale'] = scale
        if accum is not None:
            kw['accum_out'] = accum
        self.op('act', lambda e: e.activation(out=out, in_=in_, func=func, **kw), reads=r, writes=w)

    def tt(self, out, in0, in1, op, r, w, eng='dve'):
        self.op(eng, lambda e: e.tensor_tensor(out=out, in0=in0, in1=in1, op=op), reads=r, writes=w)

    def ts(self, out, in0, s1, s2, op0, op1, r, w, eng='dve', accum=None):
        if op1 is None:
            self.op(eng, lambda e: e.tensor_scalar(out=out, in0=in0, scalar1=s1, scalar2=None, op0=op0), reads=r, writes=w)
        else:
            self.op(eng, lambda e: e.tensor_scalar(out=out, in0=in0, scalar1=s1, scalar2=s2, op0=op0, op1=op1), reads=r, writes=w)

    def stt(self, out, in0, scalar, in1, op0, op1, r, w, accum=None):
        if accum is None:
            self.op('dve', lambda e: e.scalar_tensor_tensor(out=out, in0=in0, scalar=scalar, in1=in1, op0=op0, op1=op1), reads=r, writes=w)
        else:
            self.op('dve', lambda e: e.scalar_tensor_tensor(out=out, in0=in0, scalar=scalar, in1=in1, op0=op0, op1=op1, accum_out=accum), reads=r, writes=w)

    def cp(self, out, in_, r, w, eng='dve'):
        if eng == 'act':
            self.op('act', lambda e: e.activation(out=out, in_=in_, func=AF.Copy), reads=r, writes=w)
        else:
            self.op(eng, lambda e: e.tensor_copy(out=out, in_=in_), reads=r, writes=w)

    def memset(self, out, val, w, eng='pool'):
        self.op(eng, lambda e: e.memset(out, val), writes=w)

    def asel(self, out, in_, pattern, cmp, fill, base, cm, r, w):
        self.op('pool', lambda e: e.affine_select(out=out, in_=in_, pattern=pattern, compare_op=cmp, fill=fill,
                                                  base=base, channel_multiplier=cm), reads=r, writes=w)


def bc(ap, axis, n):
    a = ap.unsqueeze(axis)
    shp = list(a.shape)
    shp[axis] = n
    return a.broadcast_to(shp)


class _Stop(Exception):
    pass


def build_nc(debug=False, nblk=NBLK, do_sample=True, stop=None):
    nc = bass.Bass("TRN2", target_bir_lowering=False)

    def din(name, shape):
        return nc.dram_tensor(name, list(shape), F32, kind="ExternalInput").ap()

    def dout(name, shape):
        return nc.dram_tensor(name, list(shape), F32, kind="ExternalOutput").ap()

    x_p = din("x_p", [SEQ, D])
    x_s = din("x_s", [NS, D])
    c17 = din("c17", [NS + 1, D])
    st_S = din("st_S", [NS, 128, 8, 128])
    st_conv = din("st_conv", [NS * 3, 3072])
    st_k = din("st_k", [128, NS, 256])
    st_v = din("st_v", [128, NS, 256])
    st_ffn = din("st_ffn", [NS * 2, DFF])
    w_mod = din("w_mod", [D, 6 * D])
    b_mod = din("b_mod", [1, 6 * D])
    norm1_w = din("norm1_w", [1, D])
    norm2_w = din("norm2_w", [1, D])
    w_in = din("w_in", [D, INW])
    gdn_conv_w = din("gdn_conv_w", [4, 3072])
    gdn_a_log = din("gdn_a_log", [8])
    gdn_dt_bias = din("gdn_dt_bias", [8])
    gdn_onorm_w = din("gdn_onorm_w", [1, 128])
    w_gdn_out = din("w_gdn_out", [D, D])
    swa_sinks = din("swa_sinks", [16])
    w_swa_out = din("w_swa_out", [D, D])
    w_o = din("w_o", [D, D])
    w_ffn_gate = din("w_ffn_gate", [D, DFF])
    w_ffn_up = din("w_ffn_up", [D, DFF])
    ffn_conv_w = din("ffn_conv_w", [3, DFF])
    ffn_conv_b = din("ffn_conv_b", [1, DFF])
    w_ffn_down = din("w_ffn_down", [DFF, D])
    final_norm_w = din("final_norm_w", [D])

    y_p = dout("y_p", [SEQ, D])
    y_s = dout("y_s", [NS, D])
    o_S_p = dout("o_S_p", [128, 8, 128])
    o_S_s = dout("o_S_s", [NS, 128, 8, 128])
    o_conv_p = dout("o_conv_p", [3, 3072])
    o_conv_s = dout("o_conv_s", [NS, 3, 3072])
    o_k_p = dout("o_k_p", [128, 256])
    o_k_s = dout("o_k_s", [128, NS, 256])
    o_v_p = dout("o_v_p", [128, 256])
    o_v_s = dout("o_v_s", [128, NS, 256])
    o_ffn_p = dout("o_ffn_p", [2, DFF])
    o_ffn_s = dout("o_ffn_s", [NS, 2, DFF])

    with ExitStack() as es:
        k = K(nc, es)
        k.init_psum()
        PS = k.psum

        def dump(name, ap, tiles):
            if not debug:
                return
            o = nc.dram_tensor("dbg_" + name, list(ap.shape), ap.dtype, kind="ExternalOutput").ap()
            dt_ = Tile(None, 'dbg_' + name)
            k.tiles.append(dt_)
            k.dma('sp', o, ap, reads=tiles, semtile=dt_)

        dbgsem = k.sb("dbgsem", [1, 1])

        def ck(name):
            if stop == name:
                raise _Stop()

        try:
            identf = k.sb("identf", [128, 128])
            identb = k.sb("identb", [128, 128], BF16)
            onesf = k.sb("onesf", [128, 128])
            onesb = k.sb("onesb", [128, 128], BF16)
            Um = k.sb("Um", [64, 64])
            SLm = k.sb("SLm", [64, 64])
            SUm = k.sb("SUm", [64, 64])
            nSL = k.sb("nSL", [64, 64])
            nSU = k.sb("nSU", [64, 64])
            maskA = k.sb("maskA", [128, 256])
            maskB = k.sb("maskB", [128, 256])
            E16 = k.sb("E16", [NS + 1, 128])
            Esel = k.sb("Esel", [NS, NS, 128])
            epsc = k.sb("epsc", [128, 1])
            onec = k.sb("onec", [128, 1])

            k.memset(identf[:], 0.0, [identf])
            k.asel(identf[:], identf[:], [[-1, 128]], ALU.not_equal, 1.0, 0, 1, [identf], [identf])
            k.cp(identb[:], identf[:], [identf], [identb])
            k.memset(onesf[:], 1.0, [onesf])
            k.memset(onesb[:], 1.0, [onesb])
            k.memset(epsc[:], EPS, [epsc])
            k.memset(onec[:], 1.0, [onec])
            k.memset(Um[:], 1.0, [Um])
            k.asel(Um[:], Um[:], [[1, 64]], ALU.is_ge, 0.0, 0, -1, [Um], [Um])
            k.memset(SLm[:], 1.0, [SLm])
            k.asel(SLm[:], SLm[:], [[-1, 64]], ALU.is_ge, 0.0, -1, 1, [SLm], [SLm])
            k.memset(SUm[:], 1.0, [SUm])
            k.asel(SUm[:], SUm[:], [[1, 64]], ALU.is_ge, 0.0, -1, -1, [SUm], [SUm])
            k.ts(nSL[:], SLm[:], -1.0, None, ALU.mult, None, [SLm], [nSL])
            k.ts(nSU[:], SUm[:], -1.0, None, ALU.mult, None, [SUm], [nSU])
            k.memset(maskA[:], 0.0, [maskA])
            k.asel(maskA[:], maskA[:], [[1, 256]], ALU.is_ge, NEG, -1, -1, [maskA], [maskA])
            k.asel(maskA[:], maskA[:], [[-1, 256]], ALU.is_ge, NEG, 128, 1, [maskA], [maskA])
            k.asel(maskB[:], maskA[:], [[1, 256]], ALU.is_ge, NEG, -128, 0, [maskA], [maskB])
            k.memset(E16[:], 0.0, [E16])
            k.asel(E16[:], E16[:], [[0, 128]], ALU.not_equal, 1.0, -NS, 1, [E16], [E16])
            k.memset(Esel[:], 0.0, [Esel])
            k.asel(Esel[:], Esel[:], [[-1, NS], [0, 128]], ALU.not_equal, 1.0, 0, 1, [Esel], [Esel])

            k.pool_to = 'dve'
            modT = k.sb("modT", [128, 48, NS + 1])
            n1w = k.sb("n1w", [128, 8])
            n2w = k.sb("n2w", [128, 8])
            a1 = k.sb("a1", [128, 8])
            a2 = k.sb("a2", [128, 8])
            A1s = k.sb("A1s", [128, 8, NS])
            A2s = k.sb("A2s", [128, 8, NS])
            cwT = k.sb("cwT", [128, 24, 4])
            fcwT = k.sb("fcwT", [128, NFC, 3])
            fcbT = k.sb("fcbT", [128, NFC])
            onwT = k.sb("onwT", [128, 1])
            fnw_bc = k.sb("fnw_bc", [128, D])
            g1bc = k.sb("g1bc", [128, D])
            g2bc = k.sb("g2bc", [128, D])
            gtok1 = k.sb("gtok1", [NS + 1, D])
            gtok2 = k.sb("gtok2", [NS + 1, D])
            negA = k.sb("negA", [64, 8])
            dtb = k.sb("dtb", [64, 8])
            sinks = k.sb("sinks", [128, 16])

            W8 = [k.sb("W8_%d" % i, [128, 8, 512], BF16) for i in range(3)]
            ring = {'w8': 0, 'wd': 0}

            wlist = {'cur': W8}

            def nextw():
                wl = wlist['cur']
                t_ = wl[ring['w8'] % len(wl)]
                ring['w8'] += 1
                return t_

            def wload(src, ncols, c0=0, tile=None):
                piece = tile is not None
                if tile is None:
                    tile = nextw()
                k.dma('pool', tile[:, :, c0:c0 + ncols], src.rearrange("(c p) n -> p c n", p=128), writes=[tile], indep=piece)
                return tile

            with ExitStack() as es2:
                stage = k.sb("stage", [8, 6 * D], F32, es=es2)
                c17t = k.sb("c17t", [NS + 1, D], F32, es=es2)
                scT = k.sb("scT", [128, 8, NS + 1], BF16, es=es2)
                bmT = k.sb("bmT", [128, 48], F32, es=es2)

                def featmajor(src, r, C, dst_ap, dst_tile):
                    k.dma('sp', stage[0:r, 0:C], src, writes=[stage])
                    nchunk = C // 128
                    c = 0
                    while c < nchunk:
                        n = min(nchunk - c, 512 // r)
                        b = k.bank()
                        for j in range(n):
                            k.tr(b[:, j * r:(j + 1) * r], stage[0:r, (c + j) * 128:(c + j + 1) * 128], identf[0:r, 0:r],
                                 [stage, identf], [b])
                        if r == 1:
                            k.cp(dst_ap[:, c:c + n], b[:, 0:n], [b], [dst_tile])
                        else:
                            k.cp(dst_ap[:, c:c + n, :], b[:, 0:n * r].rearrange("p (c r) -> p c r", r=r), [b], [dst_tile])
                        k.rel(b)
                        c += n

                ck('consts')
                featmajor(b_mod, 1, 6 * D, bmT, bmT)
                featmajor(norm1_w, 1, D, n1w, n1w)
                featmajor(norm2_w, 1, D, n2w, n2w)
                featmajor(gdn_conv_w, 4, 3072, cwT, cwT)
                featmajor(ffn_conv_w, 3, DFF, fcwT, fcwT)
                featmajor(ffn_conv_b, 1, DFF, fcbT, fcbT)
                featmajor(gdn_onorm_w, 1, 128, onwT, onwT)
                ck('fm')
                k.dma('sp', fnw_bc[:], final_norm_w.partition_broadcast(128), writes=[fnw_bc])
                k.dma('sp', negA[:], gdn_a_log.partition_broadcast(64), writes=[negA])
                k.dma('sp', dtb[:], gdn_dt_bias.partition_broadcast(64), writes=[dtb])
                k.dma('sp', sinks[:], swa_sinks.partition_broadcast(128), writes=[sinks])
                k.act(negA[:], negA[:], AF.Exp, [negA], [negA])
                k.ts(negA[:], negA[:], -1.0, None, ALU.mult, None, [negA], [negA])

                ck('bcast')
                k.dma('sp', c17t[:], c17, writes=[c17t])
                k.act(c17t[:], c17t[:], AF.Silu, [c17t], [c17t])
                b = k.bank()
                for kk in range(8):
                    k.tr(b[:, kk * 17:(kk + 1) * 17], c17t[:, kk * 128:(kk + 1) * 128], identf[0:17, 0:17], [c17t, identf], [b])
                k.cp(scT[:], b[:, 0:8 * 17].rearrange("p (c r) -> p c r", r=17), [b], [scT])
                k.rel(b)
                ck('silu')
                for half in range(2):
                    b = k.bank()
                    for g in range(6):
                        wt = wload(w_mod[:, (half * 6 + g) * 512:(half * 6 + g + 1) * 512], 512)
                        for j in range(4):
                            jj = g * 4 + j
                            for kk in range(8):
                                k.mm(b[:, jj * 17:(jj + 1) * 17], wt[:, kk, j * 128:(j + 1) * 128], scT[:, kk, :], [wt, scT], [b],
                                     start=(kk == 0), stop=(kk == 7))
                    k.tt(modT[:, half * 24:(half + 1) * 24, :], b[:, 0:24 * 17].rearrange("p (c r) -> p c r", r=17),
                         bc(bmT[:, half * 24:(half + 1) * 24], 2, 17), ALU.add, [b, bmT], [modT])
                    k.rel(b)
                ck('modT')
                k.barrier()
            dump("modT", modT[:], [modT])

            k.ts(a1[:], modT[:, 8:16, NS], 1.0, None, ALU.add, None, [modT], [a1])
            k.tt(a1[:], a1[:], n1w[:], ALU.mult, [a1, n1w], [a1])
            k.ts(a2[:], modT[:, 32:40, NS], 1.0, None, ALU.add, None, [modT], [a2])
            k.tt(a2[:], a2[:], n2w[:], ALU.mult, [a2, n2w], [a2])
            k.ts(A1s[:], modT[:, 8:16, 0:NS], 1.0, None, ALU.add, None, [modT], [A1s])
            k.tt(A1s[:], A1s[:], bc(n1w[:], 2, NS), ALU.mult, [A1s, n1w], [A1s])
            k.ts(A2s[:], modT[:, 32:40, 0:NS], 1.0, None, ALU.add, None, [modT], [A2s])
            k.tt(A2s[:], A2s[:], bc(n2w[:], 2, NS), ALU.mult, [A2s, n2w], [A2s])
            for (c0, gtok, gbc) in ((16, gtok1, g1bc), (40, gtok2, g2bc)):
                b0, b1 = k.bank(), k.bank()
                for j in range(8):
                    bb = b0 if j < 4 else b1
                    k.tr(bb[0:17, (j % 4) * 128:(j % 4 + 1) * 128], modT[:, c0 + j, :], identf[:], [modT, identf], [bb])
                k.cp(gtok[:, 0:512], b0[0:17, :], [b0], [gtok])
                k.cp(gtok[:, 512:1024], b1[0:17, :], [b1], [gtok])
                for hf, bb in ((0, b0), (1, b1)):
                    k.mm(bb[:, :], E16[:], gtok[:, hf * 512:(hf + 1) * 512], [E16, gtok], [bb])
                    k.cp(gbc[:, hf * 512:(hf + 1) * 512], bb[:, :], [bb], [gbc])
                k.rel(b0, b1)
            dump("g1bc", g1bc[:], [g1bc])

            ck('derived')
            S_all = k.sb("S_all", [128, 8, 128])
            halo = k.sb("halo", [128, 24, 3])
            fhalo = k.sb("fhalo", [128, NFC, 2])
            KTl = k.sb("KTl", [128, 4, 128 + TB], BF16)
            KTh = k.sb("KTh", [128, 4, 128 + TB], BF16)
            Vtok = k.sb("Vtok", [128, 5, 512], BF16)
            k.memset(S_all[:], 0.0, [S_all])
            k.memset(halo[:], 0.0, [halo])
            k.memset(fhalo[:], 0.0, [fhalo])
            k.memset(KTl[:], 0.0, [KTl])
            k.memset(KTh[:], 0.0, [KTh])
            k.memset(Vtok[:], 0.0, [Vtok])

            xres = [k.sb("xres%d" % i, [128, D]) for i in range(4)]
            x1 = xres
            xn = k.sb("xn", [128, D], BF16)
            ss1 = k.sb("ss1", [128, 1])
            rs1 = k.sb("rs1", [128, 1])
            ss2, rs2 = ss1, rs1
            hT = k.sb("hT", [128, 8, TB], BF16)
            h2T = hT
            onT = k.sb("onT", [128, 8, TB], BF16)
            obT = k.sb("obT", [128, 8, TB], BF16)
            mixT = k.sb("mixT", [128, 8, TB], BF16)
            QT = mixT
            halo_out = k.sb("halo_out", [128, 24, 3])

            k.init_arena(74240)
            cG = lambda n, shp, dt=F32: k.carve('G', n, shp, dt)
            cA = lambda n, shp, dt=F32: k.carve('A', n, shp, dt)
            cF = lambda n, shp, dt=F32: k.carve('F', n, shp, dt)
            ba = cG("ba", [64, 8, 16])
            beta = cG("beta", [64, 8, 8])
            gg = cG("gg", [64, 8, 8])
            t64a = cG("t64a", [64, 8, 8])
            t64b = cG("t64b", [64, 8, 8])
            dd = cG("dd", [64, 64])
            ed = cG("ed", [64, 64])
            ekd = cG("ekd", [64, 64])
            bed = cG("bed", [64, 64])
            elast = cG("elast", [128, 64])
            pre = cG("pre", [128, 3 + TB])
            cv = cG("cv", [128, 3, TB])
            rqk = cG("rqk", [128, 2, TB])
            Qd2 = [cG("Qd%d" % i, [128, TB], BF16) for i in range(2)]
            Qtb = cG("Qtb", [128, TB], BF16)
            Ktb = cG("Ktb", [128, TB], BF16)
            Vtb = cG("Vtb", [128, TB], BF16)
            Sb = cG("Sb", [128, 128], BF16)
            Gp2 = [cG("Gp%d" % i, [128, TB]) for i in range(2)]
            SA = cG("SA", [64, 8, 64])
            SB = cG("SB", [64, 8, 64])
            Wm = cG("Wm", [64, 8, 64])
            Zm = cG("Zm", [64, 8, 64])
            Am2 = [cG("Am%d" % i, [64, 8, 64]) for i in range(2)]
            Amb2 = [cG("Amb%d" % i, [64, 8, 64], BF16) for i in range(2)]
            NEU = F32 if os.environ.get('NEU32', '1') == '1' else BF16
            WXH = [cG("WXH%d" % i, [64, 4, 2, 64], BF16) for i in range(2)]
            ZYH = [cG("ZYH%d" % i, [64, 4, 2, 64], BF16) for i in range(2)]
            Y32 = [cG("Y32_%d" % i, [64, 4, 64]) for i in range(2)]
            Kbd = cG("Kbd", [64, 8, 128], NEU)
            Kdec2 = [cG("Kdec%d" % i, [64, 8, 128], F32 if os.environ.get('SUPD32', '0') == '1' else BF16) for i in range(2)]
            Vb = cG("Vb", [64, 8, 128], NEU)
            osb = cG("osb", [64, 8, 128])
            uu2 = [cG("uu%d" % i, [64, 8, 128]) for i in range(2)]
            wT2 = [cG("wT%d" % i, [128, 8, 64], BF16) for i in range(2)]
            vnew = cG("vnew", [64, 128], F32 if os.environ.get('SUPD32', '0') == '1' else BF16)
            oss = cG("oss", [64, 8])
            ors = cG("ors", [64, 8])
            on1 = cG("on1", [64, 8, 128], BF16)
            Qt_ap, Kt_ap = cv[:, 0, :], cv[:, 1, :]
            Ktok = cA("Ktok", [128, 512], BF16)
            kvout = cA("kvout", [128, 512])
            sc2 = [cA("sc%d" % i, [128, 4, 256]) for i in range(2)]
            pb2 = [cA("pb%d" % i, [128, 4, 256], BF16) for i in range(2)]
            PT2 = [cA("PT%d" % i, [128, 4, 2, 128], BF16) for i in range(2)]
            mx2 = [cA("mx%d" % i, [128, 4]) for i in range(2)]
            nmx2 = [cA("nmx%d" % i, [128, 4]) for i in range(2)]
            rsum2 = [cA("rsum%d" % i, [128, 4]) for i in range(2)]
            esk2 = [cA("esk%d" % i, [128, 4]) for i in range(2)]
            actT = cF("actT", [128, NFC, TB], BF16)
            sga = cF("sga", [128, TB])
            sgb = cF("sgb", [128, TB])
            gpre = cF("gpre", [128, 2 + TB])
            gcv = cF("gcv", [128, TB])
            yt = cF("yt", [128, D])
            k.carve('FW', 'fwpad', [128, k.phase_off['F'] // 4])
            FW = [k.carve('FW', 'FW%d' % i, [128, 8, 512], BF16) for i in range(4)]
            k.phase_tiles['FW'] = k.phase_tiles['FW'][1:]
            print("arena use", k.phase_off)

            def rms_to_T(xt, xt_tile, dstT, dst_tile, acol, bcol, t):
                k.act(xn[:], xt, AF.Square, [xt_tile], [xn, ss1], accum=ss1[:])
                k.act(rs1[:], ss1[:], AF.Ln, [ss1, epsc], [rs1], bias=epsc[:], scale=1.0 / D)
                k.act(rs1[:], rs1[:], AF.Exp, [rs1], [rs1], scale=-0.5)
                k.ts(xn[:], xt, rs1[:], None, ALU.mult, None, [xt_tile, rs1], [xn])
                b = k.bank()
                bv = b[:, :].bitcast(BF16)
                for kk in range(8):
                    k.tr(bv[:, kk * 128:(kk + 1) * 128], xn[:, kk * 128:(kk + 1) * 128], identb[:], [xn, identb], [b])
                for kk in range(8):
                    k.act(dstT[:, kk, t * 128:(t + 1) * 128], bv[:, kk * 128:(kk + 1) * 128], AF.Identity,
                          [b, acol[1], bcol[1]], [dst_tile], bias=bcol[0][:, kk:kk + 1], scale=acol[0][:, kk:kk + 1])
                k.rel(b)

            hoist = {'p1': False}

            def emit_p1(t0_):
                for t in range(4):
                    k.dma('sp', xres[t][:], x_p[t0_ + t * 128:t0_ + (t + 1) * 128, :], writes=[xres[t]])
                for t in range(4):
                    rms_to_T(xres[t][:], xres[t], hT, hT, (a1, a1), (modT[:, 0:8, NS], modT), t)

            def sample_phase():
                c1 = lambda n, shp, dt=F32: k.carve('S1', n, shp, dt)
                c2 = lambda n, shp, dt=F32: k.carve('S2', n, shp, dt)
                c3 = lambda n, shp, dt=F32: k.carve('S3', n, shp, dt)
                xs = c1("xs", [NS, D])
                xs_2 = c2("xs_2", [NS, D])
                xs_3 = c3("xs_3", [NS, D])
                hTs = k.sb("hTs", [128, 8, NS], BF16)
                onTs = k.sb("onTs", [128, 8, NS], BF16)
                obTs = k.sb("obTs", [128, 8, NS], BF16)
                OH = k.sb("OH", [128, NS, NS])
                sinkcol = k.sb("sinkcol", [128, 1])
                onw_bc = k.sb("onw_bc", [NS, 128])
                hsc = k.sb("hsc", [128, 8, NS])
                k.pool_to = None
                k.memset(OH[:], 0.0, [OH])
                k.asel(OH[:], OH[:], [[1, NS], [-1, NS]], ALU.not_equal, 1.0, 0, 0, [OH], [OH])
                k.pool_to = 'dve'
                for a_ in range(8):
                    k.dma('sp', sinkcol[a_ * 16:(a_ + 1) * 16, :], swa_sinks.rearrange("(h o) -> h o", o=1), writes=[sinkcol])
                k.dma('sp', onw_bc[:], gdn_onorm_w[0].partition_broadcast(NS), writes=[onw_bc])

                def rms_T_s(src, src_tile, dst, A_, B_ap, B_tile):
                    k.act(xn[0:NS, :], src, AF.Square, [src_tile], [xn, ss1], accum=ss1[0:NS, :])
                    k.act(rs1[0:NS, :], ss1[0:NS, :], AF.Ln, [ss1, epsc], [rs1], bias=epsc[0:NS, :], scale=1.0 / D)
                    k.act(rs1[0:NS, :], rs1[0:NS, :], AF.Exp, [rs1], [rs1], scale=-0.5)
                    k.ts(xn[0:NS, :], src, rs1[0:NS, :], None, ALU.mult, None, [src_tile, rs1], [xn])
                    b = k.bank()
                    bv = b[:, :].bitcast(BF16)
                    for kk in range(8):
                        k.tr(bv[:, kk * NS:(kk + 1) * NS], xn[0:NS, kk * 128:(kk + 1) * 128], identb[0:NS, 0:NS], [xn, identb], [b])
                    pv = bv[:, 0:8 * NS].rearrange("p (c s) -> p c s", s=NS)
                    k.tt(hsc[:], pv, A_[:], ALU.mult, [b, A_], [hsc])
                    k.rel(b)
                    k.tt(dst[:], hsc[:], B_ap, ALU.add, [hsc, B_tile], [dst])

                def tok_mm(srcT, wt, c0, n, dst_ap, dst_tile, scale=None, func=None):
                    b = k.bank()
                    for kk in range(8):
                        k.mm(b[0:NS, 0:n], srcT[:, kk, :], wt[:, kk, c0:c0 + n], [srcT, wt], [b], start=(kk == 0), stop=(kk == 7))
                    if func is not None:
                        k.act(dst_ap, b[0:NS, 0:n], func, [b], [dst_tile])
                    elif scale is not None:
                        k.act(dst_ap, b[0:NS, 0:n], AF.Copy, [b], [dst_tile], scale=scale)
                    else:
                        k.cp(dst_ap, b[0:NS, 0:n], [b], [dst_tile])
                    k.rel(b)

                def to_T(src_ap, src_tile, nch, dst, dst_tile, rows=NS):
                    c = 0
                    per = 512 // rows
                    while c < nch:
                        n = min(per, nch - c)
                        b = k.bank()
                        for j in range(n):
                            k.tr(b[:, j * rows:(j + 1) * rows], src_ap[:, (c + j) * 128:(c + j + 1) * 128], identf[0:rows, 0:rows], [src_tile, identf], [b])
                        k.cp(dst[:, c:c + n, :], b[:, 0:n * rows].rearrange("p (c s) -> p c s", s=rows), [b], [dst_tile])
                        k.rel(b)
                        c += n

                def to_tok(srcT, src_tile, nch, dst_ap, dst_tile):
                    c = 0
                    while c < nch:
                        n = min(4, nch - c)
                        b = k.bank()
                        for j in range(n):
                            k.tr(b[0:NS, j * 128:(j + 1) * 128], srcT[:, c + j, :], identf[:], [src_tile, identf], [b])
                        k.cp(dst_ap[:, c * 128:(c + n) * 128], b[0:NS, 0:n * 128], [b], [dst_tile])
                        k.rel(b)
                        c += n

                qkv_p = [xres[0], xres[1], xres[2]]
                gate_s = xres[3]
                ba_s = c1("ba_s", [NS, 16])
                stc = c1("stc", [NS * 3, 3072])
                stT = c1("stT", [128, 24, NS * 3])
                newT = c1("newT", [128, 24, NS])
                cvT = c1("cvT", [128, 24, NS])
                tmpT = c1("tmpT", [128, 24, NS])
                rq_s = c1("rq_s", [128, 16, NS])
                qkp = c1("qkp", [128, 8, NS])
                beta_s = c1("beta_s", [NS, 8])
                alpha_s = c1("alpha_s", [NS, 8])
                t16a = c1("t16a", [NS, 8])
                t16b = c1("t16b", [NS, 8])
                qk_s = c1("qk_s", [NS, 8])
                v_tok = c1("v_tok", [NS, 8, 128])
                d_tok = c1("d_tok", [NS, 8, 128])
                o_tok = c1("o_tok", [NS, 8, 128])
                t_tok = c1("t_tok", [NS, 8, 128])
                oss_s = c1("oss_s", [NS, 8])
                Ss = [c1("Ss%d" % i, [128, 8, 128]) for i in range(2)]
                pK = c1("pK", [128, 8, 128])
                pQ = c1("pQ", [128, 8, 128])
                abc = c1("abc", [128, 8])

                k.dma('sp', xs[:], x_s, writes=[xs])
                rms_T_s(xs[:], xs, hTs, A1s, modT[:, 0:8, 0:NS], modT)
                for g_ in range(6):
                    wt = wload(w_in[:, OFF_QKV + g_ * 512:OFF_QKV + (g_ + 1) * 512], 512)
                    tok_mm(hTs, wt, 0, 512, qkv_p[g_ // 2][0:NS, (g_ % 2) * 512:(g_ % 2 + 1) * 512], qkv_p[g_ // 2])
                for g_ in range(2):
                    wt = wload(w_in[:, OFF_GATE + g_ * 512:OFF_GATE + (g_ + 1) * 512], 512)
                    tok_mm(hTs, wt, 0, 512, gate_s[0:NS, g_ * 512:(g_ + 1) * 512], gate_s, func=AF.Silu)
                wt = wload(w_in[:, OFF_BETA:OFF_BETA + 16], 16)
                tok_mm(hTs, wt, 0, 16, ba_s[:], ba_s)
                k.dma('sp', stc[:], st_conv, writes=[stc])
                st3 = stc[:].rearrange("(s j) c -> s j c", j=3) if False else None
                for s_ in range(NS):
                    k.dma('sp', o_conv_s[s_, 0:2, :], stc[s_ * 3 + 1:s_ * 3 + 3, :], reads=[stc])
                for p_ in range(3):
                    k.dma('sp', o_conv_s[:, 2, p_ * 1024:(p_ + 1) * 1024], qkv_p[p_][0:NS, :], reads=[qkv_p[p_]])
                to_T(stc[:], stc, 24, stT, stT, rows=NS * 3)
                for p_ in range(3):
                    to_T(qkv_p[p_][0:NS, :], qkv_p[p_], 8, newT[:, p_ * 8:(p_ + 1) * 8, :], newT)
                st4 = stT[:].rearrange("p c (s j) -> p c s j", j=3)
                k.tt(cvT[:], newT[:], bc(cwT[:, :, 3], 2, NS), ALU.mult, [newT, cwT], [cvT])
                for j_ in range(3):
                    k.tt(tmpT[:], st4[:, :, :, j_], bc(cwT[:, :, j_], 2, NS), ALU.mult, [stT, cwT], [tmpT])
                    k.tt(cvT[:], cvT[:], tmpT[:], ALU.add, [cvT, tmpT], [cvT])
                k.act(cvT[:], cvT[:], AF.Silu, [cvT], [cvT])
                k.tt(tmpT[:, 0:16, :], cvT[:, 0:16, :], cvT[:, 0:16, :], ALU.mult, [cvT], [tmpT])
                b = k.bank()
                k.mm(b[:, 0:256], onesf[:], tmpT[:, 0:16, :].rearrange("p c s -> p (c s)"), [onesf, tmpT], [b])
                k.act(rq_s[:].rearrange("p c s -> p (c s)"), b[:, 0:256], AF.Ln, [b, epsc], [rq_s], bias=epsc[:])
                k.rel(b)
                k.act(rq_s[:], rq_s[:], AF.Exp, [rq_s], [rq_s], scale=-0.5)
                k.stt(cvT[:, 0:8, :], cvT[:, 0:8, :], 128.0 ** -0.5, rq_s[:, 0:8, :], ALU.mult, ALU.mult, [cvT, rq_s], [cvT])
                k.tt(cvT[:, 8:16, :], cvT[:, 8:16, :], rq_s[:, 8:16, :], ALU.mult, [cvT, rq_s], [cvT])
                qsT, ksT, vsT = cvT[:, 0:8, :], cvT[:, 8:16, :], cvT[:, 16:24, :]
                k.act(beta_s[:], ba_s[:, 0:8], AF.Exp, [ba_s], [beta_s], scale=-1.0)
                k.ts(beta_s[:], beta_s[:], 1.0, None, ALU.add, None, [beta_s], [beta_s])
                k.op('dve', lambda e: e.reciprocal(out=beta_s[:], in_=beta_s[:]), reads=[beta_s], writes=[beta_s])
                k.tt(t16a[:], ba_s[:, 8:16], dtb[0:NS, :], ALU.add, [ba_s, dtb], [t16a])
                k.act(t16b[:], t16a[:], AF.Abs, [t16a], [t16b])
                k.act(t16b[:], t16b[:], AF.Exp, [t16b], [t16b], scale=-1.0)
                k.act(t16b[:], t16b[:], AF.Ln, [t16b, onec], [t16b], bias=onec[0:NS, :])
                k.stt(t16a[:], t16a[:], 0.0, t16b[:], ALU.max, ALU.add, [t16a, t16b], [t16a])
                k.tt(t16a[:], t16a[:], negA[0:NS, :], ALU.mult, [t16a, negA], [t16a])
                k.act(alpha_s[:], t16a[:], AF.Exp, [t16a], [alpha_s])
                to_tok(cvT[:, 16:24, :], cvT, 8, v_tok[:].rearrange("s h d -> s (h d)"), v_tok)
                k.tt(qkp[:], qsT, ksT, ALU.mult, [cvT], [qkp])
                b = k.bank()
                for h in range(8):
                    k.mm(b[0:NS, h:h + 1], qkp[:, h, :], onesf[:, 0:1], [qkp, onesf], [b])
                k.cp(qk_s[:], b[0:NS, 0:8], [b], [qk_s])
                k.rel(b)
                bks = [k.bank() for _ in range(4)]
                for s_ in range(NS):
                    S_ = Ss[s_ % 2]
                    k.dma('sp', S_[:], st_S[s_], writes=[S_])
                    k.tt(pK[:], S_[:], bc(cvT[:, 8:16, s_], 2, 128), ALU.mult, [S_, cvT], [pK], eng='pool')
                    k.tt(pQ[:], S_[:], bc(cvT[:, 0:8, s_], 2, 128), ALU.mult, [S_, cvT], [pQ])
                    for hf in range(2):
                        k.mm(bks[hf][0:NS, :], OH[:, s_, :], pK[:, hf * 4:(hf + 1) * 4, :].rearrange("p h d -> p (h d)"), [OH, pK], [bks[hf]],
                             start=(s_ == 0), stop=(s_ == NS - 1))
                        k.mm(bks[2 + hf][0:NS, :], OH[:, s_, :], pQ[:, hf * 4:(hf + 1) * 4, :].rearrange("p h d -> p (h d)"), [OH, pQ], [bks[2 + hf]],
                             start=(s_ == 0), stop=(s_ == NS - 1))
                for hf in range(2):
                    hs = slice(hf * 4, hf * 4 + 4)
                    kS = bks[hf][0:NS, :].rearrange("s (h d) -> s h d", d=128)
                    qS = bks[2 + hf][0:NS, :].rearrange("s (h d) -> s h d", d=128)
                    k.tt(t_tok[:, hs, :], kS, bc(alpha_s[:, hs], 2, 128), ALU.mult, [bks[hf], alpha_s], [t_tok])
                    k.tt(t_tok[:, hs, :], v_tok[:, hs, :], t_tok[:, hs, :], ALU.subtract, [v_tok, t_tok], [t_tok])
                    k.tt(d_tok[:, hs, :], t_tok[:, hs, :], bc(beta_s[:, hs], 2, 128), ALU.mult, [t_tok, beta_s], [d_tok])
                    k.tt(o_tok[:, hs, :], qS, bc(alpha_s[:, hs], 2, 128), ALU.mult, [bks[2 + hf], alpha_s], [o_tok])
                    k.tt(t_tok[:, hs, :], d_tok[:, hs, :], bc(qk_s[:, hs], 2, 128), ALU.mult, [d_tok, qk_s], [t_tok])
                    k.tt(o_tok[:, hs, :], o_tok[:, hs, :], t_tok[:, hs, :], ALU.add, [o_tok, t_tok], [o_tok])
                k.rel(*bks)
                k.tt(t_tok[:], o_tok[:], o_tok[:], ALU.mult, [o_tok], [t_tok])
                k.op('dve', lambda e: e.tensor_reduce(out=oss_s[:], in_=t_tok[:], axis=AX.X, op=ALU.add), reads=[t_tok], writes=[oss_s])
                k.act(oss_s[:], oss_s[:], AF.Ln, [oss_s, epsc], [oss_s], bias=epsc[0:NS, :], scale=1.0 / 128)
                k.act(oss_s[:], oss_s[:], AF.Exp, [oss_s], [oss_s], scale=-0.5)
                k.tt(o_tok[:], o_tok[:], bc(oss_s[:], 2, 128), ALU.mult, [o_tok, oss_s], [o_tok])
                k.tt(o_tok[:], o_tok[:], bc(onw_bc[:], 1, 8), ALU.mult, [o_tok, onw_bc], [o_tok])
                k.tt(o_tok[:], o_tok[:], gate_s[0:NS, :].rearrange("s (h d) -> s h d", d=128), ALU.mult, [o_tok, gate_s], [o_tok])
                to_T(o_tok[:].rearrange("s h d -> s (h d)"), o_tok, 8, newT[:, 0:8, :], newT)
                k.cp(onTs[:], newT[:, 0:8, :], [newT], [onTs])
                k.dma('sp', Ss[0][:], st_S[0], writes=[Ss[0]])
                for s_ in range(NS):
                    S_ = Ss[s_ % 2]
                    if s_ + 1 < NS:
                        k.dma('sp', Ss[(s_ + 1) % 2][:], st_S[s_ + 1], writes=[Ss[(s_ + 1) % 2]])
                    b0, b1, b2 = k.bank(), k.bank(), k.bank()
                    k.mm(b2[:, 0:8], Esel[:, s_, :], alpha_s[:], [Esel, alpha_s], [b2])
                    k.cp(abc[:], b2[:, 0:8], [b2], [abc])
                    k.mm(b0[:, :], Esel[:, s_, :], d_tok[:, 0:4, :].rearrange("s h d -> s (h d)"), [Esel, d_tok], [b0])
                    k.mm(b1[:, :], Esel[:, s_, :], d_tok[:, 4:8, :].rearrange("s h d -> s (h d)"), [Esel, d_tok], [b1])
                    k.tt(pK[:], S_[:], bc(abc[:], 2, 128), ALU.mult, [S_, abc], [pK], eng='pool')
                    k.tt(pQ[:, 0:4, :], b0[:, :].rearrange("p (h d) -> p h d", d=128), bc(cvT[:, 8:12, s_], 2, 128), ALU.mult, [b0, cvT], [pQ])
                    k.tt(pQ[:, 4:8, :], b1[:, :].rearrange("p (h d) -> p h d", d=128), bc(cvT[:, 12:16, s_], 2, 128), ALU.mult, [b1, cvT], [pQ])
                    k.rel(b0, b1, b2)
                    k.tt(S_[:], pK[:], pQ[:], ALU.add, [pK, pQ], [S_], eng='pool')
                    k.dma('sp', o_S_s[s_], S_[:], reads=[S_])

                q_s = c2("q_s", [NS, 1024])
                kv_s = c2("kv_s", [NS, 512])
                KCs = [c2("KC%d" % i, [128, NS // 2, 256]) for i in range(2)]
                VCs = [c2("VC%d" % i, [128, NS // 2, 256]) for i in range(2)]
                prd = c2("prd", [128, 16, 64])
                scT = c2("scT", [128, NS, 16])
                Pm = c2("Pm", [128, 2, 128])
                PTa = c2("PTa", [128, 2, 128])
                mx_s = c2("mx_s", [128, 2])
                nmx_s = c2("nmx_s", [128, 2])
                rs_s = c2("rs_s", [128, 2])
                es_s = c2("es_s", [128, 2])
                ob_tok = c2("ob_tok", [NS, 1024])
                obTf = c2("obTf", [128, 8, NS])
                W2x = []
                for i_ in range(3):
                    try:
                        W2x.append(c2("W2x%d" % i_, [128, 8, 512], BF16))
                    except AssertionError:
                        break
                k.switch('S1', 'S2')
                wlist['cur'] = W8 + W2x
                for g_ in range(2):
                    wt = wload(w_in[:, OFF_SQ + g_ * 512:OFF_SQ + (g_ + 1) * 512], 512)
                    tok_mm(hTs, wt, 0, 512, q_s[:, g_ * 512:(g_ + 1) * 512], q_s, scale=0.125)
                wt = wload(w_in[:, OFF_SK:OFF_SK + 512], 512)
                tok_mm(hTs, wt, 0, 512, kv_s[:], kv_s)
                for i_, q_ in ((0, 'sp'), (1, 'act')):
                    k.dma(q_, KCs[i_][0:127, :, :], st_k[1:128, i_ * 8:(i_ + 1) * 8, :], writes=[KCs[i_]])
                for i_ in range(2):
                    k.dma('pool', VCs[i_][0:127, :, :], st_v[1:128, i_ * 8:(i_ + 1) * 8, :], writes=[VCs[i_]])
                for s_ in range(NS):
                    KC_, VC_ = KCs[s_ // 8], VCs[s_ // 8]
                    k.dma('sp' if s_ < 8 else 'act', KC_[127:128, s_ % 8, :], kv_s[s_:s_ + 1, 0:256], reads=[kv_s], writes=[KC_], indep=True)
                    k.dma('pool', VC_[127:128, s_ % 8, :], kv_s[s_:s_ + 1, 256:512], reads=[kv_s], writes=[VC_], indep=True)
                for i_, q_ in ((0, 'sp'), (1, 'act')):
                    k.dma(q_, o_k_s[:, i_ * 8:(i_ + 1) * 8, :], KCs[i_][:], reads=[KCs[i_]])
                for i_ in range(2):
                    k.dma('pool', o_v_s[:, i_ * 8:(i_ + 1) * 8, :], VCs[i_][:], reads=[VCs[i_]])
                for s_ in range(NS):
                    b0, b1 = k.bank(), k.bank()
                    k.mm(b0[:, :], Esel[:, s_, :], q_s[:, 0:512], [Esel, q_s], [b0])
                    k.mm(b1[:, :], Esel[:, s_, :], q_s[:, 512:1024], [Esel, q_s], [b1])
                    for hf, bb in ((0, b0), (1, b1)):
                        KC = KCs[s_ // 8]
                        kc = KC[:, s_ % 8, hf * 128:(hf + 1) * 128].rearrange("p (g d) -> p g d", d=64)
                        k.tt(prd[:, hf * 8:(hf + 1) * 8, :].rearrange("p (g i) d -> p g i d", i=4),
                             bb[:, :].rearrange("p (g i d) -> p g i d", i=4, d=64), bc(kc, 2, 4), ALU.mult, [bb, KC], [prd])
                    k.rel(b0, b1)
                    k.op('dve', lambda e: e.tensor_reduce(out=scT[:, s_, :], in_=prd[:], axis=AX.X, op=ALU.add), reads=[prd], writes=[scT])
                b = k.bank()
                for a_ in range(2):
                    k.tr(b[:, a_ * 128:(a_ + 1) * 128], scT[:, a_ * 8:(a_ + 1) * 8, :].rearrange("p s h -> p (s h)"), identf[:], [scT, identf], [b])
                k.op('dve', lambda e: e.tensor_reduce(out=mx_s[:], in_=b[:, 0:256].rearrange("p (a q) -> p a q", q=128), axis=AX.X, op=ALU.max),
                     reads=[b], writes=[mx_s])
                k.ts(mx_s[:], mx_s[:], sinkcol[:, 0:1], None, ALU.max, None, [mx_s, sinkcol], [mx_s])
                k.ts(nmx_s[:], mx_s[:], -1.0, None, ALU.mult, None, [mx_s], [nmx_s])
                for a_ in range(2):
                    k.act(Pm[:, a_, :], b[:, a_ * 128:(a_ + 1) * 128], AF.Exp, [b, nmx_s], [Pm, rs_s], bias=nmx_s[:, a_:a_ + 1], accum=rs_s[:, a_:a_ + 1])
                k.rel(b)
                k.act(es_s[:], nmx_s[:], AF.Exp, [nmx_s, sinkcol], [es_s], bias=sinkcol[:, 0:1])
                k.tt(rs_s[:], rs_s[:], es_s[:], ALU.add, [rs_s, es_s], [rs_s])
                k.op('dve', lambda e: e.reciprocal(out=rs_s[:], in_=rs_s[:]), reads=[rs_s], writes=[rs_s])
                k.tt(Pm[:], Pm[:], bc(rs_s[:], 2, 128), ALU.mult, [Pm, rs_s], [Pm])
                b = k.bank()
                for a_ in range(2):
                    k.tr(b[:, a_ * 128:(a_ + 1) * 128], Pm[:, a_, :], identf[:], [Pm, identf], [b])
                k.cp(PTa[:].rearrange("p a q -> p (a q)"), b[:, 0:256], [b], [PTa])
                k.rel(b)
                PT3 = PTa[:].rearrange("p a (s h) -> p (a s) h", h=16)
                b0, b1 = k.bank(), k.bank()
                for s_ in range(NS):
                    for hf in range(2):
                        VC = VCs[s_ // 8]
                        vc = VC[:, s_ % 8, hf * 128:(hf + 1) * 128].rearrange("p (g d) -> p g d", d=64)
                        pt_ = PT3[:, s_, hf * 8:(hf + 1) * 8].rearrange("p (g i) -> p g i", i=4)
                        k.tt(prd[:, hf * 8:(hf + 1) * 8, :].rearrange("p (g i) d -> p g i d", i=4), bc(vc, 2, 4), bc(pt_, 3, 64), ALU.mult,
                             [VC, PTa], [prd], eng='pool')
                    k.mm(b0[0:NS, :], OH[:, s_, :], prd[:, 0:8, :].rearrange("p h d -> p (h d)"), [OH, prd], [b0], start=(s_ == 0), stop=(s_ == NS - 1))
                    k.mm(b1[0:NS, :], OH[:, s_, :], prd[:, 8:16, :].rearrange("p h d -> p (h d)"), [OH, prd], [b1], start=(s_ == 0), stop=(s_ == NS - 1))
                k.cp(ob_tok[:, 0:512], b0[0:NS, :], [b0], [ob_tok])
                k.cp(ob_tok[:, 512:1024], b1[0:NS, :], [b1], [ob_tok])
                k.rel(b0, b1)
                to_T(ob_tok[:], ob_tok, 8, obTf, obTf)
                k.cp(obTs[:], obTf[:], [obTf], [obTs])

                if nblk > 0:
                    emit_p1(0)
                    hoist['p1'] = True
                gab = c3("gab", [NS, 2048])
                yab = c3("yab", [NS, 2048])
                mix_s = c3("mix_s", [NS, 1024])
                mixTs = c3("mixTs", [128, 8, NS], BF16)
                mixTf = c3("mixTf", [128, 8, NS])
                h2Ts = c3("h2Ts", [128, 8, NS], BF16)
                gtok = c3("gtok", [NS, DFF])
                stf = c3("stf", [NS * 2, DFF])
                stfT = c3("stfT", [128, NFC, NS * 2])
                gT = c3("gT", [128, NFC, NS])
                uT = c3("uT", [128, NFC, NS])
                tT = c3("tT", [128, NFC, NS])
                aTs = c3("aTs", [128, NFC, NS], BF16)
                ys = c3("ys", [NS, 1024])
                W3x = []
                for i_ in range(3):
                    try:
                        W3x.append(c3("W3x%d" % i_, [128, 8, 512], BF16))
                    except AssertionError:
                        break
                k.switch('S2', 'S3')
                wlist['cur'] = W8 + W3x
                for g_ in range(4):
                    wt = wload(w_in[:, OFF_GA + g_ * 512:OFF_GA + (g_ + 1) * 512], 512)
                    tok_mm(hTs, wt, 0, 512, gab[:, g_ * 512:(g_ + 1) * 512], gab, func=AF.Sigmoid)
                for g_ in range(2):
                    wt = wload(w_gdn_out[:, g_ * 512:(g_ + 1) * 512], 512)
                    tok_mm(onTs, wt, 0, 512, yab[:, g_ * 512:(g_ + 1) * 512], yab)
                for g_ in range(2):
                    wt = wload(w_swa_out[:, g_ * 512:(g_ + 1) * 512], 512)
                    tok_mm(obTs, wt, 0, 512, yab[:, 1024 + g_ * 512:1024 + (g_ + 1) * 512], yab)
                k.tt(yab[:], yab[:], gab[:], ALU.mult, [yab, gab], [yab])
                k.tt(mix_s[:], yab[:, 0:1024], yab[:, 1024:2048], ALU.add, [yab], [mix_s])
                to_T(mix_s[:], mix_s, 8, mixTf, mixTf)
                k.cp(mixTs[:], mixTf[:], [mixTf], [mixTs])
                for g_ in range(2):
                    wt = wload(w_o[:, g_ * 512:(g_ + 1) * 512], 512)
                    tok_mm(mixTs, wt, 0, 512, mix_s[:, g_ * 512:(g_ + 1) * 512], mix_s)
                k.tt(mix_s[:], mix_s[:], gtok1[0:NS, :], ALU.mult, [mix_s, gtok1], [mix_s])
                k.tt(xs_3[:], xs_3[:], mix_s[:], ALU.add, [xs_3, mix_s], [xs_3])
                rms_T_s(xs_3[:], xs_3, h2Ts, A2s, modT[:, 24:32, 0:NS], modT)
                k.dma('sp', stf[:], st_ffn, writes=[stf])
                for s_ in range(NS):
                    k.dma('sp', o_ffn_s[s_, 0:1, :], stf[s_ * 2 + 1:s_ * 2 + 2, :], reads=[stf])
                to_T(stf[:], stf, NFC, stfT, stfT, rows=NS * 2)
                for (wsrc, dstT, is_gate) in ((w_ffn_gate, gT, True), (w_ffn_up, uT, False)):
                    for g_ in range(6):
                        n = 512 if g_ < 5 else DFF - 5 * 512
                        wt = wload(wsrc[:, g_ * 512:g_ * 512 + n], n)
                        if is_gate:
                            tok_mm(h2Ts, wt, 0, n, gtok[:, g_ * 512:g_ * 512 + n], gtok)
                        b = k.bank()
                        for j in range(n // 128):
                            for kk in range(8):
                                k.mm(b[:, j * NS:(j + 1) * NS], wt[:, kk, j * 128:(j + 1) * 128], h2Ts[:, kk, :], [wt, h2Ts], [b], start=(kk == 0), stop=(kk == 7))
                        k.cp(dstT[:, g_ * 4:g_ * 4 + n // 128, :], b[:, 0:(n // 128) * NS].rearrange("p (c s) -> p c s", s=NS), [b], [dstT])
                        k.rel(b)
                k.dma('sp', o_ffn_s[:, 1, :], gtok[:], reads=[gtok])
                sf4 = stfT[:].rearrange("p c (s j) -> p c s j", j=2)
                k.tt(tT[:], gT[:], bc(fcwT[:, :, 2], 2, NS), ALU.mult, [gT, fcwT], [tT])
                for j_ in range(2):
                    k.tt(gT[:], sf4[:, :, :, j_], bc(fcwT[:, :, j_], 2, NS), ALU.mult, [stfT, fcwT], [gT])
                    k.tt(tT[:], tT[:], gT[:], ALU.add, [tT, gT], [tT])
                k.tt(tT[:], tT[:], bc(fcbT[:], 2, NS), ALU.add, [tT, fcbT], [tT])
                k.act(tT[:], tT[:], AF.Silu, [tT], [tT])
                k.tt(aTs[:], tT[:], uT[:], ALU.mult, [tT, uT], [aTs])
                for hf in range(2):
                    b = k.bank()
                    for kg in range(3):
                        nk = 8 if kg < 2 else NFC - 16
                        wt = nextw()
                        k.dma('pool', wt[:, 0:nk, :], w_ffn_down[kg * 1024:kg * 1024 + nk * 128, hf * 512:(hf + 1) * 512].rearrange("(c p) n -> p c n", p=128),
                              writes=[wt])
                        for kk in range(nk):
                            kf = kg * 8 + kk
                            k.mm(b[0:NS, :], aTs[:, kf, :], wt[:, kk, :], [aTs, wt], [b], start=(kf == 0), stop=(kf == NFC - 1))
                    k.tt(mix_s[:, hf * 512:(hf + 1) * 512], b[0:NS, :], gtok2[0:NS, hf * 512:(hf + 1) * 512], ALU.mult, [b, gtok2], [mix_s])
                    k.rel(b)
                k.tt(xs_3[:], xs_3[:], mix_s[:], ALU.add, [xs_3, mix_s], [xs_3])
                k.act(ys[:], xs_3[:], AF.Square, [xs_3], [ys, ss1], accum=ss1[0:NS, :])
                k.act(rs1[0:NS, :], ss1[0:NS, :], AF.Ln, [ss1, epsc], [rs1], bias=epsc[0:NS, :], scale=1.0 / D)
                k.act(rs1[0:NS, :], rs1[0:NS, :], AF.Exp, [rs1], [rs1], scale=-0.5)
                k.stt(ys[:], xs_3[:], rs1[0:NS, :], fnw_bc[0:NS, :], ALU.mult, ALU.mult, [xs_3, rs1, fnw_bc], [ys])
                k.dma('sp', y_s, ys[:], reads=[ys])
                k.switch('S3', 'G')
                wlist['cur'] = W8

            if do_sample:
                sample_phase()

            for blk in range(nblk):
                t0 = blk * TB
                last = (blk == NBLK - 1)
                if not (blk == 0 and hoist['p1']):
                    emit_p1(t0)
                if blk == 0:
                    dump("hT", hT[:], [hT])
                if blk > 0:
                    k.switch('F', 'G')
                    k.switch('FW', 'G')
                wlist['cur'] = W8

                ck('p1')
                wt = wload(w_in[:, OFF_BETA:OFF_BETA + 16], 16)
                b = k.bank()
                for c in range(8):
                    for kk in range(8):
                        k.mm(b[0:64, c * 16:(c + 1) * 16], hT[:, kk, c * 64:(c + 1) * 64], wt[:, kk, 0:16], [hT, wt], [b],
                             start=(kk == 0), stop=(kk == 7))
                k.cp(ba[:], b[0:64, 0:128].rearrange("p (c r) -> p c r", r=16), [b], [ba])
                k.rel(b)
                k.act(beta[:], ba[:, :, 0:8], AF.Exp, [ba], [beta], scale=-1.0)
                k.ts(beta[:], beta[:], 1.0, None, ALU.add, None, [beta], [beta])
                k.op('dve', lambda e: e.reciprocal(out=beta[:], in_=beta[:]), reads=[beta], writes=[beta])
                k.tt(t64a[:], ba[:, :, 8:16], bc(dtb[:], 1, 8), ALU.add, [ba, dtb], [t64a])
                k.act(t64b[:], t64a[:], AF.Abs, [t64a], [t64b])
                k.act(t64b[:], t64b[:], AF.Exp, [t64b], [t64b], scale=-1.0)
                k.act(t64b[:], t64b[:], AF.Ln, [t64b, onec], [t64b], bias=onec[0:64, :])
                k.stt(t64a[:], t64a[:], 0.0, t64b[:], ALU.max, ALU.add, [t64a, t64b], [t64a])
                k.tt(gg[:], t64a[:], bc(negA[:], 1, 8), ALU.mult, [t64a, negA], [gg])
                ggf = gg[:].rearrange("p c h -> p (c h)")
                b = k.bank()
                k.mm(b[0:64, 0:64], Um[:], ggf, [Um, gg], [b])
                k.mm(b[:, 64:128], onesf[0:64, :], ggf, [onesf, gg], [b])
                k.cp(dd[:], b[0:64, 0:64], [b], [dd])
                k.act(ed[:], dd[:], AF.Exp, [dd], [ed])
                k.tt(ekd[:], b[0:64, 64:128], dd[:], ALU.subtract, [b, dd], [ekd])
                k.act(ekd[:], ekd[:], AF.Exp, [ekd], [ekd])
                k.act(elast[:], b[:, 64:128], AF.Exp, [b], [elast])
                k.rel(b)
                k.tt(bed[:], ed[:], beta[:].rearrange("p c h -> p (c h)"), ALU.mult, [ed, beta], [bed])
                if blk == 0:
                    dump("gg", gg[:], [gg])
                    dump("beta", beta[:], [beta])

                ck('p2')
                def gdn_front(h, hb):
                    wT, uu, Qd, Gp, Kdec, Am, Amb = wT2[hb], uu2[hb], Qd2[hb], Gp2[hb], Kdec2[hb], Am2[hb], Amb2[hb]
                    wt = nextw()
                    for part in range(3):
                        wload(w_in[:, OFF_QKV + part * 1024 + h * 128:OFF_QKV + part * 1024 + (h + 1) * 128], 128, c0=part * 128, tile=wt)
                    wload(w_in[:, OFF_GATE + h * 128:OFF_GATE + (h + 1) * 128], 128, c0=384, tile=wt)
                    for part in range(3):
                        b = k.bank()
                        for kk in range(8):
                            k.mm(b[:, :], wt[:, kk, part * 128:(part + 1) * 128], hT[:, kk, :], [wt, hT], [b], start=(kk == 0), stop=(kk == 7))
                        j = part * 8 + h
                        k.cp(pre[:, 0:3], halo[:, j, :], [halo], [pre], eng='pool')
                        k.cp(pre[:, 3:3 + TB], b[:, :], [b], [pre], eng='act')
                        k.rel(b)
                        yield
                        k.cp(halo[:, j, :], pre[:, TB:TB + 3], [pre], [halo], eng='pool')
                        k.ts(cv[:, part, :], pre[:, 0:TB], cwT[:, j, 0:1], None, ALU.mult, None, [pre, cwT], [cv])
                        for tap in range(1, 4):
                            k.stt(cv[:, part, :], pre[:, tap:tap + TB], cwT[:, j, tap:tap + 1], cv[:, part, :], ALU.mult, ALU.add,
                                  [pre, cwT, cv], [cv])
                    b = k.bank()
                    for kk in range(8):
                        k.mm(b[:, :], wt[:, kk, 384:512], hT[:, kk, :], [wt, hT], [b], start=(kk == 0), stop=(kk == 7))
                    k.act(Gp[:], b[:, :], AF.Silu, [b], [Gp])
                    k.rel(b)
                    yield
                    k.act(cv[:], cv[:], AF.Silu, [cv], [cv])
                    yield
                    for qk in range(2):
                        prebf = pre[:, 0:TB // 2].bitcast(BF16)
                        k.tt(prebf, cv[:, qk, :], cv[:, qk, :], ALU.mult, [cv], [pre], eng='pool')
                        b = k.bank()
                        k.mm(b[:, :], onesb[:], prebf, [onesb, pre], [b])
                        k.act(rqk[:, qk, :], b[:, :], AF.Ln, [b, epsc], [rqk], bias=epsc[:])
                        k.rel(b)
                        yield
                    k.act(rqk[:], rqk[:], AF.Exp, [rqk], [rqk], scale=-0.5)
                    k.stt(Qt_ap, cv[:, 0, :], 128.0 ** -0.5, rqk[:, 0, :], ALU.mult, ALU.mult, [cv, rqk], [cv])
                    k.tt(Kt_ap, cv[:, 1, :], rqk[:, 1, :], ALU.mult, [cv, rqk], [cv])
                    yield
                    k.cp(Qtb[:], Qt_ap, [cv], [Qtb], eng='pool')
                    k.cp(Ktb[:], Kt_ap, [cv], [Ktb], eng='pool')
                    k.cp(Vtb[:], cv[:, 2, :], [cv], [Vtb], eng='pool')
                    if blk == 0 and h == 0:
                        dump("Qt", Qt_ap, [cv])
                        dump("Kt", Kt_ap, [cv])
                        dump("Vt", cv[:, 2, :], [cv])
                    ck('gdn_a')
                    gh = gg[:, :, h]
                    k.tt(SA[:], bc(gh, 2, 64), bc(SLm[:], 1, 8), ALU.mult, [gg, SLm], [SA], eng='pool')
                    k.tt(SB[:], bc(gh, 2, 64), bc(Um[:], 1, 8), ALU.mult, [gg, Um], [SB], eng='pool')
                    yield
                    b = k.bank()
                    k.mm(b[0:64, :], Um[:], SA[:].rearrange("p c j -> p (c j)"), [Um, SA], [b])
                    k.act(Wm[:].rearrange("p c j -> p (c j)"), b[0:64, :], AF.Exp, [b], [Wm])
                    k.rel(b)
                    yield
                    k.tt(Wm[:], Wm[:], bc(nSL[:], 1, 8), ALU.mult, [Wm, nSL], [Wm])
                    k.tt(Wm[:], Wm[:], bc(beta[:, :, h], 2, 64), ALU.mult, [Wm, beta], [Wm])
                    yield
                    k.tt(SA[:], bc(beta[:, :, h], 2, 64), bc(identf[0:64, 0:64], 1, 8), ALU.mult, [beta, identf], [SA], eng='pool')
                    b = k.bank()
                    k.mm(b[0:64, :], SLm[:], SB[:].rearrange("p c j -> p (c j)"), [SLm, SB], [b])
                    k.act(Zm[:].rearrange("p c j -> p (c j)"), b[0:64, :], AF.Exp, [b], [Zm])
                    k.rel(b)
                    yield
                    k.tt(Am[:], Zm[:], bc(Um[:], 1, 8), ALU.mult, [Zm, Um], [Am])
                    k.tt(Zm[:], Zm[:], bc(nSU[:], 1, 8), ALU.mult, [Zm, nSU], [Zm])
                    yield
                    b = k.bank()
                    k.mm(b[0:64, :], onesf[0:64, 0:64], SA[:].rearrange("p c j -> p (c j)"), [onesf, SA], [b])
                    k.tt(Zm[:].rearrange("p c j -> p (c j)"), Zm[:].rearrange("p c j -> p (c j)"), b[0:64, :], ALU.mult, [Zm, b], [Zm])
                    k.rel(b)
                    yield
                    k.cp(Qd[:], Qt_ap, [cv], [Qd])
                    ck('gdn_b')
                    bK0, bV0 = k.bank(), k.bank()
                    bkv = bK0[:, :].bitcast(BF16)
                    bvv = bV0[:, :].bitcast(BF16)
                    for c in range(8):
                        k.tr(bkv[0:64, c * 128:(c + 1) * 128], Ktb[:, c * 64:(c + 1) * 64], identb[:], [Ktb, identb], [bK0])
                        k.tr(bvv[0:64, c * 128:(c + 1) * 128], Vtb[:, c * 64:(c + 1) * 64], identb[:], [Vtb, identb], [bV0])
                    kin = bkv[0:64, :].rearrange("p (c d) -> p c d", d=128)
                    vin = bvv[0:64, :].rearrange("p (c d) -> p c d", d=128)
                    k.tt(Kbd[:], kin, bc(bed[:].rearrange("p (c h) -> p c h", h=8)[:, :, h], 2, 128), ALU.mult, [bK0, bed], [Kbd])
                    k.tt(Kdec[:], kin, bc(ekd[:].rearrange("p (c h) -> p c h", h=8)[:, :, h], 2, 128), ALU.mult, [bK0, ekd], [Kdec])
                    k.tt(Vb[:], vin, bc(beta[:, :, h], 2, 128), ALU.mult, [bV0, beta], [Vb])
                    k.rel(bK0, bV0)
                    yield
                    bA, bB = k.bank(), k.bank()
                    for c in range(8):
                        k.mm(bA[0:64, c * 64:(c + 1) * 64], Ktb[:, c * 64:(c + 1) * 64], Ktb[:, c * 64:(c + 1) * 64], [Ktb], [bA])
                        k.mm(bB[0:64, c * 64:(c + 1) * 64], Ktb[:, c * 64:(c + 1) * 64], Qtb[:, c * 64:(c + 1) * 64], [Ktb, Qtb], [bB])
                    A3 = bA[0:64, :].rearrange("p (c j) -> p c j", j=64)
                    B3 = bB[0:64, :].rearrange("p (c j) -> p c j", j=64)
                    k.tt(Wm[:], A3, Wm[:], ALU.mult, [bA, Wm], [Wm])
                    k.tt(Zm[:], A3, Zm[:], ALU.mult, [bA, Zm], [Zm])
                    Amb_ = Am if os.environ.get('SUPD32', '0') == '1' else Amb
                    k.tt(Amb_[:], B3, Am[:], ALU.mult, [bB, Am], [Amb_])
                    k.rel(bA, bB)
                    yield
                    I4 = bc(identf[0:64, 0:64], 1, 4)
                    for hf in range(2):
                        cs = slice(hf * 4, hf * 4 + 4)
                        k.cp(ZYH[hf][:, :, 0, :], Zm[:, cs, :], [Zm], [ZYH[hf]], eng='act')
                        k.tt(ZYH[hf][:, :, 1, :], Zm[:, cs, :], I4, ALU.add, [Zm, identf], [ZYH[hf]])
                        k.cp(WXH[hf][:, :, 0, :], Wm[:, cs, :], [Wm], [WXH[hf]], eng='act')
                        k.tt(WXH[hf][:, :, 1, :], Wm[:, cs, :], I4, ALU.add, [Wm, identf], [WXH[hf]])
                    ck('gdn_c')
                    for lev in range(6):
                        for hf in range(2):
                            zy, wx = ZYH[hf], WXH[hf]
                            bZ, bW = k.bank(), k.bank()
                            for cc in range(4):
                                if lev == 0:
                                    k.mm(bZ[0:64, cc * 128:cc * 128 + 64], wx[:, cc, 0, :], zy[:, cc, 0, :], [wx, zy], [bZ])
                                    k.mm(bW[0:64, cc * 128:cc * 128 + 64], zy[:, cc, 0, :], wx[:, cc, 0, :], [wx, zy], [bW])
                                elif lev < 5:
                                    k.mm(bZ[0:64, cc * 128:(cc + 1) * 128], wx[:, cc, 0, :], zy[:, cc, :, :].rearrange("p a j -> p (a j)"), [wx, zy], [bZ])
                                    k.mm(bW[0:64, cc * 128:(cc + 1) * 128], zy[:, cc, 0, :], wx[:, cc, :, :].rearrange("p a j -> p (a j)"), [wx, zy], [bW])
                                else:
                                    k.mm(bZ[0:64, cc * 128 + 64:(cc + 1) * 128], wx[:, cc, 0, :], zy[:, cc, 1, :], [wx, zy], [bZ])
                                    k.mm(bW[0:64, cc * 128 + 64:(cc + 1) * 128], zy[:, cc, 0, :], wx[:, cc, 1, :], [wx, zy], [bW])
                            cs = slice(hf * 4, hf * 4 + 4)
                            Z4 = bZ[0:64, :].rearrange("p (c a j) -> p c a j", a=2, j=64)
                            W4 = bW[0:64, :].rearrange("p (c a j) -> p c a j", a=2, j=64)
                            if lev == 0:
                                k.cp(zy[:, :, 0, :], Z4[:, :, 0, :], [bZ], [zy], eng='act')
                                k.cp(wx[:, :, 0, :], W4[:, :, 0, :], [bW], [wx], eng='act')
                            elif lev < 5:
                                k.tt(zy[:, :, 1, :], Z4[:, :, 1, :], zy[:, :, 1, :], ALU.add, [bZ, zy], [zy])
                                k.cp(zy[:, :, 0, :], Z4[:, :, 0, :], [bZ], [zy], eng='act')
                                k.tt(wx[:, :, 1, :], W4[:, :, 1, :], wx[:, :, 1, :], ALU.add, [bW, wx], [wx])
                                k.cp(wx[:, :, 0, :], W4[:, :, 0, :], [bW], [wx], eng='act')
                            else:
                                k.tt(Y32[hf][:], Z4[:, :, 1, :], zy[:, :, 1, :], ALU.add, [bZ, zy], [Y32[hf]])
                                k.tt(SB[:, cs, :], W4[:, :, 1, :], wx[:, :, 1, :], ALU.add, [bW, wx], [SB])
                            k.rel(bZ, bW)
                            yield
                    for hf in range(2):
                        cs = slice(hf * 4, hf * 4 + 4)
                        bR = k.bank()
                        for cc in range(4):
                            k.mm(bR[0:64, cc * 64:(cc + 1) * 64], Wm[:, hf * 4 + cc, :], Y32[hf][:, cc, :], [Wm, Y32[hf]], [bR])
                        R3 = bR[0:64, 0:256].rearrange("p (c j) -> p c j", j=64)
                        k.tt(SA[:, cs, :], R3, Y32[hf][:], ALU.subtract, [bR, Y32[hf]], [SA])
                        k.rel(bR)
                        k.tt(SA[:, cs, :], SA[:, cs, :], I4, ALU.add, [SA, identf], [SA])
                        bF = k.bank()
                        for cc in range(4):
                            k.mm(bF[0:64, cc * 64:(cc + 1) * 64], SB[:, hf * 4 + cc, :], SA[:, hf * 4 + cc, :], [SB, SA], [bF])
                        k.tt(Y32[hf][:], bF[0:64, 0:256].rearrange("p (c j) -> p c j", j=64), Y32[hf][:], ALU.add, [bF, Y32[hf]], [Y32[hf]])
                        k.rel(bF)
                        yield
                    if blk == 0 and h == 0:
                        dump("Yf", Y32[0][:], [Y32[0]])
                    bU0, bU1, bWt = k.bank(), k.bank(), k.bank()
                    for c in range(8):
                        bu_ = bU0 if c < 4 else bU1
                        Yc = Y32[c // 4][:, c % 4, :]
                        k.mm(bu_[0:64, (c % 4) * 128:(c % 4 + 1) * 128], Yc, Vb[:, c, :], [Y32[c // 4], Vb], [bu_])
                        k.mm(bWt[:, c * 64:(c + 1) * 64], Kbd[:, c, :], Yc, [Kbd, Y32[c // 4]], [bWt])
                    k.cp(uu[:, 0:4, :], bU0[0:64, :].rearrange("p (c d) -> p c d", d=128), [bU0], [uu], eng='act')
                    k.cp(uu[:, 4:8, :], bU1[0:64, :].rearrange("p (c d) -> p c d", d=128), [bU1], [uu], eng='act')
                    k.cp(wT[:], bWt[:, :].rearrange("p (c j) -> p c j", j=64), [bWt], [wT])
                    k.rel(bU0, bU1, bWt)
                    yield
                    yield

                def gdn_back(h, hb):
                    wT, uu, Qd, Gp, Kdec, Am, Amb = wT2[hb], uu2[hb], Qd2[hb], Gp2[hb], Kdec2[hb], Am2[hb], Amb2[hb]
                    Amb_ = Am if os.environ.get('SUPD32', '0') == '1' else Amb
                    Sh = S_all[:, h, :]
                    k.cp(Sb[:], Sh, [S_all], [Sb], eng='pool')
                    for c in range(8):
                        col = c * 8 + h
                        b1_, b2_, b3_ = k.bank(), k.bank(), k.bank()
                        k.mm(b1_[0:64, 0:128], wT[:, c, :], Sb[:], [wT, Sb], [b1_])
                        k.mm(b2_[0:64, 0:128], Qd[:, c * 64:(c + 1) * 64], Sb[:], [Qd, Sb], [b2_])
                        k.tt(vnew[:], uu[:, c, :], b1_[0:64, 0:128], ALU.subtract, [uu, b1_], [vnew])
                        yield
                        k.mm(b2_[0:64, 128:256], Amb_[:, c, :], vnew[:], [Amb_, vnew], [b2_])
                        k.mm(b3_[:, 0:128], Kdec[:, c, :], vnew[:], [Kdec, vnew], [b3_])
                        k.stt(Sh, Sh, elast[:, col:col + 1], b3_[:, 0:128], ALU.mult, ALU.add, [S_all, elast, b3_], [S_all])
                        if c < 7:
                            k.cp(Sb[:], Sh, [S_all], [Sb], eng='pool')
                        k.act(osb[:, c, :], b2_[0:64, 0:128], AF.Copy, [b2_, ed], [osb], scale=ed[:, col:col + 1])
                        k.tt(osb[:, c, :], osb[:, c, :], b2_[0:64, 128:256], ALU.add, [osb, b2_], [osb])
                        k.rel(b1_, b2_, b3_)
                        yield
                    if blk == 0 and h == 0:
                        dump("osb", osb[:], [osb])
                    ck('gdn_e')
                    k.tt(uu[:], osb[:], osb[:], ALU.mult, [osb], [uu], eng='pool')
                    k.op('dve', lambda e: e.tensor_reduce(out=oss[:], in_=uu[:], axis=AX.X, op=ALU.add), reads=[uu], writes=[oss])
                    k.act(ors[:], oss[:], AF.Ln, [oss, epsc], [ors], bias=epsc[0:64, :], scale=1.0 / 128)
                    k.act(ors[:], ors[:], AF.Exp, [ors], [ors], scale=-0.5)
                    k.tt(on1[:], osb[:], bc(ors[:], 2, 128), ALU.mult, [osb, ors], [on1])
                    b = k.bank()
                    bv = b[:, :].bitcast(BF16)
                    for c in range(8):
                        k.tr(bv[:, c * 64:(c + 1) * 64], on1[:, c, :], identb[0:64, 0:64], [on1, identb], [b])
                    k.stt(onT[:, h, :], bv[:, 0:TB], onwT[:, 0:1], Gp[:], ALU.mult, ALU.mult, [b, Gp, onwT], [onT])
                    k.rel(b)
                    yield
                    yield

                def _drain(gl):
                    gl = [[g_, w_] for g_, w_ in gl]
                    while gl:
                        for it in list(gl):
                            for _ in range(it[1]):
                                try:
                                    next(it[0])
                                except StopIteration:
                                    gl.remove(it)
                                    break

                _drain([(gdn_front(0, 0), 1)])
                for h in range(8):
                    gl = [(gdn_back(h, h % 2), 1)]
                    if h < 7:
                        gl.append((gdn_front(h + 1, (h + 1) % 2), 2))
                    _drain(gl)
                if blk == 0:
                    dump("onT", onT[:], [onT])
                    dump("S0", S_all[:], [S_all])
                if last:
                    k.dma('sp', o_S_p, S_all[:], reads=[S_all])
                    k.cp(halo_out[:], halo[:], [halo], [halo_out], eng='pool')
                    for j_ in range(3):
                        k.dma('sp', o_conv_p[j_].rearrange("(c p) -> p c", p=128), halo_out[:, :, j_], reads=[halo_out],
                              allow_slow_non_contiguous=True)

                ck('gdn')
                k.switch('G', 'A')
                k.switch('G', 'FW')
                wt = wload(w_in[:, OFF_SK:OFF_SK + 512], 512)
                for t in range(4):
                    b = k.bank()
                    for kk in range(8):
                        k.mm(b[:, :], hT[:, kk, t * 128:(t + 1) * 128], wt[:, kk, :], [hT, wt], [b], start=(kk == 0), stop=(kk == 7))
                    ck('swa_a0')
                    kin = b[:, 0:256].rearrange("p (g d) -> p g d", d=64)
                    vin = b[:, 256:512].rearrange("p (g d) -> p g d", d=64)
                    Kt4 = Ktok[:].rearrange("p (g a d) -> p g a d", a=2, d=64)
                    Vt4 = Vtok[:, 1 + t, :].rearrange("p (g a d) -> p g a d", a=2, d=64)
                    for a_ in range(2):
                        _v = os.environ.get('SWA_VAR', '')
                        ke, ve = {'': ('act', 'dve'), 'konly': ('act', None), 'vonly': (None, 'dve'), 'kdve': ('dve', None),
                                  'both_dve': ('dve', 'dve'), 'both_act': ('act', 'act'), 'swap': ('dve', 'act')}[_v]
                        if ke:
                            k.cp(Kt4[:, :, a_, :], kin, [b], [Ktok], eng=ke)
                        if ve:
                            k.cp(Vt4[:, :, a_, :], vin, [b], [Vtok], eng=ve)
                    ck('swa_a1')
                    if last and t == 3:
                        k.cp(kvout[:], b[:, :], [b], [kvout])
                        k.dma('sp', o_k_p, kvout[:, 0:256], reads=[kvout])
                        k.dma('sp', o_v_p, kvout[:, 256:512], reads=[kvout])
                    k.rel(b)
                    b = k.bank()
                    bv = b[:, :].bitcast(BF16)
                    for g in range(4):
                        k.tr(bv[:, g * 128:(g + 1) * 128], Ktok[:, g * 128:(g + 1) * 128], identb[:], [Ktok, identb], [b])
                    ck('swa_a2')
                    k.cp(KTl[0:64, :, 128 + t * 128:128 + (t + 1) * 128], bv[0:64, 0:512].rearrange("p (g q) -> p g q", q=128), [b], [KTl])
                    k.cp(KTh[64:128, :, 128 + t * 128:128 + (t + 1) * 128], bv[64:128, 0:512].rearrange("p (g q) -> p g q", q=128), [b], [KTh])
                    k.rel(b)
                ck('swa_a')
                for half in range(2):
                    wt = wload(w_in[:, OFF_SQ + half * 512:OFF_SQ + (half + 1) * 512], 512)
                    for j in range(4):
                        b = k.bank()
                        for kk in range(8):
                            k.mm(b[:, :], wt[:, kk, j * 128:(j + 1) * 128], hT[:, kk, :], [wt, hT], [b], start=(kk == 0), stop=(kk == 7))
                        k.act(QT[:, half * 4 + j, :], b[:, :], AF.Copy, [b], [QT], scale=0.125)
                        k.rel(b)
                ck('swa_b')
                def swa_iter(t, g, sb_):
                    msk = maskB if (blk == 0 and t == 0) else maskA
                    sc, pb, PT, mx, nmx, rsum, esk = sc2[sb_], pb2[sb_], PT2[sb_], mx2[sb_], nmx2[sb_], rsum2[sb_], esk2[sb_]
                    b0, b1 = k.bank(), k.bank()
                    for i in range(4):
                        hq = g * 4 + i
                        ch, hf = hq // 2, hq % 2
                        if os.environ.get('HF0'):
                            hf = 0
                        bb = b0 if i < 2 else b1
                        KTx = KTh if hf else KTl
                        k.mm(bb[:, (i % 2) * 256:(i % 2 + 1) * 256], QT[:, ch, t * 128:(t + 1) * 128],
                             KTx[:, g, t * 128:t * 128 + 256], [QT, KTx], [bb])
                    k.tt(sc[:, 0:2, :], b0[:, :].rearrange("p (i q) -> p i q", q=256), bc(msk[:], 1, 2), ALU.add, [b0, msk], [sc])
                    k.tt(sc[:, 2:4, :], b1[:, :].rearrange("p (i q) -> p i q", q=256), bc(msk[:], 1, 2), ALU.add, [b1, msk], [sc])
                    k.rel(b0, b1)
                    yield
                    ck('swa_c')
                    k.op('dve', lambda e: e.tensor_reduce(out=mx[:], in_=sc[:], axis=AX.X, op=ALU.max), reads=[sc], writes=[mx])
                    k.tt(mx[:], mx[:], sinks[:, g * 4:(g + 1) * 4], ALU.max, [mx, sinks], [mx])
                    k.ts(nmx[:], mx[:], -1.0, None, ALU.mult, None, [mx], [nmx])
                    for i in range(4):
                        k.act(pb[:, i, :], sc[:, i, :], AF.Exp, [sc, nmx], [pb, rsum], bias=nmx[:, i:i + 1], accum=rsum[:, i:i + 1])
                    k.tt(esk[:], sinks[:, g * 4:(g + 1) * 4], mx[:], ALU.subtract, [sinks, mx], [esk])
                    k.act(esk[:], esk[:], AF.Exp, [esk], [esk])
                    k.tt(rsum[:], rsum[:], esk[:], ALU.add, [rsum, esk], [rsum])
                    k.op('dve', lambda e: e.reciprocal(out=rsum[:], in_=rsum[:]), reads=[rsum], writes=[rsum])
                    ck('swa_d')
                    k.tt(pb[:], pb[:], bc(rsum[:], 2, 256), ALU.mult, [pb, rsum], [pb])
                    yield
                    b = k.bank()
                    bv = b[:, :].bitcast(BF16)
                    for i in range(4):
                        for kt in range(2):
                            k.tr(bv[:, (i * 2 + kt) * 128:(i * 2 + kt + 1) * 128], pb[:, i, kt * 128:(kt + 1) * 128], identb[:], [pb, identb], [b])
                    ck('swa_e')
                    k.cp(PT[:].rearrange("p i a q -> p (i a q)"), bv[:, 0:1024], [b], [PT], eng='act')
                    k.rel(b)
                    yield
                    b = k.bank()
                    for i in range(4):
                        for kt in range(2):
                            k.mm(b[:, i * 128:(i + 1) * 128], Vtok[:, t + kt, g * 128:(g + 1) * 128], PT[:, i, kt, :], [Vtok, PT], [b],
                                 start=(kt == 0), stop=(kt == 1))
                    for i in range(4):
                        hq = g * 4 + i
                        ch, hf = hq // 2, hq % 2
                        k.cp(obT[hf * 64:(hf + 1) * 64, ch, t * 128:(t + 1) * 128], b[hf * 64:(hf + 1) * 64, i * 128:(i + 1) * 128], [b], [obT],
                             eng=('act' if i % 2 else 'dve'))
                    k.rel(b)
                    yield

                def swa_stream(its, sb_):
                    for (t_, g_) in its:
                        yield from swa_iter(t_, g_, sb_)

                def _drain2(gl):
                    gl = list(gl)
                    while gl:
                        for it in list(gl):
                            try:
                                next(it)
                            except StopIteration:
                                gl.remove(it)

                its_ = [(t_, g_) for t_ in range(4) for g_ in range(4)]
                _drain2([swa_stream(its_[0::2], 0), swa_stream(its_[1::2], 1)])
                ck('swa_f')
                k.cp(KTl[0:64, :, 0:128], KTl[0:64, :, TB:TB + 128], [KTl], [KTl], eng='pool')
                k.cp(KTh[64:128, :, 0:128], KTh[64:128, :, TB:TB + 128], [KTh], [KTh], eng='pool')
                k.cp(Vtok[:, 0, :], Vtok[:, 4, :], [Vtok], [Vtok], eng='pool')
                if blk == 0:
                    dump("obT", obT[:], [obT])

                ck('swa')
                k.switch('A', 'F')
                wlist['cur'] = W8 + FW
                for j in range(8):
                    wt = nextw()
                    wload(w_in[:, OFF_GA + j * 128:OFF_GA + (j + 1) * 128], 128, c0=0, tile=wt)
                    wload(w_in[:, OFF_GB + j * 128:OFF_GB + (j + 1) * 128], 128, c0=128, tile=wt)
                    wload(w_gdn_out[:, j * 128:(j + 1) * 128], 128, c0=256, tile=wt)
                    wload(w_swa_out[:, j * 128:(j + 1) * 128], 128, c0=384, tile=wt)
                    bs = [k.bank() for _ in range(4)]
                    srcs = [hT, hT, onT, obT]
                    for q in range(4):
                        for kk in range(8):
                            k.mm(bs[q][:, :], wt[:, kk, q * 128:(q + 1) * 128], srcs[q][:, kk, :], [wt, srcs[q]], [bs[q]], start=(kk == 0), stop=(kk == 7))
                    k.act(sga[:], bs[0][:, :], AF.Sigmoid, [bs[0]], [sga])
                    k.act(sgb[:], bs[1][:, :], AF.Sigmoid, [bs[1]], [sgb])
                    k.tt(sga[:], sga[:], bs[2][:, :], ALU.mult, [sga, bs[2]], [sga])
                    k.tt(sgb[:], sgb[:], bs[3][:, :], ALU.mult, [sgb, bs[3]], [sgb])
                    k.tt(mixT[:, j, :], sga[:], sgb[:], ALU.add, [sga, sgb], [mixT])
                    k.rel(*bs)
                ck('merge')
                wts = [wload(w_o[:, hf * 512:(hf + 1) * 512], 512) for hf in range(2)]
                for t in range(4):
                    for hf in range(2):
                        b = k.bank()
                        for kk in range(8):
                            k.mm(b[:, :], mixT[:, kk, t * 128:(t + 1) * 128], wts[hf][:, kk, :], [mixT, wts[hf]], [b], start=(kk == 0), stop=(kk == 7))
                        k.tt(sga[:], b[:, :], g1bc[:, hf * 512:(hf + 1) * 512], ALU.mult, [b, g1bc], [sga])
                        k.rel(b)
                        k.tt(x1[t][:, hf * 512:(hf + 1) * 512], x1[t][:, hf * 512:(hf + 1) * 512], sga[:], ALU.add, [x1[t], sga], [x1[t]], eng='pool')
                    rms_to_T(x1[t][:], x1[t], h2T, h2T, (a2, a2), (modT[:, 24:32, NS], modT), t)
                if blk == 0:
                    dump("x1", x1[0][:], [x1[0]])

                ck('wo')
                for jp in range(NFC // 2):
                    wt = nextw()
                    wload(w_ffn_gate[:, jp * 256:(jp + 1) * 256], 256, c0=0, tile=wt)
                    wload(w_ffn_up[:, jp * 256:(jp + 1) * 256], 256, c0=256, tile=wt)
                    for jj in range(2):
                        j = jp * 2 + jj
                        bg, bu = k.bank(), k.bank()
                        for kk in range(8):
                            k.mm(bg[:, :], wt[:, kk, jj * 128:(jj + 1) * 128], h2T[:, kk, :], [wt, h2T], [bg], start=(kk == 0), stop=(kk == 7))
                        for kk in range(8):
                            k.mm(bu[:, :], wt[:, kk, 256 + jj * 128:256 + (jj + 1) * 128], h2T[:, kk, :], [wt, h2T], [bu], start=(kk == 0), stop=(kk == 7))
                        k.cp(gpre[:, 0:2], fhalo[:, j, :], [fhalo], [gpre], eng='pool')
                        k.cp(gpre[:, 2:2 + TB], bg[:, :], [bg], [gpre], eng='act')
                        k.cp(fhalo[:, j, :], gpre[:, TB:TB + 2], [gpre], [fhalo], eng='pool')
                        k.ts(gcv[:], gpre[:, 0:TB], fcwT[:, j, 0:1], fcbT[:, j:j + 1], ALU.mult, ALU.add, [gpre, fcwT, fcbT], [gcv])
                        for tap in range(1, 3):
                            k.stt(gcv[:], gpre[:, tap:tap + TB], fcwT[:, j, tap:tap + 1], gcv[:], ALU.mult, ALU.add, [gpre, fcwT, gcv], [gcv])
                        k.act(gcv[:], gcv[:], AF.Silu, [gcv], [gcv])
                        k.tt(actT[:, j, :], gcv[:], bu[:, :], ALU.mult, [gcv, bu], [actT])
                        k.rel(bg, bu)
                if last:
                    for j_ in range(2):
                        k.dma('sp', o_ffn_p[j_].rearrange("(c p) -> p c", p=128), fhalo[:, :, j_], reads=[fhalo], allow_slow_non_contiguous=True)
                for hf in range(2):
                    bs = [k.bank() for _ in range(4)]
                    for kg in range(3):
                        nk = 8 if kg < 2 else NFC - 16
                        wt = nextw()
                        k.dma('pool', wt[:, 0:nk, :], w_ffn_down[kg * 1024:kg * 1024 + nk * 128, hf * 512:(hf + 1) * 512].rearrange("(c p) n -> p c n", p=128),
                              writes=[wt])
                        for kk in range(nk):
                            kf = kg * 8 + kk
                            for t in range(4):
                                k.mm(bs[t][:, :], actT[:, kf, t * 128:(t + 1) * 128], wt[:, kk, :], [actT, wt], [bs[t]], start=(kf == 0), stop=(kf == NFC - 1))
                    for t in range(4):
                        k.tt(sga[:], bs[t][:, :], g2bc[:, hf * 512:(hf + 1) * 512], ALU.mult, [bs[t], g2bc], [sga])
                        k.tt(x1[t][:, hf * 512:(hf + 1) * 512], x1[t][:, hf * 512:(hf + 1) * 512], sga[:], ALU.add, [x1[t], sga], [x1[t]], eng='pool')
                    k.rel(*bs)
                ck('ffn')
                for t in range(4):
                    k.act(yt[:], x1[t][:], AF.Square, [x1[t]], [yt, ss2], accum=ss2[:])
                    k.act(rs2[:], ss2[:], AF.Ln, [ss2, epsc], [rs2], bias=epsc[:], scale=1.0 / D)
                    k.act(rs2[:], rs2[:], AF.Exp, [rs2], [rs2], scale=-0.5)
                    k.stt(yt[:], x1[t][:], rs2[:], fnw_bc[:], ALU.mult, ALU.mult, [x1[t], rs2, fnw_bc], [yt])
                    k.dma('sp', y_p[t0 + t * 128:t0 + (t + 1) * 128, :], yt[:], reads=[yt])
        except _Stop:
            pass
        k.finish('sp')
        print("instr counts", k.cnt, "dma sems", k.ndsem)
        if os.environ.get('MMSTAT'):
            tot = sum(k.mmstat.values())
            for ln, c in sorted(k.mmstat.items(), key=lambda kv: -kv[1])[:40]:
                print("  mm line %d: %.1f us (%.1f%%)" % (ln, c / 2400.0, 100.0 * c / tot))
            print("  total est %.1f us" % (tot / 2400.0))
    return nc


OUT_NAMES = ["y_p", "y_s", "o_S_p", "o_S_s", "o_conv_p", "o_conv_s", "o_k_p", "o_k_s", "o_v_p", "o_v_s", "o_ffn_p", "o_ffn_s"]


def make_in_maps(inp, cores):
    f = lambda a: np.ascontiguousarray(a, dtype=np.float32)
    shared = {
        "w_mod": f(inp["w_mod"][0]), "b_mod": f(inp["b_mod"][0][None]), "norm1_w": f(inp["norm1_w"][0][None]),
        "norm2_w": f(inp["norm2_w"][0][None]), "w_in": f(inp["w_in"][0]), "gdn_conv_w": f(inp["gdn_conv_w"][0]),
        "gdn_a_log": f(inp["gdn_a_log"][0]), "gdn_dt_bias": f(inp["gdn_dt_bias"][0]),
        "gdn_onorm_w": f(inp["gdn_onorm_w"][0][None]), "w_gdn_out": f(inp["w_gdn_out"][0]),
        "swa_sinks": f(inp["swa_sinks"][0]), "w_swa_out": f(inp["w_swa_out"][0]), "w_o": f(inp["w_o"][0]),
        "w_ffn_gate": f(inp["w_ffn_gate"][0]), "w_ffn_up": f(inp["w_ffn_up"][0]), "ffn_conv_w": f(inp["ffn_conv_w"][0]),
        "ffn_conv_b": f(inp["ffn_conv_b"][0][None]), "w_ffn_down": f(inp["w_ffn_down"][0]),
        "final_norm_w": f(inp["final_norm_w"]),
    }
    maps = []
    for b in cores:
        s = slice(b * NS, (b + 1) * NS)
        m = dict(shared)
        m["x_p"] = f(inp["x_prompt"][b])
        m["x_s"] = f(inp["x_sample"][s, 0])
        m["c17"] = f(np.concatenate([inp["c_sample"][s], inp["c_prompt"][b:b + 1]], axis=0))
        m["st_S"] = f(np.transpose(inp["state_gdn_S"][0, s], (0, 2, 1, 3)))
        m["st_conv"] = f(inp["state_gdn_conv"][0, s].reshape(NS * 3, 3072))
        m["st_k"] = f(np.transpose(inp["cache_swa_k"][0, s].reshape(NS, 128, 256), (1, 0, 2)))
        m["st_v"] = f(np.transpose(inp["cache_swa_v"][0, s].reshape(NS, 128, 256), (1, 0, 2)))
        m["st_ffn"] = f(inp["state_ffn_conv"][0, s].reshape(NS * 2, DFF))
        maps.append(m)
    return maps


def kernel(**inp):
    nc = build_nc()
    cores = list(range(8))
    res = run_bass_kernel_spmd(nc, make_in_maps(inp, cores), core_ids=cores)
    r = res.results
    cat = lambda n: np.concatenate([r[i][n] for i in range(8)], axis=0)
    stack = lambda n: np.stack([r[i][n] for i in range(8)], axis=0)
    y_prompt = stack("y_p")
    y_sample = cat("y_s").reshape(128, 1, D)
    gS_p = np.ascontiguousarray(np.transpose(stack("o_S_p"), (0, 2, 1, 3)))[None]
    gS_s = np.ascontiguousarray(np.transpose(cat("o_S_s"), (0, 2, 1, 3)))[None]
    gc_p = stack("o_conv_p")[None]
    gc_s = cat("o_conv_s")[None]
    k_p = stack("o_k_p").reshape(1, 8, 128, 4, 64)
    k_s = np.ascontiguousarray(np.concatenate([np.transpose(r[i]["o_k_s"], (1, 0, 2)) for i in range(8)], axis=0)).reshape(1, 128, 128, 4, 64)
    v_p = stack("o_v_p").reshape(1, 8, 128, 4, 64)
    v_s = np.ascontiguousarray(np.concatenate([np.transpose(r[i]["o_v_s"], (1, 0, 2)) for i in range(8)], axis=0)).reshape(1, 128, 128, 4, 64)
    f_p = stack("o_ffn_p")[None]
    f_s = cat("o_ffn_s")[None]
    return (y_prompt, y_sample, gS_p, gS_s, gc_p, gc_s, k_p, k_s, v_p, v_s, f_p, f_s)
```

```python
import os
import numpy as np
import concourse.bass as bass
import concourse.mybir as mybir
from concourse.bass_utils import run_bass_kernel_spmd
from contextlib import ExitStack

F32 = mybir.dt.float32
BF16 = mybir.dt.bfloat16
AF = mybir.ActivationFunctionType
ALU = mybir.AluOpType
AX = mybir.AxisListType

D = 1024
SEQ = 2048
TB = 512
NBLK = SEQ // TB
NS = 16
DFF = 2816
NFC = DFF // 128
OFF_QKV, OFF_GATE, OFF_BETA, OFF_A, OFF_SQ, OFF_SK, OFF_SV, OFF_GA, OFF_GB = 0, 3072, 4096, 4104, 4112, 5136, 5392, 5648, 6672
INW = 7696
EPS = 1e-6
NEG = -30000.0


class Tile:
    def __init__(self, t, name):
        self.t = t
        self.name = name
        self.lw = None
        self.rd = {}
        self.dkey = None
        self.dcnt = 0

    def __getitem__(self, k):
        return self.t[k]


class K:
    def __init__(self, nc, es):
        self.nc = nc
        self.es = es
        self.eng = {'pe': nc.tensor, 'dve': nc.vector, 'act': nc.scalar, 'pool': nc.gpsimd, 'sp': nc.sync}
        self.sem = {}
        for e in self.eng:
            self.sem[e] = es.enter_context(nc.semaphore('s_' + e))
        self.cnt = {e: 0 for e in self.eng}
        self.seen = {e: {} for e in self.eng}
        self.ndsem = 0
        self.tiles = []
        self.free_banks = []
        self.dbg = []
        self.phase_off = {}

    def sb(self, name, shape, dt=F32, es=None):
        t = (es or self.es).enter_context(self.nc.sbuf_tensor(name, list(shape), dt))
        T = Tile(t, name)
        self.tiles.append(T)
        return T

    def view(self, ap, name):
        T = Tile(ap, name)
        self.tiles.append(T)
        return T

    def init_psum(self):
        self.psum = self.es.enter_context(self.nc.psum_tensor("psum", [128, 4096], F32))
        self.banks = []
        for i in range(8):
            T = Tile(self.psum[:, i * 512:(i + 1) * 512], "bank%d" % i)
            T.excl = True
            self.tiles.append(T)
            self.banks.append(T)
        self.free_banks = list(self.banks)

    def bank(self):
        assert self.free_banks, "out of PSUM banks"
        return self.free_banks.pop(0)

    def rel(self, *bs):
        for b in bs:
            assert b not in self.free_banks
            self.free_banks.append(b)

    def _deps(self, e, reads, writes, skip=None):
        deps = {}

        def add(kv):
            k_, v = kv
            if deps.get(k_, 0) < v:
                deps[k_] = v
        for t in reads:
            if t.lw:
                add(t.lw)
            if getattr(t, 'excl', False):
                for kv in t.rd.items():
                    if kv[0] != e:
                        add(kv)
        for t in writes:
            if t.lw:
                add(t.lw)
            for kv in t.rd.items():
                add(kv)
        for k_, v in deps.items():
            if k_ == e and e == 'pe':
                continue
            if skip is not None and k_ == skip:
                continue
            if self.seen[e].get(k_, 0) >= v:
                continue
            self.eng[e].wait_ge(self.sem[k_], v)
            self.seen[e][k_] = v

    def op(self, e, fn, reads=(), writes=()):
        if e == 'pool' and getattr(self, 'pool_to', None):
            e = self.pool_to
        self._deps(e, reads, writes)
        ins = fn(self.eng[e])
        ins.then_inc(self.sem[e], 1)
        self.cnt[e] += 1
        c = self.cnt[e]
        for t in writes:
            t.lw = (e, c)
            t.rd = {}
        for t in reads:
            if t not in writes:
                t.rd[e] = c

    def dma(self, q, out, in_, reads=(), writes=(), semtile=None, indep=False, **kw):
        T = semtile if semtile is not None else (writes[0] if writes else reads[0])
        self._deps(q, reads, writes, skip=(T.dkey if indep else None))
        if T.dkey is None:
            T.dkey = 'd%d' % self.ndsem
            self.ndsem += 1
            self.sem[T.dkey] = self.es.enter_context(self.nc.semaphore(T.dkey))
        self.eng[q].dma_start(out=out, in_=in_, **kw).then_inc(self.sem[T.dkey], 16)
        T.dcnt += 16
        for t in writes:
            t.lw = (T.dkey, T.dcnt)
            t.rd = {}
        for t in reads:
            t.rd[T.dkey] = T.dcnt

    def init_arena(self, nbytes):
        self.arena = self.es.enter_context(self.nc.sbuf_tensor("arena", [128, nbytes // 4], F32))
        self.phase_tiles = {}

    def carve(self, phase, name, shape, dt=F32):
        off = self.phase_off.get(phase, 0)
        n = 1
        for d_ in shape[1:]:
            n *= d_
        nb = n * (2 if dt == BF16 else 4)
        nb = (nb + 63) // 64 * 64
        assert off + nb <= self.arena.shape[1] * 4, "arena overflow in phase %s at %s: %d" % (phase, name, off + nb)
        ap = self.arena[0:shape[0], off // 4:(off + nb) // 4]
        if dt == BF16:
            ap = ap.bitcast(BF16)
        ap = ap[:, 0:n]
        if len(shape) == 3:
            ap = ap.rearrange("p (a b) -> p a b", b=shape[2])
        elif len(shape) == 4:
            ap = ap.rearrange("p (a b c) -> p a b c", b=shape[2], c=shape[3])
        self.phase_off[phase] = off + nb
        T = Tile(ap, name)
        self.tiles.append(T)
        self.phase_tiles.setdefault(phase, []).append(T)
        return T

    def switch(self, frm, to):
        acc = {}
        for F_ in self.phase_tiles.get(frm, []):
            if F_.lw:
                acc[F_.lw[0]] = max(acc.get(F_.lw[0], 0), F_.lw[1])
            for k_, v in F_.rd.items():
                acc[k_] = max(acc.get(k_, 0), v)
        for T in self.phase_tiles.get(to, []):
            for k_, v in acc.items():
                T.rd[k_] = max(T.rd.get(k_, 0), v)

    def barrier(self):
        for e in self.eng:
            for T in self.tiles:
                if T.dkey is not None and self.seen[e].get(T.dkey, 0) < T.dcnt:
                    self.eng[e].wait_ge(self.sem[T.dkey], T.dcnt)
                    self.seen[e][T.dkey] = T.dcnt
            for k_ in self.eng:
                if k_ != e and self.cnt[k_] > 0 and self.seen[e].get(k_, 0) < self.cnt[k_]:
                    self.eng[e].wait_ge(self.sem[k_], self.cnt[k_])
                    self.seen[e][k_] = self.cnt[k_]

    def finish(self, e='sp'):
        for T in self.tiles:
            if T.dkey is not None and self.seen[e].get(T.dkey, 0) < T.dcnt:
                self.eng[e].wait_ge(self.sem[T.dkey], T.dcnt)
                self.seen[e][T.dkey] = T.dcnt
        for k_ in self.eng:
            if k_ != e and self.cnt[k_] > 0 and self.seen[e].get(k_, 0) < self.cnt[k_]:
                self.eng[e].wait_ge(self.sem[k_], self.cnt[k_])
                self.seen[e][k_] = self.cnt[k_]

    def mm(self, out, lhsT, rhs, r, w, start=True, stop=True):
        import traceback
        ln = traceback.extract_stack(limit=2)[0].lineno
        n = 1
        for d_ in out.shape[1:]:
            n *= d_
        cyc = n * (4 if rhs.dtype == F32 else 1)
        st = self.__dict__.setdefault('mmstat', {})
        st[ln] = st.get(ln, 0) + max(cyc, 64)
        self.op('pe', lambda e: e.matmul(out, lhsT=lhsT, rhs=rhs, start=start, stop=stop), reads=r, writes=w)

    def tr(self, out, in_, ident, r, w):
        import traceback
        ln = traceback.extract_stack(limit=2)[0].lineno
        n = 1
        for d_ in out.shape[1:]:
            n *= d_
        cyc = n * (2 if in_.dtype == F32 else 1)
        st = self.__dict__.setdefault('mmstat', {})
        st[ln] = st.get(ln, 0) + max(cyc, 64)
        self.op('pe', lambda e: e.transpose(out=out, in_=in_, identity=ident), reads=r, writes=w)

    def act(self, out, in_, func, r, w, bias=None, scale=None, accum=None, eng='act'):
        kw = {}
        if bias is not None:
            kw['bias'] = bias
        if scale is not None:
            kw['scale'] = scale
        if accum is not None:
            kw['accum_out'] = accum
        self.op('act', lambda e: e.activation(out=out, in_=in_, func=func, **kw), reads=r, writes=w)

    def tt(self, out, in0, in1, op, r, w, eng='dve'):
        self.op(eng, lambda e: e.tensor_tensor(out=out, in0=in0, in1=in1, op=op), reads=r, writes=w)

    def ts(self, out, in0, s1, s2, op0, op1, r, w, eng='dve', accum=None):
        if op1 is None:
            self.op(eng, lambda e: e.tensor_scalar(out=out, in0=in0, scalar1=s1, scalar2=None, op0=op0), reads=r, writes=w)
        else:
            self.op(eng, lambda e: e.tensor_scalar(out=out, in0=in0, scalar1=s1, scalar2=s2, op0=op0, op1=op1), reads=r, writes=w)

    def stt(self, out, in0, scalar, in1, op0, op1, r, w, accum=None):
        if accum is None:
            self.op('dve', lambda e: e.scalar_tensor_tensor(out=out, in0=in0, scalar=scalar, in1=in1, op0=op0, op1=op1), reads=r, writes=w)
        else:
            self.op('dve', lambda e: e.scalar_tensor_tensor(out=out, in0=in0, scalar=scalar, in1=in1, op0=op0, op1=op1, accum_out=accum), reads=r, writes=w)

    def cp(self, out, in_, r, w, eng='dve'):
        if eng == 'act':
            self.op('act', lambda e: e.activation(out=out, in_=in_, func=AF.Copy), reads=r, writes=w)
        else:
            self.op(eng, lambda e: e.tensor_copy(out=out, in_=in_), reads=r, writes=w)

    def memset(self, out, val, w, eng='pool'):
        self.op(eng, lambda e: e.memset(out, val), writes=w)

    def asel(self, out, in_, pattern, cmp, fill, base, cm, r, w):
        self.op('pool', lambda e: e.affine_select(out=out, in_=in_, pattern=pattern, compare_op=cmp, fill=fill,
                                                  base=base, channel_multiplier=cm), reads=r, writes=w)


def bc(ap, axis, n):
    a = ap.unsqueeze(axis)
    shp = list(a.shape)
    shp[axis] = n
    return a.broadcast_to(shp)


class _Stop(Exception):
    pass


def build_nc(debug=False, nblk=NBLK, do_sample=True, stop=None):
    nc = bass.Bass("TRN2", target_bir_lowering=False)

    def din(name, shape):
        return nc.dram_tensor(name, list(shape), F32, kind="ExternalInput").ap()

    def dout(name, shape):
        return nc.dram_tensor(name, list(shape), F32, kind="ExternalOutput").ap()

    x_p = din("x_p", [SEQ, D])
    x_s = din("x_s", [NS, D])
    c17 = din("c17", [NS + 1, D])
    st_S = din("st_S", [NS, 128, 8, 128])
    st_conv = din("st_conv", [NS * 3, 3072])
    st_k = din("st_k", [128, NS, 256])
    st_v = din("st_v", [128, NS, 256])
    st_ffn = din("st_ffn", [NS * 2, DFF])
    w_mod = din("w_mod", [D, 6 * D])
    b_mod = din("b_mod", [1, 6 * D])
    norm1_w = din("norm1_w", [1, D])
    norm2_w = din("norm2_w", [1, D])
    w_in = din("w_in", [D, INW])
    gdn_conv_w = din("gdn_conv_w", [4, 3072])
    gdn_a_log = din("gdn_a_log", [8])
    gdn_dt_bias = din("gdn_dt_bias", [8])
    gdn_onorm_w = din("gdn_onorm_w", [1, 128])
    w_gdn_out = din("w_gdn_out", [D, D])
    swa_sinks = din("swa_sinks", [16])
    w_swa_out = din("w_swa_out", [D, D])
    w_o = din("w_o", [D, D])
    w_ffn_gate = din("w_ffn_gate", [D, DFF])
    w_ffn_up = din("w_ffn_up", [D, DFF])
    ffn_conv_w = din("ffn_conv_w", [3, DFF])
    ffn_conv_b = din("ffn_conv_b", [1, DFF])
    w_ffn_down = din("w_ffn_down", [DFF, D])
    final_norm_w = din("final_norm_w", [D])

    y_p = dout("y_p", [SEQ, D])
    y_s = dout("y_s", [NS, D])
    o_S_p = dout("o_S_p", [128, 8, 128])
    o_S_s = dout("o_S_s", [NS, 128, 8, 128])
    o_conv_p = dout("o_conv_p", [3, 3072])
    o_conv_s = dout("o_conv_s", [NS, 3, 3072])
    o_k_p = dout("o_k_p", [128, 256])
    o_k_s = dout("o_k_s", [128, NS, 256])
    o_v_p = dout("o_v_p", [128, 256])
    o_v_s = dout("o_v_s", [128, NS, 256])
    o_ffn_p = dout("o_ffn_p", [2, DFF])
    o_ffn_s = dout("o_ffn_s", [NS, 2, DFF])

    with ExitStack() as es:
        k = K(nc, es)
        k.init_psum()
        PS = k.psum

        def dump(name, ap, tiles):
            if not debug:
                return
            o = nc.dram_tensor("dbg_" + name, list(ap.shape), ap.dtype, kind="ExternalOutput").ap()
            dt_ = Tile(None, 'dbg_' + name)
            k.tiles.append(dt_)
            k.dma('sp', o, ap, reads=tiles, semtile=dt_)

        dbgsem = k.sb("dbgsem", [1, 1])

        def ck(name):
            if stop == name:
                raise _Stop()

        try:
            identf = k.sb("identf", [128, 128])
            identb = k.sb("identb", [128, 128], BF16)
            onesf = k.sb("onesf", [128, 128])
            onesb = k.sb("onesb", [128, 128], BF16)
            Um = k.sb("Um", [64, 64])
            SLm = k.sb("SLm", [64, 64])
            SUm = k.sb("SUm", [64, 64])
            nSL = k.sb("nSL", [64, 64])
            nSU = k.sb("nSU", [64, 64])
            maskA = k.sb("maskA", [128, 256])
            maskB = k.sb("maskB", [128, 256])
            E16 = k.sb("E16", [NS + 1, 128])
            Esel = k.sb("Esel", [NS, NS, 128])
            epsc = k.sb("epsc", [128, 1])
            onec = k.sb("onec", [128, 1])

            k.memset(identf[:], 0.0, [identf])
            k.asel(identf[:], identf[:], [[-1, 128]], ALU.not_equal, 1.0, 0, 1, [identf], [identf])
            k.cp(identb[:], identf[:], [identf], [identb])
            k.memset(onesf[:], 1.0, [onesf])
            k.memset(onesb[:], 1.0, [onesb])
            k.memset(epsc[:], EPS, [epsc])
            k.memset(onec[:], 1.0, [onec])
            k.memset(Um[:], 1.0, [Um])
            k.asel(Um[:], Um[:], [[1, 64]], ALU.is_ge, 0.0, 0, -1, [Um], [Um])
            k.memset(SLm[:], 1.0, [SLm])
            k.asel(SLm[:], SLm[:], [[-1, 64]], ALU.is_ge, 0.0, -1, 1, [SLm], [SLm])
            k.memset(SUm[:], 1.0, [SUm])
            k.asel(SUm[:], SUm[:], [[1, 64]], ALU.is_ge, 0.0, -1, -1, [SUm], [SUm])
            k.ts(nSL[:], SLm[:], -1.0, None, ALU.mult, None, [SLm], [nSL])
            k.ts(nSU[:], SUm[:], -1.0, None, ALU.mult, None, [SUm], [nSU])
            k.memset(maskA[:], 0.0, [maskA])
            k.asel(maskA[:], maskA[:], [[1, 256]], ALU.is_ge, NEG, -1, -1, [maskA], [maskA])
            k.asel(maskA[:], maskA[:], [[-1, 256]], ALU.is_ge, NEG, 128, 1, [maskA], [maskA])
            k.asel(maskB[:], maskA[:], [[1, 256]], ALU.is_ge, NEG, -128, 0, [maskA], [maskB])
            k.memset(E16[:], 0.0, [E16])
            k.asel(E16[:], E16[:], [[0, 128]], ALU.not_equal, 1.0, -NS, 1, [E16], [E16])
            k.memset(Esel[:], 0.0, [Esel])
            k.asel(Esel[:], Esel[:], [[-1, NS], [0, 128]], ALU.not_equal, 1.0, 0, 1, [Esel], [Esel])

            k.pool_to = 'dve'
            modT = k.sb("modT", [128, 48, NS + 1])
            n1w = k.sb("n1w", [128, 8])
            n2w = k.sb("n2w", [128, 8])
            a1 = k.sb("a1", [128, 8])
            a2 = k.sb("a2", [128, 8])
            A1s = k.sb("A1s", [128, 8, NS])
            A2s = k.sb("A2s", [128, 8, NS])
            cwT = k.sb("cwT", [128, 24, 4])
            fcwT = k.sb("fcwT", [128, NFC, 3])
            fcbT = k.sb("fcbT", [128, NFC])
            onwT = k.sb("onwT", [128, 1])
            fnw_bc = k.sb("fnw_bc", [128, D])
            g1bc = k.sb("g1bc", [128, D])
            g2bc = k.sb("g2bc", [128, D])
            gtok1 = k.sb("gtok1", [NS + 1, D])
            gtok2 = k.sb("gtok2", [NS + 1, D])
            negA = k.sb("negA", [64, 8])
            dtb = k.sb("dtb", [64, 8])
            sinks = k.sb("sinks", [128, 16])

            W8 = [k.sb("W8_%d" % i, [128, 8, 512], BF16) for i in range(3)]
            ring = {'w8': 0, 'wd': 0}

            wlist = {'cur': W8}

            def nextw():
                wl = wlist['cur']
                t_ = wl[ring['w8'] % len(wl)]
                ring['w8'] += 1
                return t_

            def wload(src, ncols, c0=0, tile=None):
                piece = tile is not None
                if tile is None:
                    tile = nextw()
                k.dma('pool', tile[:, :, c0:c0 + ncols], src.rearrange("(c p) n -> p c n", p=128), writes=[tile], indep=piece)
                return tile

            with ExitStack() as es2:
                stage = k.sb("stage", [8, 6 * D], F32, es=es2)
                c17t = k.sb("c17t", [NS + 1, D], F32, es=es2)
                scT = k.sb("scT", [128, 8, NS + 1], BF16, es=es2)
                bmT = k.sb("bmT", [128, 48], F32, es=es2)

                def featmajor(src, r, C, dst_ap, dst_tile):
                    k.dma('sp', stage[0:r, 0:C], src, writes=[stage])
                    nchunk = C // 128
                    c = 0
                    while c < nchunk:
                        n = min(nchunk - c, 512 // r)
                        b = k.bank()
                        for j in range(n):
                            k.tr(b[:, j * r:(j + 1) * r], stage[0:r, (c + j) * 128:(c + j + 1) * 128], identf[0:r, 0:r],
                                 [stage, identf], [b])
                        if r == 1:
                            k.cp(dst_ap[:, c:c + n], b[:, 0:n], [b], [dst_tile])
                        else:
                            k.cp(dst_ap[:, c:c + n, :], b[:, 0:n * r].rearrange("p (c r) -> p c r", r=r), [b], [dst_tile])
                        k.rel(b)
                        c += n

                ck('consts')
                featmajor(b_mod, 1, 6 * D, bmT, bmT)
                featmajor(norm1_w, 1, D, n1w, n1w)
                featmajor(norm2_w, 1, D, n2w, n2w)
                featmajor(gdn_conv_w, 4, 3072, cwT, cwT)
                featmajor(ffn_conv_w, 3, DFF, fcwT, fcwT)
                featmajor(ffn_conv_b, 1, DFF, fcbT, fcbT)
                featmajor(gdn_onorm_w, 1, 128, onwT, onwT)
                ck('fm')
                k.dma('sp', fnw_bc[:], final_norm_w.partition_broadcast(128), writes=[fnw_bc])
                k.dma('sp', negA[:], gdn_a_log.partition_broadcast(64), writes=[negA])
                k.dma('sp', dtb[:], gdn_dt_bias.partition_broadcast(64), writes=[dtb])
                k.dma('sp', sinks[:], swa_sinks.partition_broadcast(128), writes=[sinks])
                k.act(negA[:], negA[:], AF.Exp, [negA], [negA])
                k.ts(negA[:], negA[:], -1.0, None, ALU.mult, None, [negA], [negA])

                ck('bcast')
                k.dma('sp', c17t[:], c17, writes=[c17t])
                k.act(c17t[:], c17t[:], AF.Silu, [c17t], [c17t])
                b = k.bank()
                for kk in range(8):
                    k.tr(b[:, kk * 17:(kk + 1) * 17], c17t[:, kk * 128:(kk + 1) * 128], identf[0:17, 0:17], [c17t, identf], [b])
                k.cp(scT[:], b[:, 0:8 * 17].rearrange("p (c r) -> p c r", r=17), [b], [scT])
                k.rel(b)
                ck('silu')
                for half in range(2):
                    b = k.bank()
                    for g in range(6):
                        wt = wload(w_mod[:, (half * 6 + g) * 512:(half * 6 + g + 1) * 512], 512)
                        for j in range(4):
                            jj = g * 4 + j
                            for kk in range(8):
                                k.mm(b[:, jj * 17:(jj + 1) * 17], wt[:, kk, j * 128:(j + 1) * 128], scT[:, kk, :], [wt, scT], [b],
                                     start=(kk == 0), stop=(kk == 7))
                    k.tt(modT[:, half * 24:(half + 1) * 24, :], b[:, 0:24 * 17].rearrange("p (c r) -> p c r", r=17),
                         bc(bmT[:, half * 24:(half + 1) * 24], 2, 17), ALU.add, [b, bmT], [modT])
                    k.rel(b)
                ck('modT')
                k.barrier()
            dump("modT", modT[:], [modT])

            k.ts(a1[:], modT[:, 8:16, NS], 1.0, None, ALU.add, None, [modT], [a1])
            k.tt(a1[:], a1[:], n1w[:], ALU.mult, [a1, n1w], [a1])
            k.ts(a2[:], modT[:, 32:40, NS], 1.0, None, ALU.add, None, [modT], [a2])
            k.tt(a2[:], a2[:], n2w[:], ALU.mult, [a2, n2w], [a2])
            k.ts(A1s[:], modT[:, 8:16, 0:NS], 1.0, None, ALU.add, None, [modT], [A1s])
            k.tt(A1s[:], A1s[:], bc(n1w[:], 2, NS), ALU.mult, [A1s, n1w], [A1s])
            k.ts(A2s[:], modT[:, 32:40, 0:NS], 1.0, None, ALU.add, None, [modT], [A2s])
            k.tt(A2s[:], A2s[:], bc(n2w[:], 2, NS), ALU.mult, [A2s, n2w], [A2s])
            for (c0, gtok, gbc) in ((16, gtok1, g1bc), (40, gtok2, g2bc)):
                b0, b1 = k.bank(), k.bank()
                for j in range(8):
                    bb = b0 if j < 4 else b1
                    k.tr(bb[0:17, (j % 4) * 128:(j % 4 + 1) * 128], modT[:, c0 + j, :], identf[:], [modT, identf], [bb])
                k.cp(gtok[:, 0:512], b0[0:17, :], [b0], [gtok])
                k.cp(gtok[:, 512:1024], b1[0:17, :], [b1], [gtok])
                for hf, bb in ((0, b0), (1, b1)):
                    k.mm(bb[:, :], E16[:], gtok[:, hf * 512:(hf + 1) * 512], [E16, gtok], [bb])
                    k.cp(gbc[:, hf * 512:(hf + 1) * 512], bb[:, :], [bb], [gbc])
                k.rel(b0, b1)
            dump("g1bc", g1bc[:], [g1bc])

            ck('derived')
            S_all = k.sb("S_all", [128, 8, 128])
            halo = k.sb("halo", [128, 24, 3])
            fhalo = k.sb("fhalo", [128, NFC, 2])
            KTl = k.sb("KTl", [128, 4, 128 + TB], BF16)
            KTh = k.sb("KTh", [128, 4, 128 + TB], BF16)
            Vtok = k.sb("Vtok", [128, 5, 512], BF16)
            k.memset(S_all[:], 0.0, [S_all])
            k.memset(halo[:], 0.0, [halo])
            k.memset(fhalo[:], 0.0, [fhalo])
            k.memset(KTl[:], 0.0, [KTl])
            k.memset(KTh[:], 0.0, [KTh])
            k.memset(Vtok[:], 0.0, [Vtok])

            xres = [k.sb("xres%d" % i, [128, D]) for i in range(4)]
            x1 = xres
            xn = k.sb("xn", [128, D], BF16)
            ss1 = k.sb("ss1", [128, 1])
            rs1 = k.sb("rs1", [128, 1])
            ss2, rs2 = ss1, rs1
            hT = k.sb("hT", [128, 8, TB], BF16)
            h2T = hT
            onT = k.sb("onT", [128, 8, TB], BF16)
            obT = k.sb("obT", [128, 8, TB], BF16)
            mixT = k.sb("mixT", [128, 8, TB], BF16)
            QT = mixT
            halo_out = k.sb("halo_out", [128, 24, 3])

            k.init_arena(74240)
            cG = lambda n, shp, dt=F32: k.carve('G', n, shp, dt)
            cA = lambda n, shp, dt=F32: k.carve('A', n, shp, dt)
            cF = lambda n, shp, dt=F32: k.carve('F', n, shp, dt)
            ba = cG("ba", [64, 8, 16])
            beta = cG("beta", [64, 8, 8])
            gg = cG("gg", [64, 8, 8])
            t64a = cG("t64a", [64, 8, 8])
            t64b = cG("t64b", [64, 8, 8])
            dd = cG("dd", [64, 64])
            ed = cG("ed", [64, 64])
            ekd = cG("ekd", [64, 64])
            bed = cG("bed", [64, 64])
            elast = cG("elast", [128, 64])
            pre = cG("pre", [128, 3 + TB])
            cv = cG("cv", [128, 3, TB])
            rqk = cG("rqk", [128, 2, TB])
            Qd2 = [cG("Qd%d" % i, [128, TB], BF16) for i in range(2)]
            Qtb = cG("Qtb", [128, TB], BF16)
            Ktb = cG("Ktb", [128, TB], BF16)
            Vtb = cG("Vtb", [128, TB], BF16)
            Sb = cG("Sb", [128, 128], BF16)
            Gp2 = [cG("Gp%d" % i, [128, TB]) for i in range(2)]
            SA = cG("SA", [64, 8, 64])
            SB = cG("SB", [64, 8, 64])
            Wm = cG("Wm", [64, 8, 64])
            Zm = cG("Zm", [64, 8, 64])
            Am2 = [cG("Am%d" % i, [64, 8, 64]) for i in range(2)]
            Amb2 = [cG("Amb%d" % i, [64, 8, 64], BF16) for i in range(2)]
            NEU = BF16
            WXH = [cG("WXH%d" % i, [64, 4, 2, 64], BF16) for i in range(2)]
            ZYH = [cG("ZYH%d" % i, [64, 4, 2, 64], BF16) for i in range(2)]
            Y32 = [cG("Y32_%d" % i, [64, 4, 64]) for i in range(2)]
            Yb = [cG("Yb_%d" % i, [64, 4, 64], BF16) for i in range(2)]
            Kbd = cG("Kbd", [64, 8, 128], NEU)
            Kdec2 = [cG("Kdec%d" % i, [64, 8, 128], F32 if os.environ.get('SUPD32', '0') == '1' else BF16) for i in range(2)]
            Vb = cG("Vb", [64, 8, 128], NEU)
            osb = cG("osb", [64, 8, 128])
            uu2 = [cG("uu%d" % i, [64, 8, 128]) for i in range(2)]
            wT2 = [cG("wT%d" % i, [128, 8, 64], BF16) for i in range(2)]
            vnew = cG("vnew", [64, 128], F32 if os.environ.get('SUPD32', '0') == '1' else BF16)
            oss = cG("oss", [64, 8])
            ors = cG("ors", [64, 8])
            on1 = cG("on1", [64, 8, 128], BF16)
            Qt_ap, Kt_ap = cv[:, 0, :], cv[:, 1, :]
            Ktok = cA("Ktok", [128, 512], BF16)
            kvout = cA("kvout", [128, 512])
            sc2 = [cA("sc%d" % i, [128, 4, 256]) for i in range(2)]
            pb2 = [cA("pb%d" % i, [128, 4, 256], BF16) for i in range(2)]
            PT2 = [cA("PT%d" % i, [128, 4, 2, 128], BF16) for i in range(2)]
            mx2 = [cA("mx%d" % i, [128, 4]) for i in range(2)]
            nmx2 = [cA("nmx%d" % i, [128, 4]) for i in range(2)]
            rsum2 = [cA("rsum%d" % i, [128, 4]) for i in range(2)]
            esk2 = [cA("esk%d" % i, [128, 4]) for i in range(2)]
            actT = cF("actT", [128, NFC, TB], BF16)
            sga = cF("sga", [128, TB])
            sgb = cF("sgb", [128, TB])
            gpre = cF("gpre", [128, 2 + TB])
            gcv = cF("gcv", [128, TB])
            yt = cF("yt", [128, D])
            k.carve('FW', 'fwpad', [128, k.phase_off['F'] // 4])
            FW = [k.carve('FW', 'FW%d' % i, [128, 8, 512], BF16) for i in range(4)]
            k.phase_tiles['FW'] = k.phase_tiles['FW'][1:]
            print("arena use", k.phase_off)

            def rms_to_T(xt, xt_tile, dstT, dst_tile, acol, bcol, t):
                k.act(xn[:], xt, AF.Square, [xt_tile], [xn, ss1], accum=ss1[:])
                k.act(rs1[:], ss1[:], AF.Ln, [ss1, epsc], [rs1], bias=epsc[:], scale=1.0 / D)
                k.act(rs1[:], rs1[:], AF.Exp, [rs1], [rs1], scale=-0.5)
                k.ts(xn[:], xt, rs1[:], None, ALU.mult, None, [xt_tile, rs1], [xn])
                b = k.bank()
                bv = b[:, :].bitcast(BF16)
                for kk in range(8):
                    k.tr(bv[:, kk * 128:(kk + 1) * 128], xn[:, kk * 128:(kk + 1) * 128], identb[:], [xn, identb], [b])
                for kk in range(8):
                    k.act(dstT[:, kk, t * 128:(t + 1) * 128], bv[:, kk * 128:(kk + 1) * 128], AF.Identity,
                          [b, acol[1], bcol[1]], [dst_tile], bias=bcol[0][:, kk:kk + 1], scale=acol[0][:, kk:kk + 1])
                k.rel(b)

            hoist = {'p1': False}

            def emit_p1(t0_):
                for t in range(4):
                    k.dma('sp', xres[t][:], x_p[t0_ + t * 128:t0_ + (t + 1) * 128, :], writes=[xres[t]])
                for t in range(4):
                    rms_to_T(xres[t][:], xres[t], hT, hT, (a1, a1), (modT[:, 0:8, NS], modT), t)

            def sample_phase():
                c1 = lambda n, shp, dt=F32: k.carve('S1', n, shp, dt)
                c2 = lambda n, shp, dt=F32: k.carve('S2', n, shp, dt)
                c3 = lambda n, shp, dt=F32: k.carve('S3', n, shp, dt)
                xs = c1("xs", [NS, D])
                xs_2 = c2("xs_2", [NS, D])
                xs_3 = c3("xs_3", [NS, D])
                hTs = k.sb("hTs", [128, 8, NS], BF16)
                onTs = k.sb("onTs", [128, 8, NS], BF16)
                obTs = k.sb("obTs", [128, 8, NS], BF16)
                OH = k.sb("OH", [128, NS, NS])
                sinkcol = k.sb("sinkcol", [128, 1])
                onw_bc = k.sb("onw_bc", [NS, 128])
                hsc = k.sb("hsc", [128, 8, NS])
                k.pool_to = None
                k.memset(OH[:], 0.0, [OH])
                k.asel(OH[:], OH[:], [[1, NS], [-1, NS]], ALU.not_equal, 1.0, 0, 0, [OH], [OH])
                k.pool_to = 'dve'
                for a_ in range(8):
                    k.dma('sp', sinkcol[a_ * 16:(a_ + 1) * 16, :], swa_sinks.rearrange("(h o) -> h o", o=1), writes=[sinkcol])
                k.dma('sp', onw_bc[:], gdn_onorm_w[0].partition_broadcast(NS), writes=[onw_bc])

                def rms_T_s(src, src_tile, dst, A_, B_ap, B_tile):
                    k.act(xn[0:NS, :], src, AF.Square, [src_tile], [xn, ss1], accum=ss1[0:NS, :])
                    k.act(rs1[0:NS, :], ss1[0:NS, :], AF.Ln, [ss1, epsc], [rs1], bias=epsc[0:NS, :], scale=1.0 / D)
                    k.act(rs1[0:NS, :], rs1[0:NS, :], AF.Exp, [rs1], [rs1], scale=-0.5)
                    k.ts(xn[0:NS, :], src, rs1[0:NS, :], None, ALU.mult, None, [src_tile, rs1], [xn])
                    b = k.bank()
                    bv = b[:, :].bitcast(BF16)
                    for kk in range(8):
                        k.tr(bv[:, kk * NS:(kk + 1) * NS], xn[0:NS, kk * 128:(kk + 1) * 128], identb[0:NS, 0:NS], [xn, identb], [b])
                    pv = bv[:, 0:8 * NS].rearrange("p (c s) -> p c s", s=NS)
                    k.tt(hsc[:], pv, A_[:], ALU.mult, [b, A_], [hsc])
                    k.rel(b)
                    k.tt(dst[:], hsc[:], B_ap, ALU.add, [hsc, B_tile], [dst])

                def tok_mm(srcT, wt, c0, n, dst_ap, dst_tile, scale=None, func=None):
                    b = k.bank()
                    for kk in range(8):
                        k.mm(b[0:NS, 0:n], srcT[:, kk, :], wt[:, kk, c0:c0 + n], [srcT, wt], [b], start=(kk == 0), stop=(kk == 7))
                    if func is not None:
                        k.act(dst_ap, b[0:NS, 0:n], func, [b], [dst_tile])
                    elif scale is not None:
                        k.act(dst_ap, b[0:NS, 0:n], AF.Copy, [b], [dst_tile], scale=scale)
                    else:
                        k.cp(dst_ap, b[0:NS, 0:n], [b], [dst_tile])
                    k.rel(b)

                def to_T(src_ap, src_tile, nch, dst, dst_tile, rows=NS):
                    c = 0
                    per = 512 // rows
                    while c < nch:
                        n = min(per, nch - c)
                        b = k.bank()
                        for j in range(n):
                            k.tr(b[:, j * rows:(j + 1) * rows], src_ap[:, (c + j) * 128:(c + j + 1) * 128], identf[0:rows, 0:rows], [src_tile, identf], [b])
                        k.cp(dst[:, c:c + n, :], b[:, 0:n * rows].rearrange("p (c s) -> p c s", s=rows), [b], [dst_tile])
                        k.rel(b)
                        c += n

                def to_tok(srcT, src_tile, nch, dst_ap, dst_tile):
                    c = 0
                    while c < nch:
                        n = min(4, nch - c)
                        b = k.bank()
                        for j in range(n):
                            k.tr(b[0:NS, j * 128:(j + 1) * 128], srcT[:, c + j, :], identf[:], [src_tile, identf], [b])
                        k.cp(dst_ap[:, c * 128:(c + n) * 128], b[0:NS, 0:n * 128], [b], [dst_tile])
                        k.rel(b)
                        c += n

                qkv_p = [xres[0], xres[1], xres[2]]
                gate_s = xres[3]
                ba_s = c1("ba_s", [NS, 16])
                stc = c1("stc", [NS * 3, 3072])
                stT = c1("stT", [128, 24, NS * 3])
                newT = c1("newT", [128, 24, NS])
                cvT = c1("cvT", [128, 24, NS])
                tmpT = c1("tmpT", [128, 24, NS])
                rq_s = c1("rq_s", [128, 16, NS])
                qkp = c1("qkp", [128, 8, NS])
                beta_s = c1("beta_s", [NS, 8])
                alpha_s = c1("alpha_s", [NS, 8])
                t16a = c1("t16a", [NS, 8])
                t16b = c1("t16b", [NS, 8])
                qk_s = c1("qk_s", [NS, 8])
                v_tok = c1("v_tok", [NS, 8, 128])
                d_tok = c1("d_tok", [NS, 8, 128])
                o_tok = c1("o_tok", [NS, 8, 128])
                t_tok = c1("t_tok", [NS, 8, 128])
                oss_s = c1("oss_s", [NS, 8])
                Ss = [c1("Ss%d" % i, [128, 8, 128]) for i in range(2)]
                pK = c1("pK", [128, 8, 128])
                pQ = c1("pQ", [128, 8, 128])
                abc = c1("abc", [128, 8])

                k.dma('sp', xs[:], x_s, writes=[xs])
                rms_T_s(xs[:], xs, hTs, A1s, modT[:, 0:8, 0:NS], modT)
                for g_ in range(6):
                    wt = wload(w_in[:, OFF_QKV + g_ * 512:OFF_QKV + (g_ + 1) * 512], 512)
                    tok_mm(hTs, wt, 0, 512, qkv_p[g_ // 2][0:NS, (g_ % 2) * 512:(g_ % 2 + 1) * 512], qkv_p[g_ // 2])
                for g_ in range(2):
                    wt = wload(w_in[:, OFF_GATE + g_ * 512:OFF_GATE + (g_ + 1) * 512], 512)
                    tok_mm(hTs, wt, 0, 512, gate_s[0:NS, g_ * 512:(g_ + 1) * 512], gate_s, func=AF.Silu)
                wt = wload(w_in[:, OFF_BETA:OFF_BETA + 16], 16)
                tok_mm(hTs, wt, 0, 16, ba_s[:], ba_s)
                k.dma('sp', stc[:], st_conv, writes=[stc])
                st3 = stc[:].rearrange("(s j) c -> s j c", j=3) if False else None
                for s_ in range(NS):
                    k.dma('sp', o_conv_s[s_, 0:2, :], stc[s_ * 3 + 1:s_ * 3 + 3, :], reads=[stc])
                for p_ in range(3):
                    k.dma('sp', o_conv_s[:, 2, p_ * 1024:(p_ + 1) * 1024], qkv_p[p_][0:NS, :], reads=[qkv_p[p_]])
                to_T(stc[:], stc, 24, stT, stT, rows=NS * 3)
                for p_ in range(3):
                    to_T(qkv_p[p_][0:NS, :], qkv_p[p_], 8, newT[:, p_ * 8:(p_ + 1) * 8, :], newT)
                st4 = stT[:].rearrange("p c (s j) -> p c s j", j=3)
                k.tt(cvT[:], newT[:], bc(cwT[:, :, 3], 2, NS), ALU.mult, [newT, cwT], [cvT])
                for j_ in range(3):
                    k.tt(tmpT[:], st4[:, :, :, j_], bc(cwT[:, :, j_], 2, NS), ALU.mult, [stT, cwT], [tmpT])
                    k.tt(cvT[:], cvT[:], tmpT[:], ALU.add, [cvT, tmpT], [cvT])
                k.act(cvT[:], cvT[:], AF.Silu, [cvT], [cvT])
                k.tt(tmpT[:, 0:16, :], cvT[:, 0:16, :], cvT[:, 0:16, :], ALU.mult, [cvT], [tmpT])
                b = k.bank()
                k.mm(b[:, 0:256], onesf[:], tmpT[:, 0:16, :].rearrange("p c s -> p (c s)"), [onesf, tmpT], [b])
                k.act(rq_s[:].rearrange("p c s -> p (c s)"), b[:, 0:256], AF.Ln, [b, epsc], [rq_s], bias=epsc[:])
                k.rel(b)
                k.act(rq_s[:], rq_s[:], AF.Exp, [rq_s], [rq_s], scale=-0.5)
                k.stt(cvT[:, 0:8, :], cvT[:, 0:8, :], 128.0 ** -0.5, rq_s[:, 0:8, :], ALU.mult, ALU.mult, [cvT, rq_s], [cvT])
                k.tt(cvT[:, 8:16, :], cvT[:, 8:16, :], rq_s[:, 8:16, :], ALU.mult, [cvT, rq_s], [cvT])
                qsT, ksT, vsT = cvT[:, 0:8, :], cvT[:, 8:16, :], cvT[:, 16:24, :]
                k.act(beta_s[:], ba_s[:, 0:8], AF.Exp, [ba_s], [beta_s], scale=-1.0)
                k.ts(beta_s[:], beta_s[:], 1.0, None, ALU.add, None, [beta_s], [beta_s])
                k.op('dve', lambda e: e.reciprocal(out=beta_s[:], in_=beta_s[:]), reads=[beta_s], writes=[beta_s])
                k.tt(t16a[:], ba_s[:, 8:16], dtb[0:NS, :], ALU.add, [ba_s, dtb], [t16a])
                k.act(t16b[:], t16a[:], AF.Abs, [t16a], [t16b])
                k.act(t16b[:], t16b[:], AF.Exp, [t16b], [t16b], scale=-1.0)
                k.act(t16b[:], t16b[:], AF.Ln, [t16b, onec], [t16b], bias=onec[0:NS, :])
                k.stt(t16a[:], t16a[:], 0.0, t16b[:], ALU.max, ALU.add, [t16a, t16b], [t16a])
                k.tt(t16a[:], t16a[:], negA[0:NS, :], ALU.mult, [t16a, negA], [t16a])
                k.act(alpha_s[:], t16a[:], AF.Exp, [t16a], [alpha_s])
                to_tok(cvT[:, 16:24, :], cvT, 8, v_tok[:].rearrange("s h d -> s (h d)"), v_tok)
                k.tt(qkp[:], qsT, ksT, ALU.mult, [cvT], [qkp])
                b = k.bank()
                for h in range(8):
                    k.mm(b[0:NS, h:h + 1], qkp[:, h, :], onesf[:, 0:1], [qkp, onesf], [b])
                k.cp(qk_s[:], b[0:NS, 0:8], [b], [qk_s])
                k.rel(b)
                bks = [k.bank() for _ in range(4)]
                for s_ in range(NS):
                    S_ = Ss[s_ % 2]
                    k.dma('sp', S_[:], st_S[s_], writes=[S_])
                    k.tt(pK[:], S_[:], bc(cvT[:, 8:16, s_], 2, 128), ALU.mult, [S_, cvT], [pK], eng='pool')
                    k.tt(pQ[:], S_[:], bc(cvT[:, 0:8, s_], 2, 128), ALU.mult, [S_, cvT], [pQ])
                    for hf in range(2):
                        k.mm(bks[hf][0:NS, :], OH[:, s_, :], pK[:, hf * 4:(hf + 1) * 4, :].rearrange("p h d -> p (h d)"), [OH, pK], [bks[hf]],
                             start=(s_ == 0), stop=(s_ == NS - 1))
                        k.mm(bks[2 + hf][0:NS, :], OH[:, s_, :], pQ[:, hf * 4:(hf + 1) * 4, :].rearrange("p h d -> p (h d)"), [OH, pQ], [bks[2 + hf]],
                             start=(s_ == 0), stop=(s_ == NS - 1))
                for hf in range(2):
                    hs = slice(hf * 4, hf * 4 + 4)
                    kS = bks[hf][0:NS, :].rearrange("s (h d) -> s h d", d=128)
                    qS = bks[2 + hf][0:NS, :].rearrange("s (h d) -> s h d", d=128)
                    k.tt(t_tok[:, hs, :], kS, bc(alpha_s[:, hs], 2, 128), ALU.mult, [bks[hf], alpha_s], [t_tok])
                    k.tt(t_tok[:, hs, :], v_tok[:, hs, :], t_tok[:, hs, :], ALU.subtract, [v_tok, t_tok], [t_tok])
                    k.tt(d_tok[:, hs, :], t_tok[:, hs, :], bc(beta_s[:, hs], 2, 128), ALU.mult, [t_tok, beta_s], [d_tok])
                    k.tt(o_tok[:, hs, :], qS, bc(alpha_s[:, hs], 2, 128), ALU.mult, [bks[2 + hf], alpha_s], [o_tok])
                    k.tt(t_tok[:, hs, :], d_tok[:, hs, :], bc(qk_s[:, hs], 2, 128), ALU.mult, [d_tok, qk_s], [t_tok])
                    k.tt(o_tok[:, hs, :], o_tok[:, hs, :], t_tok[:, hs, :], ALU.add, [o_tok, t_tok], [o_tok])
                k.rel(*bks)
                k.tt(t_tok[:], o_tok[:], o_tok[:], ALU.mult, [o_tok], [t_tok])
                k.op('dve', lambda e: e.tensor_reduce(out=oss_s[:], in_=t_tok[:], axis=AX.X, op=ALU.add), reads=[t_tok], writes=[oss_s])
                k.act(oss_s[:], oss_s[:], AF.Ln, [oss_s, epsc], [oss_s], bias=epsc[0:NS, :], scale=1.0 / 128)
                k.act(oss_s[:], oss_s[:], AF.Exp, [oss_s], [oss_s], scale=-0.5)
                k.tt(o_tok[:], o_tok[:], bc(oss_s[:], 2, 128), ALU.mult, [o_tok, oss_s], [o_tok])
                k.tt(o_tok[:], o_tok[:], bc(onw_bc[:], 1, 8), ALU.mult, [o_tok, onw_bc], [o_tok])
                k.tt(o_tok[:], o_tok[:], gate_s[0:NS, :].rearrange("s (h d) -> s h d", d=128), ALU.mult, [o_tok, gate_s], [o_tok])
                to_T(o_tok[:].rearrange("s h d -> s (h d)"), o_tok, 8, newT[:, 0:8, :], newT)
                k.cp(onTs[:], newT[:, 0:8, :], [newT], [onTs])
                k.dma('sp', Ss[0][:], st_S[0], writes=[Ss[0]])
                for s_ in range(NS):
                    S_ = Ss[s_ % 2]
                    if s_ + 1 < NS:
                        k.dma('sp', Ss[(s_ + 1) % 2][:], st_S[s_ + 1], writes=[Ss[(s_ + 1) % 2]])
                    b0, b1, b2 = k.bank(), k.bank(), k.bank()
                    k.mm(b2[:, 0:8], Esel[:, s_, :], alpha_s[:], [Esel, alpha_s], [b2])
                    k.cp(abc[:], b2[:, 0:8], [b2], [abc])
                    k.mm(b0[:, :], Esel[:, s_, :], d_tok[:, 0:4, :].rearrange("s h d -> s (h d)"), [Esel, d_tok], [b0])
                    k.mm(b1[:, :], Esel[:, s_, :], d_tok[:, 4:8, :].rearrange("s h d -> s (h d)"), [Esel, d_tok], [b1])
                    k.tt(pK[:], S_[:], bc(abc[:], 2, 128), ALU.mult, [S_, abc], [pK], eng='pool')
                    k.tt(pQ[:, 0:4, :], b0[:, :].rearrange("p (h d) -> p h d", d=128), bc(cvT[:, 8:12, s_], 2, 128), ALU.mult, [b0, cvT], [pQ])
                    k.tt(pQ[:, 4:8, :], b1[:, :].rearrange("p (h d) -> p h d", d=128), bc(cvT[:, 12:16, s_], 2, 128), ALU.mult, [b1, cvT], [pQ])
                    k.rel(b0, b1, b2)
                    k.tt(S_[:], pK[:], pQ[:], ALU.add, [pK, pQ], [S_], eng='pool')
                    k.dma('sp', o_S_s[s_], S_[:], reads=[S_])

                q_s = c2("q_s", [NS, 1024])
                kv_s = c2("kv_s", [NS, 512])
                KCs = [c2("KC%d" % i, [128, NS // 2, 256]) for i in range(2)]
                VCs = [c2("VC%d" % i, [128, NS // 2, 256]) for i in range(2)]
                prd = c2("prd", [128, 16, 64])
                scT = c2("scT", [128, NS, 16])
                Pm = c2("Pm", [128, 2, 128])
                PTa = c2("PTa", [128, 2, 128])
                mx_s = c2("mx_s", [128, 2])
                nmx_s = c2("nmx_s", [128, 2])
                rs_s = c2("rs_s", [128, 2])
                es_s = c2("es_s", [128, 2])
                ob_tok = c2("ob_tok", [NS, 1024])
                obTf = c2("obTf", [128, 8, NS])
                W2x = []
                for i_ in range(3):
                    try:
                        W2x.append(c2("W2x%d" % i_, [128, 8, 512], BF16))
                    except AssertionError:
                        break
                k.switch('S1', 'S2')
                wlist['cur'] = W8 + W2x
                for g_ in range(2):
                    wt = wload(w_in[:, OFF_SQ + g_ * 512:OFF_SQ + (g_ + 1) * 512], 512)
                    tok_mm(hTs, wt, 0, 512, q_s[:, g_ * 512:(g_ + 1) * 512], q_s, scale=0.125)
                wt = wload(w_in[:, OFF_SK:OFF_SK + 512], 512)
                tok_mm(hTs, wt, 0, 512, kv_s[:], kv_s)
                for i_, q_ in ((0, 'sp'), (1, 'act')):
                    k.dma(q_, KCs[i_][0:127, :, :], st_k[1:128, i_ * 8:(i_ + 1) * 8, :], writes=[KCs[i_]])
                for i_ in range(2):
                    k.dma('pool', VCs[i_][0:127, :, :], st_v[1:128, i_ * 8:(i_ + 1) * 8, :], writes=[VCs[i_]])
                for s_ in range(NS):
                    KC_, VC_ = KCs[s_ // 8], VCs[s_ // 8]
                    k.dma('sp' if s_ < 8 else 'act', KC_[127:128, s_ % 8, :], kv_s[s_:s_ + 1, 0:256], reads=[kv_s], writes=[KC_], indep=True)
                    k.dma('pool', VC_[127:128, s_ % 8, :], kv_s[s_:s_ + 1, 256:512], reads=[kv_s], writes=[VC_], indep=True)
                for i_, q_ in ((0, 'sp'), (1, 'act')):
                    k.dma(q_, o_k_s[:, i_ * 8:(i_ + 1) * 8, :], KCs[i_][:], reads=[KCs[i_]])
                for i_ in range(2):
                    k.dma('pool', o_v_s[:, i_ * 8:(i_ + 1) * 8, :], VCs[i_][:], reads=[VCs[i_]])
                for s_ in range(NS):
                    b0, b1 = k.bank(), k.bank()
                    k.mm(b0[:, :], Esel[:, s_, :], q_s[:, 0:512], [Esel, q_s], [b0])
                    k.mm(b1[:, :], Esel[:, s_, :], q_s[:, 512:1024], [Esel, q_s], [b1])
                    for hf, bb in ((0, b0), (1, b1)):
                        KC = KCs[s_ // 8]
                        kc = KC[:, s_ % 8, hf * 128:(hf + 1) * 128].rearrange("p (g d) -> p g d", d=64)
                        k.tt(prd[:, hf * 8:(hf + 1) * 8, :].rearrange("p (g i) d -> p g i d", i=4),
                             bb[:, :].rearrange("p (g i d) -> p g i d", i=4, d=64), bc(kc, 2, 4), ALU.mult, [bb, KC], [prd])
                    k.rel(b0, b1)
                    k.op('dve', lambda e: e.tensor_reduce(out=scT[:, s_, :], in_=prd[:], axis=AX.X, op=ALU.add), reads=[prd], writes=[scT])
                b = k.bank()
                for a_ in range(2):
                    k.tr(b[:, a_ * 128:(a_ + 1) * 128], scT[:, a_ * 8:(a_ + 1) * 8, :].rearrange("p s h -> p (s h)"), identf[:], [scT, identf], [b])
                k.op('dve', lambda e: e.tensor_reduce(out=mx_s[:], in_=b[:, 0:256].rearrange("p (a q) -> p a q", q=128), axis=AX.X, op=ALU.max),
                     reads=[b], writes=[mx_s])
                k.ts(mx_s[:], mx_s[:], sinkcol[:, 0:1], None, ALU.max, None, [mx_s, sinkcol], [mx_s])
                k.ts(nmx_s[:], mx_s[:], -1.0, None, ALU.mult, None, [mx_s], [nmx_s])
                for a_ in range(2):
                    k.act(Pm[:, a_, :], b[:, a_ * 128:(a_ + 1) * 128], AF.Exp, [b, nmx_s], [Pm, rs_s], bias=nmx_s[:, a_:a_ + 1], accum=rs_s[:, a_:a_ + 1])
                k.rel(b)
                k.act(es_s[:], nmx_s[:], AF.Exp, [nmx_s, sinkcol], [es_s], bias=sinkcol[:, 0:1])
                k.tt(rs_s[:], rs_s[:], es_s[:], ALU.add, [rs_s, es_s], [rs_s])
                k.op('dve', lambda e: e.reciprocal(out=rs_s[:], in_=rs_s[:]), reads=[rs_s], writes=[rs_s])
                k.tt(Pm[:], Pm[:], bc(rs_s[:], 2, 128), ALU.mult, [Pm, rs_s], [Pm])
                b = k.bank()
                for a_ in range(2):
                    k.tr(b[:, a_ * 128:(a_ + 1) * 128], Pm[:, a_, :], identf[:], [Pm, identf], [b])
                k.cp(PTa[:].rearrange("p a q -> p (a q)"), b[:, 0:256], [b], [PTa])
                k.rel(b)
                PT3 = PTa[:].rearrange("p a (s h) -> p (a s) h", h=16)
                b0, b1 = k.bank(), k.bank()
                for s_ in range(NS):
                    for hf in range(2):
                        VC = VCs[s_ // 8]
                        vc = VC[:, s_ % 8, hf * 128:(hf + 1) * 128].rearrange("p (g d) -> p g d", d=64)
                        pt_ = PT3[:, s_, hf * 8:(hf + 1) * 8].rearrange("p (g i) -> p g i", i=4)
                        k.tt(prd[:, hf * 8:(hf + 1) * 8, :].rearrange("p (g i) d -> p g i d", i=4), bc(vc, 2, 4), bc(pt_, 3, 64), ALU.mult,
                             [VC, PTa], [prd], eng='pool')
                    k.mm(b0[0:NS, :], OH[:, s_, :], prd[:, 0:8, :].rearrange("p h d -> p (h d)"), [OH, prd], [b0], start=(s_ == 0), stop=(s_ == NS - 1))
                    k.mm(b1[0:NS, :], OH[:, s_, :], prd[:, 8:16, :].rearrange("p h d -> p (h d)"), [OH, prd], [b1], start=(s_ == 0), stop=(s_ == NS - 1))
                k.cp(ob_tok[:, 0:512], b0[0:NS, :], [b0], [ob_tok])
                k.cp(ob_tok[:, 512:1024], b1[0:NS, :], [b1], [ob_tok])
                k.rel(b0, b1)
                to_T(ob_tok[:], ob_tok, 8, obTf, obTf)
                k.cp(obTs[:], obTf[:], [obTf], [obTs])

                if nblk > 0:
                    emit_p1(0)
                    hoist['p1'] = True
                gab = c3("gab", [NS, 2048])
                yab = c3("yab", [NS, 2048])
                mix_s = c3("mix_s", [NS, 1024])
                mixTs = c3("mixTs", [128, 8, NS], BF16)
                mixTf = c3("mixTf", [128, 8, NS])
                h2Ts = c3("h2Ts", [128, 8, NS], BF16)
                gtok = c3("gtok", [NS, DFF])
                stf = c3("stf", [NS * 2, DFF])
                stfT = c3("stfT", [128, NFC, NS * 2])
                gT = c3("gT", [128, NFC, NS])
                uT = c3("uT", [128, NFC, NS])
                tT = c3("tT", [128, NFC, NS])
                aTs = c3("aTs", [128, NFC, NS], BF16)
                ys = c3("ys", [NS, 1024])
                W3x = []
                for i_ in range(3):
                    try:
                        W3x.append(c3("W3x%d" % i_, [128, 8, 512], BF16))
                    except AssertionError:
                        break
                k.switch('S2', 'S3')
                wlist['cur'] = W8 + W3x
                for g_ in range(4):
                    wt = wload(w_in[:, OFF_GA + g_ * 512:OFF_GA + (g_ + 1) * 512], 512)
                    tok_mm(hTs, wt, 0, 512, gab[:, g_ * 512:(g_ + 1) * 512], gab, func=AF.Sigmoid)
                for g_ in range(2):
                    wt = wload(w_gdn_out[:, g_ * 512:(g_ + 1) * 512], 512)
                    tok_mm(onTs, wt, 0, 512, yab[:, g_ * 512:(g_ + 1) * 512], yab)
                for g_ in range(2):
                    wt = wload(w_swa_out[:, g_ * 512:(g_ + 1) * 512], 512)
                    tok_mm(obTs, wt, 0, 512, yab[:, 1024 + g_ * 512:1024 + (g_ + 1) * 512], yab)
                k.tt(yab[:], yab[:], gab[:], ALU.mult, [yab, gab], [yab])
                k.tt(mix_s[:], yab[:, 0:1024], yab[:, 1024:2048], ALU.add, [yab], [mix_s])
                to_T(mix_s[:], mix_s, 8, mixTf, mixTf)
                k.cp(mixTs[:], mixTf[:], [mixTf], [mixTs])
                for g_ in range(2):
                    wt = wload(w_o[:, g_ * 512:(g_ + 1) * 512], 512)
                    tok_mm(mixTs, wt, 0, 512, mix_s[:, g_ * 512:(g_ + 1) * 512], mix_s)
                k.tt(mix_s[:], mix_s[:], gtok1[0:NS, :], ALU.mult, [mix_s, gtok1], [mix_s])
                k.tt(xs_3[:], xs_3[:], mix_s[:], ALU.add, [xs_3, mix_s], [xs_3])
                rms_T_s(xs_3[:], xs_3, h2Ts, A2s, modT[:, 24:32, 0:NS], modT)
                k.dma('sp', stf[:], st_ffn, writes=[stf])
                for s_ in range(NS):
                    k.dma('sp', o_ffn_s[s_, 0:1, :], stf[s_ * 2 + 1:s_ * 2 + 2, :], reads=[stf])
                to_T(stf[:], stf, NFC, stfT, stfT, rows=NS * 2)
                for (wsrc, dstT, is_gate) in ((w_ffn_gate, gT, True), (w_ffn_up, uT, False)):
                    for g_ in range(6):
                        n = 512 if g_ < 5 else DFF - 5 * 512
                        wt = wload(wsrc[:, g_ * 512:g_ * 512 + n], n)
                        if is_gate:
                            tok_mm(h2Ts, wt, 0, n, gtok[:, g_ * 512:g_ * 512 + n], gtok)
                        b = k.bank()
                        for j in range(n // 128):
                            for kk in range(8):
                                k.mm(b[:, j * NS:(j + 1) * NS], wt[:, kk, j * 128:(j + 1) * 128], h2Ts[:, kk, :], [wt, h2Ts], [b], start=(kk == 0), stop=(kk == 7))
                        k.cp(dstT[:, g_ * 4:g_ * 4 + n // 128, :], b[:, 0:(n // 128) * NS].rearrange("p (c s) -> p c s", s=NS), [b], [dstT])
                        k.rel(b)
                k.dma('sp', o_ffn_s[:, 1, :], gtok[:], reads=[gtok])
                sf4 = stfT[:].rearrange("p c (s j) -> p c s j", j=2)
                k.tt(tT[:], gT[:], bc(fcwT[:, :, 2], 2, NS), ALU.mult, [gT, fcwT], [tT])
                for j_ in range(2):
                    k.tt(gT[:], sf4[:, :, :, j_], bc(fcwT[:, :, j_], 2, NS), ALU.mult, [stfT, fcwT], [gT])
                    k.tt(tT[:], tT[:], gT[:], ALU.add, [tT, gT], [tT])
                k.tt(tT[:], tT[:], bc(fcbT[:], 2, NS), ALU.add, [tT, fcbT], [tT])
                k.act(tT[:], tT[:], AF.Silu, [tT], [tT])
                k.tt(aTs[:], tT[:], uT[:], ALU.mult, [tT, uT], [aTs])
                for hf in range(2):
                    b = k.bank()
                    for kg in range(3):
                        nk = 8 if kg < 2 else NFC - 16
                        wt = nextw()
                        k.dma('pool', wt[:, 0:nk, :], w_ffn_down[kg * 1024:kg * 1024 + nk * 128, hf * 512:(hf + 1) * 512].rearrange("(c p) n -> p c n", p=128),
                              writes=[wt])
                        for kk in range(nk):
                            kf = kg * 8 + kk
                            k.mm(b[0:NS, :], aTs[:, kf, :], wt[:, kk, :], [aTs, wt], [b], start=(kf == 0), stop=(kf == NFC - 1))
                    k.tt(mix_s[:, hf * 512:(hf + 1) * 512], b[0:NS, :], gtok2[0:NS, hf * 512:(hf + 1) * 512], ALU.mult, [b, gtok2], [mix_s])
                    k.rel(b)
                k.tt(xs_3[:], xs_3[:], mix_s[:], ALU.add, [xs_3, mix_s], [xs_3])
                k.act(ys[:], xs_3[:], AF.Square, [xs_3], [ys, ss1], accum=ss1[0:NS, :])
                k.act(rs1[0:NS, :], ss1[0:NS, :], AF.Ln, [ss1, epsc], [rs1], bias=epsc[0:NS, :], scale=1.0 / D)
                k.act(rs1[0:NS, :], rs1[0:NS, :], AF.Exp, [rs1], [rs1], scale=-0.5)
                k.stt(ys[:], xs_3[:], rs1[0:NS, :], fnw_bc[0:NS, :], ALU.mult, ALU.mult, [xs_3, rs1, fnw_bc], [ys])
                k.dma('sp', y_s, ys[:], reads=[ys])
                k.switch('S3', 'G')
                wlist['cur'] = W8

            if do_sample:
                sample_phase()

            for blk in range(nblk):
                t0 = blk * TB
                last = (blk == NBLK - 1)
                if not (blk == 0 and hoist['p1']):
                    emit_p1(t0)
                if blk == 0:
                    dump("hT", hT[:], [hT])
                if blk > 0:
                    k.switch('F', 'G')
                    k.switch('FW', 'G')
                wlist['cur'] = W8

                ck('p1')
                wt = wload(w_in[:, OFF_BETA:OFF_BETA + 16], 16)
                b = k.bank()
                for c in range(8):
                    for kk in range(8):
                        k.mm(b[0:64, c * 16:(c + 1) * 16], hT[:, kk, c * 64:(c + 1) * 64], wt[:, kk, 0:16], [hT, wt], [b],
                             start=(kk == 0), stop=(kk == 7))
                k.cp(ba[:], b[0:64, 0:128].rearrange("p (c r) -> p c r", r=16), [b], [ba])
                k.rel(b)
                k.act(beta[:], ba[:, :, 0:8], AF.Exp, [ba], [beta], scale=-1.0)
                k.ts(beta[:], beta[:], 1.0, None, ALU.add, None, [beta], [beta])
                k.op('dve', lambda e: e.reciprocal(out=beta[:], in_=beta[:]), reads=[beta], writes=[beta])
                k.tt(t64a[:], ba[:, :, 8:16], bc(dtb[:], 1, 8), ALU.add, [ba, dtb], [t64a])
                k.act(t64b[:], t64a[:], AF.Abs, [t64a], [t64b])
                k.act(t64b[:], t64b[:], AF.Exp, [t64b], [t64b], scale=-1.0)
                k.act(t64b[:], t64b[:], AF.Ln, [t64b, onec], [t64b], bias=onec[0:64, :])
                k.stt(t64a[:], t64a[:], 0.0, t64b[:], ALU.max, ALU.add, [t64a, t64b], [t64a])
                k.tt(gg[:], t64a[:], bc(negA[:], 1, 8), ALU.mult, [t64a, negA], [gg])
                ggf = gg[:].rearrange("p c h -> p (c h)")
                b = k.bank()
                k.mm(b[0:64, 0:64], Um[:], ggf, [Um, gg], [b])
                k.mm(b[:, 64:128], onesf[0:64, :], ggf, [onesf, gg], [b])
                k.cp(dd[:], b[0:64, 0:64], [b], [dd])
                k.act(ed[:], dd[:], AF.Exp, [dd], [ed])
                k.tt(ekd[:], b[0:64, 64:128], dd[:], ALU.subtract, [b, dd], [ekd])
                k.act(ekd[:], ekd[:], AF.Exp, [ekd], [ekd])
                k.act(elast[:], b[:, 64:128], AF.Exp, [b], [elast])
                k.rel(b)
                k.tt(bed[:], ed[:], beta[:].rearrange("p c h -> p (c h)"), ALU.mult, [ed, beta], [bed])
                if blk == 0:
                    dump("gg", gg[:], [gg])
                    dump("beta", beta[:], [beta])

                ck('p2')
                def gdn_front(h, hb):
                    wT, uu, Qd, Gp, Kdec, Am, Amb = wT2[hb], uu2[hb], Qd2[hb], Gp2[hb], Kdec2[hb], Am2[hb], Amb2[hb]
                    wt = nextw()
                    for part in range(3):
                        wload(w_in[:, OFF_QKV + part * 1024 + h * 128:OFF_QKV + part * 1024 + (h + 1) * 128], 128, c0=part * 128, tile=wt)
                    wload(w_in[:, OFF_GATE + h * 128:OFF_GATE + (h + 1) * 128], 128, c0=384, tile=wt)
                    for part in range(3):
                        b = k.bank()
                        for kk in range(8):
                            k.mm(b[:, :], wt[:, kk, part * 128:(part + 1) * 128], hT[:, kk, :], [wt, hT], [b], start=(kk == 0), stop=(kk == 7))
                        j = part * 8 + h
                        k.cp(pre[:, 0:3], halo[:, j, :], [halo], [pre], eng='pool')
                        k.cp(pre[:, 3:3 + TB], b[:, :], [b], [pre], eng='act')
                        k.rel(b)
                        yield
                        k.cp(halo[:, j, :], pre[:, TB:TB + 3], [pre], [halo], eng='pool')
                        k.ts(cv[:, part, :], pre[:, 0:TB], cwT[:, j, 0:1], None, ALU.mult, None, [pre, cwT], [cv])
                        for tap in range(1, 4):
                            k.stt(cv[:, part, :], pre[:, tap:tap + TB], cwT[:, j, tap:tap + 1], cv[:, part, :], ALU.mult, ALU.add,
                                  [pre, cwT, cv], [cv])
                    b = k.bank()
                    for kk in range(8):
                        k.mm(b[:, :], wt[:, kk, 384:512], hT[:, kk, :], [wt, hT], [b], start=(kk == 0), stop=(kk == 7))
                    k.act(Gp[:], b[:, :], AF.Silu, [b], [Gp])
                    k.rel(b)
                    yield
                    k.act(cv[:], cv[:], AF.Silu, [cv], [cv])
                    yield
                    for qk in range(2):
                        prebf = pre[:, 0:TB // 2].bitcast(BF16)
                        k.tt(prebf, cv[:, qk, :], cv[:, qk, :], ALU.mult, [cv], [pre], eng='pool')
                        b = k.bank()
                        k.mm(b[:, :], onesb[:], prebf, [onesb, pre], [b])
                        k.act(rqk[:, qk, :], b[:, :], AF.Ln, [b, epsc], [rqk], bias=epsc[:])
                        k.rel(b)
                        yield
                    k.act(rqk[:], rqk[:], AF.Exp, [rqk], [rqk], scale=-0.5)
                    k.stt(Qt_ap, cv[:, 0, :], 128.0 ** -0.5, rqk[:, 0, :], ALU.mult, ALU.mult, [cv, rqk], [cv])
                    k.tt(Kt_ap, cv[:, 1, :], rqk[:, 1, :], ALU.mult, [cv, rqk], [cv])
                    yield
                    k.cp(Qtb[:], Qt_ap, [cv], [Qtb], eng='pool')
                    k.cp(Ktb[:], Kt_ap, [cv], [Ktb], eng='pool')
                    k.cp(Vtb[:], cv[:, 2, :], [cv], [Vtb], eng='pool')
                    if blk == 0 and h == 0:
                        dump("Qt", Qt_ap, [cv])
                        dump("Kt", Kt_ap, [cv])
                        dump("Vt", cv[:, 2, :], [cv])
                    ck('gdn_a')
                    gh = gg[:, :, h]
                    k.tt(SA[:], bc(gh, 2, 64), bc(SLm[:], 1, 8), ALU.mult, [gg, SLm], [SA], eng='pool')
                    k.tt(SB[:], bc(gh, 2, 64), bc(Um[:], 1, 8), ALU.mult, [gg, Um], [SB], eng='pool')
                    yield
                    b = k.bank()
                    k.mm(b[0:64, :], Um[:], SA[:].rearrange("p c j -> p (c j)"), [Um, SA], [b])
                    k.act(Wm[:].rearrange("p c j -> p (c j)"), b[0:64, :], AF.Exp, [b], [Wm])
                    k.rel(b)
                    yield
                    k.tt(Wm[:], Wm[:], bc(nSL[:], 1, 8), ALU.mult, [Wm, nSL], [Wm])
                    k.tt(Wm[:], Wm[:], bc(beta[:, :, h], 2, 64), ALU.mult, [Wm, beta], [Wm])
                    yield
                    k.tt(SA[:], bc(beta[:, :, h], 2, 64), bc(identf[0:64, 0:64], 1, 8), ALU.mult, [beta, identf], [SA], eng='pool')
                    b = k.bank()
                    k.mm(b[0:64, :], SLm[:], SB[:].rearrange("p c j -> p (c j)"), [SLm, SB], [b])
                    k.act(Zm[:].rearrange("p c j -> p (c j)"), b[0:64, :], AF.Exp, [b], [Zm])
                    k.rel(b)
                    yield
                    k.tt(Am[:], Zm[:], bc(Um[:], 1, 8), ALU.mult, [Zm, Um], [Am])
                    k.tt(Zm[:], Zm[:], bc(nSU[:], 1, 8), ALU.mult, [Zm, nSU], [Zm])
                    yield
                    b = k.bank()
                    k.mm(b[0:64, :], onesf[0:64, 0:64], SA[:].rearrange("p c j -> p (c j)"), [onesf, SA], [b])
                    k.tt(Zm[:].rearrange("p c j -> p (c j)"), Zm[:].rearrange("p c j -> p (c j)"), b[0:64, :], ALU.mult, [Zm, b], [Zm])
                    k.rel(b)
                    yield
                    k.cp(Qd[:], Qt_ap, [cv], [Qd])
                    ck('gdn_b')
                    bK0, bV0 = k.bank(), k.bank()
                    bkv = bK0[:, :].bitcast(BF16)
                    bvv = bV0[:, :].bitcast(BF16)
                    for c in range(8):
                        k.tr(bkv[0:64, c * 128:(c + 1) * 128], Ktb[:, c * 64:(c + 1) * 64], identb[:], [Ktb, identb], [bK0])
                        k.tr(bvv[0:64, c * 128:(c + 1) * 128], Vtb[:, c * 64:(c + 1) * 64], identb[:], [Vtb, identb], [bV0])
                    kin = bkv[0:64, :].rearrange("p (c d) -> p c d", d=128)
                    vin = bvv[0:64, :].rearrange("p (c d) -> p c d", d=128)
                    k.tt(Kbd[:], kin, bc(bed[:].rearrange("p (c h) -> p c h", h=8)[:, :, h], 2, 128), ALU.mult, [bK0, bed], [Kbd])
                    k.tt(Kdec[:], kin, bc(ekd[:].rearrange("p (c h) -> p c h", h=8)[:, :, h], 2, 128), ALU.mult, [bK0, ekd], [Kdec])
                    k.tt(Vb[:], vin, bc(beta[:, :, h], 2, 128), ALU.mult, [bV0, beta], [Vb])
                    k.rel(bK0, bV0)
                    yield
                    bA, bB = k.bank(), k.bank()
                    for c in range(8):
                        k.mm(bA[0:64, c * 64:(c + 1) * 64], Ktb[:, c * 64:(c + 1) * 64], Ktb[:, c * 64:(c + 1) * 64], [Ktb], [bA])
                        k.mm(bB[0:64, c * 64:(c + 1) * 64], Ktb[:, c * 64:(c + 1) * 64], Qtb[:, c * 64:(c + 1) * 64], [Ktb, Qtb], [bB])
                    A3 = bA[0:64, :].rearrange("p (c j) -> p c j", j=64)
                    B3 = bB[0:64, :].rearrange("p (c j) -> p c j", j=64)
                    k.tt(Wm[:], A3, Wm[:], ALU.mult, [bA, Wm], [Wm])
                    k.tt(Zm[:], A3, Zm[:], ALU.mult, [bA, Zm], [Zm])
                    Amb_ = Am if os.environ.get('SUPD32', '0') == '1' else Amb
                    k.tt(Amb_[:], B3, Am[:], ALU.mult, [bB, Am], [Amb_])
                    k.rel(bA, bB)
                    yield
                    I4 = bc(identf[0:64, 0:64], 1, 4)
                    for hf in range(2):
                        cs = slice(hf * 4, hf * 4 + 4)
                        k.cp(ZYH[hf][:, :, 0, :], Zm[:, cs, :], [Zm], [ZYH[hf]], eng='act')
                        k.tt(ZYH[hf][:, :, 1, :], Zm[:, cs, :], I4, ALU.add, [Zm, identf], [ZYH[hf]])
                        k.cp(WXH[hf][:, :, 0, :], Wm[:, cs, :], [Wm], [WXH[hf]], eng='act')
                        k.tt(WXH[hf][:, :, 1, :], Wm[:, cs, :], I4, ALU.add, [Wm, identf], [WXH[hf]])
                    ck('gdn_c')
                    for lev in range(6):
                        for hf in range(2):
                            zy, wx = ZYH[hf], WXH[hf]
                            bZ, bW = k.bank(), k.bank()
                            for cc in range(4):
                                if lev == 0:
                                    k.mm(bZ[0:64, cc * 128:cc * 128 + 64], wx[:, cc, 0, :], zy[:, cc, 0, :], [wx, zy], [bZ])
                                    k.mm(bW[0:64, cc * 128:cc * 128 + 64], zy[:, cc, 0, :], wx[:, cc, 0, :], [wx, zy], [bW])
                                elif lev < 5:
                                    k.mm(bZ[0:64, cc * 128:(cc + 1) * 128], wx[:, cc, 0, :], zy[:, cc, :, :].rearrange("p a j -> p (a j)"), [wx, zy], [bZ])
                                    k.mm(bW[0:64, cc * 128:(cc + 1) * 128], zy[:, cc, 0, :], wx[:, cc, :, :].rearrange("p a j -> p (a j)"), [wx, zy], [bW])
                                else:
                                    k.mm(bZ[0:64, cc * 128 + 64:(cc + 1) * 128], wx[:, cc, 0, :], zy[:, cc, 1, :], [wx, zy], [bZ])
                                    k.mm(bW[0:64, cc * 128 + 64:(cc + 1) * 128], zy[:, cc, 0, :], wx[:, cc, 1, :], [wx, zy], [bW])
                            cs = slice(hf * 4, hf * 4 + 4)
                            Z4 = bZ[0:64, :].rearrange("p (c a j) -> p c a j", a=2, j=64)
                            W4 = bW[0:64, :].rearrange("p (c a j) -> p c a j", a=2, j=64)
                            if lev == 0:
                                k.cp(zy[:, :, 0, :], Z4[:, :, 0, :], [bZ], [zy], eng='act')
                                k.cp(wx[:, :, 0, :], W4[:, :, 0, :], [bW], [wx], eng='act')
                            elif lev < 5:
                                k.tt(zy[:, :, 1, :], Z4[:, :, 1, :], zy[:, :, 1, :], ALU.add, [bZ, zy], [zy])
                                k.cp(zy[:, :, 0, :], Z4[:, :, 0, :], [bZ], [zy], eng='act')
                                k.tt(wx[:, :, 1, :], W4[:, :, 1, :], wx[:, :, 1, :], ALU.add, [bW, wx], [wx])
                                k.cp(wx[:, :, 0, :], W4[:, :, 0, :], [bW], [wx], eng='act')
                            else:
                                k.tt(Y32[hf][:], Z4[:, :, 1, :], zy[:, :, 1, :], ALU.add, [bZ, zy], [Y32[hf]])
                                k.tt(SB[:, cs, :], W4[:, :, 1, :], wx[:, :, 1, :], ALU.add, [bW, wx], [SB])
                            k.rel(bZ, bW)
                            yield
                    for hf in range(2):
                        cs = slice(hf * 4, hf * 4 + 4)
                        bR = k.bank()
                        for cc in range(4):
                            k.mm(bR[0:64, cc * 64:(cc + 1) * 64], Wm[:, hf * 4 + cc, :], Y32[hf][:, cc, :], [Wm, Y32[hf]], [bR])
                        R3 = bR[0:64, 0:256].rearrange("p (c j) -> p c j", j=64)
                        k.tt(SA[:, cs, :], R3, Y32[hf][:], ALU.subtract, [bR, Y32[hf]], [SA])
                        k.rel(bR)
                        k.tt(SA[:, cs, :], SA[:, cs, :], I4, ALU.add, [SA, identf], [SA])
                        bF = k.bank()
                        for cc in range(4):
                            k.mm(bF[0:64, cc * 64:(cc + 1) * 64], SB[:, hf * 4 + cc, :], SA[:, hf * 4 + cc, :], [SB, SA], [bF])
                        k.tt(Yb[hf][:], bF[0:64, 0:256].rearrange("p (c j) -> p c j", j=64), Y32[hf][:], ALU.add, [bF, Y32[hf]], [Yb[hf]])
                        k.rel(bF)
                        yield
                    if blk == 0 and h == 0:
                        dump("Yf", Yb[0][:], [Yb[0]])
                    bU0, bU1, bWt = k.bank(), k.bank(), k.bank()
                    for c in range(8):
                        bu_ = bU0 if c < 4 else bU1
                        Yc = Yb[c // 4][:, c % 4, :]
                        k.mm(bu_[0:64, (c % 4) * 128:(c % 4 + 1) * 128], Yc, Vb[:, c, :], [Yb[c // 4], Vb], [bu_])
                        k.mm(bWt[:, c * 64:(c + 1) * 64], Kbd[:, c, :], Yc, [Kbd, Yb[c // 4]], [bWt])
                    k.cp(uu[:, 0:4, :], bU0[0:64, :].rearrange("p (c d) -> p c d", d=128), [bU0], [uu], eng='act')
                    k.cp(uu[:, 4:8, :], bU1[0:64, :].rearrange("p (c d) -> p c d", d=128), [bU1], [uu], eng='act')
                    k.cp(wT[:], bWt[:, :].rearrange("p (c j) -> p c j", j=64), [bWt], [wT])
                    k.rel(bU0, bU1, bWt)
                    yield
                    yield

                def gdn_back(h, hb):
                    wT, uu, Qd, Gp, Kdec, Am, Amb = wT2[hb], uu2[hb], Qd2[hb], Gp2[hb], Kdec2[hb], Am2[hb], Amb2[hb]
                    Amb_ = Am if os.environ.get('SUPD32', '0') == '1' else Amb
                    Sh = S_all[:, h, :]
                    k.cp(Sb[:], Sh, [S_all], [Sb], eng='pool')
                    for c in range(8):
                        col = c * 8 + h
                        b1_, b2_, b3_ = k.bank(), k.bank(), k.bank()
                        k.mm(b1_[0:64, 0:128], wT[:, c, :], Sb[:], [wT, Sb], [b1_])
                        k.mm(b2_[0:64, 0:128], Qd[:, c * 64:(c + 1) * 64], Sb[:], [Qd, Sb], [b2_])
                        k.tt(vnew[:], uu[:, c, :], b1_[0:64, 0:128], ALU.subtract, [uu, b1_], [vnew])
                        yield
                        k.mm(b2_[0:64, 128:256], Amb_[:, c, :], vnew[:], [Amb_, vnew], [b2_])
                        k.mm(b3_[:, 0:128], Kdec[:, c, :], vnew[:], [Kdec, vnew], [b3_])
                        k.stt(Sh, Sh, elast[:, col:col + 1], b3_[:, 0:128], ALU.mult, ALU.add, [S_all, elast, b3_], [S_all])
                        if c < 7:
                            k.cp(Sb[:], Sh, [S_all], [Sb], eng='pool')
                        k.act(osb[:, c, :], b2_[0:64, 0:128], AF.Copy, [b2_, ed], [osb], scale=ed[:, col:col + 1])
                        k.tt(osb[:, c, :], osb[:, c, :], b2_[0:64, 128:256], ALU.add, [osb, b2_], [osb])
                        k.rel(b1_, b2_, b3_)
                        yield
                    if blk == 0 and h == 0:
                        dump("osb", osb[:], [osb])
                    ck('gdn_e')
                    k.tt(uu[:], osb[:], osb[:], ALU.mult, [osb], [uu], eng='pool')
                    k.op('dve', lambda e: e.tensor_reduce(out=oss[:], in_=uu[:], axis=AX.X, op=ALU.add), reads=[uu], writes=[oss])
                    k.act(ors[:], oss[:], AF.Ln, [oss, epsc], [ors], bias=epsc[0:64, :], scale=1.0 / 128)
                    k.act(ors[:], ors[:], AF.Exp, [ors], [ors], scale=-0.5)
                    k.tt(on1[:], osb[:], bc(ors[:], 2, 128), ALU.mult, [osb, ors], [on1])
                    b = k.bank()
                    bv = b[:, :].bitcast(BF16)
                    for c in range(8):
                        k.tr(bv[:, c * 64:(c + 1) * 64], on1[:, c, :], identb[0:64, 0:64], [on1, identb], [b])
                    k.stt(onT[:, h, :], bv[:, 0:TB], onwT[:, 0:1], Gp[:], ALU.mult, ALU.mult, [b, Gp, onwT], [onT])
                    k.rel(b)
                    yield
                    yield

                def _drain(gl):
                    gl = [[g_, w_] for g_, w_ in gl]
                    while gl:
                        for it in list(gl):
                            for _ in range(it[1]):
                                try:
                                    next(it[0])
                                except StopIteration:
                                    gl.remove(it)
                                    break

                _drain([(gdn_front(0, 0), 1)])
                for h in range(8):
                    gl = [(gdn_back(h, h % 2), 1)]
                    if h < 7:
                        gl.append((gdn_front(h + 1, (h + 1) % 2), 2))
                    _drain(gl)
                if blk == 0:
                    dump("onT", onT[:], [onT])
                    dump("S0", S_all[:], [S_all])
                if last:
                    k.dma('sp', o_S_p, S_all[:], reads=[S_all])
                    k.cp(halo_out[:], halo[:], [halo], [halo_out], eng='pool')
                    for j_ in range(3):
                        k.dma('sp', o_conv_p[j_].rearrange("(c p) -> p c", p=128), halo_out[:, :, j_], reads=[halo_out],
                              allow_slow_non_contiguous=True)

                ck('gdn')
                k.switch('G', 'A')
                k.switch('G', 'FW')
                wt = wload(w_in[:, OFF_SK:OFF_SK + 512], 512)
                for t in range(4):
                    b = k.bank()
                    for kk in range(8):
                        k.mm(b[:, :], hT[:, kk, t * 128:(t + 1) * 128], wt[:, kk, :], [hT, wt], [b], start=(kk == 0), stop=(kk == 7))
                    ck('swa_a0')
                    kin = b[:, 0:256].rearrange("p (g d) -> p g d", d=64)
                    vin = b[:, 256:512].rearrange("p (g d) -> p g d", d=64)
                    Kt4 = Ktok[:].rearrange("p (g a d) -> p g a d", a=2, d=64)
                    Vt4 = Vtok[:, 1 + t, :].rearrange("p (g a d) -> p g a d", a=2, d=64)
                    for a_ in range(2):
                        _v = os.environ.get('SWA_VAR', '')
                        ke, ve = {'': ('act', 'dve'), 'konly': ('act', None), 'vonly': (None, 'dve'), 'kdve': ('dve', None),
                                  'both_dve': ('dve', 'dve'), 'both_act': ('act', 'act'), 'swap': ('dve', 'act')}[_v]
                        if ke:
                            k.cp(Kt4[:, :, a_, :], kin, [b], [Ktok], eng=ke)
                        if ve:
                            k.cp(Vt4[:, :, a_, :], vin, [b], [Vtok], eng=ve)
                    ck('swa_a1')
                    if last and t == 3:
                        k.cp(kvout[:], b[:, :], [b], [kvout])
                        k.dma('sp', o_k_p, kvout[:, 0:256], reads=[kvout])
                        k.dma('sp', o_v_p, kvout[:, 256:512], reads=[kvout])
                    k.rel(b)
                    b = k.bank()
                    bv = b[:, :].bitcast(BF16)
                    for g in range(4):
                        k.tr(bv[:, g * 128:(g + 1) * 128], Ktok[:, g * 128:(g + 1) * 128], identb[:], [Ktok, identb], [b])
                    ck('swa_a2')
                    k.cp(KTl[0:64, :, 128 + t * 128:128 + (t + 1) * 128], bv[0:64, 0:512].rearrange("p (g q) -> p g q", q=128), [b], [KTl])
                    k.cp(KTh[64:128, :, 128 + t * 128:128 + (t + 1) * 128], bv[64:128, 0:512].rearrange("p (g q) -> p g q", q=128), [b], [KTh])
                    k.rel(b)
                ck('swa_a')
                for half in range(2):
                    wt = wload(w_in[:, OFF_SQ + half * 512:OFF_SQ + (half + 1) * 512], 512)
                    for j in range(4):
                        b = k.bank()
                        for kk in range(8):
                            k.mm(b[:, :], wt[:, kk, j * 128:(j + 1) * 128], hT[:, kk, :], [wt, hT], [b], start=(kk == 0), stop=(kk == 7))
                        k.act(QT[:, half * 4 + j, :], b[:, :], AF.Copy, [b], [QT], scale=0.125)
                        k.rel(b)
                ck('swa_b')
                def swa_iter(t, g, sb_):
                    msk = maskB if (blk == 0 and t == 0) else maskA
                    sc, pb, PT, mx, nmx, rsum, esk = sc2[sb_], pb2[sb_], PT2[sb_], mx2[sb_], nmx2[sb_], rsum2[sb_], esk2[sb_]
                    b0, b1 = k.bank(), k.bank()
                    for i in range(4):
                        hq = g * 4 + i
                        ch, hf = hq // 2, hq % 2
                        if os.environ.get('HF0'):
                            hf = 0
                        bb = b0 if i < 2 else b1
                        KTx = KTh if hf else KTl
                        k.mm(bb[:, (i % 2) * 256:(i % 2 + 1) * 256], QT[:, ch, t * 128:(t + 1) * 128],
                             KTx[:, g, t * 128:t * 128 + 256], [QT, KTx], [bb])
                    k.tt(sc[:, 0:2, :], b0[:, :].rearrange("p (i q) -> p i q", q=256), bc(msk[:], 1, 2), ALU.add, [b0, msk], [sc])
                    k.tt(sc[:, 2:4, :], b1[:, :].rearrange("p (i q) -> p i q", q=256), bc(msk[:], 1, 2), ALU.add, [b1, msk], [sc])
                    k.rel(b0, b1)
                    yield
                    ck('swa_c')
                    k.op('dve', lambda e: e.tensor_reduce(out=mx[:], in_=sc[:], axis=AX.X, op=ALU.max), reads=[sc], writes=[mx])
                    k.tt(mx[:], mx[:], sinks[:, g * 4:(g + 1) * 4], ALU.max, [mx, sinks], [mx])
                    k.ts(nmx[:], mx[:], -1.0, None, ALU.mult, None, [mx], [nmx])
                    for i in range(4):
                        k.act(pb[:, i, :], sc[:, i, :], AF.Exp, [sc, nmx], [pb, rsum], bias=nmx[:, i:i + 1], accum=rsum[:, i:i + 1])
                    k.tt(esk[:], sinks[:, g * 4:(g + 1) * 4], mx[:], ALU.subtract, [sinks, mx], [esk])
                    k.act(esk[:], esk[:], AF.Exp, [esk], [esk])
                    k.tt(rsum[:], rsum[:], esk[:], ALU.add, [rsum, esk], [rsum])
                    k.op('dve', lambda e: e.reciprocal(out=rsum[:], in_=rsum[:]), reads=[rsum], writes=[rsum])
                    ck('swa_d')
                    k.tt(pb[:], pb[:], bc(rsum[:], 2, 256), ALU.mult, [pb, rsum], [pb])
                    yield
                    b = k.bank()
                    bv = b[:, :].bitcast(BF16)
                    for i in range(4):
                        for kt in range(2):
                            k.tr(bv[:, (i * 2 + kt) * 128:(i * 2 + kt + 1) * 128], pb[:, i, kt * 128:(kt + 1) * 128], identb[:], [pb, identb], [b])
                    ck('swa_e')
                    k.cp(PT[:].rearrange("p i a q -> p (i a q)"), bv[:, 0:1024], [b], [PT], eng='act')
                    k.rel(b)
                    yield
                    b = k.bank()
                    for i in range(4):
                        for kt in range(2):
                            k.mm(b[:, i * 128:(i + 1) * 128], Vtok[:, t + kt, g * 128:(g + 1) * 128], PT[:, i, kt, :], [Vtok, PT], [b],
                                 start=(kt == 0), stop=(kt == 1))
                    for i in range(4):
                        hq = g * 4 + i
                        ch, hf = hq // 2, hq % 2
                        k.cp(obT[hf * 64:(hf + 1) * 64, ch, t * 128:(t + 1) * 128], b[hf * 64:(hf + 1) * 64, i * 128:(i + 1) * 128], [b], [obT],
                             eng=('act' if i % 2 else 'dve'))
                    k.rel(b)
                    yield

                def swa_stream(its, sb_):
                    for (t_, g_) in its:
                        yield from swa_iter(t_, g_, sb_)

                def _drain2(gl):
                    gl = list(gl)
                    while gl:
                        for it in list(gl):
                            try:
                                next(it)
                            except StopIteration:
                                gl.remove(it)

                its_ = [(t_, g_) for t_ in range(4) for g_ in range(4)]
                _drain2([swa_stream(its_[0::2], 0), swa_stream(its_[1::2], 1)])
                ck('swa_f')
                k.cp(KTl[0:64, :, 0:128], KTl[0:64, :, TB:TB + 128], [KTl], [KTl], eng='pool')
                k.cp(KTh[64:128, :, 0:128], KTh[64:128, :, TB:TB + 128], [KTh], [KTh], eng='pool')
                k.cp(Vtok[:, 0, :], Vtok[:, 4, :], [Vtok], [Vtok], eng='pool')
                if blk == 0:
                    dump("obT", obT[:], [obT])

                ck('swa')
                k.switch('A', 'F')
                wlist['cur'] = W8 + FW
                for j in range(8):
                    wt = nextw()
                    wload(w_in[:, OFF_GA + j * 128:OFF_GA + (j + 1) * 128], 128, c0=0, tile=wt)
                    wload(w_in[:, OFF_GB + j * 128:OFF_GB + (j + 1) * 128], 128, c0=128, tile=wt)
                    wload(w_gdn_out[:, j * 128:(j + 1) * 128], 128, c0=256, tile=wt)
                    wload(w_swa_out[:, j * 128:(j + 1) * 128], 128, c0=384, tile=wt)
                    bs = [k.bank() for _ in range(4)]
                    srcs = [hT, hT, onT, obT]
                    for q in range(4):
                        for kk in range(8):
                            k.mm(bs[q][:, :], wt[:, kk, q * 128:(q + 1) * 128], srcs[q][:, kk, :], [wt, srcs[q]], [bs[q]], start=(kk == 0), stop=(kk == 7))
                    k.act(sga[:], bs[0][:, :], AF.Sigmoid, [bs[0]], [sga])
                    k.act(sgb[:], bs[1][:, :], AF.Sigmoid, [bs[1]], [sgb])
                    k.tt(sga[:], sga[:], bs[2][:, :], ALU.mult, [sga, bs[2]], [sga])
                    k.tt(sgb[:], sgb[:], bs[3][:, :], ALU.mult, [sgb, bs[3]], [sgb])
                    k.tt(mixT[:, j, :], sga[:], sgb[:], ALU.add, [sga, sgb], [mixT])
                    k.rel(*bs)
                ck('merge')
                wts = [wload(w_o[:, hf * 512:(hf + 1) * 512], 512) for hf in range(2)]
                for t in range(4):
                    for hf in range(2):
                        b = k.bank()
                        for kk in range(8):
                            k.mm(b[:, :], mixT[:, kk, t * 128:(t + 1) * 128], wts[hf][:, kk, :], [mixT, wts[hf]], [b], start=(kk == 0), stop=(kk == 7))
                        k.tt(sga[:], b[:, :], g1bc[:, hf * 512:(hf + 1) * 512], ALU.mult, [b, g1bc], [sga])
                        k.rel(b)
                        k.tt(x1[t][:, hf * 512:(hf + 1) * 512], x1[t][:, hf * 512:(hf + 1) * 512], sga[:], ALU.add, [x1[t], sga], [x1[t]], eng='pool')
                    rms_to_T(x1[t][:], x1[t], h2T, h2T, (a2, a2), (modT[:, 24:32, NS], modT), t)
                if blk == 0:
                    dump("x1", x1[0][:], [x1[0]])

                ck('wo')
                for jp in range(NFC // 2):
                    wt = nextw()
                    wload(w_ffn_gate[:, jp * 256:(jp + 1) * 256], 256, c0=0, tile=wt)
                    wload(w_ffn_up[:, jp * 256:(jp + 1) * 256], 256, c0=256, tile=wt)
                    for jj in range(2):
                        j = jp * 2 + jj
                        bg, bu = k.bank(), k.bank()
                        for kk in range(8):
                            k.mm(bg[:, :], wt[:, kk, jj * 128:(jj + 1) * 128], h2T[:, kk, :], [wt, h2T], [bg], start=(kk == 0), stop=(kk == 7))
                        for kk in range(8):
                            k.mm(bu[:, :], wt[:, kk, 256 + jj * 128:256 + (jj + 1) * 128], h2T[:, kk, :], [wt, h2T], [bu], start=(kk == 0), stop=(kk == 7))
                        k.cp(gpre[:, 0:2], fhalo[:, j, :], [fhalo], [gpre], eng='pool')
                        k.cp(gpre[:, 2:2 + TB], bg[:, :], [bg], [gpre], eng='act')
                        k.cp(fhalo[:, j, :], gpre[:, TB:TB + 2], [gpre], [fhalo], eng='pool')
                        k.ts(gcv[:], gpre[:, 0:TB], fcwT[:, j, 0:1], fcbT[:, j:j + 1], ALU.mult, ALU.add, [gpre, fcwT, fcbT], [gcv])
                        for tap in range(1, 3):
                            k.stt(gcv[:], gpre[:, tap:tap + TB], fcwT[:, j, tap:tap + 1], gcv[:], ALU.mult, ALU.add, [gpre, fcwT, gcv], [gcv])
                        k.act(gcv[:], gcv[:], AF.Silu, [gcv], [gcv])
                        k.tt(actT[:, j, :], gcv[:], bu[:, :], ALU.mult, [gcv, bu], [actT])
                        k.rel(bg, bu)
                if last:
                    for j_ in range(2):
                        k.dma('sp', o_ffn_p[j_].rearrange("(c p) -> p c", p=128), fhalo[:, :, j_], reads=[fhalo], allow_slow_non_contiguous=True)
                for hf in range(2):
                    bs = [k.bank() for _ in range(4)]
                    for kg in range(3):
                        nk = 8 if kg < 2 else NFC - 16
                        wt = nextw()
                        k.dma('pool', wt[:, 0:nk, :], w_ffn_down[kg * 1024:kg * 1024 + nk * 128, hf * 512:(hf + 1) * 512].rearrange("(c p) n -> p c n", p=128),
                              writes=[wt])
                        for kk in range(nk):
                            kf = kg * 8 + kk
                            for t in range(4):
                                k.mm(bs[t][:, :], actT[:, kf, t * 128:(t + 1) * 128], wt[:, kk, :], [actT, wt], [bs[t]], start=(kf == 0), stop=(kf == NFC - 1))
                    for t in range(4):
                        k.tt(sga[:], bs[t][:, :], g2bc[:, hf * 512:(hf + 1) * 512], ALU.mult, [bs[t], g2bc], [sga])
                        k.tt(x1[t][:, hf * 512:(hf + 1) * 512], x1[t][:, hf * 512:(hf + 1) * 512], sga[:], ALU.add, [x1[t], sga], [x1[t]], eng='pool')
                    k.rel(*bs)
                ck('ffn')
                for t in range(4):
                    k.act(yt[:], x1[t][:], AF.Square, [x1[t]], [yt, ss2], accum=ss2[:])
                    k.act(rs2[:], ss2[:], AF.Ln, [ss2, epsc], [rs2], bias=epsc[:], scale=1.0 / D)
                    k.act(rs2[:], rs2[:], AF.Exp, [rs2], [rs2], scale=-0.5)
                    k.stt(yt[:], x1[t][:], rs2[:], fnw_bc[:], ALU.mult, ALU.mult, [x1[t], rs2, fnw_bc], [yt])
                    k.dma('sp', y_p[t0 + t * 128:t0 + (t + 1) * 128, :], yt[:], reads=[yt])
        except _Stop:
            pass
        k.finish('sp')
        print("instr counts", k.cnt, "dma sems", k.ndsem)
        if os.environ.get('MMSTAT'):
            tot = sum(k.mmstat.values())
            for ln, c in sorted(k.mmstat.items(), key=lambda kv: -kv[1])[:40]:
                print("  mm line %d: %.1f us (%.1f%%)" % (ln, c / 2400.0, 100.0 * c / tot))
            print("  total est %.1f us" % (tot / 2400.0))
    return nc


OUT_NAMES = ["y_p", "y_s", "o_S_p", "o_S_s", "o_conv_p", "o_conv_s", "o_k_p", "o_k_s", "o_v_p", "o_v_s", "o_ffn_p", "o_ffn_s"]


def make_in_maps(inp, cores):
    f = lambda a: np.ascontiguousarray(a, dtype=np.float32)
    shared = {
        "w_mod": f(inp["w_mod"][0]), "b_mod": f(inp["b_mod"][0][None]), "norm1_w": f(inp["norm1_w"][0][None]),
        "norm2_w": f(inp["norm2_w"][0][None]), "w_in": f(inp["w_in"][0]), "gdn_conv_w": f(inp["gdn_conv_w"][0]),
        "gdn_a_log": f(inp["gdn_a_log"][0]), "gdn_dt_bias": f(inp["gdn_dt_bias"][0]),
        "gdn_onorm_w": f(inp["gdn_onorm_w"][0][None]), "w_gdn_out": f(inp["w_gdn_out"][0]),
        "swa_sinks": f(inp["swa_sinks"][0]), "w_swa_out": f(inp["w_swa_out"][0]), "w_o": f(inp["w_o"][0]),
        "w_ffn_gate": f(inp["w_ffn_gate"][0]), "w_ffn_up": f(inp["w_ffn_up"][0]), "ffn_conv_w": f(inp["ffn_conv_w"][0]),
        "ffn_conv_b": f(inp["ffn_conv_b"][0][None]), "w_ffn_down": f(inp["w_ffn_down"][0]),
        "final_norm_w": f(inp["final_norm_w"]),
    }
    maps = []
    for b in cores:
        s = slice(b * NS, (b + 1) * NS)
        m = dict(shared)
        m["x_p"] = f(inp["x_prompt"][b])
        m["x_s"] = f(inp["x_sample"][s, 0])
        m["c17"] = f(np.concatenate([inp["c_sample"][s], inp["c_prompt"][b:b + 1]], axis=0))
        m["st_S"] = f(np.transpose(inp["state_gdn_S"][0, s], (0, 2, 1, 3)))
        m["st_conv"] = f(inp["state_gdn_conv"][0, s].reshape(NS * 3, 3072))
        m["st_k"] = f(np.transpose(inp["cache_swa_k"][0, s].reshape(NS, 128, 256), (1, 0, 2)))
        m["st_v"] = f(np.transpose(inp["cache_swa_v"][0, s].reshape(NS, 128, 256), (1, 0, 2)))
        m["st_ffn"] = f(inp["state_ffn_conv"][0, s].reshape(NS * 2, DFF))
        maps.append(m)
    return maps


def kernel(**inp):
    nc = build_nc()
    cores = list(range(8))
    res = run_bass_kernel_spmd(nc, make_in_maps(inp, cores), core_ids=cores)
    r = res.results
    cat = lambda n: np.concatenate([r[i][n] for i in range(8)], axis=0)
    stack = lambda n: np.stack([r[i][n] for i in range(8)], axis=0)
    y_prompt = stack("y_p")
    y_sample = cat("y_s").reshape(128, 1, D)
    gS_p = np.ascontiguousarray(np.transpose(stack("o_S_p"), (0, 2, 1, 3)))[None]
    gS_s = np.ascontiguousarray(np.transpose(cat("o_S_s"), (0, 2, 1, 3)))[None]
    gc_p = stack("o_conv_p")[None]
    gc_s = cat("o_conv_s")[None]
    k_p = stack("o_k_p").reshape(1, 8, 128, 4, 64)
    k_s = np.ascontiguousarray(np.concatenate([np.transpose(r[i]["o_k_s"], (1, 0, 2)) for i in range(8)], axis=0)).reshape(1, 128, 128, 4, 64)
    v_p = stack("o_v_p").reshape(1, 8, 128, 4, 64)
    v_s = np.ascontiguousarray(np.concatenate([np.transpose(r[i]["o_v_s"], (1, 0, 2)) for i in range(8)], axis=0)).reshape(1, 128, 128, 4, 64)
    f_p = stack("o_ffn_p")[None]
    f_s = cat("o_ffn_s")[None]
    return (y_prompt, y_sample, gS_p, gS_s, gc_p, gc_s, k_p, k_s, v_p, v_s, f_p, f_s)
```

```python
import os
import numpy as np
import concourse.bass as bass
import concourse.mybir as mybir
from concourse.bass_utils import run_bass_kernel_spmd
from contextlib import ExitStack

F32 = mybir.dt.float32
BF16 = mybir.dt.bfloat16
AF = mybir.ActivationFunctionType
ALU = mybir.AluOpType
AX = mybir.AxisListType

D = 1024
SEQ = 2048
TB = 512
NBLK = SEQ // TB
NS = 16
DFF = 2816
NFC = DFF // 128
OFF_QKV, OFF_GATE, OFF_BETA, OFF_A, OFF_SQ, OFF_SK, OFF_SV, OFF_GA, OFF_GB = 0, 3072, 4096, 4104, 4112, 5136, 5392, 5648, 6672
INW = 7696
EPS = 1e-6
NEG = -30000.0


class Tile:
    def __init__(self, t, name):
        self.t = t
        self.name = name
        self.lw = None
        self.rd = {}
        self.dkey = None
        self.dcnt = 0

    def __getitem__(self, k):
        return self.t[k]


class K:
    def __init__(self, nc, es):
        self.nc = nc
        self.es = es
        self.eng = {'pe': nc.tensor, 'dve': nc.vector, 'act': nc.scalar, 'pool': nc.gpsimd, 'sp': nc.sync}
        self.sem = {}
        for e in self.eng:
            self.sem[e] = es.enter_context(nc.semaphore('s_' + e))
        self.cnt = {e: 0 for e in self.eng}
        self.seen = {e: {} for e in self.eng}
        self.ndsem = 0
        self.tiles = []
        self.free_banks = []
        self.dbg = []
        self.phase_off = {}

    def sb(self, name, shape, dt=F32, es=None):
        t = (es or self.es).enter_context(self.nc.sbuf_tensor(name, list(shape), dt))
        T = Tile(t, name)
        self.tiles.append(T)
        return T

    def view(self, ap, name):
        T = Tile(ap, name)
        self.tiles.append(T)
        return T

    def init_psum(self):
        self.psum = self.es.enter_context(self.nc.psum_tensor("psum", [128, 4096], F32))
        self.banks = []
        for i in range(8):
            T = Tile(self.psum[:, i * 512:(i + 1) * 512], "bank%d" % i)
            T.excl = True
            self.tiles.append(T)
            self.banks.append(T)
        self.free_banks = list(self.banks)

    def bank(self):
        assert self.free_banks, "out of PSUM banks"
        return self.free_banks.pop(0)

    def rel(self, *bs):
        for b in bs:
            assert b not in self.free_banks
            self.free_banks.append(b)

    def _deps(self, e, reads, writes, skip=None):
        deps = {}

        def add(kv):
            k_, v = kv
            if deps.get(k_, 0) < v:
                deps[k_] = v
        for t in reads:
            if t.lw:
                add(t.lw)
            if getattr(t, 'excl', False):
                for kv in t.rd.items():
                    if kv[0] != e:
                        add(kv)
        for t in writes:
            if t.lw:
                add(t.lw)
            for kv in t.rd.items():
                add(kv)
        for k_, v in deps.items():
            if k_ == e and e == 'pe':
                continue
            if skip is not None and k_ == skip:
                continue
            if self.seen[e].get(k_, 0) >= v:
                continue
            self.eng[e].wait_ge(self.sem[k_], v)
            self.seen[e][k_] = v

    def op(self, e, fn, reads=(), writes=()):
        if e == 'pool' and getattr(self, 'pool_to', None):
            e = self.pool_to
        self._deps(e, reads, writes)
        ins = fn(self.eng[e])
        ins.then_inc(self.sem[e], 1)
        self.cnt[e] += 1
        c = self.cnt[e]
        for t in writes:
            t.lw = (e, c)
            t.rd = {}
        for t in reads:
            if t not in writes:
                t.rd[e] = c

    def dma(self, q, out, in_, reads=(), writes=(), semtile=None, indep=False, **kw):
        T = semtile if semtile is not None else (writes[0] if writes else reads[0])
        self._deps(q, reads, writes, skip=(T.dkey if indep else None))
        if T.dkey is None:
            T.dkey = 'd%d' % self.ndsem
            self.ndsem += 1
            self.sem[T.dkey] = self.es.enter_context(self.nc.semaphore(T.dkey))
        self.eng[q].dma_start(out=out, in_=in_, **kw).then_inc(self.sem[T.dkey], 16)
        T.dcnt += 16
        for t in writes:
            t.lw = (T.dkey, T.dcnt)
            t.rd = {}
        for t in reads:
            t.rd[T.dkey] = T.dcnt

    def init_arena(self, nbytes):
        self.arena = self.es.enter_context(self.nc.sbuf_tensor("arena", [128, nbytes // 4], F32))
        self.phase_tiles = {}

    def carve(self, phase, name, shape, dt=F32):
        off = self.phase_off.get(phase, 0)
        n = 1
        for d_ in shape[1:]:
            n *= d_
        nb = n * (2 if dt == BF16 else 4)
        nb = (nb + 63) // 64 * 64
        assert off + nb <= self.arena.shape[1] * 4, "arena overflow in phase %s at %s: %d" % (phase, name, off + nb)
        ap = self.arena[0:shape[0], off // 4:(off + nb) // 4]
        if dt == BF16:
            ap = ap.bitcast(BF16)
        ap = ap[:, 0:n]
        if len(shape) == 3:
            ap = ap.rearrange("p (a b) -> p a b", b=shape[2])
        elif len(shape) == 4:
            ap = ap.rearrange("p (a b c) -> p a b c", b=shape[2], c=shape[3])
        self.phase_off[phase] = off + nb
        T = Tile(ap, name)
        self.tiles.append(T)
        self.phase_tiles.setdefault(phase, []).append(T)
        return T

    def switch(self, frm, to):
        acc = {}
        for F_ in self.phase_tiles.get(frm, []):
            if F_.lw:
                acc[F_.lw[0]] = max(acc.get(F_.lw[0], 0), F_.lw[1])
            for k_, v in F_.rd.items():
                acc[k_] = max(acc.get(k_, 0), v)
        for T in self.phase_tiles.get(to, []):
            for k_, v in acc.items():
                T.rd[k_] = max(T.rd.get(k_, 0), v)

    def barrier(self):
        for e in self.eng:
            for T in self.tiles:
                if T.dkey is not None and self.seen[e].get(T.dkey, 0) < T.dcnt:
                    self.eng[e].wait_ge(self.sem[T.dkey], T.dcnt)
                    self.seen[e][T.dkey] = T.dcnt
            for k_ in self.eng:
                if k_ != e and self.cnt[k_] > 0 and self.seen[e].get(k_, 0) < self.cnt[k_]:
                    self.eng[e].wait_ge(self.sem[k_], self.cnt[k_])
                    self.seen[e][k_] = self.cnt[k_]

    def finish(self, e='sp'):
        for T in self.tiles:
            if T.dkey is not None and self.seen[e].get(T.dkey, 0) < T.dcnt:
                self.eng[e].wait_ge(self.sem[T.dkey], T.dcnt)
                self.seen[e][T.dkey] = T.dcnt
        for k_ in self.eng:
            if k_ != e and self.cnt[k_] > 0 and self.seen[e].get(k_, 0) < self.cnt[k_]:
                self.eng[e].wait_ge(self.sem[k_], self.cnt[k_])
                self.seen[e][k_] = self.cnt[k_]

    def mm(self, out, lhsT, rhs, r, w, start=True, stop=True):
        import traceback
        ln = traceback.extract_stack(limit=2)[0].lineno
        n = 1
        for d_ in out.shape[1:]:
            n *= d_
        cyc = n * (4 if rhs.dtype == F32 else 1)
        st = self.__dict__.setdefault('mmstat', {})
        st[ln] = st.get(ln, 0) + max(cyc, 64)
        self.op('pe', lambda e: e.matmul(out, lhsT=lhsT, rhs=rhs, start=start, stop=stop), reads=r, writes=w)

    def tr(self, out, in_, ident, r, w):
        import traceback
        ln = traceback.extract_stack(limit=2)[0].lineno
        n = 1
        for d_ in out.shape[1:]:
            n *= d_
        cyc = n * (2 if in_.dtype == F32 else 1)
        st = self.__dict__.setdefault('mmstat', {})
        st[ln] = st.get(ln, 0) + max(cyc, 64)
        self.op('pe', lambda e: e.transpose(out=out, in_=in_, identity=ident), reads=r, writes=w)

    def act(self, out, in_, func, r, w, bias=None, scale=None, accum=None, eng='act'):
        kw = {}
        if bias is not None:
            kw['bias'] = bias
        if scale is not None:
            kw['scale'] = scale
        if accum is not None:
            kw['accum_out'] = accum
        self.op('act', lambda e: e.activation(out=out, in_=in_, func=func, **kw), reads=r, writes=w)

    def tt(self, out, in0, in1, op, r, w, eng='dve'):
        self.op(eng, lambda e: e.tensor_tensor(out=out, in0=in0, in1=in1, op=op), reads=r, writes=w)

    def ts(self, out, in0, s1, s2, op0, op1, r, w, eng='dve', accum=None):
        if op1 is None:
            self.op(eng, lambda e: e.tensor_scalar(out=out, in0=in0, scalar1=s1, scalar2=None, op0=op0), reads=r, writes=w)
        else:
            self.op(eng, lambda e: e.tensor_scalar(out=out, in0=in0, scalar1=s1, scalar2=s2, op0=op0, op1=op1), reads=r, writes=w)

    def stt(self, out, in0, scalar, in1, op0, op1, r, w, accum=None):
        if accum is None:
            self.op('dve', lambda e: e.scalar_tensor_tensor(out=out, in0=in0, scalar=scalar, in1=in1, op0=op0, op1=op1), reads=r, writes=w)
        else:
            self.op('dve', lambda e: e.scalar_tensor_tensor(out=out, in0=in0, scalar=scalar, in1=in1, op0=op0, op1=op1, accum_out=accum), reads=r, writes=w)

    def cp(self, out, in_, r, w, eng='dve'):
        if eng == 'act':
            self.op('act', lambda e: e.activation(out=out, in_=in_, func=AF.Copy), reads=r, writes=w)
        else:
            self.op(eng, lambda e: e.tensor_copy(out=out, in_=in_), reads=r, writes=w)

    def memset(self, out, val, w, eng='pool'):
        self.op(eng, lambda e: e.memset(out, val), writes=w)

    def asel(self, out, in_, pattern, cmp, fill, base, cm, r, w):
        self.op('pool', lambda e: e.affine_select(out=out, in_=in_, pattern=pattern, compare_op=cmp, fill=fill,
                                                  base=base, channel_multiplier=cm), reads=r, writes=w)


def bc(ap, axis, n):
    a = ap.unsqueeze(axis)
    shp = list(a.shape)
    shp[axis] = n
    return a.broadcast_to(shp)


class _Stop(Exception):
    pass


def build_nc(debug=False, nblk=NBLK, do_sample=True, stop=None):
    nc = bass.Bass("TRN2", target_bir_lowering=False)

    def din(name, shape):
        return nc.dram_tensor(name, list(shape), F32, kind="ExternalInput").ap()

    def dout(name, shape):
        return nc.dram_tensor(name, list(shape), F32, kind="ExternalOutput").ap()

    x_p = din("x_p", [SEQ, D])
    x_s = din("x_s", [NS, D])
    c17 = din("c17", [NS + 1, D])
    st_S = din("st_S", [NS, 128, 8, 128])
    st_conv = din("st_conv", [NS * 3, 3072])
    st_k = din("st_k", [128, NS, 256])
    st_v = din("st_v", [128, NS, 256])
    st_ffn = din("st_ffn", [NS * 2, DFF])
    w_mod = din("w_mod", [D, 6 * D])
    b_mod = din("b_mod", [1, 6 * D])
    norm1_w = din("norm1_w", [1, D])
    norm2_w = din("norm2_w", [1, D])
    w_in = din("w_in", [D, INW])
    gdn_conv_w = din("gdn_conv_w", [4, 3072])
    gdn_a_log = din("gdn_a_log", [8])
    gdn_dt_bias = din("gdn_dt_bias", [8])
    gdn_onorm_w = din("gdn_onorm_w", [1, 128])
    w_gdn_out = din("w_gdn_out", [D, D])
    swa_sinks = din("swa_sinks", [16])
    w_swa_out = din("w_swa_out", [D, D])
    w_o = din("w_o", [D, D])
    w_ffn_gate = din("w_ffn_gate", [D, DFF])
    w_ffn_up = din("w_ffn_up", [D, DFF])
    ffn_conv_w = din("ffn_conv_w", [3, DFF])
    ffn_conv_b = din("ffn_conv_b", [1, DFF])
    w_ffn_down = din("w_ffn_down", [DFF, D])
    final_norm_w = din("final_norm_w", [D])

    y_p = dout("y_p", [SEQ, D])
    y_s = dout("y_s", [NS, D])
    o_S_p = dout("o_S_p", [128, 8, 128])
    o_S_s = dout("o_S_s", [NS, 128, 8, 128])
    o_conv_p = dout("o_conv_p", [3, 3072])
    o_conv_s = dout("o_conv_s", [NS, 3, 3072])
    o_k_p = dout("o_k_p", [128, 256])
    o_k_s = dout("o_k_s", [128, NS, 256])
    o_v_p = dout("o_v_p", [128, 256])
    o_v_s = dout("o_v_s", [128, NS, 256])
    o_ffn_p = dout("o_ffn_p", [2, DFF])
    o_ffn_s = dout("o_ffn_s", [NS, 2, DFF])

    with ExitStack() as es:
        k = K(nc, es)
        k.init_psum()
        PS = k.psum

        def dump(name, ap, tiles):
            if not debug:
                return
            o = nc.dram_tensor("dbg_" + name, list(ap.shape), ap.dtype, kind="ExternalOutput").ap()
            dt_ = Tile(None, 'dbg_' + name)
            k.tiles.append(dt_)
            k.dma('sp', o, ap, reads=tiles, semtile=dt_)

        dbgsem = k.sb("dbgsem", [1, 1])

        def ck(name):
            if stop == name:
                raise _Stop()

        try:
            identf = k.sb("identf", [128, 128])
            identb = k.sb("identb", [128, 128], BF16)
            onesf = k.sb("onesf", [128, 128])
            onesb = k.sb("onesb", [128, 128], BF16)
            Um = k.sb("Um", [64, 64])
            SLm = k.sb("SLm", [64, 64])
            SUm = k.sb("SUm", [64, 64])
            nSL = k.sb("nSL", [64, 64])
            nSU = k.sb("nSU", [64, 64])
            maskA = k.sb("maskA", [128, 256])
            maskB = k.sb("maskB", [128, 256])
            E16 = k.sb("E16", [NS + 1, 128])
            Esel = k.sb("Esel", [NS, NS, 128])
            epsc = k.sb("epsc", [128, 1])
            onec = k.sb("onec", [128, 1])

            k.memset(identf[:], 0.0, [identf])
            k.asel(identf[:], identf[:], [[-1, 128]], ALU.not_equal, 1.0, 0, 1, [identf], [identf])
            k.cp(identb[:], identf[:], [identf], [identb])
            k.memset(onesf[:], 1.0, [onesf])
            k.memset(onesb[:], 1.0, [onesb])
            k.memset(epsc[:], EPS, [epsc])
            k.memset(onec[:], 1.0, [onec])
            k.memset(Um[:], 1.0, [Um])
            k.asel(Um[:], Um[:], [[1, 64]], ALU.is_ge, 0.0, 0, -1, [Um], [Um])
            k.memset(SLm[:], 1.0, [SLm])
            k.asel(SLm[:], SLm[:], [[-1, 64]], ALU.is_ge, 0.0, -1, 1, [SLm], [SLm])
            k.memset(SUm[:], 1.0, [SUm])
            k.asel(SUm[:], SUm[:], [[1, 64]], ALU.is_ge, 0.0, -1, -1, [SUm], [SUm])
            k.ts(nSL[:], SLm[:], -1.0, None, ALU.mult, None, [SLm], [nSL])
            k.ts(nSU[:], SUm[:], -1.0, None, ALU.mult, None, [SUm], [nSU])
            k.memset(maskA[:], 0.0, [maskA])
            k.asel(maskA[:], maskA[:], [[1, 256]], ALU.is_ge, NEG, -1, -1, [maskA], [maskA])
            k.asel(maskA[:], maskA[:], [[-1, 256]], ALU.is_ge, NEG, 128, 1, [maskA], [maskA])
            k.asel(maskB[:], maskA[:], [[1, 256]], ALU.is_ge, NEG, -128, 0, [maskA], [maskB])
            k.memset(E16[:], 0.0, [E16])
            k.asel(E16[:], E16[:], [[0, 128]], ALU.not_equal, 1.0, -NS, 1, [E16], [E16])
            k.memset(Esel[:], 0.0, [Esel])
            k.asel(Esel[:], Esel[:], [[-1, NS], [0, 128]], ALU.not_equal, 1.0, 0, 1, [Esel], [Esel])

            k.pool_to = 'dve'
            modT = k.sb("modT", [128, 48, NS + 1])
            n1w = k.sb("n1w", [128, 8])
            n2w = k.sb("n2w", [128, 8])
            a1 = k.sb("a1", [128, 8])
            a2 = k.sb("a2", [128, 8])
            A1s = k.sb("A1s", [128, 8, NS])
            A2s = k.sb("A2s", [128, 8, NS])
            cwT = k.sb("cwT", [128, 24, 4])
            fcwT = k.sb("fcwT", [128, NFC, 3])
            fcbT = k.sb("fcbT", [128, NFC])
            onwT = k.sb("onwT", [128, 1])
            fnw_bc = k.sb("fnw_bc", [128, D])
            g1bc = k.sb("g1bc", [128, D])
            g2bc = k.sb("g2bc", [128, D])
            gtok1 = k.sb("gtok1", [NS + 1, D])
            gtok2 = k.sb("gtok2", [NS + 1, D])
            negA = k.sb("negA", [64, 8])
            dtb = k.sb("dtb", [64, 8])
            sinks = k.sb("sinks", [128, 16])

            W8 = [k.sb("W8_%d" % i, [128, 8, 512], BF16) for i in range(3)]
            ring = {'w8': 0, 'wd': 0}

            wlist = {'cur': W8}

            def nextw():
                wl = wlist['cur']
                t_ = wl[ring['w8'] % len(wl)]
                ring['w8'] += 1
                return t_

            def wload(src, ncols, c0=0, tile=None):
                piece = tile is not None
                if tile is None:
                    tile = nextw()
                k.dma('pool', tile[:, :, c0:c0 + ncols], src.rearrange("(c p) n -> p c n", p=128), writes=[tile], indep=piece)
                return tile

            with ExitStack() as es2:
                stage = k.sb("stage", [8, 6 * D], F32, es=es2)
                c17t = k.sb("c17t", [NS + 1, D], F32, es=es2)
                scT = k.sb("scT", [128, 8, NS + 1], BF16, es=es2)
                bmT = k.sb("bmT", [128, 48], F32, es=es2)

                def featmajor(src, r, C, dst_ap, dst_tile):
                    k.dma('sp', stage[0:r, 0:C], src, writes=[stage])
                    nchunk = C // 128
                    c = 0
                    while c < nchunk:
                        n = min(nchunk - c, 512 // r)
                        b = k.bank()
                        for j in range(n):
                            k.tr(b[:, j * r:(j + 1) * r], stage[0:r, (c + j) * 128:(c + j + 1) * 128], identf[0:r, 0:r],
                                 [stage, identf], [b])
                        if r == 1:
                            k.cp(dst_ap[:, c:c + n], b[:, 0:n], [b], [dst_tile])
                        else:
                            k.cp(dst_ap[:, c:c + n, :], b[:, 0:n * r].rearrange("p (c r) -> p c r", r=r), [b], [dst_tile])
                        k.rel(b)
                        c += n

                ck('consts')
                featmajor(b_mod, 1, 6 * D, bmT, bmT)
                featmajor(norm1_w, 1, D, n1w, n1w)
                featmajor(norm2_w, 1, D, n2w, n2w)
                featmajor(gdn_conv_w, 4, 3072, cwT, cwT)
                featmajor(ffn_conv_w, 3, DFF, fcwT, fcwT)
                featmajor(ffn_conv_b, 1, DFF, fcbT, fcbT)
                featmajor(gdn_onorm_w, 1, 128, onwT, onwT)
                ck('fm')
                k.dma('sp', fnw_bc[:], final_norm_w.partition_broadcast(128), writes=[fnw_bc])
                k.dma('sp', negA[:], gdn_a_log.partition_broadcast(64), writes=[negA])
                k.dma('sp', dtb[:], gdn_dt_bias.partition_broadcast(64), writes=[dtb])
                k.dma('sp', sinks[:], swa_sinks.partition_broadcast(128), writes=[sinks])
                k.act(negA[:], negA[:], AF.Exp, [negA], [negA])
                k.ts(negA[:], negA[:], -1.0, None, ALU.mult, None, [negA], [negA])

                ck('bcast')
                k.dma('sp', c17t[:], c17, writes=[c17t])
                k.act(c17t[:], c17t[:], AF.Silu, [c17t], [c17t])
                b = k.bank()
                for kk in range(8):
                    k.tr(b[:, kk * 17:(kk + 1) * 17], c17t[:, kk * 128:(kk + 1) * 128], identf[0:17, 0:17], [c17t, identf], [b])
                k.cp(scT[:], b[:, 0:8 * 17].rearrange("p (c r) -> p c r", r=17), [b], [scT])
                k.rel(b)
                ck('silu')
                for half in range(2):
                    b = k.bank()
                    for g in range(6):
                        wt = wload(w_mod[:, (half * 6 + g) * 512:(half * 6 + g + 1) * 512], 512)
                        for j in range(4):
                            jj = g * 4 + j
                            for kk in range(8):
                                k.mm(b[:, jj * 17:(jj + 1) * 17], wt[:, kk, j * 128:(j + 1) * 128], scT[:, kk, :], [wt, scT], [b],
                                     start=(kk == 0), stop=(kk == 7))
                    k.tt(modT[:, half * 24:(half + 1) * 24, :], b[:, 0:24 * 17].rearrange("p (c r) -> p c r", r=17),
                         bc(bmT[:, half * 24:(half + 1) * 24], 2, 17), ALU.add, [b, bmT], [modT])
                    k.rel(b)
                ck('modT')
                k.barrier()
            dump("modT", modT[:], [modT])

            k.ts(a1[:], modT[:, 8:16, NS], 1.0, None, ALU.add, None, [modT], [a1])
            k.tt(a1[:], a1[:], n1w[:], ALU.mult, [a1, n1w], [a1])
            k.ts(a2[:], modT[:, 32:40, NS], 1.0, None, ALU.add, None, [modT], [a2])
            k.tt(a2[:], a2[:], n2w[:], ALU.mult, [a2, n2w], [a2])
            k.ts(A1s[:], modT[:, 8:16, 0:NS], 1.0, None, ALU.add, None, [modT], [A1s])
            k.tt(A1s[:], A1s[:], bc(n1w[:], 2, NS), ALU.mult, [A1s, n1w], [A1s])
            k.ts(A2s[:], modT[:, 32:40, 0:NS], 1.0, None, ALU.add, None, [modT], [A2s])
            k.tt(A2s[:], A2s[:], bc(n2w[:], 2, NS), ALU.mult, [A2s, n2w], [A2s])
            for (c0, gtok, gbc) in ((16, gtok1, g1bc), (40, gtok2, g2bc)):
                b0, b1 = k.bank(), k.bank()
                for j in range(8):
                    bb = b0 if j < 4 else b1
                    k.tr(bb[0:17, (j % 4) * 128:(j % 4 + 1) * 128], modT[:, c0 + j, :], identf[:], [modT, identf], [bb])
                k.cp(gtok[:, 0:512], b0[0:17, :], [b0], [gtok])
                k.cp(gtok[:, 512:1024], b1[0:17, :], [b1], [gtok])
                for hf, bb in ((0, b0), (1, b1)):
                    k.mm(bb[:, :], E16[:], gtok[:, hf * 512:(hf + 1) * 512], [E16, gtok], [bb])
                    k.cp(gbc[:, hf * 512:(hf + 1) * 512], bb[:, :], [bb], [gbc])
                k.rel(b0, b1)
            dump("g1bc", g1bc[:], [g1bc])

            ck('derived')
            S_all = k.sb("S_all", [128, 8, 128])
            halo = k.sb("halo", [128, 24, 3])
            fhalo = k.sb("fhalo", [128, NFC, 2])
            KTl = k.sb("KTl", [128, 4, 128 + TB], BF16)
            KTh = k.sb("KTh", [128, 4, 128 + TB], BF16)
            Vtok = k.sb("Vtok", [128, 5, 512], BF16)
            k.memset(S_all[:], 0.0, [S_all])
            k.memset(halo[:], 0.0, [halo])
            k.memset(fhalo[:], 0.0, [fhalo])
            k.memset(KTl[:], 0.0, [KTl])
            k.memset(KTh[:], 0.0, [KTh])
            k.memset(Vtok[:], 0.0, [Vtok])

            xres = [k.sb("xres%d" % i, [128, D]) for i in range(4)]
            x1 = xres
            xn = k.sb("xn", [128, D], BF16)
            ss1 = k.sb("ss1", [128, 1])
            rs1 = k.sb("rs1", [128, 1])
            ss2, rs2 = ss1, rs1
            hT = k.sb("hT", [128, 8, TB], BF16)
            h2T = hT
            onT = k.sb("onT", [128, 8, TB], BF16)
            obT = k.sb("obT", [128, 8, TB], BF16)
            mixT = k.sb("mixT", [128, 8, TB], BF16)
            QT = mixT
            halo_out = k.sb("halo_out", [128, 24, 3])

            k.init_arena(74240)
            cG = lambda n, shp, dt=F32: k.carve('G', n, shp, dt)
            cA = lambda n, shp, dt=F32: k.carve('A', n, shp, dt)
            cF = lambda n, shp, dt=F32: k.carve('F', n, shp, dt)
            ba = cG("ba", [64, 8, 16])
            beta = cG("beta", [64, 8, 8])
            gg = cG("gg", [64, 8, 8])
            t64a = cG("t64a", [64, 8, 8])
            t64b = cG("t64b", [64, 8, 8])
            dd = cG("dd", [64, 64])
            ed = cG("ed", [64, 64])
            ekd = cG("ekd", [64, 64])
            bed = cG("bed", [64, 64])
            elast = cG("elast", [128, 64])
            pre = cG("pre", [128, 3 + TB])
            cv = cG("cv", [128, 3, TB])
            rqk = cG("rqk", [128, 2, TB])
            Qd2 = [cG("Qd%d" % i, [128, TB], BF16) for i in range(2)]
            Qtb = cG("Qtb", [128, TB], BF16)
            Ktb = cG("Ktb", [128, TB], BF16)
            Vtb = cG("Vtb", [128, TB], BF16)
            Sb = cG("Sb", [128, 128], BF16)
            Gp2 = [cG("Gp%d" % i, [128, TB]) for i in range(2)]
            SA = cG("SA", [64, 8, 64])
            SB = cG("SB", [64, 8, 64])
            Wm = cG("Wm", [64, 8, 64])
            Zm = cG("Zm", [64, 8, 64])
            Am2 = [cG("Am%d" % i, [64, 8, 64]) for i in range(2)]
            Amb2 = [cG("Amb%d" % i, [64, 8, 64], BF16) for i in range(2)]
            NEU = BF16
            WXH = [cG("WXH%d" % i, [64, 4, 2, 64], BF16) for i in range(2)]
            ZYH = [cG("ZYH%d" % i, [64, 4, 2, 64], BF16) for i in range(2)]
            Y32 = [cG("Y32_%d" % i, [64, 4, 64]) for i in range(2)]
            ghi = cG("ghi", [64, 8, 8], BF16)
            glo = cG("glo", [64, 8, 8], BF16)
            bhi = cG("bhi", [64, 8, 8], BF16)
            blo = cG("blo", [64, 8, 8], BF16)
            Umb = cG("Umb", [64, 64], BF16)
            SLmb = cG("SLmb", [64, 64], BF16)
            Yb = [cG("Yb_%d" % i, [64, 4, 64], BF16) for i in range(2)]
            Kbd = cG("Kbd", [64, 8, 128], NEU)
            Kdec2 = [cG("Kdec%d" % i, [64, 8, 128], F32 if os.environ.get('SUPD32', '0') == '1' else BF16) for i in range(2)]
            Vb = cG("Vb", [64, 8, 128], NEU)
            osb = cG("osb", [64, 8, 128])
            uu2 = [cG("uu%d" % i, [64, 8, 128]) for i in range(2)]
            wT2 = [cG("wT%d" % i, [128, 8, 64], BF16) for i in range(2)]
            vnew = cG("vnew", [64, 128], F32 if os.environ.get('SUPD32', '0') == '1' else BF16)
            oss = cG("oss", [64, 8])
            ors = cG("ors", [64, 8])
            on1 = cG("on1", [64, 8, 128], BF16)
            Qt_ap, Kt_ap = cv[:, 0, :], cv[:, 1, :]
            Ktok = cA("Ktok", [128, 512], BF16)
            kvout = cA("kvout", [128, 512])
            sc2 = [cA("sc%d" % i, [128, 4, 256]) for i in range(2)]
            pb2 = [cA("pb%d" % i, [128, 4, 256], BF16) for i in range(2)]
            PT2 = [cA("PT%d" % i, [128, 4, 2, 128], BF16) for i in range(2)]
            mx2 = [cA("mx%d" % i, [128, 4]) for i in range(2)]
            nmx2 = [cA("nmx%d" % i, [128, 4]) for i in range(2)]
            rsum2 = [cA("rsum%d" % i, [128, 4]) for i in range(2)]
            esk2 = [cA("esk%d" % i, [128, 4]) for i in range(2)]
            actT = cF("actT", [128, NFC, TB], BF16)
            sga = cF("sga", [128, TB])
            sgb = cF("sgb", [128, TB])
            gpre = cF("gpre", [128, 2 + TB])
            gcv = cF("gcv", [128, TB])
            yt = cF("yt", [128, D])
            k.carve('FW', 'fwpad', [128, k.phase_off['F'] // 4])
            FW = [k.carve('FW', 'FW%d' % i, [128, 8, 512], BF16) for i in range(4)]
            k.phase_tiles['FW'] = k.phase_tiles['FW'][1:]
            print("arena use", k.phase_off)

            def rms_to_T(xt, xt_tile, dstT, dst_tile, acol, bcol, t):
                k.act(xn[:], xt, AF.Square, [xt_tile], [xn, ss1], accum=ss1[:])
                k.act(rs1[:], ss1[:], AF.Ln, [ss1, epsc], [rs1], bias=epsc[:], scale=1.0 / D)
                k.act(rs1[:], rs1[:], AF.Exp, [rs1], [rs1], scale=-0.5)
                k.ts(xn[:], xt, rs1[:], None, ALU.mult, None, [xt_tile, rs1], [xn])
                b = k.bank()
                bv = b[:, :].bitcast(BF16)
                for kk in range(8):
                    k.tr(bv[:, kk * 128:(kk + 1) * 128], xn[:, kk * 128:(kk + 1) * 128], identb[:], [xn, identb], [b])
                for kk in range(8):
                    k.act(dstT[:, kk, t * 128:(t + 1) * 128], bv[:, kk * 128:(kk + 1) * 128], AF.Identity,
                          [b, acol[1], bcol[1]], [dst_tile], bias=bcol[0][:, kk:kk + 1], scale=acol[0][:, kk:kk + 1])
                k.rel(b)

            hoist = {'p1': False}

            def emit_p1(t0_):
                for t in range(4):
                    k.dma('sp', xres[t][:], x_p[t0_ + t * 128:t0_ + (t + 1) * 128, :], writes=[xres[t]])
                for t in range(4):
                    rms_to_T(xres[t][:], xres[t], hT, hT, (a1, a1), (modT[:, 0:8, NS], modT), t)

            def sample_phase():
                c1 = lambda n, shp, dt=F32: k.carve('S1', n, shp, dt)
                c2 = lambda n, shp, dt=F32: k.carve('S2', n, shp, dt)
                c3 = lambda n, shp, dt=F32: k.carve('S3', n, shp, dt)
                xs = c1("xs", [NS, D])
                xs_2 = c2("xs_2", [NS, D])
                xs_3 = c3("xs_3", [NS, D])
                hTs = k.sb("hTs", [128, 8, NS], BF16)
                onTs = k.sb("onTs", [128, 8, NS], BF16)
                obTs = k.sb("obTs", [128, 8, NS], BF16)
                OH = k.sb("OH", [128, NS, NS])
                sinkcol = k.sb("sinkcol", [128, 1])
                onw_bc = k.sb("onw_bc", [NS, 128])
                hsc = k.sb("hsc", [128, 8, NS])
                k.pool_to = None
                k.memset(OH[:], 0.0, [OH])
                k.asel(OH[:], OH[:], [[1, NS], [-1, NS]], ALU.not_equal, 1.0, 0, 0, [OH], [OH])
                k.pool_to = 'dve'
                for a_ in range(8):
                    k.dma('sp', sinkcol[a_ * 16:(a_ + 1) * 16, :], swa_sinks.rearrange("(h o) -> h o", o=1), writes=[sinkcol])
                k.dma('sp', onw_bc[:], gdn_onorm_w[0].partition_broadcast(NS), writes=[onw_bc])

                def rms_T_s(src, src_tile, dst, A_, B_ap, B_tile):
                    k.act(xn[0:NS, :], src, AF.Square, [src_tile], [xn, ss1], accum=ss1[0:NS, :])
                    k.act(rs1[0:NS, :], ss1[0:NS, :], AF.Ln, [ss1, epsc], [rs1], bias=epsc[0:NS, :], scale=1.0 / D)
                    k.act(rs1[0:NS, :], rs1[0:NS, :], AF.Exp, [rs1], [rs1], scale=-0.5)
                    k.ts(xn[0:NS, :], src, rs1[0:NS, :], None, ALU.mult, None, [src_tile, rs1], [xn])
                    b = k.bank()
                    bv = b[:, :].bitcast(BF16)
                    for kk in range(8):
                        k.tr(bv[:, kk * NS:(kk + 1) * NS], xn[0:NS, kk * 128:(kk + 1) * 128], identb[0:NS, 0:NS], [xn, identb], [b])
                    pv = bv[:, 0:8 * NS].rearrange("p (c s) -> p c s", s=NS)
                    k.tt(hsc[:], pv, A_[:], ALU.mult, [b, A_], [hsc])
                    k.rel(b)
                    k.tt(dst[:], hsc[:], B_ap, ALU.add, [hsc, B_tile], [dst])

                def tok_mm(srcT, wt, c0, n, dst_ap, dst_tile, scale=None, func=None):
                    b = k.bank()
                    for kk in range(8):
                        k.mm(b[0:NS, 0:n], srcT[:, kk, :], wt[:, kk, c0:c0 + n], [srcT, wt], [b], start=(kk == 0), stop=(kk == 7))
                    if func is not None:
                        k.act(dst_ap, b[0:NS, 0:n], func, [b], [dst_tile])
                    elif scale is not None:
                        k.act(dst_ap, b[0:NS, 0:n], AF.Copy, [b], [dst_tile], scale=scale)
                    else:
                        k.cp(dst_ap, b[0:NS, 0:n], [b], [dst_tile])
                    k.rel(b)

                def to_T(src_ap, src_tile, nch, dst, dst_tile, rows=NS):
                    c = 0
                    per = 512 // rows
                    while c < nch:
                        n = min(per, nch - c)
                        b = k.bank()
                        for j in range(n):
                            k.tr(b[:, j * rows:(j + 1) * rows], src_ap[:, (c + j) * 128:(c + j + 1) * 128], identf[0:rows, 0:rows], [src_tile, identf], [b])
                        k.cp(dst[:, c:c + n, :], b[:, 0:n * rows].rearrange("p (c s) -> p c s", s=rows), [b], [dst_tile])
                        k.rel(b)
                        c += n

                def to_tok(srcT, src_tile, nch, dst_ap, dst_tile):
                    c = 0
                    while c < nch:
                        n = min(4, nch - c)
                        b = k.bank()
                        for j in range(n):
                            k.tr(b[0:NS, j * 128:(j + 1) * 128], srcT[:, c + j, :], identf[:], [src_tile, identf], [b])
                        k.cp(dst_ap[:, c * 128:(c + n) * 128], b[0:NS, 0:n * 128], [b], [dst_tile])
                        k.rel(b)
                        c += n

                qkv_p = [xres[0], xres[1], xres[2]]
                gate_s = xres[3]
                ba_s = c1("ba_s", [NS, 16])
                stc = c1("stc", [NS * 3, 3072])
                stT = c1("stT", [128, 24, NS * 3])
                newT = c1("newT", [128, 24, NS])
                cvT = c1("cvT", [128, 24, NS])
                tmpT = c1("tmpT", [128, 24, NS])
                rq_s = c1("rq_s", [128, 16, NS])
                qkp = c1("qkp", [128, 8, NS])
                beta_s = c1("beta_s", [NS, 8])
                alpha_s = c1("alpha_s", [NS, 8])
                t16a = c1("t16a", [NS, 8])
                t16b = c1("t16b", [NS, 8])
                qk_s = c1("qk_s", [NS, 8])
                v_tok = c1("v_tok", [NS, 8, 128])
                d_tok = c1("d_tok", [NS, 8, 128])
                o_tok = c1("o_tok", [NS, 8, 128])
                t_tok = c1("t_tok", [NS, 8, 128])
                oss_s = c1("oss_s", [NS, 8])
                Ss = [c1("Ss%d" % i, [128, 8, 128]) for i in range(2)]
                pK = c1("pK", [128, 8, 128])
                pQ = c1("pQ", [128, 8, 128])
                abc = c1("abc", [128, 8])

                k.dma('sp', xs[:], x_s, writes=[xs])
                rms_T_s(xs[:], xs, hTs, A1s, modT[:, 0:8, 0:NS], modT)
                for g_ in range(6):
                    wt = wload(w_in[:, OFF_QKV + g_ * 512:OFF_QKV + (g_ + 1) * 512], 512)
                    tok_mm(hTs, wt, 0, 512, qkv_p[g_ // 2][0:NS, (g_ % 2) * 512:(g_ % 2 + 1) * 512], qkv_p[g_ // 2])
                for g_ in range(2):
                    wt = wload(w_in[:, OFF_GATE + g_ * 512:OFF_GATE + (g_ + 1) * 512], 512)
                    tok_mm(hTs, wt, 0, 512, gate_s[0:NS, g_ * 512:(g_ + 1) * 512], gate_s, func=AF.Silu)
                wt = wload(w_in[:, OFF_BETA:OFF_BETA + 16], 16)
                tok_mm(hTs, wt, 0, 16, ba_s[:], ba_s)
                k.dma('sp', stc[:], st_conv, writes=[stc])
                st3 = stc[:].rearrange("(s j) c -> s j c", j=3) if False else None
                for s_ in range(NS):
                    k.dma('sp', o_conv_s[s_, 0:2, :], stc[s_ * 3 + 1:s_ * 3 + 3, :], reads=[stc])
                for p_ in range(3):
                    k.dma('sp', o_conv_s[:, 2, p_ * 1024:(p_ + 1) * 1024], qkv_p[p_][0:NS, :], reads=[qkv_p[p_]])
                to_T(stc[:], stc, 24, stT, stT, rows=NS * 3)
                for p_ in range(3):
                    to_T(qkv_p[p_][0:NS, :], qkv_p[p_], 8, newT[:, p_ * 8:(p_ + 1) * 8, :], newT)
                st4 = stT[:].rearrange("p c (s j) -> p c s j", j=3)
                k.tt(cvT[:], newT[:], bc(cwT[:, :, 3], 2, NS), ALU.mult, [newT, cwT], [cvT])
                for j_ in range(3):
                    k.tt(tmpT[:], st4[:, :, :, j_], bc(cwT[:, :, j_], 2, NS), ALU.mult, [stT, cwT], [tmpT])
                    k.tt(cvT[:], cvT[:], tmpT[:], ALU.add, [cvT, tmpT], [cvT])
                k.act(cvT[:], cvT[:], AF.Silu, [cvT], [cvT])
                k.tt(tmpT[:, 0:16, :], cvT[:, 0:16, :], cvT[:, 0:16, :], ALU.mult, [cvT], [tmpT])
                b = k.bank()
                k.mm(b[:, 0:256], onesf[:], tmpT[:, 0:16, :].rearrange("p c s -> p (c s)"), [onesf, tmpT], [b])
                k.act(rq_s[:].rearrange("p c s -> p (c s)"), b[:, 0:256], AF.Ln, [b, epsc], [rq_s], bias=epsc[:])
                k.rel(b)
                k.act(rq_s[:], rq_s[:], AF.Exp, [rq_s], [rq_s], scale=-0.5)
                k.stt(cvT[:, 0:8, :], cvT[:, 0:8, :], 128.0 ** -0.5, rq_s[:, 0:8, :], ALU.mult, ALU.mult, [cvT, rq_s], [cvT])
                k.tt(cvT[:, 8:16, :], cvT[:, 8:16, :], rq_s[:, 8:16, :], ALU.mult, [cvT, rq_s], [cvT])
                qsT, ksT, vsT = cvT[:, 0:8, :], cvT[:, 8:16, :], cvT[:, 16:24, :]
                k.act(beta_s[:], ba_s[:, 0:8], AF.Exp, [ba_s], [beta_s], scale=-1.0)
                k.ts(beta_s[:], beta_s[:], 1.0, None, ALU.add, None, [beta_s], [beta_s])
                k.op('dve', lambda e: e.reciprocal(out=beta_s[:], in_=beta_s[:]), reads=[beta_s], writes=[beta_s])
                k.tt(t16a[:], ba_s[:, 8:16], dtb[0:NS, :], ALU.add, [ba_s, dtb], [t16a])
                k.act(t16b[:], t16a[:], AF.Abs, [t16a], [t16b])
                k.act(t16b[:], t16b[:], AF.Exp, [t16b], [t16b], scale=-1.0)
                k.act(t16b[:], t16b[:], AF.Ln, [t16b, onec], [t16b], bias=onec[0:NS, :])
                k.stt(t16a[:], t16a[:], 0.0, t16b[:], ALU.max, ALU.add, [t16a, t16b], [t16a])
                k.tt(t16a[:], t16a[:], negA[0:NS, :], ALU.mult, [t16a, negA], [t16a])
                k.act(alpha_s[:], t16a[:], AF.Exp, [t16a], [alpha_s])
                to_tok(cvT[:, 16:24, :], cvT, 8, v_tok[:].rearrange("s h d -> s (h d)"), v_tok)
                k.tt(qkp[:], qsT, ksT, ALU.mult, [cvT], [qkp])
                b = k.bank()
                for h in range(8):
                    k.mm(b[0:NS, h:h + 1], qkp[:, h, :], onesf[:, 0:1], [qkp, onesf], [b])
                k.cp(qk_s[:], b[0:NS, 0:8], [b], [qk_s])
                k.rel(b)
                bks = [k.bank() for _ in range(4)]
                for s_ in range(NS):
                    S_ = Ss[s_ % 2]
                    k.dma('sp', S_[:], st_S[s_], writes=[S_])
                    k.tt(pK[:], S_[:], bc(cvT[:, 8:16, s_], 2, 128), ALU.mult, [S_, cvT], [pK], eng='pool')
                    k.tt(pQ[:], S_[:], bc(cvT[:, 0:8, s_], 2, 128), ALU.mult, [S_, cvT], [pQ])
                    for hf in range(2):
                        k.mm(bks[hf][0:NS, :], OH[:, s_, :], pK[:, hf * 4:(hf + 1) * 4, :].rearrange("p h d -> p (h d)"), [OH, pK], [bks[hf]],
                             start=(s_ == 0), stop=(s_ == NS - 1))
                        k.mm(bks[2 + hf][0:NS, :], OH[:, s_, :], pQ[:, hf * 4:(hf + 1) * 4, :].rearrange("p h d -> p (h d)"), [OH, pQ], [bks[2 + hf]],
                             start=(s_ == 0), stop=(s_ == NS - 1))
                for hf in range(2):
                    hs = slice(hf * 4, hf * 4 + 4)
                    kS = bks[hf][0:NS, :].rearrange("s (h d) -> s h d", d=128)
                    qS = bks[2 + hf][0:NS, :].rearrange("s (h d) -> s h d", d=128)
                    k.tt(t_tok[:, hs, :], kS, bc(alpha_s[:, hs], 2, 128), ALU.mult, [bks[hf], alpha_s], [t_tok])
                    k.tt(t_tok[:, hs, :], v_tok[:, hs, :], t_tok[:, hs, :], ALU.subtract, [v_tok, t_tok], [t_tok])
                    k.tt(d_tok[:, hs, :], t_tok[:, hs, :], bc(beta_s[:, hs], 2, 128), ALU.mult, [t_tok, beta_s], [d_tok])
                    k.tt(o_tok[:, hs, :], qS, bc(alpha_s[:, hs], 2, 128), ALU.mult, [bks[2 + hf], alpha_s], [o_tok])
                    k.tt(t_tok[:, hs, :], d_tok[:, hs, :], bc(qk_s[:, hs], 2, 128), ALU.mult, [d_tok, qk_s], [t_tok])
                    k.tt(o_tok[:, hs, :], o_tok[:, hs, :], t_tok[:, hs, :], ALU.add, [o_tok, t_tok], [o_tok])
                k.rel(*bks)
                k.tt(t_tok[:], o_tok[:], o_tok[:], ALU.mult, [o_tok], [t_tok])
                k.op('dve', lambda e: e.tensor_reduce(out=oss_s[:], in_=t_tok[:], axis=AX.X, op=ALU.add), reads=[t_tok], writes=[oss_s])
                k.act(oss_s[:], oss_s[:], AF.Ln, [oss_s, epsc], [oss_s], bias=epsc[0:NS, :], scale=1.0 / 128)
                k.act(oss_s[:], oss_s[:], AF.Exp, [oss_s], [oss_s], scale=-0.5)
                k.tt(o_tok[:], o_tok[:], bc(oss_s[:], 2, 128), ALU.mult, [o_tok, oss_s], [o_tok])
                k.tt(o_tok[:], o_tok[:], bc(onw_bc[:], 1, 8), ALU.mult, [o_tok, onw_bc], [o_tok])
                k.tt(o_tok[:], o_tok[:], gate_s[0:NS, :].rearrange("s (h d) -> s h d", d=128), ALU.mult, [o_tok, gate_s], [o_tok])
                to_T(o_tok[:].rearrange("s h d -> s (h d)"), o_tok, 8, newT[:, 0:8, :], newT)
                k.cp(onTs[:], newT[:, 0:8, :], [newT], [onTs])
                k.dma('sp', Ss[0][:], st_S[0], writes=[Ss[0]])
                for s_ in range(NS):
                    S_ = Ss[s_ % 2]
                    if s_ + 1 < NS:
                        k.dma('sp', Ss[(s_ + 1) % 2][:], st_S[s_ + 1], writes=[Ss[(s_ + 1) % 2]])
                    b0, b1, b2 = k.bank(), k.bank(), k.bank()
                    k.mm(b2[:, 0:8], Esel[:, s_, :], alpha_s[:], [Esel, alpha_s], [b2])
                    k.cp(abc[:], b2[:, 0:8], [b2], [abc])
                    k.mm(b0[:, :], Esel[:, s_, :], d_tok[:, 0:4, :].rearrange("s h d -> s (h d)"), [Esel, d_tok], [b0])
                    k.mm(b1[:, :], Esel[:, s_, :], d_tok[:, 4:8, :].rearrange("s h d -> s (h d)"), [Esel, d_tok], [b1])
                    k.tt(pK[:], S_[:], bc(abc[:], 2, 128), ALU.mult, [S_, abc], [pK], eng='pool')
                    k.tt(pQ[:, 0:4, :], b0[:, :].rearrange("p (h d) -> p h d", d=128), bc(cvT[:, 8:12, s_], 2, 128), ALU.mult, [b0, cvT], [pQ])
                    k.tt(pQ[:, 4:8, :], b1[:, :].rearrange("p (h d) -> p h d", d=128), bc(cvT[:, 12:16, s_], 2, 128), ALU.mult, [b1, cvT], [pQ])
                    k.rel(b0, b1, b2)
                    k.tt(S_[:], pK[:], pQ[:], ALU.add, [pK, pQ], [S_], eng='pool')
                    k.dma('sp', o_S_s[s_], S_[:], reads=[S_])

                q_s = c2("q_s", [NS, 1024])
                kv_s = c2("kv_s", [NS, 512])
                KCs = [c2("KC%d" % i, [128, NS // 2, 256]) for i in range(2)]
                VCs = [c2("VC%d" % i, [128, NS // 2, 256]) for i in range(2)]
                prd = c2("prd", [128, 16, 64])
                scT = c2("scT", [128, NS, 16])
                Pm = c2("Pm", [128, 2, 128])
                PTa = c2("PTa", [128, 2, 128])
                mx_s = c2("mx_s", [128, 2])
                nmx_s = c2("nmx_s", [128, 2])
                rs_s = c2("rs_s", [128, 2])
                es_s = c2("es_s", [128, 2])
                ob_tok = c2("ob_tok", [NS, 1024])
                obTf = c2("obTf", [128, 8, NS])
                W2x = []
                for i_ in range(3):
                    try:
                        W2x.append(c2("W2x%d" % i_, [128, 8, 512], BF16))
                    except AssertionError:
                        break
                k.switch('S1', 'S2')
                wlist['cur'] = W8 + W2x
                for g_ in range(2):
                    wt = wload(w_in[:, OFF_SQ + g_ * 512:OFF_SQ + (g_ + 1) * 512], 512)
                    tok_mm(hTs, wt, 0, 512, q_s[:, g_ * 512:(g_ + 1) * 512], q_s, scale=0.125)
                wt = wload(w_in[:, OFF_SK:OFF_SK + 512], 512)
                tok_mm(hTs, wt, 0, 512, kv_s[:], kv_s)
                for i_, q_ in ((0, 'sp'), (1, 'act')):
                    k.dma(q_, KCs[i_][0:127, :, :], st_k[1:128, i_ * 8:(i_ + 1) * 8, :], writes=[KCs[i_]])
                for i_ in range(2):
                    k.dma('pool', VCs[i_][0:127, :, :], st_v[1:128, i_ * 8:(i_ + 1) * 8, :], writes=[VCs[i_]])
                for s_ in range(NS):
                    KC_, VC_ = KCs[s_ // 8], VCs[s_ // 8]
                    k.dma('sp' if s_ < 8 else 'act', KC_[127:128, s_ % 8, :], kv_s[s_:s_ + 1, 0:256], reads=[kv_s], writes=[KC_], indep=True)
                    k.dma('pool', VC_[127:128, s_ % 8, :], kv_s[s_:s_ + 1, 256:512], reads=[kv_s], writes=[VC_], indep=True)
                for i_, q_ in ((0, 'sp'), (1, 'act')):
                    k.dma(q_, o_k_s[:, i_ * 8:(i_ + 1) * 8, :], KCs[i_][:], reads=[KCs[i_]])
                for i_ in range(2):
                    k.dma('pool', o_v_s[:, i_ * 8:(i_ + 1) * 8, :], VCs[i_][:], reads=[VCs[i_]])
                for s_ in range(NS):
                    b0, b1 = k.bank(), k.bank()
                    k.mm(b0[:, :], Esel[:, s_, :], q_s[:, 0:512], [Esel, q_s], [b0])
                    k.mm(b1[:, :], Esel[:, s_, :], q_s[:, 512:1024], [Esel, q_s], [b1])
                    for hf, bb in ((0, b0), (1, b1)):
                        KC = KCs[s_ // 8]
                        kc = KC[:, s_ % 8, hf * 128:(hf + 1) * 128].rearrange("p (g d) -> p g d", d=64)
                        k.tt(prd[:, hf * 8:(hf + 1) * 8, :].rearrange("p (g i) d -> p g i d", i=4),
                             bb[:, :].rearrange("p (g i d) -> p g i d", i=4, d=64), bc(kc, 2, 4), ALU.mult, [bb, KC], [prd])
                    k.rel(b0, b1)
                    k.op('dve', lambda e: e.tensor_reduce(out=scT[:, s_, :], in_=prd[:], axis=AX.X, op=ALU.add), reads=[prd], writes=[scT])
                b = k.bank()
                for a_ in range(2):
                    k.tr(b[:, a_ * 128:(a_ + 1) * 128], scT[:, a_ * 8:(a_ + 1) * 8, :].rearrange("p s h -> p (s h)"), identf[:], [scT, identf], [b])
                k.op('dve', lambda e: e.tensor_reduce(out=mx_s[:], in_=b[:, 0:256].rearrange("p (a q) -> p a q", q=128), axis=AX.X, op=ALU.max),
                     reads=[b], writes=[mx_s])
                k.ts(mx_s[:], mx_s[:], sinkcol[:, 0:1], None, ALU.max, None, [mx_s, sinkcol], [mx_s])
                k.ts(nmx_s[:], mx_s[:], -1.0, None, ALU.mult, None, [mx_s], [nmx_s])
                for a_ in range(2):
                    k.act(Pm[:, a_, :], b[:, a_ * 128:(a_ + 1) * 128], AF.Exp, [b, nmx_s], [Pm, rs_s], bias=nmx_s[:, a_:a_ + 1], accum=rs_s[:, a_:a_ + 1])
                k.rel(b)
                k.act(es_s[:], nmx_s[:], AF.Exp, [nmx_s, sinkcol], [es_s], bias=sinkcol[:, 0:1])
                k.tt(rs_s[:], rs_s[:], es_s[:], ALU.add, [rs_s, es_s], [rs_s])
                k.op('dve', lambda e: e.reciprocal(out=rs_s[:], in_=rs_s[:]), reads=[rs_s], writes=[rs_s])
                k.tt(Pm[:], Pm[:], bc(rs_s[:], 2, 128), ALU.mult, [Pm, rs_s], [Pm])
                b = k.bank()
                for a_ in range(2):
                    k.tr(b[:, a_ * 128:(a_ + 1) * 128], Pm[:, a_, :], identf[:], [Pm, identf], [b])
                k.cp(PTa[:].rearrange("p a q -> p (a q)"), b[:, 0:256], [b], [PTa])
                k.rel(b)
                PT3 = PTa[:].rearrange("p a (s h) -> p (a s) h", h=16)
                b0, b1 = k.bank(), k.bank()
                for s_ in range(NS):
                    for hf in range(2):
                        VC = VCs[s_ // 8]
                        vc = VC[:, s_ % 8, hf * 128:(hf + 1) * 128].rearrange("p (g d) -> p g d", d=64)
                        pt_ = PT3[:, s_, hf * 8:(hf + 1) * 8].rearrange("p (g i) -> p g i", i=4)
                        k.tt(prd[:, hf * 8:(hf + 1) * 8, :].rearrange("p (g i) d -> p g i d", i=4), bc(vc, 2, 4), bc(pt_, 3, 64), ALU.mult,
                             [VC, PTa], [prd], eng='pool')
                    k.mm(b0[0:NS, :], OH[:, s_, :], prd[:, 0:8, :].rearrange("p h d -> p (h d)"), [OH, prd], [b0], start=(s_ == 0), stop=(s_ == NS - 1))
                    k.mm(b1[0:NS, :], OH[:, s_, :], prd[:, 8:16, :].rearrange("p h d -> p (h d)"), [OH, prd], [b1], start=(s_ == 0), stop=(s_ == NS - 1))
                k.cp(ob_tok[:, 0:512], b0[0:NS, :], [b0], [ob_tok])
                k.cp(ob_tok[:, 512:1024], b1[0:NS, :], [b1], [ob_tok])
                k.rel(b0, b1)
                to_T(ob_tok[:], ob_tok, 8, obTf, obTf)
                k.cp(obTs[:], obTf[:], [obTf], [obTs])

                if nblk > 0:
                    emit_p1(0)
                    hoist['p1'] = True
                gab = c3("gab", [NS, 2048])
                yab = c3("yab", [NS, 2048])
                mix_s = c3("mix_s", [NS, 1024])
                mixTs = c3("mixTs", [128, 8, NS], BF16)
                mixTf = c3("mixTf", [128, 8, NS])
                h2Ts = c3("h2Ts", [128, 8, NS], BF16)
                gtok = c3("gtok", [NS, DFF])
                stf = c3("stf", [NS * 2, DFF])
                stfT = c3("stfT", [128, NFC, NS * 2])
                gT = c3("gT", [128, NFC, NS])
                uT = c3("uT", [128, NFC, NS])
                tT = c3("tT", [128, NFC, NS])
                aTs = c3("aTs", [128, NFC, NS], BF16)
                ys = c3("ys", [NS, 1024])
                W3x = []
                for i_ in range(3):
                    try:
                        W3x.append(c3("W3x%d" % i_, [128, 8, 512], BF16))
                    except AssertionError:
                        break
                k.switch('S2', 'S3')
                wlist['cur'] = W8 + W3x
                for g_ in range(4):
                    wt = wload(w_in[:, OFF_GA + g_ * 512:OFF_GA + (g_ + 1) * 512], 512)
                    tok_mm(hTs, wt, 0, 512, gab[:, g_ * 512:(g_ + 1) * 512], gab, func=AF.Sigmoid)
                for g_ in range(2):
                    wt = wload(w_gdn_out[:, g_ * 512:(g_ + 1) * 512], 512)
                    tok_mm(onTs, wt, 0, 512, yab[:, g_ * 512:(g_ + 1) * 512], yab)
                for g_ in range(2):
                    wt = wload(w_swa_out[:, g_ * 512:(g_ + 1) * 512], 512)
                    tok_mm(obTs, wt, 0, 512, yab[:, 1024 + g_ * 512:1024 + (g_ + 1) * 512], yab)
                k.tt(yab[:], yab[:], gab[:], ALU.mult, [yab, gab], [yab])
                k.tt(mix_s[:], yab[:, 0:1024], yab[:, 1024:2048], ALU.add, [yab], [mix_s])
                to_T(mix_s[:], mix_s, 8, mixTf, mixTf)
                k.cp(mixTs[:], mixTf[:], [mixTf], [mixTs])
                for g_ in range(2):
                    wt = wload(w_o[:, g_ * 512:(g_ + 1) * 512], 512)
                    tok_mm(mixTs, wt, 0, 512, mix_s[:, g_ * 512:(g_ + 1) * 512], mix_s)
                k.tt(mix_s[:], mix_s[:], gtok1[0:NS, :], ALU.mult, [mix_s, gtok1], [mix_s])
                k.tt(xs_3[:], xs_3[:], mix_s[:], ALU.add, [xs_3, mix_s], [xs_3])
                rms_T_s(xs_3[:], xs_3, h2Ts, A2s, modT[:, 24:32, 0:NS], modT)
                k.dma('sp', stf[:], st_ffn, writes=[stf])
                for s_ in range(NS):
                    k.dma('sp', o_ffn_s[s_, 0:1, :], stf[s_ * 2 + 1:s_ * 2 + 2, :], reads=[stf])
                to_T(stf[:], stf, NFC, stfT, stfT, rows=NS * 2)
                for (wsrc, dstT, is_gate) in ((w_ffn_gate, gT, True), (w_ffn_up, uT, False)):
                    for g_ in range(6):
                        n = 512 if g_ < 5 else DFF - 5 * 512
                        wt = wload(wsrc[:, g_ * 512:g_ * 512 + n], n)
                        if is_gate:
                            tok_mm(h2Ts, wt, 0, n, gtok[:, g_ * 512:g_ * 512 + n], gtok)
                        b = k.bank()
                        for j in range(n // 128):
                            for kk in range(8):
                                k.mm(b[:, j * NS:(j + 1) * NS], wt[:, kk, j * 128:(j + 1) * 128], h2Ts[:, kk, :], [wt, h2Ts], [b], start=(kk == 0), stop=(kk == 7))
                        k.cp(dstT[:, g_ * 4:g_ * 4 + n // 128, :], b[:, 0:(n // 128) * NS].rearrange("p (c s) -> p c s", s=NS), [b], [dstT])
                        k.rel(b)
                k.dma('sp', o_ffn_s[:, 1, :], gtok[:], reads=[gtok])
                sf4 = stfT[:].rearrange("p c (s j) -> p c s j", j=2)
                k.tt(tT[:], gT[:], bc(fcwT[:, :, 2], 2, NS), ALU.mult, [gT, fcwT], [tT])
                for j_ in range(2):
                    k.tt(gT[:], sf4[:, :, :, j_], bc(fcwT[:, :, j_], 2, NS), ALU.mult, [stfT, fcwT], [gT])
                    k.tt(tT[:], tT[:], gT[:], ALU.add, [tT, gT], [tT])
                k.tt(tT[:], tT[:], bc(fcbT[:], 2, NS), ALU.add, [tT, fcbT], [tT])
                k.act(tT[:], tT[:], AF.Silu, [tT], [tT])
                k.tt(aTs[:], tT[:], uT[:], ALU.mult, [tT, uT], [aTs])
                for hf in range(2):
                    b = k.bank()
                    for kg in range(3):
                        nk = 8 if kg < 2 else NFC - 16
                        wt = nextw()
                        k.dma('pool', wt[:, 0:nk, :], w_ffn_down[kg * 1024:kg * 1024 + nk * 128, hf * 512:(hf + 1) * 512].rearrange("(c p) n -> p c n", p=128),
                              writes=[wt])
                        for kk in range(nk):
                            kf = kg * 8 + kk
                            k.mm(b[0:NS, :], aTs[:, kf, :], wt[:, kk, :], [aTs, wt], [b], start=(kf == 0), stop=(kf == NFC - 1))
                    k.tt(mix_s[:, hf * 512:(hf + 1) * 512], b[0:NS, :], gtok2[0:NS, hf * 512:(hf + 1) * 512], ALU.mult, [b, gtok2], [mix_s])
                    k.rel(b)
                k.tt(xs_3[:], xs_3[:], mix_s[:], ALU.add, [xs_3, mix_s], [xs_3])
                k.act(ys[:], xs_3[:], AF.Square, [xs_3], [ys, ss1], accum=ss1[0:NS, :])
                k.act(rs1[0:NS, :], ss1[0:NS, :], AF.Ln, [ss1, epsc], [rs1], bias=epsc[0:NS, :], scale=1.0 / D)
                k.act(rs1[0:NS, :], rs1[0:NS, :], AF.Exp, [rs1], [rs1], scale=-0.5)
                k.stt(ys[:], xs_3[:], rs1[0:NS, :], fnw_bc[0:NS, :], ALU.mult, ALU.mult, [xs_3, rs1, fnw_bc], [ys])
                k.dma('sp', y_s, ys[:], reads=[ys])
                k.switch('S3', 'G')
                wlist['cur'] = W8

            if do_sample:
                sample_phase()

            for blk in range(nblk):
                t0 = blk * TB
                last = (blk == NBLK - 1)
                if not (blk == 0 and hoist['p1']):
                    emit_p1(t0)
                if blk == 0:
                    dump("hT", hT[:], [hT])
                if blk > 0:
                    k.switch('F', 'G')
                    k.switch('FW', 'G')
                wlist['cur'] = W8

                ck('p1')
                wt = wload(w_in[:, OFF_BETA:OFF_BETA + 16], 16)
                b = k.bank()
                for c in range(8):
                    for kk in range(8):
                        k.mm(b[0:64, c * 16:(c + 1) * 16], hT[:, kk, c * 64:(c + 1) * 64], wt[:, kk, 0:16], [hT, wt], [b],
                             start=(kk == 0), stop=(kk == 7))
                k.cp(ba[:], b[0:64, 0:128].rearrange("p (c r) -> p c r", r=16), [b], [ba])
                k.rel(b)
                k.act(beta[:], ba[:, :, 0:8], AF.Exp, [ba], [beta], scale=-1.0)
                k.ts(beta[:], beta[:], 1.0, None, ALU.add, None, [beta], [beta])
                k.op('dve', lambda e: e.reciprocal(out=beta[:], in_=beta[:]), reads=[beta], writes=[beta])
                k.tt(t64a[:], ba[:, :, 8:16], bc(dtb[:], 1, 8), ALU.add, [ba, dtb], [t64a])
                k.act(t64b[:], t64a[:], AF.Abs, [t64a], [t64b])
                k.act(t64b[:], t64b[:], AF.Exp, [t64b], [t64b], scale=-1.0)
                k.act(t64b[:], t64b[:], AF.Ln, [t64b, onec], [t64b], bias=onec[0:64, :])
                k.stt(t64a[:], t64a[:], 0.0, t64b[:], ALU.max, ALU.add, [t64a, t64b], [t64a])
                k.tt(gg[:], t64a[:], bc(negA[:], 1, 8), ALU.mult, [t64a, negA], [gg])
                ggf = gg[:].rearrange("p c h -> p (c h)")
                b = k.bank()
                k.mm(b[0:64, 0:64], Um[:], ggf, [Um, gg], [b])
                k.mm(b[:, 64:128], onesf[0:64, :], ggf, [onesf, gg], [b])
                k.cp(dd[:], b[0:64, 0:64], [b], [dd])
                k.act(ed[:], dd[:], AF.Exp, [dd], [ed])
                k.tt(ekd[:], b[0:64, 64:128], dd[:], ALU.subtract, [b, dd], [ekd])
                k.act(ekd[:], ekd[:], AF.Exp, [ekd], [ekd])
                k.act(elast[:], b[:, 64:128], AF.Exp, [b], [elast])
                k.rel(b)
                k.tt(bed[:], ed[:], beta[:].rearrange("p c h -> p (c h)"), ALU.mult, [ed, beta], [bed])
                k.cp(ghi[:], gg[:], [gg], [ghi])
                k.tt(glo[:], gg[:], ghi[:], ALU.subtract, [gg, ghi], [glo])
                k.cp(bhi[:], beta[:], [beta], [bhi])
                k.tt(blo[:], beta[:], bhi[:], ALU.subtract, [beta, bhi], [blo])
                k.cp(Umb[:], Um[:], [Um], [Umb])
                k.cp(SLmb[:], SLm[:], [SLm], [SLmb])
                if blk == 0:
                    dump("gg", gg[:], [gg])
                    dump("beta", beta[:], [beta])

                ck('p2')
                def gdn_front(h, hb):
                    wT, uu, Qd, Gp, Kdec, Am, Amb = wT2[hb], uu2[hb], Qd2[hb], Gp2[hb], Kdec2[hb], Am2[hb], Amb2[hb]
                    wt = nextw()
                    for part in range(3):
                        wload(w_in[:, OFF_QKV + part * 1024 + h * 128:OFF_QKV + part * 1024 + (h + 1) * 128], 128, c0=part * 128, tile=wt)
                    wload(w_in[:, OFF_GATE + h * 128:OFF_GATE + (h + 1) * 128], 128, c0=384, tile=wt)
                    for part in range(3):
                        b = k.bank()
                        for kk in range(8):
                            k.mm(b[:, :], wt[:, kk, part * 128:(part + 1) * 128], hT[:, kk, :], [wt, hT], [b], start=(kk == 0), stop=(kk == 7))
                        j = part * 8 + h
                        k.cp(pre[:, 0:3], halo[:, j, :], [halo], [pre], eng='pool')
                        k.cp(pre[:, 3:3 + TB], b[:, :], [b], [pre], eng='act')
                        k.rel(b)
                        yield
                        k.cp(halo[:, j, :], pre[:, TB:TB + 3], [pre], [halo], eng='pool')
                        k.ts(cv[:, part, :], pre[:, 0:TB], cwT[:, j, 0:1], None, ALU.mult, None, [pre, cwT], [cv])
                        for tap in range(1, 4):
                            k.stt(cv[:, part, :], pre[:, tap:tap + TB], cwT[:, j, tap:tap + 1], cv[:, part, :], ALU.mult, ALU.add,
                                  [pre, cwT, cv], [cv])
                    b = k.bank()
                    for kk in range(8):
                        k.mm(b[:, :], wt[:, kk, 384:512], hT[:, kk, :], [wt, hT], [b], start=(kk == 0), stop=(kk == 7))
                    k.act(Gp[:], b[:, :], AF.Silu, [b], [Gp])
                    k.rel(b)
                    yield
                    k.act(cv[:], cv[:], AF.Silu, [cv], [cv])
                    yield
                    for qk in range(2):
                        prebf = pre[:, 0:TB // 2].bitcast(BF16)
                        k.tt(prebf, cv[:, qk, :], cv[:, qk, :], ALU.mult, [cv], [pre], eng='pool')
                        b = k.bank()
                        k.mm(b[:, :], onesb[:], prebf, [onesb, pre], [b])
                        k.act(rqk[:, qk, :], b[:, :], AF.Ln, [b, epsc], [rqk], bias=epsc[:])
                        k.rel(b)
                        yield
                    k.act(rqk[:], rqk[:], AF.Exp, [rqk], [rqk], scale=-0.5)
                    k.stt(Qt_ap, cv[:, 0, :], 128.0 ** -0.5, rqk[:, 0, :], ALU.mult, ALU.mult, [cv, rqk], [cv])
                    k.tt(Kt_ap, cv[:, 1, :], rqk[:, 1, :], ALU.mult, [cv, rqk], [cv])
                    yield
                    k.cp(Qtb[:], Qt_ap, [cv], [Qtb], eng='pool')
                    k.cp(Ktb[:], Kt_ap, [cv], [Ktb], eng='pool')
                    k.cp(Vtb[:], cv[:, 2, :], [cv], [Vtb], eng='pool')
                    if blk == 0 and h == 0:
                        dump("Qt", Qt_ap, [cv])
                        dump("Kt", Kt_ap, [cv])
                        dump("Vt", cv[:, 2, :], [cv])
                    ck('gdn_a')
                    gh = gg[:, :, h]
                    SAv = SA[:].rearrange("p c j -> p (c j)").bitcast(BF16)
                    SBv = SB[:].rearrange("p c j -> p (c j)").bitcast(BF16)
                    SA_h = SAv[:, 0:512].rearrange("p (c j) -> p c j", j=64)
                    SA_l = SAv[:, 512:1024].rearrange("p (c j) -> p c j", j=64)
                    SB_h = SBv[:, 0:512].rearrange("p (c j) -> p c j", j=64)
                    SB_l = SBv[:, 512:1024].rearrange("p (c j) -> p c j", j=64)
                    k.tt(SA_h, bc(ghi[:, :, h], 2, 64), bc(SLm[:], 1, 8), ALU.mult, [ghi, SLm], [SA])
                    k.tt(SA_l, bc(glo[:, :, h], 2, 64), bc(SLm[:], 1, 8), ALU.mult, [glo, SLm], [SA])
                    k.tt(SB_h, bc(ghi[:, :, h], 2, 64), bc(Um[:], 1, 8), ALU.mult, [ghi, Um], [SB])
                    k.tt(SB_l, bc(glo[:, :, h], 2, 64), bc(Um[:], 1, 8), ALU.mult, [glo, Um], [SB])
                    yield
                    b = k.bank()
                    k.mm(b[0:64, :], Umb[:], SAv[:, 0:512], [Umb, SA], [b], start=True, stop=False)
                    k.mm(b[0:64, :], Umb[:], SAv[:, 512:1024], [Umb, SA], [b], start=False, stop=True)
                    k.act(Wm[:].rearrange("p c j -> p (c j)"), b[0:64, :], AF.Exp, [b], [Wm])
                    k.rel(b)
                    yield
                    k.tt(Wm[:], Wm[:], bc(nSL[:], 1, 8), ALU.mult, [Wm, nSL], [Wm])
                    k.tt(Wm[:], Wm[:], bc(beta[:, :, h], 2, 64), ALU.mult, [Wm, beta], [Wm])
                    yield
                    k.tt(SA_h, bc(bhi[:, :, h], 2, 64), bc(identf[0:64, 0:64], 1, 8), ALU.mult, [bhi, identf], [SA])
                    k.tt(SA_l, bc(blo[:, :, h], 2, 64), bc(identf[0:64, 0:64], 1, 8), ALU.mult, [blo, identf], [SA])
                    b = k.bank()
                    k.mm(b[0:64, :], SLmb[:], SBv[:, 0:512], [SLmb, SB], [b], start=True, stop=False)
                    k.mm(b[0:64, :], SLmb[:], SBv[:, 512:1024], [SLmb, SB], [b], start=False, stop=True)
                    k.act(Zm[:].rearrange("p c j -> p (c j)"), b[0:64, :], AF.Exp, [b], [Zm])
                    k.rel(b)
                    yield
                    k.tt(Am[:], Zm[:], bc(Um[:], 1, 8), ALU.mult, [Zm, Um], [Am])
                    k.tt(Zm[:], Zm[:], bc(nSU[:], 1, 8), ALU.mult, [Zm, nSU], [Zm])
                    yield
                    b = k.bank()
                    k.mm(b[0:64, :], onesb[0:64, 0:64], SAv[:, 0:512], [onesb, SA], [b], start=True, stop=False)
                    k.mm(b[0:64, :], onesb[0:64, 0:64], SAv[:, 512:1024], [onesb, SA], [b], start=False, stop=True)
                    k.tt(Zm[:].rearrange("p c j -> p (c j)"), Zm[:].rearrange("p c j -> p (c j)"), b[0:64, :], ALU.mult, [Zm, b], [Zm])
                    k.rel(b)
                    yield
                    k.cp(Qd[:], Qt_ap, [cv], [Qd])
                    ck('gdn_b')
                    bK0, bV0 = k.bank(), k.bank()
                    bkv = bK0[:, :].bitcast(BF16)
                    bvv = bV0[:, :].bitcast(BF16)
                    for c in range(8):
                        k.tr(bkv[0:64, c * 128:(c + 1) * 128], Ktb[:, c * 64:(c + 1) * 64], identb[:], [Ktb, identb], [bK0])
                        k.tr(bvv[0:64, c * 128:(c + 1) * 128], Vtb[:, c * 64:(c + 1) * 64], identb[:], [Vtb, identb], [bV0])
                    kin = bkv[0:64, :].rearrange("p (c d) -> p c d", d=128)
                    vin = bvv[0:64, :].rearrange("p (c d) -> p c d", d=128)
                    k.tt(Kbd[:], kin, bc(bed[:].rearrange("p (c h) -> p c h", h=8)[:, :, h], 2, 128), ALU.mult, [bK0, bed], [Kbd])
                    k.tt(Kdec[:], kin, bc(ekd[:].rearrange("p (c h) -> p c h", h=8)[:, :, h], 2, 128), ALU.mult, [bK0, ekd], [Kdec])
                    k.tt(Vb[:], vin, bc(beta[:, :, h], 2, 128), ALU.mult, [bV0, beta], [Vb])
                    k.rel(bK0, bV0)
                    yield
                    bA, bB = k.bank(), k.bank()
                    for c in range(8):
                        k.mm(bA[0:64, c * 64:(c + 1) * 64], Ktb[:, c * 64:(c + 1) * 64], Ktb[:, c * 64:(c + 1) * 64], [Ktb], [bA])
                        k.mm(bB[0:64, c * 64:(c + 1) * 64], Ktb[:, c * 64:(c + 1) * 64], Qtb[:, c * 64:(c + 1) * 64], [Ktb, Qtb], [bB])
                    A3 = bA[0:64, :].rearrange("p (c j) -> p c j", j=64)
                    B3 = bB[0:64, :].rearrange("p (c j) -> p c j", j=64)
                    k.tt(Wm[:], A3, Wm[:], ALU.mult, [bA, Wm], [Wm])
                    k.tt(Zm[:], A3, Zm[:], ALU.mult, [bA, Zm], [Zm])
                    Amb_ = Am if os.environ.get('SUPD32', '0') == '1' else Amb
                    k.tt(Amb_[:], B3, Am[:], ALU.mult, [bB, Am], [Amb_])
                    k.rel(bA, bB)
                    yield
                    I4 = bc(identf[0:64, 0:64], 1, 4)
                    for hf in range(2):
                        cs = slice(hf * 4, hf * 4 + 4)
                        k.cp(ZYH[hf][:, :, 0, :], Zm[:, cs, :], [Zm], [ZYH[hf]], eng='act')
                        k.tt(ZYH[hf][:, :, 1, :], Zm[:, cs, :], I4, ALU.add, [Zm, identf], [ZYH[hf]])
                        k.cp(WXH[hf][:, :, 0, :], Wm[:, cs, :], [Wm], [WXH[hf]], eng='act')
                        k.tt(WXH[hf][:, :, 1, :], Wm[:, cs, :], I4, ALU.add, [Wm, identf], [WXH[hf]])
                    ck('gdn_c')
                    for lev in range(6):
                        for hf in range(2):
                            zy, wx = ZYH[hf], WXH[hf]
                            bZ, bW = k.bank(), k.bank()
                            for cc in range(4):
                                if lev == 0:
                                    k.mm(bZ[0:64, cc * 128:cc * 128 + 64], wx[:, cc, 0, :], zy[:, cc, 0, :], [wx, zy], [bZ])
                                    k.mm(bW[0:64, cc * 128:cc * 128 + 64], zy[:, cc, 0, :], wx[:, cc, 0, :], [wx, zy], [bW])
                                elif lev < 5:
                                    k.mm(bZ[0:64, cc * 128:(cc + 1) * 128], wx[:, cc, 0, :], zy[:, cc, :, :].rearrange("p a j -> p (a j)"), [wx, zy], [bZ])
                                    k.mm(bW[0:64, cc * 128:(cc + 1) * 128], zy[:, cc, 0, :], wx[:, cc, :, :].rearrange("p a j -> p (a j)"), [wx, zy], [bW])
                                else:
                                    k.mm(bZ[0:64, cc * 128 + 64:(cc + 1) * 128], wx[:, cc, 0, :], zy[:, cc, 1, :], [wx, zy], [bZ])
                                    k.mm(bW[0:64, cc * 128 + 64:(cc + 1) * 128], zy[:, cc, 0, :], wx[:, cc, 1, :], [wx, zy], [bW])
                            cs = slice(hf * 4, hf * 4 + 4)
                            Z4 = bZ[0:64, :].rearrange("p (c a j) -> p c a j", a=2, j=64)
                            W4 = bW[0:64, :].rearrange("p (c a j) -> p c a j", a=2, j=64)
                            if lev == 0:
                                k.cp(zy[:, :, 0, :], Z4[:, :, 0, :], [bZ], [zy], eng='act')
                                k.cp(wx[:, :, 0, :], W4[:, :, 0, :], [bW], [wx], eng='act')
                            elif lev < 5:
                                k.tt(zy[:, :, 1, :], Z4[:, :, 1, :], zy[:, :, 1, :], ALU.add, [bZ, zy], [zy])
                                k.cp(zy[:, :, 0, :], Z4[:, :, 0, :], [bZ], [zy], eng='act')
                                k.tt(wx[:, :, 1, :], W4[:, :, 1, :], wx[:, :, 1, :], ALU.add, [bW, wx], [wx])
                                k.cp(wx[:, :, 0, :], W4[:, :, 0, :], [bW], [wx], eng='act')
                            else:
                                k.tt(Y32[hf][:], Z4[:, :, 1, :], zy[:, :, 1, :], ALU.add, [bZ, zy], [Y32[hf]])
                                k.tt(SB[:, cs, :], W4[:, :, 1, :], wx[:, :, 1, :], ALU.add, [bW, wx], [SB])
                            k.rel(bZ, bW)
                            yield
                    for hf in range(2):
                        cs = slice(hf * 4, hf * 4 + 4)
                        bR = k.bank()
                        for cc in range(4):
                            k.mm(bR[0:64, cc * 64:(cc + 1) * 64], Wm[:, hf * 4 + cc, :], Y32[hf][:, cc, :], [Wm, Y32[hf]], [bR])
                        R3 = bR[0:64, 0:256].rearrange("p (c j) -> p c j", j=64)
                        k.tt(SA[:, cs, :], R3, Y32[hf][:], ALU.subtract, [bR, Y32[hf]], [SA])
                        k.rel(bR)
                        k.tt(SA[:, cs, :], SA[:, cs, :], I4, ALU.add, [SA, identf], [SA])
                        bF = k.bank()
                        for cc in range(4):
                            k.mm(bF[0:64, cc * 64:(cc + 1) * 64], SB[:, hf * 4 + cc, :], SA[:, hf * 4 + cc, :], [SB, SA], [bF])
                        k.tt(Yb[hf][:], bF[0:64, 0:256].rearrange("p (c j) -> p c j", j=64), Y32[hf][:], ALU.add, [bF, Y32[hf]], [Yb[hf]])
                        k.rel(bF)
                        yield
                    if blk == 0 and h == 0:
                        dump("Yf", Yb[0][:], [Yb[0]])
                    bU0, bU1, bWt = k.bank(), k.bank(), k.bank()
                    for c in range(8):
                        bu_ = bU0 if c < 4 else bU1
                        Yc = Yb[c // 4][:, c % 4, :]
                        k.mm(bu_[0:64, (c % 4) * 128:(c % 4 + 1) * 128], Yc, Vb[:, c, :], [Yb[c // 4], Vb], [bu_])
                        k.mm(bWt[:, c * 64:(c + 1) * 64], Kbd[:, c, :], Yc, [Kbd, Yb[c // 4]], [bWt])
                    k.cp(uu[:, 0:4, :], bU0[0:64, :].rearrange("p (c d) -> p c d", d=128), [bU0], [uu], eng='act')
                    k.cp(uu[:, 4:8, :], bU1[0:64, :].rearrange("p (c d) -> p c d", d=128), [bU1], [uu], eng='act')
                    k.cp(wT[:], bWt[:, :].rearrange("p (c j) -> p c j", j=64), [bWt], [wT])
                    k.rel(bU0, bU1, bWt)
                    yield
                    yield

                def gdn_back(h, hb):
                    wT, uu, Qd, Gp, Kdec, Am, Amb = wT2[hb], uu2[hb], Qd2[hb], Gp2[hb], Kdec2[hb], Am2[hb], Amb2[hb]
                    Amb_ = Am if os.environ.get('SUPD32', '0') == '1' else Amb
                    Sh = S_all[:, h, :]
                    k.cp(Sb[:], Sh, [S_all], [Sb], eng='pool')
                    for c in range(8):
                        col = c * 8 + h
                        b1_, b2_, b3_ = k.bank(), k.bank(), k.bank()
                        k.mm(b1_[0:64, 0:128], wT[:, c, :], Sb[:], [wT, Sb], [b1_])
                        k.mm(b2_[0:64, 0:128], Qd[:, c * 64:(c + 1) * 64], Sb[:], [Qd, Sb], [b2_])
                        k.tt(vnew[:], uu[:, c, :], b1_[0:64, 0:128], ALU.subtract, [uu, b1_], [vnew])
                        yield
                        k.mm(b2_[0:64, 128:256], Amb_[:, c, :], vnew[:], [Amb_, vnew], [b2_])
                        k.mm(b3_[:, 0:128], Kdec[:, c, :], vnew[:], [Kdec, vnew], [b3_])
                        k.stt(Sh, Sh, elast[:, col:col + 1], b3_[:, 0:128], ALU.mult, ALU.add, [S_all, elast, b3_], [S_all])
                        if c < 7:
                            k.cp(Sb[:], Sh, [S_all], [Sb], eng='pool')
                        k.act(osb[:, c, :], b2_[0:64, 0:128], AF.Copy, [b2_, ed], [osb], scale=ed[:, col:col + 1])
                        k.tt(osb[:, c, :], osb[:, c, :], b2_[0:64, 128:256], ALU.add, [osb, b2_], [osb])
                        k.rel(b1_, b2_, b3_)
                        yield
                    if blk == 0 and h == 0:
                        dump("osb", osb[:], [osb])
                    ck('gdn_e')
                    k.tt(uu[:], osb[:], osb[:], ALU.mult, [osb], [uu], eng='pool')
                    k.op('dve', lambda e: e.tensor_reduce(out=oss[:], in_=uu[:], axis=AX.X, op=ALU.add), reads=[uu], writes=[oss])
                    k.act(ors[:], oss[:], AF.Ln, [oss, epsc], [ors], bias=epsc[0:64, :], scale=1.0 / 128)
                    k.act(ors[:], ors[:], AF.Exp, [ors], [ors], scale=-0.5)
                    k.tt(on1[:], osb[:], bc(ors[:], 2, 128), ALU.mult, [osb, ors], [on1])
                    b = k.bank()
                    bv = b[:, :].bitcast(BF16)
                    for c in range(8):
                        k.tr(bv[:, c * 64:(c + 1) * 64], on1[:, c, :], identb[0:64, 0:64], [on1, identb], [b])
                    k.stt(onT[:, h, :], bv[:, 0:TB], onwT[:, 0:1], Gp[:], ALU.mult, ALU.mult, [b, Gp, onwT], [onT])
                    k.rel(b)
                    yield
                    yield

                def _drain(gl):
                    gl = [[g_, w_] for g_, w_ in gl]
                    while gl:
                        for it in list(gl):
                            for _ in range(it[1]):
                                try:
                                    next(it[0])
                                except StopIteration:
                                    gl.remove(it)
                                    break

                _drain([(gdn_front(0, 0), 1)])
                for h in range(8):
                    gl = [(gdn_back(h, h % 2), 1)]
                    if h < 7:
                        gl.append((gdn_front(h + 1, (h + 1) % 2), 2))
                    _drain(gl)
                if blk == 0:
                    dump("onT", onT[:], [onT])
                    dump("S0", S_all[:], [S_all])
                if last:
                    k.dma('sp', o_S_p, S_all[:], reads=[S_all])
                    k.cp(halo_out[:], halo[:], [halo], [halo_out], eng='pool')
                    for j_ in range(3):
                        k.dma('sp', o_conv_p[j_].rearrange("(c p) -> p c", p=128), halo_out[:, :, j_], reads=[halo_out],
                              allow_slow_non_contiguous=True)

                ck('gdn')
                k.switch('G', 'A')
                k.switch('G', 'FW')
                wt = wload(w_in[:, OFF_SK:OFF_SK + 512], 512)
                for t in range(4):
                    b = k.bank()
                    for kk in range(8):
                        k.mm(b[:, :], hT[:, kk, t * 128:(t + 1) * 128], wt[:, kk, :], [hT, wt], [b], start=(kk == 0), stop=(kk == 7))
                    ck('swa_a0')
                    kin = b[:, 0:256].rearrange("p (g d) -> p g d", d=64)
                    vin = b[:, 256:512].rearrange("p (g d) -> p g d", d=64)
                    Kt4 = Ktok[:].rearrange("p (g a d) -> p g a d", a=2, d=64)
                    Vt4 = Vtok[:, 1 + t, :].rearrange("p (g a d) -> p g a d", a=2, d=64)
                    for a_ in range(2):
                        _v = os.environ.get('SWA_VAR', '')
                        ke, ve = {'': ('act', 'dve'), 'konly': ('act', None), 'vonly': (None, 'dve'), 'kdve': ('dve', None),
                                  'both_dve': ('dve', 'dve'), 'both_act': ('act', 'act'), 'swap': ('dve', 'act')}[_v]
                        if ke:
                            k.cp(Kt4[:, :, a_, :], kin, [b], [Ktok], eng=ke)
                        if ve:
                            k.cp(Vt4[:, :, a_, :], vin, [b], [Vtok], eng=ve)
                    ck('swa_a1')
                    if last and t == 3:
                        k.cp(kvout[:], b[:, :], [b], [kvout])
                        k.dma('sp', o_k_p, kvout[:, 0:256], reads=[kvout])
                        k.dma('sp', o_v_p, kvout[:, 256:512], reads=[kvout])
                    k.rel(b)
                    b = k.bank()
                    bv = b[:, :].bitcast(BF16)
                    for g in range(4):
                        k.tr(bv[:, g * 128:(g + 1) * 128], Ktok[:, g * 128:(g + 1) * 128], identb[:], [Ktok, identb], [b])
                    ck('swa_a2')
                    k.cp(KTl[0:64, :, 128 + t * 128:128 + (t + 1) * 128], bv[0:64, 0:512].rearrange("p (g q) -> p g q", q=128), [b], [KTl])
                    k.cp(KTh[64:128, :, 128 + t * 128:128 + (t + 1) * 128], bv[64:128, 0:512].rearrange("p (g q) -> p g q", q=128), [b], [KTh])
                    k.rel(b)
                ck('swa_a')
                for half in range(2):
                    wt = wload(w_in[:, OFF_SQ + half * 512:OFF_SQ + (half + 1) * 512], 512)
                    for j in range(4):
                        b = k.bank()
                        for kk in range(8):
                            k.mm(b[:, :], wt[:, kk, j * 128:(j + 1) * 128], hT[:, kk, :], [wt, hT], [b], start=(kk == 0), stop=(kk == 7))
                        k.act(QT[:, half * 4 + j, :], b[:, :], AF.Copy, [b], [QT], scale=0.125)
                        k.rel(b)
                ck('swa_b')
                def swa_iter(t, g, sb_):
                    msk = maskB if (blk == 0 and t == 0) else maskA
                    sc, pb, PT, mx, nmx, rsum, esk = sc2[sb_], pb2[sb_], PT2[sb_], mx2[sb_], nmx2[sb_], rsum2[sb_], esk2[sb_]
                    b0, b1 = k.bank(), k.bank()
                    for i in range(4):
                        hq = g * 4 + i
                        ch, hf = hq // 2, hq % 2
                        if os.environ.get('HF0'):
                            hf = 0
                        bb = b0 if i < 2 else b1
                        KTx = KTh if hf else KTl
                        k.mm(bb[:, (i % 2) * 256:(i % 2 + 1) * 256], QT[:, ch, t * 128:(t + 1) * 128],
                             KTx[:, g, t * 128:t * 128 + 256], [QT, KTx], [bb])
                    k.tt(sc[:, 0:2, :], b0[:, :].rearrange("p (i q) -> p i q", q=256), bc(msk[:], 1, 2), ALU.add, [b0, msk], [sc])
                    k.tt(sc[:, 2:4, :], b1[:, :].rearrange("p (i q) -> p i q", q=256), bc(msk[:], 1, 2), ALU.add, [b1, msk], [sc])
                    k.rel(b0, b1)
                    yield
                    ck('swa_c')
                    k.op('dve', lambda e: e.tensor_reduce(out=mx[:], in_=sc[:], axis=AX.X, op=ALU.max), reads=[sc], writes=[mx])
                    k.tt(mx[:], mx[:], sinks[:, g * 4:(g + 1) * 4], ALU.max, [mx, sinks], [mx])
                    k.ts(nmx[:], mx[:], -1.0, None, ALU.mult, None, [mx], [nmx])
                    for i in range(4):
                        k.act(pb[:, i, :], sc[:, i, :], AF.Exp, [sc, nmx], [pb, rsum], bias=nmx[:, i:i + 1], accum=rsum[:, i:i + 1])
                    k.tt(esk[:], sinks[:, g * 4:(g + 1) * 4], mx[:], ALU.subtract, [sinks, mx], [esk])
                    k.act(esk[:], esk[:], AF.Exp, [esk], [esk])
                    k.tt(rsum[:], rsum[:], esk[:], ALU.add, [rsum, esk], [rsum])
                    k.op('dve', lambda e: e.reciprocal(out=rsum[:], in_=rsum[:]), reads=[rsum], writes=[rsum])
                    ck('swa_d')
                    k.tt(pb[:], pb[:], bc(rsum[:], 2, 256), ALU.mult, [pb, rsum], [pb])
                    yield
                    b = k.bank()
                    bv = b[:, :].bitcast(BF16)
                    for i in range(4):
                        for kt in range(2):
                            k.tr(bv[:, (i * 2 + kt) * 128:(i * 2 + kt + 1) * 128], pb[:, i, kt * 128:(kt + 1) * 128], identb[:], [pb, identb], [b])
                    ck('swa_e')
                    k.cp(PT[:].rearrange("p i a q -> p (i a q)"), bv[:, 0:1024], [b], [PT], eng='act')
                    k.rel(b)
                    yield
                    b = k.bank()
                    for i in range(4):
                        for kt in range(2):
                            k.mm(b[:, i * 128:(i + 1) * 128], Vtok[:, t + kt, g * 128:(g + 1) * 128], PT[:, i, kt, :], [Vtok, PT], [b],
                                 start=(kt == 0), stop=(kt == 1))
                    for i in range(4):
                        hq = g * 4 + i
                        ch, hf = hq // 2, hq % 2
                        k.cp(obT[hf * 64:(hf + 1) * 64, ch, t * 128:(t + 1) * 128], b[hf * 64:(hf + 1) * 64, i * 128:(i + 1) * 128], [b], [obT],
                             eng=('act' if i % 2 else 'dve'))
                    k.rel(b)
                    yield

                def swa_stream(its, sb_):
                    for (t_, g_) in its:
                        yield from swa_iter(t_, g_, sb_)

                def _drain2(gl):
                    gl = list(gl)
                    while gl:
                        for it in list(gl):
                            try:
                                next(it)
                            except StopIteration:
                                gl.remove(it)

                its_ = [(t_, g_) for t_ in range(4) for g_ in range(4)]
                _drain2([swa_stream(its_[0::2], 0), swa_stream(its_[1::2], 1)])
                ck('swa_f')
                k.cp(KTl[0:64, :, 0:128], KTl[0:64, :, TB:TB + 128], [KTl], [KTl], eng='pool')
                k.cp(KTh[64:128, :, 0:128], KTh[64:128, :, TB:TB + 128], [KTh], [KTh], eng='pool')
                k.cp(Vtok[:, 0, :], Vtok[:, 4, :], [Vtok], [Vtok], eng='pool')
                if blk == 0:
                    dump("obT", obT[:], [obT])

                ck('swa')
                k.switch('A', 'F')
                wlist['cur'] = W8 + FW
                for j in range(8):
                    wt = nextw()
                    wload(w_in[:, OFF_GA + j * 128:OFF_GA + (j + 1) * 128], 128, c0=0, tile=wt)
                    wload(w_in[:, OFF_GB + j * 128:OFF_GB + (j + 1) * 128], 128, c0=128, tile=wt)
                    wload(w_gdn_out[:, j * 128:(j + 1) * 128], 128, c0=256, tile=wt)
                    wload(w_swa_out[:, j * 128:(j + 1) * 128], 128, c0=384, tile=wt)
                    bs = [k.bank() for _ in range(4)]
                    srcs = [hT, hT, onT, obT]
                    for q in range(4):
                        for kk in range(8):
                            k.mm(bs[q][:, :], wt[:, kk, q * 128:(q + 1) * 128], srcs[q][:, kk, :], [wt, srcs[q]], [bs[q]], start=(kk == 0), stop=(kk == 7))
                    k.act(sga[:], bs[0][:, :], AF.Sigmoid, [bs[0]], [sga])
                    k.act(sgb[:], bs[1][:, :], AF.Sigmoid, [bs[1]], [sgb])
                    k.tt(sga[:], sga[:], bs[2][:, :], ALU.mult, [sga, bs[2]], [sga])
                    k.tt(sgb[:], sgb[:], bs[3][:, :], ALU.mult, [sgb, bs[3]], [sgb])
                    k.tt(mixT[:, j, :], sga[:], sgb[:], ALU.add, [sga, sgb], [mixT])
                    k.rel(*bs)
                ck('merge')
                wts = [wload(w_o[:, hf * 512:(hf + 1) * 512], 512) for hf in range(2)]
                for t in range(4):
                    for hf in range(2):
                        b = k.bank()
                        for kk in range(8):
                            k.mm(b[:, :], mixT[:, kk, t * 128:(t + 1) * 128], wts[hf][:, kk, :], [mixT, wts[hf]], [b], start=(kk == 0), stop=(kk == 7))
                        k.tt(sga[:], b[:, :], g1bc[:, hf * 512:(hf + 1) * 512], ALU.mult, [b, g1bc], [sga])
                        k.rel(b)
                        k.tt(x1[t][:, hf * 512:(hf + 1) * 512], x1[t][:, hf * 512:(hf + 1) * 512], sga[:], ALU.add, [x1[t], sga], [x1[t]], eng='pool')
                    rms_to_T(x1[t][:], x1[t], h2T, h2T, (a2, a2), (modT[:, 24:32, NS], modT), t)
                if blk == 0:
                    dump("x1", x1[0][:], [x1[0]])

                ck('wo')
                for jp in range(NFC // 2):
                    wt = nextw()
                    wload(w_ffn_gate[:, jp * 256:(jp + 1) * 256], 256, c0=0, tile=wt)
                    wload(w_ffn_up[:, jp * 256:(jp + 1) * 256], 256, c0=256, tile=wt)
                    for jj in range(2):
                        j = jp * 2 + jj
                        bg, bu = k.bank(), k.bank()
                        for kk in range(8):
                            k.mm(bg[:, :], wt[:, kk, jj * 128:(jj + 1) * 128], h2T[:, kk, :], [wt, h2T], [bg], start=(kk == 0), stop=(kk == 7))
                        for kk in range(8):
                            k.mm(bu[:, :], wt[:, kk, 256 + jj * 128:256 + (jj + 1) * 128], h2T[:, kk, :], [wt, h2T], [bu], start=(kk == 0), stop=(kk == 7))
                        k.cp(gpre[:, 0:2], fhalo[:, j, :], [fhalo], [gpre], eng='pool')
                        k.cp(gpre[:, 2:2 + TB], bg[:, :], [bg], [gpre], eng='act')
                        k.cp(fhalo[:, j, :], gpre[:, TB:TB + 2], [gpre], [fhalo], eng='pool')
                        k.ts(gcv[:], gpre[:, 0:TB], fcwT[:, j, 0:1], fcbT[:, j:j + 1], ALU.mult, ALU.add, [gpre, fcwT, fcbT], [gcv])
                        for tap in range(1, 3):
                            k.stt(gcv[:], gpre[:, tap:tap + TB], fcwT[:, j, tap:tap + 1], gcv[:], ALU.mult, ALU.add, [gpre, fcwT, gcv], [gcv])
                        k.act(gcv[:], gcv[:], AF.Silu, [gcv], [gcv])
                        k.tt(actT[:, j, :], gcv[:], bu[:, :], ALU.mult, [gcv, bu], [actT])
                        k.rel(bg, bu)
                if last:
                    for j_ in range(2):
                        k.dma('sp', o_ffn_p[j_].rearrange("(c p) -> p c", p=128), fhalo[:, :, j_], reads=[fhalo], allow_slow_non_contiguous=True)
                for hf in range(2):
                    bs = [k.bank() for _ in range(4)]
                    for kg in range(3):
                        nk = 8 if kg < 2 else NFC - 16
                        wt = nextw()
                        k.dma('pool', wt[:, 0:nk, :], w_ffn_down[kg * 1024:kg * 1024 + nk * 128, hf * 512:(hf + 1) * 512].rearrange("(c p) n -> p c n", p=128),
                              writes=[wt])
                        for kk in range(nk):
                            kf = kg * 8 + kk
                            for t in range(4):
                                k.mm(bs[t][:, :], actT[:, kf, t * 128:(t + 1) * 128], wt[:, kk, :], [actT, wt], [bs[t]], start=(kf == 0), stop=(kf == NFC - 1))
                    for t in range(4):
                        k.tt(sga[:], bs[t][:, :], g2bc[:, hf * 512:(hf + 1) * 512], ALU.mult, [bs[t], g2bc], [sga])
                        k.tt(x1[t][:, hf * 512:(hf + 1) * 512], x1[t][:, hf * 512:(hf + 1) * 512], sga[:], ALU.add, [x1[t], sga], [x1[t]], eng='pool')
                    k.rel(*bs)
                ck('ffn')
                for t in range(4):
                    k.act(yt[:], x1[t][:], AF.Square, [x1[t]], [yt, ss2], accum=ss2[:])
                    k.act(rs2[:], ss2[:], AF.Ln, [ss2, epsc], [rs2], bias=epsc[:], scale=1.0 / D)
                    k.act(rs2[:], rs2[:], AF.Exp, [rs2], [rs2], scale=-0.5)
                    k.stt(yt[:], x1[t][:], rs2[:], fnw_bc[:], ALU.mult, ALU.mult, [x1[t], rs2, fnw_bc], [yt])
                    k.dma('sp', y_p[t0 + t * 128:t0 + (t + 1) * 128, :], yt[:], reads=[yt])
        except _Stop:
            pass
        k.finish('sp')
        print("instr counts", k.cnt, "dma sems", k.ndsem)
        if os.environ.get('MMSTAT'):
            tot = sum(k.mmstat.values())
            for ln, c in sorted(k.mmstat.items(), key=lambda kv: -kv[1])[:40]:
                print("  mm line %d: %.1f us (%.1f%%)" % (ln, c / 2400.0, 100.0 * c / tot))
            print("  total est %.1f us" % (tot / 2400.0))
    return nc


OUT_NAMES = ["y_p", "y_s", "o_S_p", "o_S_s", "o_conv_p", "o_conv_s", "o_k_p", "o_k_s", "o_v_p", "o_v_s", "o_ffn_p", "o_ffn_s"]


def make_in_maps(inp, cores):
    f = lambda a: np.ascontiguousarray(a, dtype=np.float32)
    shared = {
        "w_mod": f(inp["w_mod"][0]), "b_mod": f(inp["b_mod"][0][None]), "norm1_w": f(inp["norm1_w"][0][None]),
        "norm2_w": f(inp["norm2_w"][0][None]), "w_in": f(inp["w_in"][0]), "gdn_conv_w": f(inp["gdn_conv_w"][0]),
        "gdn_a_log": f(inp["gdn_a_log"][0]), "gdn_dt_bias": f(inp["gdn_dt_bias"][0]),
        "gdn_onorm_w": f(inp["gdn_onorm_w"][0][None]), "w_gdn_out": f(inp["w_gdn_out"][0]),
        "swa_sinks": f(inp["swa_sinks"][0]), "w_swa_out": f(inp["w_swa_out"][0]), "w_o": f(inp["w_o"][0]),
        "w_ffn_gate": f(inp["w_ffn_gate"][0]), "w_ffn_up": f(inp["w_ffn_up"][0]), "ffn_conv_w": f(inp["ffn_conv_w"][0]),
        "ffn_conv_b": f(inp["ffn_conv_b"][0][None]), "w_ffn_down": f(inp["w_ffn_down"][0]),
        "final_norm_w": f(inp["final_norm_w"]),
    }
    maps = []
    for b in cores:
        s = slice(b * NS, (b + 1) * NS)
        m = dict(shared)
        m["x_p"] = f(inp["x_prompt"][b])
        m["x_s"] = f(inp["x_sample"][s, 0])
        m["c17"] = f(np.concatenate([inp["c_sample"][s], inp["c_prompt"][b:b + 1]], axis=0))
        m["st_S"] = f(np.transpose(inp["state_gdn_S"][0, s], (0, 2, 1, 3)))
        m["st_conv"] = f(inp["state_gdn_conv"][0, s].reshape(NS * 3, 3072))
        m["st_k"] = f(np.transpose(inp["cache_swa_k"][0, s].reshape(NS, 128, 256), (1, 0, 2)))
        m["st_v"] = f(np.transpose(inp["cache_swa_v"][0, s].reshape(NS, 128, 256), (1, 0, 2)))
        m["st_ffn"] = f(inp["state_ffn_conv"][0, s].reshape(NS * 2, DFF))
        maps.append(m)
    return maps


def kernel(**inp):
    nc = build_nc()
    cores = list(range(8))
    res = run_bass_kernel_spmd(nc, make_in_maps(inp, cores), core_ids=cores)
    r = res.results
    cat = lambda n: np.concatenate([r[i][n] for i in range(8)], axis=0)
    stack = lambda n: np.stack([r[i][n] for i in range(8)], axis=0)
    y_prompt = stack("y_p")
    y_sample = cat("y_s").reshape(128, 1, D)
    gS_p = np.ascontiguousarray(np.transpose(stack("o_S_p"), (0, 2, 1, 3)))[None]
    gS_s = np.ascontiguousarray(np.transpose(cat("o_S_s"), (0, 2, 1, 3)))[None]
    gc_p = stack("o_conv_p")[None]
    gc_s = cat("o_conv_s")[None]
    k_p = stack("o_k_p").reshape(1, 8, 128, 4, 64)
    k_s = np.ascontiguousarray(np.concatenate([np.transpose(r[i]["o_k_s"], (1, 0, 2)) for i in range(8)], axis=0)).reshape(1, 128, 128, 4, 64)
    v_p = stack("o_v_p").reshape(1, 8, 128, 4, 64)
    v_s = np.ascontiguousarray(np.concatenate([np.transpose(r[i]["o_v_s"], (1, 0, 2)) for i in range(8)], axis=0)).reshape(1, 128, 128, 4, 64)
    f_p = stack("o_ffn_p")[None]
    f_s = cat("o_ffn_s")[None]
    return (y_prompt, y_sample, gS_p, gS_s, gc_p, gc_s, k_p, k_s, v_p, v_s, f_p, f_s)
```

```python
import os
import numpy as np
import concourse.bass as bass
import concourse.mybir as mybir
from concourse.bass_utils import run_bass_kernel_spmd
from contextlib import ExitStack

F32 = mybir.dt.float32
BF16 = mybir.dt.bfloat16
AF = mybir.ActivationFunctionType
ALU = mybir.AluOpType
AX = mybir.AxisListType

D = 1024
SEQ = 2048
TB = 512
NBLK = SEQ // TB
NS = 16
DFF = 2816
NFC = DFF // 128
OFF_QKV, OFF_GATE, OFF_BETA, OFF_A, OFF_SQ, OFF_SK, OFF_SV, OFF_GA, OFF_GB = 0, 3072, 4096, 4104, 4112, 5136, 5392, 5648, 6672
INW = 7696
EPS = 1e-6
NEG = -30000.0


class Tile:
    def __init__(self, t, name):
        self.t = t
        self.name = name
        self.lw = None
        self.rd = {}
        self.dkey = None
        self.dcnt = 0

    def __getitem__(self, k):
        return self.t[k]


class K:
    def __init__(self, nc, es):
        self.nc = nc
        self.es = es
        self.eng = {'pe': nc.tensor, 'dve': nc.vector, 'act': nc.scalar, 'pool': nc.gpsimd, 'sp': nc.sync}
        self.sem = {}
        for e in self.eng:
            self.sem[e] = es.enter_context(nc.semaphore('s_' + e))
        self.cnt = {e: 0 for e in self.eng}
        self.seen = {e: {} for e in self.eng}
        self.ndsem = 0
        self.tiles = []
        self.free_banks = []
        self.dbg = []
        self.phase_off = {}

    def sb(self, name, shape, dt=F32, es=None):
        t = (es or self.es).enter_context(self.nc.sbuf_tensor(name, list(shape), dt))
        T = Tile(t, name)
        self.tiles.append(T)
        return T

    def view(self, ap, name):
        T = Tile(ap, name)
        self.tiles.append(T)
        return T

    def init_psum(self):
        self.psum = self.es.enter_context(self.nc.psum_tensor("psum", [128, 4096], F32))
        self.banks = []
        for i in range(8):
            T = Tile(self.psum[:, i * 512:(i + 1) * 512], "bank%d" % i)
            T.excl = True
            self.tiles.append(T)
            self.banks.append(T)
        self.free_banks = list(self.banks)

    def bank(self):
        assert self.free_banks, "out of PSUM banks"
        return self.free_banks.pop(0)

    def rel(self, *bs):
        for b in bs:
            assert b not in self.free_banks
            self.free_banks.append(b)

    def _deps(self, e, reads, writes, skip=None):
        deps = {}

        def add(kv):
            k_, v = kv
            if deps.get(k_, 0) < v:
                deps[k_] = v
        for t in reads:
            if t.lw:
                add(t.lw)
            if getattr(t, 'excl', False):
                for kv in t.rd.items():
                    if kv[0] != e:
                        add(kv)
        for t in writes:
            if t.lw:
                add(t.lw)
            for kv in t.rd.items():
                add(kv)
        for k_, v in deps.items():
            if k_ == e and e == 'pe':
                continue
            if skip is not None and k_ == skip:
                continue
            if self.seen[e].get(k_, 0) >= v:
                continue
            self.eng[e].wait_ge(self.sem[k_], v)
            self.seen[e][k_] = v

    def op(self, e, fn, reads=(), writes=()):
        if e == 'pool' and getattr(self, 'pool_to', None):
            e = self.pool_to
        self._deps(e, reads, writes)
        ins = fn(self.eng[e])
        ins.then_inc(self.sem[e], 1)
        self.cnt[e] += 1
        c = self.cnt[e]
        for t in writes:
            t.lw = (e, c)
            t.rd = {}
        for t in reads:
            if t not in writes:
                t.rd[e] = c

    def dma(self, q, out, in_, reads=(), writes=(), semtile=None, indep=False, **kw):
        T = semtile if semtile is not None else (writes[0] if writes else reads[0])
        self._deps(q, reads, writes, skip=(T.dkey if indep else None))
        if T.dkey is None:
            T.dkey = 'd%d' % self.ndsem
            self.ndsem += 1
            self.sem[T.dkey] = self.es.enter_context(self.nc.semaphore(T.dkey))
        self.eng[q].dma_start(out=out, in_=in_, **kw).then_inc(self.sem[T.dkey], 16)
        T.dcnt += 16
        for t in writes:
            t.lw = (T.dkey, T.dcnt)
            t.rd = {}
        for t in reads:
            t.rd[T.dkey] = T.dcnt

    def init_arena(self, nbytes):
        self.arena = self.es.enter_context(self.nc.sbuf_tensor("arena", [128, nbytes // 4], F32))
        self.phase_tiles = {}

    def carve(self, phase, name, shape, dt=F32):
        off = self.phase_off.get(phase, 0)
        n = 1
        for d_ in shape[1:]:
            n *= d_
        nb = n * (2 if dt == BF16 else 4)
        nb = (nb + 63) // 64 * 64
        assert off + nb <= self.arena.shape[1] * 4, "arena overflow in phase %s at %s: %d" % (phase, name, off + nb)
        ap = self.arena[0:shape[0], off // 4:(off + nb) // 4]
        if dt == BF16:
            ap = ap.bitcast(BF16)
        ap = ap[:, 0:n]
        if len(shape) == 3:
            ap = ap.rearrange("p (a b) -> p a b", b=shape[2])
        elif len(shape) == 4:
            ap = ap.rearrange("p (a b c) -> p a b c", b=shape[2], c=shape[3])
        self.phase_off[phase] = off + nb
        T = Tile(ap, name)
        self.tiles.append(T)
        self.phase_tiles.setdefault(phase, []).append(T)
        return T

    def switch(self, frm, to):
        acc = {}
        for F_ in self.phase_tiles.get(frm, []):
            if F_.lw:
                acc[F_.lw[0]] = max(acc.get(F_.lw[0], 0), F_.lw[1])
            for k_, v in F_.rd.items():
                acc[k_] = max(acc.get(k_, 0), v)
        for T in self.phase_tiles.get(to, []):
            for k_, v in acc.items():
                T.rd[k_] = max(T.rd.get(k_, 0), v)

    def barrier(self):
        for e in self.eng:
            for T in self.tiles:
                if T.dkey is not None and self.seen[e].get(T.dkey, 0) < T.dcnt:
                    self.eng[e].wait_ge(self.sem[T.dkey], T.dcnt)
                    self.seen[e][T.dkey] = T.dcnt
            for k_ in self.eng:
                if k_ != e and self.cnt[k_] > 0 and self.seen[e].get(k_, 0) < self.cnt[k_]:
                    self.eng[e].wait_ge(self.sem[k_], self.cnt[k_])
                    self.seen[e][k_] = self.cnt[k_]

    def finish(self, e='sp'):
        for T in self.tiles:
            if T.dkey is not None and self.seen[e].get(T.dkey, 0) < T.dcnt:
                self.eng[e].wait_ge(self.sem[T.dkey], T.dcnt)
                self.seen[e][T.dkey] = T.dcnt
        for k_ in self.eng:
            if k_ != e and self.cnt[k_] > 0 and self.seen[e].get(k_, 0) < self.cnt[k_]:
                self.eng[e].wait_ge(self.sem[k_], self.cnt[k_])
                self.seen[e][k_] = self.cnt[k_]

    def mm(self, out, lhsT, rhs, r, w, start=True, stop=True):
        import traceback
        ln = traceback.extract_stack(limit=2)[0].lineno
        n = 1
        for d_ in out.shape[1:]:
            n *= d_
        cyc = n * (4 if rhs.dtype == F32 else 1)
        st = self.__dict__.setdefault('mmstat', {})
        st[ln] = st.get(ln, 0) + max(cyc, 64)
        self.op('pe', lambda e: e.matmul(out, lhsT=lhsT, rhs=rhs, start=start, stop=stop), reads=r, writes=w)

    def tr(self, out, in_, ident, r, w):
        import traceback
        ln = traceback.extract_stack(limit=2)[0].lineno
        n = 1
        for d_ in out.shape[1:]:
            n *= d_
        cyc = n * (2 if in_.dtype == F32 else 1)
        st = self.__dict__.setdefault('mmstat', {})
        st[ln] = st.get(ln, 0) + max(cyc, 64)
        self.op('pe', lambda e: e.transpose(out=out, in_=in_, identity=ident), reads=r, writes=w)

    def act(self, out, in_, func, r, w, bias=None, scale=None, accum=None, eng='act'):
        kw = {}
        if bias is not None:
            kw['bias'] = bias
        if scale is not None:
            kw['scale'] = scale
        if accum is not None:
            kw['accum_out'] = accum
        self.op('act', lambda e: e.activation(out=out, in_=in_, func=func, **kw), reads=r, writes=w)

    def tt(self, out, in0, in1, op, r, w, eng='dve'):
        self.op(eng, lambda e: e.tensor_tensor(out=out, in0=in0, in1=in1, op=op), reads=r, writes=w)

    def ts(self, out, in0, s1, s2, op0, op1, r, w, eng='dve', accum=None):
        if op1 is None:
            self.op(eng, lambda e: e.tensor_scalar(out=out, in0=in0, scalar1=s1, scalar2=None, op0=op0), reads=r, writes=w)
        else:
            self.op(eng, lambda e: e.tensor_scalar(out=out, in0=in0, scalar1=s1, scalar2=s2, op0=op0, op1=op1), reads=r, writes=w)

    def stt(self, out, in0, scalar, in1, op0, op1, r, w, accum=None):
        if accum is None:
            self.op('dve', lambda e: e.scalar_tensor_tensor(out=out, in0=in0, scalar=scalar, in1=in1, op0=op0, op1=op1), reads=r, writes=w)
        else:
            self.op('dve', lambda e: e.scalar_tensor_tensor(out=out, in0=in0, scalar=scalar, in1=in1, op0=op0, op1=op1, accum_out=accum), reads=r, writes=w)

    def cp(self, out, in_, r, w, eng='dve'):
        if eng == 'act':
            self.op('act', lambda e: e.activation(out=out, in_=in_, func=AF.Copy), reads=r, writes=w)
        else:
            self.op(eng, lambda e: e.tensor_copy(out=out, in_=in_), reads=r, writes=w)

    def memset(self, out, val, w, eng='pool'):
        self.op(eng, lambda e: e.memset(out, val), writes=w)

    def asel(self, out, in_, pattern, cmp, fill, base, cm, r, w):
        self.op('pool', lambda e: e.affine_select(out=out, in_=in_, pattern=pattern, compare_op=cmp, fill=fill,
                                                  base=base, channel_multiplier=cm), reads=r, writes=w)


def bc(ap, axis, n):
    a = ap.unsqueeze(axis)
    shp = list(a.shape)
    shp[axis] = n
    return a.broadcast_to(shp)


class _Stop(Exception):
    pass


def build_nc(debug=False, nblk=NBLK, do_sample=True, stop=None):
    nc = bass.Bass("TRN2", target_bir_lowering=False)

    def din(name, shape):
        return nc.dram_tensor(name, list(shape), F32, kind="ExternalInput").ap()

    def dout(name, shape):
        return nc.dram_tensor(name, list(shape), F32, kind="ExternalOutput").ap()

    x_p = din("x_p", [SEQ, D])
    x_s = din("x_s", [NS, D])
    c17 = din("c17", [NS + 1, D])
    st_S = din("st_S", [NS, 128, 8, 128])
    st_conv = din("st_conv", [NS * 3, 3072])
    st_k = din("st_k", [128, NS, 256])
    st_v = din("st_v", [128, NS, 256])
    st_ffn = din("st_ffn", [NS * 2, DFF])
    w_mod = din("w_mod", [D, 6 * D])
    b_mod = din("b_mod", [1, 6 * D])
    norm1_w = din("norm1_w", [1, D])
    norm2_w = din("norm2_w", [1, D])
    w_in = din("w_in", [D, INW])
    gdn_conv_w = din("gdn_conv_w", [4, 3072])
    gdn_a_log = din("gdn_a_log", [8])
    gdn_dt_bias = din("gdn_dt_bias", [8])
    gdn_onorm_w = din("gdn_onorm_w", [1, 128])
    w_gdn_out = din("w_gdn_out", [D, D])
    swa_sinks = din("swa_sinks", [16])
    w_swa_out = din("w_swa_out", [D, D])
    w_o = din("w_o", [D, D])
    w_ffn_gate = din("w_ffn_gate", [D, DFF])
    w_ffn_up = din("w_ffn_up", [D, DFF])
    ffn_conv_w = din("ffn_conv_w", [3, DFF])
    ffn_conv_b = din("ffn_conv_b", [1, DFF])
    w_ffn_down = din("w_ffn_down", [DFF, D])
    final_norm_w = din("final_norm_w", [D])

    y_p = dout("y_p", [SEQ, D])
    y_s = dout("y_s", [NS, D])
    o_S_p = dout("o_S_p", [128, 8, 128])
    o_S_s = dout("o_S_s", [NS, 128, 8, 128])
    o_conv_p = dout("o_conv_p", [3, 3072])
    o_conv_s = dout("o_conv_s", [NS, 3, 3072])
    o_k_p = dout("o_k_p", [128, 256])
    o_k_s = dout("o_k_s", [128, NS, 256])
    o_v_p = dout("o_v_p", [128, 256])
    o_v_s = dout("o_v_s", [128, NS, 256])
    o_ffn_p = dout("o_ffn_p", [2, DFF])
    o_ffn_s = dout("o_ffn_s", [NS, 2, DFF])

    with ExitStack() as es:
        k = K(nc, es)
        k.init_psum()
        PS = k.psum

        def dump(name, ap, tiles):
            if not debug:
                return
            o = nc.dram_tensor("dbg_" + name, list(ap.shape), ap.dtype, kind="ExternalOutput").ap()
            dt_ = Tile(None, 'dbg_' + name)
            k.tiles.append(dt_)
            k.dma('sp', o, ap, reads=tiles, semtile=dt_)

        dbgsem = k.sb("dbgsem", [1, 1])

        def ck(name):
            if stop == name:
                raise _Stop()

        try:
            identf = k.sb("identf", [128, 128])
            identb = k.sb("identb", [128, 128], BF16)
            onesf = k.sb("onesf", [128, 128])
            onesb = k.sb("onesb", [128, 128], BF16)
            Um = k.sb("Um", [64, 64])
            SLm = k.sb("SLm", [64, 64])
            SUm = k.sb("SUm", [64, 64])
            nSL = k.sb("nSL", [64, 64])
            nSU = k.sb("nSU", [64, 64])
            maskA = k.sb("maskA", [128, 256])
            maskB = k.sb("maskB", [128, 256])
            E16 = k.sb("E16", [NS + 1, 128])
            Esel = k.sb("Esel", [NS, NS, 128])
            epsc = k.sb("epsc", [128, 1])
            onec = k.sb("onec", [128, 1])

            k.memset(identf[:], 0.0, [identf])
            k.asel(identf[:], identf[:], [[-1, 128]], ALU.not_equal, 1.0, 0, 1, [identf], [identf])
            k.cp(identb[:], identf[:], [identf], [identb])
            k.memset(onesf[:], 1.0, [onesf])
            k.memset(onesb[:], 1.0, [onesb])
            k.memset(epsc[:], EPS, [epsc])
            k.memset(onec[:], 1.0, [onec])
            k.memset(Um[:], 1.0, [Um])
            k.asel(Um[:], Um[:], [[1, 64]], ALU.is_ge, 0.0, 0, -1, [Um], [Um])
            k.memset(SLm[:], 1.0, [SLm])
            k.asel(SLm[:], SLm[:], [[-1, 64]], ALU.is_ge, 0.0, -1, 1, [SLm], [SLm])
            k.memset(SUm[:], 1.0, [SUm])
            k.asel(SUm[:], SUm[:], [[1, 64]], ALU.is_ge, 0.0, -1, -1, [SUm], [SUm])
            k.ts(nSL[:], SLm[:], -1.0, None, ALU.mult, None, [SLm], [nSL])
            k.ts(nSU[:], SUm[:], -1.0, None, ALU.mult, None, [SUm], [nSU])
            k.memset(maskA[:], 0.0, [maskA])
            k.asel(maskA[:], maskA[:], [[1, 256]], ALU.is_ge, NEG, -1, -1, [maskA], [maskA])
            k.asel(maskA[:], maskA[:], [[-1, 256]], ALU.is_ge, NEG, 128, 1, [maskA], [maskA])
            k.asel(maskB[:], maskA[:], [[1, 256]], ALU.is_ge, NEG, -128, 0, [maskA], [maskB])
            k.memset(E16[:], 0.0, [E16])
            k.asel(E16[:], E16[:], [[0, 128]], ALU.not_equal, 1.0, -NS, 1, [E16], [E16])
            k.memset(Esel[:], 0.0, [Esel])
            k.asel(Esel[:], Esel[:], [[-1, NS], [0, 128]], ALU.not_equal, 1.0, 0, 1, [Esel], [Esel])

            k.pool_to = 'dve'
            modT = k.sb("modT", [128, 48, NS + 1])
            n1w = k.sb("n1w", [128, 8])
            n2w = k.sb("n2w", [128, 8])
            a1 = k.sb("a1", [128, 8])
            a2 = k.sb("a2", [128, 8])
            A1s = k.sb("A1s", [128, 8, NS])
            A2s = k.sb("A2s", [128, 8, NS])
            cwT = k.sb("cwT", [128, 24, 4])
            fcwT = k.sb("fcwT", [128, NFC, 3])
            fcbT = k.sb("fcbT", [128, NFC])
            onwT = k.sb("onwT", [128, 1])
            fnw_bc = k.sb("fnw_bc", [128, D])
            g1bc = k.sb("g1bc", [128, D])
            g2bc = k.sb("g2bc", [128, D])
            gtok1 = k.sb("gtok1", [NS + 1, D])
            gtok2 = k.sb("gtok2", [NS + 1, D])
            negA = k.sb("negA", [64, 8])
            dtb = k.sb("dtb", [64, 8])
            sinks = k.sb("sinks", [128, 16])

            W8 = [k.sb("W8_%d" % i, [128, 8, 512], BF16) for i in range(3)]
            ring = {'w8': 0, 'wd': 0}

            wlist = {'cur': W8}

            def nextw():
                wl = wlist['cur']
                t_ = wl[ring['w8'] % len(wl)]
                ring['w8'] += 1
                return t_

            def wload(src, ncols, c0=0, tile=None):
                piece = tile is not None
                if tile is None:
                    tile = nextw()
                k.dma('pool', tile[:, :, c0:c0 + ncols], src.rearrange("(c p) n -> p c n", p=128), writes=[tile], indep=piece)
                return tile

            with ExitStack() as es2:
                stage = k.sb("stage", [8, 6 * D], F32, es=es2)
                c17t = k.sb("c17t", [NS + 1, D], F32, es=es2)
                scT = k.sb("scT", [128, 8, NS + 1], BF16, es=es2)
                bmT = k.sb("bmT", [128, 48], F32, es=es2)

                def featmajor(src, r, C, dst_ap, dst_tile):
                    k.dma('sp', stage[0:r, 0:C], src, writes=[stage])
                    nchunk = C // 128
                    c = 0
                    while c < nchunk:
                        n = min(nchunk - c, 512 // r)
                        b = k.bank()
                        for j in range(n):
                            k.tr(b[:, j * r:(j + 1) * r], stage[0:r, (c + j) * 128:(c + j + 1) * 128], identf[0:r, 0:r],
                                 [stage, identf], [b])
                        if r == 1:
                            k.cp(dst_ap[:, c:c + n], b[:, 0:n], [b], [dst_tile])
                        else:
                            k.cp(dst_ap[:, c:c + n, :], b[:, 0:n * r].rearrange("p (c r) -> p c r", r=r), [b], [dst_tile])
                        k.rel(b)
                        c += n

                ck('consts')
                featmajor(b_mod, 1, 6 * D, bmT, bmT)
                featmajor(norm1_w, 1, D, n1w, n1w)
                featmajor(norm2_w, 1, D, n2w, n2w)
                featmajor(gdn_conv_w, 4, 3072, cwT, cwT)
                featmajor(ffn_conv_w, 3, DFF, fcwT, fcwT)
                featmajor(ffn_conv_b, 1, DFF, fcbT, fcbT)
                featmajor(gdn_onorm_w, 1, 128, onwT, onwT)
                ck('fm')
                k.dma('sp', fnw_bc[:], final_norm_w.partition_broadcast(128), writes=[fnw_bc])
                k.dma('sp', negA[:], gdn_a_log.partition_broadcast(64), writes=[negA])
                k.dma('sp', dtb[:], gdn_dt_bias.partition_broadcast(64), writes=[dtb])
                k.dma('sp', sinks[:], swa_sinks.partition_broadcast(128), writes=[sinks])
                k.act(negA[:], negA[:], AF.Exp, [negA], [negA])
                k.ts(negA[:], negA[:], -1.0, None, ALU.mult, None, [negA], [negA])

                ck('bcast')
                k.dma('sp', c17t[:], c17, writes=[c17t])
                k.act(c17t[:], c17t[:], AF.Silu, [c17t], [c17t])
                b = k.bank()
                for kk in range(8):
                    k.tr(b[:, kk * 17:(kk + 1) * 17], c17t[:, kk * 128:(kk + 1) * 128], identf[0:17, 0:17], [c17t, identf], [b])
                k.cp(scT[:], b[:, 0:8 * 17].rearrange("p (c r) -> p c r", r=17), [b], [scT])
                k.rel(b)
                ck('silu')
                for half in range(2):
                    b = k.bank()
                    for g in range(6):
                        wt = wload(w_mod[:, (half * 6 + g) * 512:(half * 6 + g + 1) * 512], 512)
                        for j in range(4):
                            jj = g * 4 + j
                            for kk in range(8):
                                k.mm(b[:, jj * 17:(jj + 1) * 17], wt[:, kk, j * 128:(j + 1) * 128], scT[:, kk, :], [wt, scT], [b],
                                     start=(kk == 0), stop=(kk == 7))
                    k.tt(modT[:, half * 24:(half + 1) * 24, :], b[:, 0:24 * 17].rearrange("p (c r) -> p c r", r=17),
                         bc(bmT[:, half * 24:(half + 1) * 24], 2, 17), ALU.add, [b, bmT], [modT])
                    k.rel(b)
                ck('modT')
                k.barrier()
            dump("modT", modT[:], [modT])

            k.ts(a1[:], modT[:, 8:16, NS], 1.0, None, ALU.add, None, [modT], [a1])
            k.tt(a1[:], a1[:], n1w[:], ALU.mult, [a1, n1w], [a1])
            k.ts(a2[:], modT[:, 32:40, NS], 1.0, None, ALU.add, None, [modT], [a2])
            k.tt(a2[:], a2[:], n2w[:], ALU.mult, [a2, n2w], [a2])
            k.ts(A1s[:], modT[:, 8:16, 0:NS], 1.0, None, ALU.add, None, [modT], [A1s])
            k.tt(A1s[:], A1s[:], bc(n1w[:], 2, NS), ALU.mult, [A1s, n1w], [A1s])
            k.ts(A2s[:], modT[:, 32:40, 0:NS], 1.0, None, ALU.add, None, [modT], [A2s])
            k.tt(A2s[:], A2s[:], bc(n2w[:], 2, NS), ALU.mult, [A2s, n2w], [A2s])
            for (c0, gtok, gbc) in ((16, gtok1, g1bc), (40, gtok2, g2bc)):
                b0, b1 = k.bank(), k.bank()
                for j in range(8):
                    bb = b0 if j < 4 else b1
                    k.tr(bb[0:17, (j % 4) * 128:(j % 4 + 1) * 128], modT[:, c0 + j, :], identf[:], [modT, identf], [bb])
                k.cp(gtok[:, 0:512], b0[0:17, :], [b0], [gtok])
                k.cp(gtok[:, 512:1024], b1[0:17, :], [b1], [gtok])
                for hf, bb in ((0, b0), (1, b1)):
                    k.mm(bb[:, :], E16[:], gtok[:, hf * 512:(hf + 1) * 512], [E16, gtok], [bb])
                    k.cp(gbc[:, hf * 512:(hf + 1) * 512], bb[:, :], [bb], [gbc])
                k.rel(b0, b1)
            dump("g1bc", g1bc[:], [g1bc])

            ck('derived')
            S_all = k.sb("S_all", [128, 8, 128])
            halo = k.sb("halo", [128, 24, 3])
            fhalo = k.sb("fhalo", [128, NFC, 2])
            KTl = k.sb("KTl", [128, 4, 128 + TB], BF16)
            KTh = k.sb("KTh", [128, 4, 128 + TB], BF16)
            Vtok = k.sb("Vtok", [128, 5, 512], BF16)
            k.memset(S_all[:], 0.0, [S_all])
            k.memset(halo[:], 0.0, [halo])
            k.memset(fhalo[:], 0.0, [fhalo])
            k.memset(KTl[:], 0.0, [KTl])
            k.memset(KTh[:], 0.0, [KTh])
            k.memset(Vtok[:], 0.0, [Vtok])

            xres = [k.sb("xres%d" % i, [128, D]) for i in range(4)]
            x1 = xres
            xn = k.sb("xn", [128, D], BF16)
            ss1 = k.sb("ss1", [128, 1])
            rs1 = k.sb("rs1", [128, 1])
            ss2, rs2 = ss1, rs1
            hT = k.sb("hT", [128, 8, TB], BF16)
            h2T = hT
            onT = k.sb("onT", [128, 8, TB], BF16)
            obT = k.sb("obT", [128, 8, TB], BF16)
            mixT = k.sb("mixT", [128, 8, TB], BF16)
            QT = mixT
            halo_out = k.sb("halo_out", [128, 24, 3])

            k.init_arena(74240)
            cG = lambda n, shp, dt=F32: k.carve('G', n, shp, dt)
            cA = lambda n, shp, dt=F32: k.carve('A', n, shp, dt)
            cF = lambda n, shp, dt=F32: k.carve('F', n, shp, dt)
            ba = cG("ba", [64, 8, 16])
            beta = cG("beta", [64, 8, 8])
            gg = cG("gg", [64, 8, 8])
            t64a = cG("t64a", [64, 8, 8])
            t64b = cG("t64b", [64, 8, 8])
            dd = cG("dd", [64, 64])
            ed = cG("ed", [64, 64])
            ekd = cG("ekd", [64, 64])
            bed = cG("bed", [64, 64])
            elast = cG("elast", [128, 64])
            pre = cG("pre", [128, 3 + TB])
            cv = cG("cv", [128, 3, TB])
            rqk = cG("rqk", [128, 2, TB])
            Qd2 = [cG("Qd%d" % i, [128, TB], BF16) for i in range(2)]
            Qtb = cG("Qtb", [128, TB], BF16)
            Ktb = cG("Ktb", [128, TB], BF16)
            Vtb = cG("Vtb", [128, TB], BF16)
            Sb = cG("Sb", [128, 128], BF16)
            Gp2 = [cG("Gp%d" % i, [128, TB]) for i in range(2)]
            SA = cG("SA", [64, 8, 64])
            SB = cG("SB", [64, 8, 64])
            Wm = cG("Wm", [64, 8, 64])
            Zm = cG("Zm", [64, 8, 64])
            Am2 = [cG("Am%d" % i, [64, 8, 64]) for i in range(2)]
            Amb2 = [cG("Amb%d" % i, [64, 8, 64], BF16) for i in range(2)]
            NEU = BF16
            WXH = [cG("WXH%d" % i, [64, 4, 2, 64], BF16) for i in range(2)]
            ZYH = [cG("ZYH%d" % i, [64, 4, 2, 64], BF16) for i in range(2)]
            Y32 = [cG("Y32_%d" % i, [64, 4, 64]) for i in range(2)]
            Yb = [cG("Yb_%d" % i, [64, 4, 64], BF16) for i in range(2)]
            Kbd = cG("Kbd", [64, 8, 128], NEU)
            Kdec2 = [cG("Kdec%d" % i, [64, 8, 128], F32 if os.environ.get('SUPD32', '0') == '1' else BF16) for i in range(2)]
            Vb = cG("Vb", [64, 8, 128], NEU)
            osb = cG("osb", [64, 8, 128])
            uu2 = [cG("uu%d" % i, [64, 8, 128]) for i in range(2)]
            wT2 = [cG("wT%d" % i, [128, 8, 64], BF16) for i in range(2)]
            vnew = cG("vnew", [64, 128], F32 if os.environ.get('SUPD32', '0') == '1' else BF16)
            oss = cG("oss", [64, 8])
            ors = cG("ors", [64, 8])
            on1 = cG("on1", [64, 8, 128], BF16)
            Qt_ap, Kt_ap = cv[:, 0, :], cv[:, 1, :]
            Ktok = cA("Ktok", [128, 512], BF16)
            kvout = cA("kvout", [128, 512])
            sc2 = [cA("sc%d" % i, [128, 4, 256]) for i in range(2)]
            pb2 = [cA("pb%d" % i, [128, 4, 256], BF16) for i in range(2)]
            PT2 = [cA("PT%d" % i, [128, 4, 2, 128], BF16) for i in range(2)]
            mx2 = [cA("mx%d" % i, [128, 4]) for i in range(2)]
            nmx2 = [cA("nmx%d" % i, [128, 4]) for i in range(2)]
            rsum2 = [cA("rsum%d" % i, [128, 4]) for i in range(2)]
            esk2 = [cA("esk%d" % i, [128, 4]) for i in range(2)]
            actT = cF("actT", [128, NFC, TB], BF16)
            sga = cF("sga", [128, TB])
            sgb = cF("sgb", [128, TB])
            gpre = cF("gpre", [128, 2 + TB])
            gcv = cF("gcv", [128, TB])
            yt = cF("yt", [128, D])
            k.carve('FW', 'fwpad', [128, k.phase_off['F'] // 4])
            FW = [k.carve('FW', 'FW%d' % i, [128, 8, 512], BF16) for i in range(4)]
            k.phase_tiles['FW'] = k.phase_tiles['FW'][1:]
            print("arena use", k.phase_off)

            def rms_to_T(xt, xt_tile, dstT, dst_tile, acol, bcol, t):
                k.act(xn[:], xt, AF.Square, [xt_tile], [xn, ss1], accum=ss1[:])
                k.act(rs1[:], ss1[:], AF.Ln, [ss1, epsc], [rs1], bias=epsc[:], scale=1.0 / D)
                k.act(rs1[:], rs1[:], AF.Exp, [rs1], [rs1], scale=-0.5)
                k.ts(xn[:], xt, rs1[:], None, ALU.mult, None, [xt_tile, rs1], [xn])
                b = k.bank()
                bv = b[:, :].bitcast(BF16)
                for kk in range(8):
                    k.tr(bv[:, kk * 128:(kk + 1) * 128], xn[:, kk * 128:(kk + 1) * 128], identb[:], [xn, identb], [b])
                for kk in range(8):
                    k.act(dstT[:, kk, t * 128:(t + 1) * 128], bv[:, kk * 128:(kk + 1) * 128], AF.Identity,
                          [b, acol[1], bcol[1]], [dst_tile], bias=bcol[0][:, kk:kk + 1], scale=acol[0][:, kk:kk + 1])
                k.rel(b)

            hoist = {'p1': False}

            def emit_p1(t0_):
                for t in range(4):
                    k.dma('sp', xres[t][:], x_p[t0_ + t * 128:t0_ + (t + 1) * 128, :], writes=[xres[t]])
                for t in range(4):
                    rms_to_T(xres[t][:], xres[t], hT, hT, (a1, a1), (modT[:, 0:8, NS], modT), t)

            def sample_phase():
                c1 = lambda n, shp, dt=F32: k.carve('S1', n, shp, dt)
                c2 = lambda n, shp, dt=F32: k.carve('S2', n, shp, dt)
                c3 = lambda n, shp, dt=F32: k.carve('S3', n, shp, dt)
                xs = c1("xs", [NS, D])
                xs_2 = c2("xs_2", [NS, D])
                xs_3 = c3("xs_3", [NS, D])
                hTs = k.sb("hTs", [128, 8, NS], BF16)
                onTs = k.sb("onTs", [128, 8, NS], BF16)
                obTs = k.sb("obTs", [128, 8, NS], BF16)
                OH = k.sb("OH", [128, NS, NS])
                sinkcol = k.sb("sinkcol", [128, 1])
                onw_bc = k.sb("onw_bc", [NS, 128])
                hsc = k.sb("hsc", [128, 8, NS])
                k.pool_to = None
                k.memset(OH[:], 0.0, [OH])
                k.asel(OH[:], OH[:], [[1, NS], [-1, NS]], ALU.not_equal, 1.0, 0, 0, [OH], [OH])
                k.pool_to = 'dve'
                for a_ in range(8):
                    k.dma('sp', sinkcol[a_ * 16:(a_ + 1) * 16, :], swa_sinks.rearrange("(h o) -> h o", o=1), writes=[sinkcol])
                k.dma('sp', onw_bc[:], gdn_onorm_w[0].partition_broadcast(NS), writes=[onw_bc])

                def rms_T_s(src, src_tile, dst, A_, B_ap, B_tile):
                    k.act(xn[0:NS, :], src, AF.Square, [src_tile], [xn, ss1], accum=ss1[0:NS, :])
                    k.act(rs1[0:NS, :], ss1[0:NS, :], AF.Ln, [ss1, epsc], [rs1], bias=epsc[0:NS, :], scale=1.0 / D)
                    k.act(rs1[0:NS, :], rs1[0:NS, :], AF.Exp, [rs1], [rs1], scale=-0.5)
                    k.ts(xn[0:NS, :], src, rs1[0:NS, :], None, ALU.mult, None, [src_tile, rs1], [xn])
                    b = k.bank()
                    bv = b[:, :].bitcast(BF16)
                    for kk in range(8):
                        k.tr(bv[:, kk * NS:(kk + 1) * NS], xn[0:NS, kk * 128:(kk + 1) * 128], identb[0:NS, 0:NS], [xn, identb], [b])
                    pv = bv[:, 0:8 * NS].rearrange("p (c s) -> p c s", s=NS)
                    k.tt(hsc[:], pv, A_[:], ALU.mult, [b, A_], [hsc])
                    k.rel(b)
                    k.tt(dst[:], hsc[:], B_ap, ALU.add, [hsc, B_tile], [dst])

                def tok_mm(srcT, wt, c0, n, dst_ap, dst_tile, scale=None, func=None):
                    b = k.bank()
                    for kk in range(8):
                        k.mm(b[0:NS, 0:n], srcT[:, kk, :], wt[:, kk, c0:c0 + n], [srcT, wt], [b], start=(kk == 0), stop=(kk == 7))
                    if func is not None:
                        k.act(dst_ap, b[0:NS, 0:n], func, [b], [dst_tile])
                    elif scale is not None:
                        k.act(dst_ap, b[0:NS, 0:n], AF.Copy, [b], [dst_tile], scale=scale)
                    else:
                        k.cp(dst_ap, b[0:NS, 0:n], [b], [dst_tile])
                    k.rel(b)

                def to_T(src_ap, src_tile, nch, dst, dst_tile, rows=NS):
                    c = 0
                    per = 512 // rows
                    while c < nch:
                        n = min(per, nch - c)
                        b = k.bank()
                        for j in range(n):
                            k.tr(b[:, j * rows:(j + 1) * rows], src_ap[:, (c + j) * 128:(c + j + 1) * 128], identf[0:rows, 0:rows], [src_tile, identf], [b])
                        k.cp(dst[:, c:c + n, :], b[:, 0:n * rows].rearrange("p (c s) -> p c s", s=rows), [b], [dst_tile])
                        k.rel(b)
                        c += n

                def to_tok(srcT, src_tile, nch, dst_ap, dst_tile):
                    c = 0
                    while c < nch:
                        n = min(4, nch - c)
                        b = k.bank()
                        for j in range(n):
                            k.tr(b[0:NS, j * 128:(j + 1) * 128], srcT[:, c + j, :], identf[:], [src_tile, identf], [b])
                        k.cp(dst_ap[:, c * 128:(c + n) * 128], b[0:NS, 0:n * 128], [b], [dst_tile])
                        k.rel(b)
                        c += n

                qkv_p = [xres[0], xres[1], xres[2]]
                gate_s = xres[3]
                ba_s = c1("ba_s", [NS, 16])
                stc = c1("stc", [NS * 3, 3072])
                stT = c1("stT", [128, 24, NS * 3])
                newT = c1("newT", [128, 24, NS])
                cvT = c1("cvT", [128, 24, NS])
                tmpT = c1("tmpT", [128, 24, NS])
                rq_s = c1("rq_s", [128, 16, NS])
                qkp = c1("qkp", [128, 8, NS])
                beta_s = c1("beta_s", [NS, 8])
                alpha_s = c1("alpha_s", [NS, 8])
                t16a = c1("t16a", [NS, 8])
                t16b = c1("t16b", [NS, 8])
                qk_s = c1("qk_s", [NS, 8])
                v_tok = c1("v_tok", [NS, 8, 128])
                d_tok = c1("d_tok", [NS, 8, 128])
                o_tok = c1("o_tok", [NS, 8, 128])
                t_tok = c1("t_tok", [NS, 8, 128])
                oss_s = c1("oss_s", [NS, 8])
                Ss = [c1("Ss%d" % i, [128, 8, 128]) for i in range(2)]
                pK = c1("pK", [128, 8, 128])
                pQ = c1("pQ", [128, 8, 128])
                abc = c1("abc", [128, 8])
                pKb = c1("pKb", [128, 8, 128], BF16)
                pQb = c1("pQb", [128, 8, 128], BF16)
                OHb = c1("OHb", [128, NS, NS], BF16)
                k.cp(OHb[:], OH[:], [OH], [OHb])

                k.dma('sp', xs[:], x_s, writes=[xs])
                rms_T_s(xs[:], xs, hTs, A1s, modT[:, 0:8, 0:NS], modT)
                for g_ in range(6):
                    wt = wload(w_in[:, OFF_QKV + g_ * 512:OFF_QKV + (g_ + 1) * 512], 512)
                    tok_mm(hTs, wt, 0, 512, qkv_p[g_ // 2][0:NS, (g_ % 2) * 512:(g_ % 2 + 1) * 512], qkv_p[g_ // 2])
                for g_ in range(2):
                    wt = wload(w_in[:, OFF_GATE + g_ * 512:OFF_GATE + (g_ + 1) * 512], 512)
                    tok_mm(hTs, wt, 0, 512, gate_s[0:NS, g_ * 512:(g_ + 1) * 512], gate_s, func=AF.Silu)
                wt = wload(w_in[:, OFF_BETA:OFF_BETA + 16], 16)
                tok_mm(hTs, wt, 0, 16, ba_s[:], ba_s)
                k.dma('sp', stc[:], st_conv, writes=[stc])
                st3 = stc[:].rearrange("(s j) c -> s j c", j=3) if False else None
                for s_ in range(NS):
                    k.dma('sp', o_conv_s[s_, 0:2, :], stc[s_ * 3 + 1:s_ * 3 + 3, :], reads=[stc])
                for p_ in range(3):
                    k.dma('sp', o_conv_s[:, 2, p_ * 1024:(p_ + 1) * 1024], qkv_p[p_][0:NS, :], reads=[qkv_p[p_]])
                to_T(stc[:], stc, 24, stT, stT, rows=NS * 3)
                for p_ in range(3):
                    to_T(qkv_p[p_][0:NS, :], qkv_p[p_], 8, newT[:, p_ * 8:(p_ + 1) * 8, :], newT)
                st4 = stT[:].rearrange("p c (s j) -> p c s j", j=3)
                k.tt(cvT[:], newT[:], bc(cwT[:, :, 3], 2, NS), ALU.mult, [newT, cwT], [cvT])
                for j_ in range(3):
                    k.tt(tmpT[:], st4[:, :, :, j_], bc(cwT[:, :, j_], 2, NS), ALU.mult, [stT, cwT], [tmpT])
                    k.tt(cvT[:], cvT[:], tmpT[:], ALU.add, [cvT, tmpT], [cvT])
                k.act(cvT[:], cvT[:], AF.Silu, [cvT], [cvT])
                k.tt(tmpT[:, 0:16, :], cvT[:, 0:16, :], cvT[:, 0:16, :], ALU.mult, [cvT], [tmpT])
                b = k.bank()
                k.mm(b[:, 0:256], onesf[:], tmpT[:, 0:16, :].rearrange("p c s -> p (c s)"), [onesf, tmpT], [b])
                k.act(rq_s[:].rearrange("p c s -> p (c s)"), b[:, 0:256], AF.Ln, [b, epsc], [rq_s], bias=epsc[:])
                k.rel(b)
                k.act(rq_s[:], rq_s[:], AF.Exp, [rq_s], [rq_s], scale=-0.5)
                k.stt(cvT[:, 0:8, :], cvT[:, 0:8, :], 128.0 ** -0.5, rq_s[:, 0:8, :], ALU.mult, ALU.mult, [cvT, rq_s], [cvT])
                k.tt(cvT[:, 8:16, :], cvT[:, 8:16, :], rq_s[:, 8:16, :], ALU.mult, [cvT, rq_s], [cvT])
                qsT, ksT, vsT = cvT[:, 0:8, :], cvT[:, 8:16, :], cvT[:, 16:24, :]
                k.act(beta_s[:], ba_s[:, 0:8], AF.Exp, [ba_s], [beta_s], scale=-1.0)
                k.ts(beta_s[:], beta_s[:], 1.0, None, ALU.add, None, [beta_s], [beta_s])
                k.op('dve', lambda e: e.reciprocal(out=beta_s[:], in_=beta_s[:]), reads=[beta_s], writes=[beta_s])
                k.tt(t16a[:], ba_s[:, 8:16], dtb[0:NS, :], ALU.add, [ba_s, dtb], [t16a])
                k.act(t16b[:], t16a[:], AF.Abs, [t16a], [t16b])
                k.act(t16b[:], t16b[:], AF.Exp, [t16b], [t16b], scale=-1.0)
                k.act(t16b[:], t16b[:], AF.Ln, [t16b, onec], [t16b], bias=onec[0:NS, :])
                k.stt(t16a[:], t16a[:], 0.0, t16b[:], ALU.max, ALU.add, [t16a, t16b], [t16a])
                k.tt(t16a[:], t16a[:], negA[0:NS, :], ALU.mult, [t16a, negA], [t16a])
                k.act(alpha_s[:], t16a[:], AF.Exp, [t16a], [alpha_s])
                to_tok(cvT[:, 16:24, :], cvT, 8, v_tok[:].rearrange("s h d -> s (h d)"), v_tok)
                k.tt(qkp[:], qsT, ksT, ALU.mult, [cvT], [qkp])
                b = k.bank()
                for h in range(8):
                    k.mm(b[0:NS, h:h + 1], qkp[:, h, :], onesf[:, 0:1], [qkp, onesf], [b])
                k.cp(qk_s[:], b[0:NS, 0:8], [b], [qk_s])
                k.rel(b)
                bks = [k.bank() for _ in range(4)]
                for s_ in range(NS):
                    S_ = Ss[s_ % 2]
                    k.dma('sp', S_[:], st_S[s_], writes=[S_])
                    k.tt(pKb[:], S_[:], bc(cvT[:, 8:16, s_], 2, 128), ALU.mult, [S_, cvT], [pKb], eng='pool')
                    k.tt(pQb[:], S_[:], bc(cvT[:, 0:8, s_], 2, 128), ALU.mult, [S_, cvT], [pQb])
                    for hf in range(2):
                        k.mm(bks[hf][0:NS, :], OHb[:, s_, :], pKb[:, hf * 4:(hf + 1) * 4, :].rearrange("p h d -> p (h d)"), [OHb, pKb], [bks[hf]],
                             start=(s_ == 0), stop=(s_ == NS - 1))
                        k.mm(bks[2 + hf][0:NS, :], OHb[:, s_, :], pQb[:, hf * 4:(hf + 1) * 4, :].rearrange("p h d -> p (h d)"), [OHb, pQb], [bks[2 + hf]],
                             start=(s_ == 0), stop=(s_ == NS - 1))
                for hf in range(2):
                    hs = slice(hf * 4, hf * 4 + 4)
                    kS = bks[hf][0:NS, :].rearrange("s (h d) -> s h d", d=128)
                    qS = bks[2 + hf][0:NS, :].rearrange("s (h d) -> s h d", d=128)
                    k.tt(t_tok[:, hs, :], kS, bc(alpha_s[:, hs], 2, 128), ALU.mult, [bks[hf], alpha_s], [t_tok])
                    k.tt(t_tok[:, hs, :], v_tok[:, hs, :], t_tok[:, hs, :], ALU.subtract, [v_tok, t_tok], [t_tok])
                    k.tt(d_tok[:, hs, :], t_tok[:, hs, :], bc(beta_s[:, hs], 2, 128), ALU.mult, [t_tok, beta_s], [d_tok])
                    k.tt(o_tok[:, hs, :], qS, bc(alpha_s[:, hs], 2, 128), ALU.mult, [bks[2 + hf], alpha_s], [o_tok])
                    k.tt(t_tok[:, hs, :], d_tok[:, hs, :], bc(qk_s[:, hs], 2, 128), ALU.mult, [d_tok, qk_s], [t_tok])
                    k.tt(o_tok[:, hs, :], o_tok[:, hs, :], t_tok[:, hs, :], ALU.add, [o_tok, t_tok], [o_tok])
                k.rel(*bks)
                k.tt(t_tok[:], o_tok[:], o_tok[:], ALU.mult, [o_tok], [t_tok])
                k.op('dve', lambda e: e.tensor_reduce(out=oss_s[:], in_=t_tok[:], axis=AX.X, op=ALU.add), reads=[t_tok], writes=[oss_s])
                k.act(oss_s[:], oss_s[:], AF.Ln, [oss_s, epsc], [oss_s], bias=epsc[0:NS, :], scale=1.0 / 128)
                k.act(oss_s[:], oss_s[:], AF.Exp, [oss_s], [oss_s], scale=-0.5)
                k.tt(o_tok[:], o_tok[:], bc(oss_s[:], 2, 128), ALU.mult, [o_tok, oss_s], [o_tok])
                k.tt(o_tok[:], o_tok[:], bc(onw_bc[:], 1, 8), ALU.mult, [o_tok, onw_bc], [o_tok])
                k.tt(o_tok[:], o_tok[:], gate_s[0:NS, :].rearrange("s (h d) -> s h d", d=128), ALU.mult, [o_tok, gate_s], [o_tok])
                to_T(o_tok[:].rearrange("s h d -> s (h d)"), o_tok, 8, newT[:, 0:8, :], newT)
                k.cp(onTs[:], newT[:, 0:8, :], [newT], [onTs])
                k.dma('sp', Ss[0][:], st_S[0], writes=[Ss[0]])
                for s_ in range(NS):
                    S_ = Ss[s_ % 2]
                    if s_ + 1 < NS:
                        k.dma('sp', Ss[(s_ + 1) % 2][:], st_S[s_ + 1], writes=[Ss[(s_ + 1) % 2]])
                    b0, b1, b2 = k.bank(), k.bank(), k.bank()
                    k.mm(b2[:, 0:8], Esel[:, s_, :], alpha_s[:], [Esel, alpha_s], [b2])
                    k.cp(abc[:], b2[:, 0:8], [b2], [abc])
                    k.mm(b0[:, :], Esel[:, s_, :], d_tok[:, 0:4, :].rearrange("s h d -> s (h d)"), [Esel, d_tok], [b0])
                    k.mm(b1[:, :], Esel[:, s_, :], d_tok[:, 4:8, :].rearrange("s h d -> s (h d)"), [Esel, d_tok], [b1])
                    k.tt(pK[:], S_[:], bc(abc[:], 2, 128), ALU.mult, [S_, abc], [pK], eng='pool')
                    k.tt(pQ[:, 0:4, :], b0[:, :].rearrange("p (h d) -> p h d", d=128), bc(cvT[:, 8:12, s_], 2, 128), ALU.mult, [b0, cvT], [pQ])
                    k.tt(pQ[:, 4:8, :], b1[:, :].rearrange("p (h d) -> p h d", d=128), bc(cvT[:, 12:16, s_], 2, 128), ALU.mult, [b1, cvT], [pQ])
                    k.rel(b0, b1, b2)
                    k.tt(S_[:], pK[:], pQ[:], ALU.add, [pK, pQ], [S_], eng='pool')
                    k.dma('sp', o_S_s[s_], S_[:], reads=[S_])

                q_s = c2("q_s", [NS, 1024])
                kv_s = c2("kv_s", [NS, 512])
                KCs = [c2("KC%d" % i, [128, NS // 2, 256]) for i in range(2)]
                VCs = [c2("VC%d" % i, [128, NS // 2, 256]) for i in range(2)]
                prd = c2("prd", [128, 16, 64])
                scT = c2("scT", [128, NS, 16])
                Pm = c2("Pm", [128, 2, 128])
                PTa = c2("PTa", [128, 2, 128])
                mx_s = c2("mx_s", [128, 2])
                nmx_s = c2("nmx_s", [128, 2])
                rs_s = c2("rs_s", [128, 2])
                es_s = c2("es_s", [128, 2])
                ob_tok = c2("ob_tok", [NS, 1024])
                obTf = c2("obTf", [128, 8, NS])
                W2x = []
                for i_ in range(3):
                    try:
                        W2x.append(c2("W2x%d" % i_, [128, 8, 512], BF16))
                    except AssertionError:
                        break
                k.switch('S1', 'S2')
                wlist['cur'] = W8 + W2x
                for g_ in range(2):
                    wt = wload(w_in[:, OFF_SQ + g_ * 512:OFF_SQ + (g_ + 1) * 512], 512)
                    tok_mm(hTs, wt, 0, 512, q_s[:, g_ * 512:(g_ + 1) * 512], q_s, scale=0.125)
                wt = wload(w_in[:, OFF_SK:OFF_SK + 512], 512)
                tok_mm(hTs, wt, 0, 512, kv_s[:], kv_s)
                for i_, q_ in ((0, 'sp'), (1, 'act')):
                    k.dma(q_, KCs[i_][0:127, :, :], st_k[1:128, i_ * 8:(i_ + 1) * 8, :], writes=[KCs[i_]])
                for i_ in range(2):
                    k.dma('pool', VCs[i_][0:127, :, :], st_v[1:128, i_ * 8:(i_ + 1) * 8, :], writes=[VCs[i_]])
                for s_ in range(NS):
                    KC_, VC_ = KCs[s_ // 8], VCs[s_ // 8]
                    k.dma('sp' if s_ < 8 else 'act', KC_[127:128, s_ % 8, :], kv_s[s_:s_ + 1, 0:256], reads=[kv_s], writes=[KC_], indep=True)
                    k.dma('pool', VC_[127:128, s_ % 8, :], kv_s[s_:s_ + 1, 256:512], reads=[kv_s], writes=[VC_], indep=True)
                for i_, q_ in ((0, 'sp'), (1, 'act')):
                    k.dma(q_, o_k_s[:, i_ * 8:(i_ + 1) * 8, :], KCs[i_][:], reads=[KCs[i_]])
                for i_ in range(2):
                    k.dma('pool', o_v_s[:, i_ * 8:(i_ + 1) * 8, :], VCs[i_][:], reads=[VCs[i_]])
                for s_ in range(NS):
                    b0, b1 = k.bank(), k.bank()
                    k.mm(b0[:, :], Esel[:, s_, :], q_s[:, 0:512], [Esel, q_s], [b0])
                    k.mm(b1[:, :], Esel[:, s_, :], q_s[:, 512:1024], [Esel, q_s], [b1])
                    for hf, bb in ((0, b0), (1, b1)):
                        KC = KCs[s_ // 8]
                        kc = KC[:, s_ % 8, hf * 128:(hf + 1) * 128].rearrange("p (g d) -> p g d", d=64)
                        k.tt(prd[:, hf * 8:(hf + 1) * 8, :].rearrange("p (g i) d -> p g i d", i=4),
                             bb[:, :].rearrange("p (g i d) -> p g i d", i=4, d=64), bc(kc, 2, 4), ALU.mult, [bb, KC], [prd])
                    k.rel(b0, b1)
                    k.op('dve', lambda e: e.tensor_reduce(out=scT[:, s_, :], in_=prd[:], axis=AX.X, op=ALU.add), reads=[prd], writes=[scT])
                b = k.bank()
                for a_ in range(2):
                    k.tr(b[:, a_ * 128:(a_ + 1) * 128], scT[:, a_ * 8:(a_ + 1) * 8, :].rearrange("p s h -> p (s h)"), identf[:], [scT, identf], [b])
                k.op('dve', lambda e: e.tensor_reduce(out=mx_s[:], in_=b[:, 0:256].rearrange("p (a q) -> p a q", q=128), axis=AX.X, op=ALU.max),
                     reads=[b], writes=[mx_s])
                k.ts(mx_s[:], mx_s[:], sinkcol[:, 0:1], None, ALU.max, None, [mx_s, sinkcol], [mx_s])
                k.ts(nmx_s[:], mx_s[:], -1.0, None, ALU.mult, None, [mx_s], [nmx_s])
                for a_ in range(2):
                    k.act(Pm[:, a_, :], b[:, a_ * 128:(a_ + 1) * 128], AF.Exp, [b, nmx_s], [Pm, rs_s], bias=nmx_s[:, a_:a_ + 1], accum=rs_s[:, a_:a_ + 1])
                k.rel(b)
                k.act(es_s[:], nmx_s[:], AF.Exp, [nmx_s, sinkcol], [es_s], bias=sinkcol[:, 0:1])
                k.tt(rs_s[:], rs_s[:], es_s[:], ALU.add, [rs_s, es_s], [rs_s])
                k.op('dve', lambda e: e.reciprocal(out=rs_s[:], in_=rs_s[:]), reads=[rs_s], writes=[rs_s])
                k.tt(Pm[:], Pm[:], bc(rs_s[:], 2, 128), ALU.mult, [Pm, rs_s], [Pm])
                b = k.bank()
                for a_ in range(2):
                    k.tr(b[:, a_ * 128:(a_ + 1) * 128], Pm[:, a_, :], identf[:], [Pm, identf], [b])
                k.cp(PTa[:].rearrange("p a q -> p (a q)"), b[:, 0:256], [b], [PTa])
                k.rel(b)
                PT3 = PTa[:].rearrange("p a (s h) -> p (a s) h", h=16)
                b0, b1 = k.bank(), k.bank()
                for s_ in range(NS):
                    for hf in range(2):
                        VC = VCs[s_ // 8]
                        vc = VC[:, s_ % 8, hf * 128:(hf + 1) * 128].rearrange("p (g d) -> p g d", d=64)
                        pt_ = PT3[:, s_, hf * 8:(hf + 1) * 8].rearrange("p (g i) -> p g i", i=4)
                        k.tt(prd[:, hf * 8:(hf + 1) * 8, :].rearrange("p (g i) d -> p g i d", i=4), bc(vc, 2, 4), bc(pt_, 3, 64), ALU.mult,
                             [VC, PTa], [prd], eng='pool')
                    k.mm(b0[0:NS, :], OH[:, s_, :], prd[:, 0:8, :].rearrange("p h d -> p (h d)"), [OH, prd], [b0], start=(s_ == 0), stop=(s_ == NS - 1))
                    k.mm(b1[0:NS, :], OH[:, s_, :], prd[:, 8:16, :].rearrange("p h d -> p (h d)"), [OH, prd], [b1], start=(s_ == 0), stop=(s_ == NS - 1))
                k.cp(ob_tok[:, 0:512], b0[0:NS, :], [b0], [ob_tok])
                k.cp(ob_tok[:, 512:1024], b1[0:NS, :], [b1], [ob_tok])
                k.rel(b0, b1)
                to_T(ob_tok[:], ob_tok, 8, obTf, obTf)
                k.cp(obTs[:], obTf[:], [obTf], [obTs])

                if nblk > 0:
                    emit_p1(0)
                    hoist['p1'] = True
                gab = c3("gab", [NS, 2048])
                yab = c3("yab", [NS, 2048])
                mix_s = c3("mix_s", [NS, 1024])
                mixTs = c3("mixTs", [128, 8, NS], BF16)
                mixTf = c3("mixTf", [128, 8, NS])
                h2Ts = c3("h2Ts", [128, 8, NS], BF16)
                gtok = c3("gtok", [NS, DFF])
                stf = c3("stf", [NS * 2, DFF])
                stfT = c3("stfT", [128, NFC, NS * 2])
                gT = c3("gT", [128, NFC, NS])
                uT = c3("uT", [128, NFC, NS])
                tT = c3("tT", [128, NFC, NS])
                aTs = c3("aTs", [128, NFC, NS], BF16)
                ys = c3("ys", [NS, 1024])
                W3x = []
                for i_ in range(3):
                    try:
                        W3x.append(c3("W3x%d" % i_, [128, 8, 512], BF16))
                    except AssertionError:
                        break
                k.switch('S2', 'S3')
                wlist['cur'] = W8 + W3x
                for g_ in range(4):
                    wt = wload(w_in[:, OFF_GA + g_ * 512:OFF_GA + (g_ + 1) * 512], 512)
                    tok_mm(hTs, wt, 0, 512, gab[:, g_ * 512:(g_ + 1) * 512], gab, func=AF.Sigmoid)
                for g_ in range(2):
                    wt = wload(w_gdn_out[:, g_ * 512:(g_ + 1) * 512], 512)
                    tok_mm(onTs, wt, 0, 512, yab[:, g_ * 512:(g_ + 1) * 512], yab)
                for g_ in range(2):
                    wt = wload(w_swa_out[:, g_ * 512:(g_ + 1) * 512], 512)
                    tok_mm(obTs, wt, 0, 512, yab[:, 1024 + g_ * 512:1024 + (g_ + 1) * 512], yab)
                k.tt(yab[:], yab[:], gab[:], ALU.mult, [yab, gab], [yab])
                k.tt(mix_s[:], yab[:, 0:1024], yab[:, 1024:2048], ALU.add, [yab], [mix_s])
                to_T(mix_s[:], mix_s, 8, mixTf, mixTf)
                k.cp(mixTs[:], mixTf[:], [mixTf], [mixTs])
                for g_ in range(2):
                    wt = wload(w_o[:, g_ * 512:(g_ + 1) * 512], 512)
                    tok_mm(mixTs, wt, 0, 512, mix_s[:, g_ * 512:(g_ + 1) * 512], mix_s)
                k.tt(mix_s[:], mix_s[:], gtok1[0:NS, :], ALU.mult, [mix_s, gtok1], [mix_s])
                k.tt(xs_3[:], xs_3[:], mix_s[:], ALU.add, [xs_3, mix_s], [xs_3])
                rms_T_s(xs_3[:], xs_3, h2Ts, A2s, modT[:, 24:32, 0:NS], modT)
                k.dma('sp', stf[:], st_ffn, writes=[stf])
                for s_ in range(NS):
                    k.dma('sp', o_ffn_s[s_, 0:1, :], stf[s_ * 2 + 1:s_ * 2 + 2, :], reads=[stf])
                to_T(stf[:], stf, NFC, stfT, stfT, rows=NS * 2)
                for (wsrc, dstT, is_gate) in ((w_ffn_gate, gT, True), (w_ffn_up, uT, False)):
                    for g_ in range(6):
                        n = 512 if g_ < 5 else DFF - 5 * 512
                        wt = wload(wsrc[:, g_ * 512:g_ * 512 + n], n)
                        if is_gate:
                            tok_mm(h2Ts, wt, 0, n, gtok[:, g_ * 512:g_ * 512 + n], gtok)
                        b = k.bank()
                        for j in range(n // 128):
                            for kk in range(8):
                                k.mm(b[:, j * NS:(j + 1) * NS], wt[:, kk, j * 128:(j + 1) * 128], h2Ts[:, kk, :], [wt, h2Ts], [b], start=(kk == 0), stop=(kk == 7))
                        k.cp(dstT[:, g_ * 4:g_ * 4 + n // 128, :], b[:, 0:(n // 128) * NS].rearrange("p (c s) -> p c s", s=NS), [b], [dstT])
                        k.rel(b)
                k.dma('sp', o_ffn_s[:, 1, :], gtok[:], reads=[gtok])
                sf4 = stfT[:].rearrange("p c (s j) -> p c s j", j=2)
                k.tt(tT[:], gT[:], bc(fcwT[:, :, 2], 2, NS), ALU.mult, [gT, fcwT], [tT])
                for j_ in range(2):
                    k.tt(gT[:], sf4[:, :, :, j_], bc(fcwT[:, :, j_], 2, NS), ALU.mult, [stfT, fcwT], [gT])
                    k.tt(tT[:], tT[:], gT[:], ALU.add, [tT, gT], [tT])
                k.tt(tT[:], tT[:], bc(fcbT[:], 2, NS), ALU.add, [tT, fcbT], [tT])
                k.act(tT[:], tT[:], AF.Silu, [tT], [tT])
                k.tt(aTs[:], tT[:], uT[:], ALU.mult, [tT, uT], [aTs])
                for hf in range(2):
                    b = k.bank()
                    for kg in range(3):
                        nk = 8 if kg < 2 else NFC - 16
                        wt = nextw()
                        k.dma('pool', wt[:, 0:nk, :], w_ffn_down[kg * 1024:kg * 1024 + nk * 128, hf * 512:(hf + 1) * 512].rearrange("(c p) n -> p c n", p=128),
                              writes=[wt])
                        for kk in range(nk):
                            kf = kg * 8 + kk
                            k.mm(b[0:NS, :], aTs[:, kf, :], wt[:, kk, :], [aTs, wt], [b], start=(kf == 0), stop=(kf == NFC - 1))
                    k.tt(mix_s[:, hf * 512:(hf + 1) * 512], b[0:NS, :], gtok2[0:NS, hf * 512:(hf + 1) * 512], ALU.mult, [b, gtok2], [mix_s])
                    k.rel(b)
                k.tt(xs_3[:], xs_3[:], mix_s[:], ALU.add, [xs_3, mix_s], [xs_3])
                k.act(ys[:], xs_3[:], AF.Square, [xs_3], [ys, ss1], accum=ss1[0:NS, :])
                k.act(rs1[0:NS, :], ss1[0:NS, :], AF.Ln, [ss1, epsc], [rs1], bias=epsc[0:NS, :], scale=1.0 / D)
                k.act(rs1[0:NS, :], rs1[0:NS, :], AF.Exp, [rs1], [rs1], scale=-0.5)
                k.stt(ys[:], xs_3[:], rs1[0:NS, :], fnw_bc[0:NS, :], ALU.mult, ALU.mult, [xs_3, rs1, fnw_bc], [ys])
                k.dma('sp', y_s, ys[:], reads=[ys])
                k.switch('S3', 'G')
                wlist['cur'] = W8

            if do_sample:
                sample_phase()

            for blk in range(nblk):
                t0 = blk * TB
                last = (blk == NBLK - 1)
                if not (blk == 0 and hoist['p1']):
                    emit_p1(t0)
                if blk == 0:
                    dump("hT", hT[:], [hT])
                if blk > 0:
                    k.switch('F', 'G')
                    k.switch('FW', 'G')
                wlist['cur'] = W8

                ck('p1')
                wt = wload(w_in[:, OFF_BETA:OFF_BETA + 16], 16)
                b = k.bank()
                for c in range(8):
                    for kk in range(8):
                        k.mm(b[0:64, c * 16:(c + 1) * 16], hT[:, kk, c * 64:(c + 1) * 64], wt[:, kk, 0:16], [hT, wt], [b],
                             start=(kk == 0), stop=(kk == 7))
                k.cp(ba[:], b[0:64, 0:128].rearrange("p (c r) -> p c r", r=16), [b], [ba])
                k.rel(b)
                k.act(beta[:], ba[:, :, 0:8], AF.Exp, [ba], [beta], scale=-1.0)
                k.ts(beta[:], beta[:], 1.0, None, ALU.add, None, [beta], [beta])
                k.op('dve', lambda e: e.reciprocal(out=beta[:], in_=beta[:]), reads=[beta], writes=[beta])
                k.tt(t64a[:], ba[:, :, 8:16], bc(dtb[:], 1, 8), ALU.add, [ba, dtb], [t64a])
                k.act(t64b[:], t64a[:], AF.Abs, [t64a], [t64b])
                k.act(t64b[:], t64b[:], AF.Exp, [t64b], [t64b], scale=-1.0)
                k.act(t64b[:], t64b[:], AF.Ln, [t64b, onec], [t64b], bias=onec[0:64, :])
                k.stt(t64a[:], t64a[:], 0.0, t64b[:], ALU.max, ALU.add, [t64a, t64b], [t64a])
                k.tt(gg[:], t64a[:], bc(negA[:], 1, 8), ALU.mult, [t64a, negA], [gg])
                ggf = gg[:].rearrange("p c h -> p (c h)")
                b = k.bank()
                k.mm(b[0:64, 0:64], Um[:], ggf, [Um, gg], [b])
                k.mm(b[:, 64:128], onesf[0:64, :], ggf, [onesf, gg], [b])
                k.cp(dd[:], b[0:64, 0:64], [b], [dd])
                k.act(ed[:], dd[:], AF.Exp, [dd], [ed])
                k.tt(ekd[:], b[0:64, 64:128], dd[:], ALU.subtract, [b, dd], [ekd])
                k.act(ekd[:], ekd[:], AF.Exp, [ekd], [ekd])
                k.act(elast[:], b[:, 64:128], AF.Exp, [b], [elast])
                k.rel(b)
                k.tt(bed[:], ed[:], beta[:].rearrange("p c h -> p (c h)"), ALU.mult, [ed, beta], [bed])
                if blk == 0:
                    dump("gg", gg[:], [gg])
                    dump("beta", beta[:], [beta])

                ck('p2')
                def gdn_front(h, hb):
                    wT, uu, Qd, Gp, Kdec, Am, Amb = wT2[hb], uu2[hb], Qd2[hb], Gp2[hb], Kdec2[hb], Am2[hb], Amb2[hb]
                    wt = nextw()
                    for part in range(3):
                        wload(w_in[:, OFF_QKV + part * 1024 + h * 128:OFF_QKV + part * 1024 + (h + 1) * 128], 128, c0=part * 128, tile=wt)
                    wload(w_in[:, OFF_GATE + h * 128:OFF_GATE + (h + 1) * 128], 128, c0=384, tile=wt)
                    for part in range(3):
                        b = k.bank()
                        for kk in range(8):
                            k.mm(b[:, :], wt[:, kk, part * 128:(part + 1) * 128], hT[:, kk, :], [wt, hT], [b], start=(kk == 0), stop=(kk == 7))
                        j = part * 8 + h
                        k.cp(pre[:, 0:3], halo[:, j, :], [halo], [pre], eng='pool')
                        k.cp(pre[:, 3:3 + TB], b[:, :], [b], [pre], eng='act')
                        k.rel(b)
                        yield
                        k.cp(halo[:, j, :], pre[:, TB:TB + 3], [pre], [halo], eng='pool')
                        k.ts(cv[:, part, :], pre[:, 0:TB], cwT[:, j, 0:1], None, ALU.mult, None, [pre, cwT], [cv])
                        for tap in range(1, 4):
                            k.stt(cv[:, part, :], pre[:, tap:tap + TB], cwT[:, j, tap:tap + 1], cv[:, part, :], ALU.mult, ALU.add,
                                  [pre, cwT, cv], [cv])
                    b = k.bank()
                    for kk in range(8):
                        k.mm(b[:, :], wt[:, kk, 384:512], hT[:, kk, :], [wt, hT], [b], start=(kk == 0), stop=(kk == 7))
                    k.act(Gp[:], b[:, :], AF.Silu, [b], [Gp])
                    k.rel(b)
                    yield
                    k.act(cv[:], cv[:], AF.Silu, [cv], [cv])
                    yield
                    for qk in range(2):
                        prebf = pre[:, 0:TB // 2].bitcast(BF16)
                        k.tt(prebf, cv[:, qk, :], cv[:, qk, :], ALU.mult, [cv], [pre], eng='pool')
                        b = k.bank()
                        k.mm(b[:, :], onesb[:], prebf, [onesb, pre], [b])
                        k.act(rqk[:, qk, :], b[:, :], AF.Ln, [b, epsc], [rqk], bias=epsc[:])
                        k.rel(b)
                        yield
                    k.act(rqk[:], rqk[:], AF.Exp, [rqk], [rqk], scale=-0.5)
                    k.stt(Qt_ap, cv[:, 0, :], 128.0 ** -0.5, rqk[:, 0, :], ALU.mult, ALU.mult, [cv, rqk], [cv])
                    k.tt(Kt_ap, cv[:, 1, :], rqk[:, 1, :], ALU.mult, [cv, rqk], [cv])
                    yield
                    k.cp(Qtb[:], Qt_ap, [cv], [Qtb], eng='pool')
                    k.cp(Ktb[:], Kt_ap, [cv], [Ktb], eng='pool')
                    k.cp(Vtb[:], cv[:, 2, :], [cv], [Vtb], eng='pool')
                    if blk == 0 and h == 0:
                        dump("Qt", Qt_ap, [cv])
                        dump("Kt", Kt_ap, [cv])
                        dump("Vt", cv[:, 2, :], [cv])
                    ck('gdn_a')
                    gh = gg[:, :, h]
                    k.tt(SA[:], bc(gh, 2, 64), bc(SLm[:], 1, 8), ALU.mult, [gg, SLm], [SA], eng='pool')
                    k.tt(SB[:], bc(gh, 2, 64), bc(Um[:], 1, 8), ALU.mult, [gg, Um], [SB], eng='pool')
                    yield
                    b = k.bank()
                    k.mm(b[0:64, :], Um[:], SA[:].rearrange("p c j -> p (c j)"), [Um, SA], [b])
                    k.act(Wm[:].rearrange("p c j -> p (c j)"), b[0:64, :], AF.Exp, [b], [Wm])
                    k.rel(b)
                    yield
                    k.tt(Wm[:], Wm[:], bc(nSL[:], 1, 8), ALU.mult, [Wm, nSL], [Wm])
                    k.tt(Wm[:], Wm[:], bc(beta[:, :, h], 2, 64), ALU.mult, [Wm, beta], [Wm])
                    yield
                    k.tt(SA[:], bc(beta[:, :, h], 2, 64), bc(identf[0:64, 0:64], 1, 8), ALU.mult, [beta, identf], [SA], eng='pool')
                    b = k.bank()
                    k.mm(b[0:64, :], SLm[:], SB[:].rearrange("p c j -> p (c j)"), [SLm, SB], [b])
                    k.act(Zm[:].rearrange("p c j -> p (c j)"), b[0:64, :], AF.Exp, [b], [Zm])
                    k.rel(b)
                    yield
                    k.tt(Am[:], Zm[:], bc(Um[:], 1, 8), ALU.mult, [Zm, Um], [Am])
                    k.tt(Zm[:], Zm[:], bc(nSU[:], 1, 8), ALU.mult, [Zm, nSU], [Zm])
                    yield
                    b = k.bank()
                    k.mm(b[0:64, :], onesf[0:64, 0:64], SA[:].rearrange("p c j -> p (c j)"), [onesf, SA], [b])
                    k.tt(Zm[:].rearrange("p c j -> p (c j)"), Zm[:].rearrange("p c j -> p (c j)"), b[0:64, :], ALU.mult, [Zm, b], [Zm])
                    k.rel(b)
                    yield
                    k.cp(Qd[:], Qt_ap, [cv], [Qd])
                    ck('gdn_b')
                    bK0, bV0 = k.bank(), k.bank()
                    bkv = bK0[:, :].bitcast(BF16)
                    bvv = bV0[:, :].bitcast(BF16)
                    for c in range(8):
                        k.tr(bkv[0:64, c * 128:(c + 1) * 128], Ktb[:, c * 64:(c + 1) * 64], identb[:], [Ktb, identb], [bK0])
                        k.tr(bvv[0:64, c * 128:(c + 1) * 128], Vtb[:, c * 64:(c + 1) * 64], identb[:], [Vtb, identb], [bV0])
                    kin = bkv[0:64, :].rearrange("p (c d) -> p c d", d=128)
                    vin = bvv[0:64, :].rearrange("p (c d) -> p c d", d=128)
                    k.tt(Kbd[:], kin, bc(bed[:].rearrange("p (c h) -> p c h", h=8)[:, :, h], 2, 128), ALU.mult, [bK0, bed], [Kbd])
                    k.tt(Kdec[:], kin, bc(ekd[:].rearrange("p (c h) -> p c h", h=8)[:, :, h], 2, 128), ALU.mult, [bK0, ekd], [Kdec])
                    k.tt(Vb[:], vin, bc(beta[:, :, h], 2, 128), ALU.mult, [bV0, beta], [Vb])
                    k.rel(bK0, bV0)
                    yield
                    bA, bB = k.bank(), k.bank()
                    for c in range(8):
                        k.mm(bA[0:64, c * 64:(c + 1) * 64], Ktb[:, c * 64:(c + 1) * 64], Ktb[:, c * 64:(c + 1) * 64], [Ktb], [bA])
                        k.mm(bB[0:64, c * 64:(c + 1) * 64], Ktb[:, c * 64:(c + 1) * 64], Qtb[:, c * 64:(c + 1) * 64], [Ktb, Qtb], [bB])
                    A3 = bA[0:64, :].rearrange("p (c j) -> p c j", j=64)
                    B3 = bB[0:64, :].rearrange("p (c j) -> p c j", j=64)
                    k.tt(Wm[:], A3, Wm[:], ALU.mult, [bA, Wm], [Wm])
                    k.tt(Zm[:], A3, Zm[:], ALU.mult, [bA, Zm], [Zm])
                    Amb_ = Am if os.environ.get('SUPD32', '0') == '1' else Amb
                    k.tt(Amb_[:], B3, Am[:], ALU.mult, [bB, Am], [Amb_])
                    k.rel(bA, bB)
                    yield
                    I4 = bc(identf[0:64, 0:64], 1, 4)
                    for hf in range(2):
                        cs = slice(hf * 4, hf * 4 + 4)
                        k.cp(ZYH[hf][:, :, 0, :], Zm[:, cs, :], [Zm], [ZYH[hf]], eng='act')
                        k.tt(ZYH[hf][:, :, 1, :], Zm[:, cs, :], I4, ALU.add, [Zm, identf], [ZYH[hf]])
                        k.cp(WXH[hf][:, :, 0, :], Wm[:, cs, :], [Wm], [WXH[hf]], eng='act')
                        k.tt(WXH[hf][:, :, 1, :], Wm[:, cs, :], I4, ALU.add, [Wm, identf], [WXH[hf]])
                    ck('gdn_c')
                    for lev in range(6):
                        for hf in range(2):
                            zy, wx = ZYH[hf], WXH[hf]
                            bZ, bW = k.bank(), k.bank()
                            for cc in range(4):
                                if lev == 0:
                                    k.mm(bZ[0:64, cc * 128:cc * 128 + 64], wx[:, cc, 0, :], zy[:, cc, 0, :], [wx, zy], [bZ])
                                    k.mm(bW[0:64, cc * 128:cc * 128 + 64], zy[:, cc, 0, :], wx[:, cc, 0, :], [wx, zy], [bW])
                                elif lev < 5:
                                    k.mm(bZ[0:64, cc * 128:(cc + 1) * 128], wx[:, cc, 0, :], zy[:, cc, :, :].rearrange("p a j -> p (a j)"), [wx, zy], [bZ])
                                    k.mm(bW[0:64, cc * 128:(cc + 1) * 128], zy[:, cc, 0, :], wx[:, cc, :, :].rearrange("p a j -> p (a j)"), [wx, zy], [bW])
                                else:
                                    k.mm(bZ[0:64, cc * 128 + 64:(cc + 1) * 128], wx[:, cc, 0, :], zy[:, cc, 1, :], [wx, zy], [bZ])
                                    k.mm(bW[0:64, cc * 128 + 64:(cc + 1) * 128], zy[:, cc, 0, :], wx[:, cc, 1, :], [wx, zy], [bW])
                            cs = slice(hf * 4, hf * 4 + 4)
                            Z4 = bZ[0:64, :].rearrange("p (c a j) -> p c a j", a=2, j=64)
                            W4 = bW[0:64, :].rearrange("p (c a j) -> p c a j", a=2, j=64)
                            if lev == 0:
                                k.cp(zy[:, :, 0, :], Z4[:, :, 0, :], [bZ], [zy], eng='act')
                                k.cp(wx[:, :, 0, :], W4[:, :, 0, :], [bW], [wx], eng='act')
                            elif lev < 5:
                                k.tt(zy[:, :, 1, :], Z4[:, :, 1, :], zy[:, :, 1, :], ALU.add, [bZ, zy], [zy])
                                k.cp(zy[:, :, 0, :], Z4[:, :, 0, :], [bZ], [zy], eng='act')
                                k.tt(wx[:, :, 1, :], W4[:, :, 1, :], wx[:, :, 1, :], ALU.add, [bW, wx], [wx])
                                k.cp(wx[:, :, 0, :], W4[:, :, 0, :], [bW], [wx], eng='act')
                            else:
                                k.tt(Y32[hf][:], Z4[:, :, 1, :], zy[:, :, 1, :], ALU.add, [bZ, zy], [Y32[hf]])
                                k.tt(SB[:, cs, :], W4[:, :, 1, :], wx[:, :, 1, :], ALU.add, [bW, wx], [SB])
                            k.rel(bZ, bW)
                            yield
                    for hf in range(2):
                        cs = slice(hf * 4, hf * 4 + 4)
                        bR = k.bank()
                        for cc in range(4):
                            k.mm(bR[0:64, cc * 64:(cc + 1) * 64], Wm[:, hf * 4 + cc, :], Y32[hf][:, cc, :], [Wm, Y32[hf]], [bR])
                        R3 = bR[0:64, 0:256].rearrange("p (c j) -> p c j", j=64)
                        k.tt(SA[:, cs, :], R3, Y32[hf][:], ALU.subtract, [bR, Y32[hf]], [SA])
                        k.rel(bR)
                        k.tt(SA[:, cs, :], SA[:, cs, :], I4, ALU.add, [SA, identf], [SA])
                        bF = k.bank()
                        for cc in range(4):
                            k.mm(bF[0:64, cc * 64:(cc + 1) * 64], SB[:, hf * 4 + cc, :], SA[:, hf * 4 + cc, :], [SB, SA], [bF])
                        k.tt(Yb[hf][:], bF[0:64, 0:256].rearrange("p (c j) -> p c j", j=64), Y32[hf][:], ALU.add, [bF, Y32[hf]], [Yb[hf]])
                        k.rel(bF)
                        yield
                    if blk == 0 and h == 0:
                        dump("Yf", Yb[0][:], [Yb[0]])
                    bU0, bU1, bWt = k.bank(), k.bank(), k.bank()
                    for c in range(8):
                        bu_ = bU0 if c < 4 else bU1
                        Yc = Yb[c // 4][:, c % 4, :]
                        k.mm(bu_[0:64, (c % 4) * 128:(c % 4 + 1) * 128], Yc, Vb[:, c, :], [Yb[c // 4], Vb], [bu_])
                        k.mm(bWt[:, c * 64:(c + 1) * 64], Kbd[:, c, :], Yc, [Kbd, Yb[c // 4]], [bWt])
                    k.cp(uu[:, 0:4, :], bU0[0:64, :].rearrange("p (c d) -> p c d", d=128), [bU0], [uu], eng='act')
                    k.cp(uu[:, 4:8, :], bU1[0:64, :].rearrange("p (c d) -> p c d", d=128), [bU1], [uu], eng='act')
                    k.cp(wT[:], bWt[:, :].rearrange("p (c j) -> p c j", j=64), [bWt], [wT])
                    k.rel(bU0, bU1, bWt)
                    yield
                    yield

                def gdn_back(h, hb):
                    wT, uu, Qd, Gp, Kdec, Am, Amb = wT2[hb], uu2[hb], Qd2[hb], Gp2[hb], Kdec2[hb], Am2[hb], Amb2[hb]
                    Amb_ = Am if os.environ.get('SUPD32', '0') == '1' else Amb
                    Sh = S_all[:, h, :]
                    k.cp(Sb[:], Sh, [S_all], [Sb], eng='pool')
                    for c in range(8):
                        col = c * 8 + h
                        b1_, b2_, b3_ = k.bank(), k.bank(), k.bank()
                        k.mm(b1_[0:64, 0:128], wT[:, c, :], Sb[:], [wT, Sb], [b1_])
                        k.mm(b2_[0:64, 0:128], Qd[:, c * 64:(c + 1) * 64], Sb[:], [Qd, Sb], [b2_])
                        k.tt(vnew[:], uu[:, c, :], b1_[0:64, 0:128], ALU.subtract, [uu, b1_], [vnew])
                        yield
                        k.mm(b2_[0:64, 128:256], Amb_[:, c, :], vnew[:], [Amb_, vnew], [b2_])
                        k.mm(b3_[:, 0:128], Kdec[:, c, :], vnew[:], [Kdec, vnew], [b3_])
                        k.stt(Sh, Sh, elast[:, col:col + 1], b3_[:, 0:128], ALU.mult, ALU.add, [S_all, elast, b3_], [S_all])
                        if c < 7:
                            k.cp(Sb[:], Sh, [S_all], [Sb], eng='pool')
                        k.act(osb[:, c, :], b2_[0:64, 0:128], AF.Copy, [b2_, ed], [osb], scale=ed[:, col:col + 1])
                        k.tt(osb[:, c, :], osb[:, c, :], b2_[0:64, 128:256], ALU.add, [osb, b2_], [osb])
                        k.rel(b1_, b2_, b3_)
                        yield
                    if blk == 0 and h == 0:
                        dump("osb", osb[:], [osb])
                    ck('gdn_e')
                    k.tt(uu[:], osb[:], osb[:], ALU.mult, [osb], [uu], eng='pool')
                    k.op('dve', lambda e: e.tensor_reduce(out=oss[:], in_=uu[:], axis=AX.X, op=ALU.add), reads=[uu], writes=[oss])
                    k.act(ors[:], oss[:], AF.Ln, [oss, epsc], [ors], bias=epsc[0:64, :], scale=1.0 / 128)
                    k.act(ors[:], ors[:], AF.Exp, [ors], [ors], scale=-0.5)
                    k.tt(on1[:], osb[:], bc(ors[:], 2, 128), ALU.mult, [osb, ors], [on1])
                    b = k.bank()
                    bv = b[:, :].bitcast(BF16)
                    for c in range(8):
                        k.tr(bv[:, c * 64:(c + 1) * 64], on1[:, c, :], identb[0:64, 0:64], [on1, identb], [b])
                    k.stt(onT[:, h, :], bv[:, 0:TB], onwT[:, 0:1], Gp[:], ALU.mult, ALU.mult, [b, Gp, onwT], [onT])
                    k.rel(b)
                    yield
                    yield

                def _drain(gl):
                    gl = [[g_, w_] for g_, w_ in gl]
                    while gl:
                        for it in list(gl):
                            for _ in range(it[1]):
                                try:
                                    next(it[0])
                                except StopIteration:
                                    gl.remove(it)
                                    break

                _drain([(gdn_front(0, 0), 1)])
                for h in range(8):
                    gl = [(gdn_back(h, h % 2), 1)]
                    if h < 7:
                        gl.append((gdn_front(h + 1, (h + 1) % 2), 2))
                    _drain(gl)
                if blk == 0:
                    dump("onT", onT[:], [onT])
                    dump("S0", S_all[:], [S_all])
                if last:
                    k.dma('sp', o_S_p, S_all[:], reads=[S_all])
                    k.cp(halo_out[:], halo[:], [halo], [halo_out], eng='pool')
                    for j_ in range(3):
                        k.dma('sp', o_conv_p[j_].rearrange("(c p) -> p c", p=128), halo_out[:, :, j_], reads=[halo_out],
                              allow_slow_non_contiguous=True)

                ck('gdn')
                k.switch('G', 'A')
                k.switch('G', 'FW')
                wt = wload(w_in[:, OFF_SK:OFF_SK + 512], 512)
                for t in range(4):
                    b = k.bank()
                    for kk in range(8):
                        k.mm(b[:, :], hT[:, kk, t * 128:(t + 1) * 128], wt[:, kk, :], [hT, wt], [b], start=(kk == 0), stop=(kk == 7))
                    ck('swa_a0')
                    kin = b[:, 0:256].rearrange("p (g d) -> p g d", d=64)
                    vin = b[:, 256:512].rearrange("p (g d) -> p g d", d=64)
                    Kt4 = Ktok[:].rearrange("p (g a d) -> p g a d", a=2, d=64)
                    Vt4 = Vtok[:, 1 + t, :].rearrange("p (g a d) -> p g a d", a=2, d=64)
                    for a_ in range(2):
                        _v = os.environ.get('SWA_VAR', '')
                        ke, ve = {'': ('act', 'dve'), 'konly': ('act', None), 'vonly': (None, 'dve'), 'kdve': ('dve', None),
                                  'both_dve': ('dve', 'dve'), 'both_act': ('act', 'act'), 'swap': ('dve', 'act')}[_v]
                        if ke:
                            k.cp(Kt4[:, :, a_, :], kin, [b], [Ktok], eng=ke)
                        if ve:
                            k.cp(Vt4[:, :, a_, :], vin, [b], [Vtok], eng=ve)
                    ck('swa_a1')
                    if last and t == 3:
                        k.cp(kvout[:], b[:, :], [b], [kvout])
                        k.dma('sp', o_k_p, kvout[:, 0:256], reads=[kvout])
                        k.dma('sp', o_v_p, kvout[:, 256:512], reads=[kvout])
                    k.rel(b)
                    b = k.bank()
                    bv = b[:, :].bitcast(BF16)
                    for g in range(4):
                        k.tr(bv[:, g * 128:(g + 1) * 128], Ktok[:, g * 128:(g + 1) * 128], identb[:], [Ktok, identb], [b])
                    ck('swa_a2')
                    k.cp(KTl[0:64, :, 128 + t * 128:128 + (t + 1) * 128], bv[0:64, 0:512].rearrange("p (g q) -> p g q", q=128), [b], [KTl])
                    k.cp(KTh[64:128, :, 128 + t * 128:128 + (t + 1) * 128], bv[64:128, 0:512].rearrange("p (g q) -> p g q", q=128), [b], [KTh])
                    k.rel(b)
                ck('swa_a')
                for half in range(2):
                    wt = wload(w_in[:, OFF_SQ + half * 512:OFF_SQ + (half + 1) * 512], 512)
                    for j in range(4):
                        b = k.bank()
                        for kk in range(8):
                            k.mm(b[:, :], wt[:, kk, j * 128:(j + 1) * 128], hT[:, kk, :], [wt, hT], [b], start=(kk == 0), stop=(kk == 7))
                        k.act(QT[:, half * 4 + j, :], b[:, :], AF.Copy, [b], [QT], scale=0.125)
                        k.rel(b)
                ck('swa_b')
                def swa_iter(t, g, sb_):
                    msk = maskB if (blk == 0 and t == 0) else maskA
                    sc, pb, PT, mx, nmx, rsum, esk = sc2[sb_], pb2[sb_], PT2[sb_], mx2[sb_], nmx2[sb_], rsum2[sb_], esk2[sb_]
                    b0, b1 = k.bank(), k.bank()
                    for i in range(4):
                        hq = g * 4 + i
                        ch, hf = hq // 2, hq % 2
                        if os.environ.get('HF0'):
                            hf = 0
                        bb = b0 if i < 2 else b1
                        KTx = KTh if hf else KTl
                        k.mm(bb[:, (i % 2) * 256:(i % 2 + 1) * 256], QT[:, ch, t * 128:(t + 1) * 128],
                             KTx[:, g, t * 128:t * 128 + 256], [QT, KTx], [bb])
                    k.tt(sc[:, 0:2, :], b0[:, :].rearrange("p (i q) -> p i q", q=256), bc(msk[:], 1, 2), ALU.add, [b0, msk], [sc])
                    k.tt(sc[:, 2:4, :], b1[:, :].rearrange("p (i q) -> p i q", q=256), bc(msk[:], 1, 2), ALU.add, [b1, msk], [sc])
                    k.rel(b0, b1)
                    yield
                    ck('swa_c')
                    k.op('dve', lambda e: e.tensor_reduce(out=mx[:], in_=sc[:], axis=AX.X, op=ALU.max), reads=[sc], writes=[mx])
                    k.tt(mx[:], mx[:], sinks[:, g * 4:(g + 1) * 4], ALU.max, [mx, sinks], [mx])
                    k.ts(nmx[:], mx[:], -1.0, None, ALU.mult, None, [mx], [nmx])
                    for i in range(4):
                        k.act(pb[:, i, :], sc[:, i, :], AF.Exp, [sc, nmx], [pb, rsum], bias=nmx[:, i:i + 1], accum=rsum[:, i:i + 1])
                    k.tt(esk[:], sinks[:, g * 4:(g + 1) * 4], mx[:], ALU.subtract, [sinks, mx], [esk])
                    k.act(esk[:], esk[:], AF.Exp, [esk], [esk])
                    k.tt(rsum[:], rsum[:], esk[:], ALU.add, [rsum, esk], [rsum])
                    k.op('dve', lambda e: e.reciprocal(out=rsum[:], in_=rsum[:]), reads=[rsum], writes=[rsum])
                    ck('swa_d')
                    k.tt(pb[:], pb[:], bc(rsum[:], 2, 256), ALU.mult, [pb, rsum], [pb])
                    yield
                    b = k.bank()
                    bv = b[:, :].bitcast(BF16)
                    for i in range(4):
                        for kt in range(2):
                            k.tr(bv[:, (i * 2 + kt) * 128:(i * 2 + kt + 1) * 128], pb[:, i, kt * 128:(kt + 1) * 128], identb[:], [pb, identb], [b])
                    ck('swa_e')
                    k.cp(PT[:].rearrange("p i a q -> p (i a q)"), bv[:, 0:1024], [b], [PT], eng='act')
                    k.rel(b)
                    yield
                    b = k.bank()
                    for i in range(4):
                        for kt in range(2):
                            k.mm(b[:, i * 128:(i + 1) * 128], Vtok[:, t + kt, g * 128:(g + 1) * 128], PT[:, i, kt, :], [Vtok, PT], [b],
                                 start=(kt == 0), stop=(kt == 1))
                    for i in range(4):
                        hq = g * 4 + i
                        ch, hf = hq // 2, hq % 2
                        k.cp(obT[hf * 64:(hf + 1) * 64, ch, t * 128:(t + 1) * 128], b[hf * 64:(hf + 1) * 64, i * 128:(i + 1) * 128], [b], [obT],
                             eng=('act' if i % 2 else 'dve'))
                    k.rel(b)
                    yield

                def swa_stream(its, sb_):
                    for (t_, g_) in its:
                        yield from swa_iter(t_, g_, sb_)

                def _drain2(gl):
                    gl = list(gl)
                    while gl:
                        for it in list(gl):
                            try:
                                next(it)
                            except StopIteration:
                                gl.remove(it)

                its_ = [(t_, g_) for t_ in range(4) for g_ in range(4)]
                _drain2([swa_stream(its_[0::2], 0), swa_stream(its_[1::2], 1)])
                ck('swa_f')
                k.cp(KTl[0:64, :, 0:128], KTl[0:64, :, TB:TB + 128], [KTl], [KTl], eng='pool')
                k.cp(KTh[64:128, :, 0:128], KTh[64:128, :, TB:TB + 128], [KTh], [KTh], eng='pool')
                k.cp(Vtok[:, 0, :], Vtok[:, 4, :], [Vtok], [Vtok], eng='pool')
                if blk == 0:
                    dump("obT", obT[:], [obT])

                ck('swa')
                k.switch('A', 'F')
                wlist['cur'] = W8 + FW
                for j in range(8):
                    wt = nextw()
                    wload(w_in[:, OFF_GA + j * 128:OFF_GA + (j + 1) * 128], 128, c0=0, tile=wt)
                    wload(w_in[:, OFF_GB + j * 128:OFF_GB + (j + 1) * 128], 128, c0=128, tile=wt)
                    wload(w_gdn_out[:, j * 128:(j + 1) * 128], 128, c0=256, tile=wt)
                    wload(w_swa_out[:, j * 128:(j + 1) * 128], 128, c0=384, tile=wt)
                    bs = [k.bank() for _ in range(4)]
                    srcs = [hT, hT, onT, obT]
                    for q in range(4):
                        for kk in range(8):
                            k.mm(bs[q][:, :], wt[:, kk, q * 128:(q + 1) * 128], srcs[q][:, kk, :], [wt, srcs[q]], [bs[q]], start=(kk == 0), stop=(kk == 7))
                    k.act(sga[:], bs[0][:, :], AF.Sigmoid, [bs[0]], [sga])
                    k.act(sgb[:], bs[1][:, :], AF.Sigmoid, [bs[1]], [sgb])
                    k.tt(sga[:], sga[:], bs[2][:, :], ALU.mult, [sga, bs[2]], [sga])
                    k.tt(sgb[:], sgb[:], bs[3][:, :], ALU.mult, [sgb, bs[3]], [sgb])
                    k.tt(mixT[:, j, :], sga[:], sgb[:], ALU.add, [sga, sgb], [mixT])
                    k.rel(*bs)
                ck('merge')
                wts = [wload(w_o[:, hf * 512:(hf + 1) * 512], 512) for hf in range(2)]
                for t in range(4):
                    for hf in range(2):
                        b = k.bank()
                        for kk in range(8):
                            k.mm(b[:, :], mixT[:, kk, t * 128:(t + 1) * 128], wts[hf][:, kk, :], [mixT, wts[hf]], [b], start=(kk == 0), stop=(kk == 7))
                        k.tt(sga[:], b[:, :], g1bc[:, hf * 512:(hf + 1) * 512], ALU.mult, [b, g1bc], [sga])
                        k.rel(b)
                        k.tt(x1[t][:, hf * 512:(hf + 1) * 512], x1[t][:, hf * 512:(hf + 1) * 512], sga[:], ALU.add, [x1[t], sga], [x1[t]], eng='pool')
                    rms_to_T(x1[t][:], x1[t], h2T, h2T, (a2, a2), (modT[:, 24:32, NS], modT), t)
                if blk == 0:
                    dump("x1", x1[0][:], [x1[0]])

                ck('wo')
                for jp in range(NFC // 2):
                    wt = nextw()
                    wload(w_ffn_gate[:, jp * 256:(jp + 1) * 256], 256, c0=0, tile=wt)
                    wload(w_ffn_up[:, jp * 256:(jp + 1) * 256], 256, c0=256, tile=wt)
                    for jj in range(2):
                        j = jp * 2 + jj
                        bg, bu = k.bank(), k.bank()
                        for kk in range(8):
                            k.mm(bg[:, :], wt[:, kk, jj * 128:(jj + 1) * 128], h2T[:, kk, :], [wt, h2T], [bg], start=(kk == 0), stop=(kk == 7))
                        for kk in range(8):
                            k.mm(bu[:, :], wt[:, kk, 256 + jj * 128:256 + (jj + 1) * 128], h2T[:, kk, :], [wt, h2T], [bu], start=(kk == 0), stop=(kk == 7))
                        k.cp(gpre[:, 0:2], fhalo[:, j, :], [fhalo], [gpre], eng='pool')
                        k.cp(gpre[:, 2:2 + TB], bg[:, :], [bg], [gpre], eng='act')
                        k.cp(fhalo[:, j, :], gpre[:, TB:TB + 2], [gpre], [fhalo], eng='pool')
                        k.ts(gcv[:], gpre[:, 0:TB], fcwT[:, j, 0:1], fcbT[:, j:j + 1], ALU.mult, ALU.add, [gpre, fcwT, fcbT], [gcv])
                        for tap in range(1, 3):
                            k.stt(gcv[:], gpre[:, tap:tap + TB], fcwT[:, j, tap:tap + 1], gcv[:], ALU.mult, ALU.add, [gpre, fcwT, gcv], [gcv])
                        k.act(gcv[:], gcv[:], AF.Silu, [gcv], [gcv])
                        k.tt(actT[:, j, :], gcv[:], bu[:, :], ALU.mult, [gcv, bu], [actT])
                        k.rel(bg, bu)
                if last:
                    for j_ in range(2):
                        k.dma('sp', o_ffn_p[j_].rearrange("(c p) -> p c", p=128), fhalo[:, :, j_], reads=[fhalo], allow_slow_non_contiguous=True)
                for hf in range(2):
                    bs = [k.bank() for _ in range(4)]
                    for kg in range(3):
                        nk = 8 if kg < 2 else NFC - 16
                        wt = nextw()
                        k.dma('pool', wt[:, 0:nk, :], w_ffn_down[kg * 1024:kg * 1024 + nk * 128, hf * 512:(hf + 1) * 512].rearrange("(c p) n -> p c n", p=128),
                              writes=[wt])
                        for kk in range(nk):
                            kf = kg * 8 + kk
                            for t in range(4):
                                k.mm(bs[t][:, :], actT[:, kf, t * 128:(t + 1) * 128], wt[:, kk, :], [actT, wt], [bs[t]], start=(kf == 0), stop=(kf == NFC - 1))
                    for t in range(4):
                        k.tt(sga[:], bs[t][:, :], g2bc[:, hf * 512:(hf + 1) * 512], ALU.mult, [bs[t], g2bc], [sga])
                        k.tt(x1[t][:, hf * 512:(hf + 1) * 512], x1[t][:, hf * 512:(hf + 1) * 512], sga[:], ALU.add, [x1[t], sga], [x1[t]], eng='pool')
                    k.rel(*bs)
                ck('ffn')
                for t in range(4):
                    k.act(yt[:], x1[t][:], AF.Square, [x1[t]], [yt, ss2], accum=ss2[:])
                    k.act(rs2[:], ss2[:], AF.Ln, [ss2, epsc], [rs2], bias=epsc[:], scale=1.0 / D)
                    k.act(rs2[:], rs2[:], AF.Exp, [rs2], [rs2], scale=-0.5)
                    k.stt(yt[:], x1[t][:], rs2[:], fnw_bc[:], ALU.mult, ALU.mult, [x1[t], rs2, fnw_bc], [yt])
                    k.dma('sp', y_p[t0 + t * 128:t0 + (t + 1) * 128, :], yt[:], reads=[yt])
        except _Stop:
            pass
        k.finish('sp')
        print("instr counts", k.cnt, "dma sems", k.ndsem)
        if os.environ.get('MMSTAT'):
            tot = sum(k.mmstat.values())
            for ln, c in sorted(k.mmstat.items(), key=lambda kv: -kv[1])[:40]:
                print("  mm line %d: %.1f us (%.1f%%)" % (ln, c / 2400.0, 100.0 * c / tot))
            print("  total est %.1f us" % (tot / 2400.0))
    return nc


OUT_NAMES = ["y_p", "y_s", "o_S_p", "o_S_s", "o_conv_p", "o_conv_s", "o_k_p", "o_k_s", "o_v_p", "o_v_s", "o_ffn_p", "o_ffn_s"]


def make_in_maps(inp, cores):
    f = lambda a: np.ascontiguousarray(a, dtype=np.float32)
    shared = {
        "w_mod": f(inp["w_mod"][0]), "b_mod": f(inp["b_mod"][0][None]), "norm1_w": f(inp["norm1_w"][0][None]),
        "norm2_w": f(inp["norm2_w"][0][None]), "w_in": f(inp["w_in"][0]), "gdn_conv_w": f(inp["gdn_conv_w"][0]),
        "gdn_a_log": f(inp["gdn_a_log"][0]), "gdn_dt_bias": f(inp["gdn_dt_bias"][0]),
        "gdn_onorm_w": f(inp["gdn_onorm_w"][0][None]), "w_gdn_out": f(inp["w_gdn_out"][0]),
        "swa_sinks": f(inp["swa_sinks"][0]), "w_swa_out": f(inp["w_swa_out"][0]), "w_o": f(inp["w_o"][0]),
        "w_ffn_gate": f(inp["w_ffn_gate"][0]), "w_ffn_up": f(inp["w_ffn_up"][0]), "ffn_conv_w": f(inp["ffn_conv_w"][0]),
        "ffn_conv_b": f(inp["ffn_conv_b"][0][None]), "w_ffn_down": f(inp["w_ffn_down"][0]),
        "final_norm_w": f(inp["final_norm_w"]),
    }
    maps = []
    for b in cores:
        s = slice(b * NS, (b + 1) * NS)
        m = dict(shared)
        m["x_p"] = f(inp["x_prompt"][b])
        m["x_s"] = f(inp["x_sample"][s, 0])
        m["c17"] = f(np.concatenate([inp["c_sample"][s], inp["c_prompt"][b:b + 1]], axis=0))
        m["st_S"] = f(np.transpose(inp["state_gdn_S"][0, s], (0, 2, 1, 3)))
        m["st_conv"] = f(inp["state_gdn_conv"][0, s].reshape(NS * 3, 3072))
        m["st_k"] = f(np.transpose(inp["cache_swa_k"][0, s].reshape(NS, 128, 256), (1, 0, 2)))
        m["st_v"] = f(np.transpose(inp["cache_swa_v"][0, s].reshape(NS, 128, 256), (1, 0, 2)))
        m["st_ffn"] = f(inp["state_ffn_conv"][0, s].reshape(NS * 2, DFF))
        maps.append(m)
    return maps


def kernel(**inp):
    nc = build_nc()
    cores = list(range(8))
    res = run_bass_kernel_spmd(nc, make_in_maps(inp, cores), core_ids=cores)
    r = res.results
    cat = lambda n: np.concatenate([r[i][n] for i in range(8)], axis=0)
    stack = lambda n: np.stack([r[i][n] for i in range(8)], axis=0)
    y_prompt = stack("y_p")
    y_sample = cat("y_s").reshape(128, 1, D)
    gS_p = np.ascontiguousarray(np.transpose(stack("o_S_p"), (0, 2, 1, 3)))[None]
    gS_s = np.ascontiguousarray(np.transpose(cat("o_S_s"), (0, 2, 1, 3)))[None]
    gc_p = stack("o_conv_p")[None]
    gc_s = cat("o_conv_s")[None]
    k_p = stack("o_k_p").reshape(1, 8, 128, 4, 64)
    k_s = np.ascontiguousarray(np.concatenate([np.transpose(r[i]["o_k_s"], (1, 0, 2)) for i in range(8)], axis=0)).reshape(1, 128, 128, 4, 64)
    v_p = stack("o_v_p").reshape(1, 8, 128, 4, 64)
    v_s = np.ascontiguousarray(np.concatenate([np.transpose(r[i]["o_v_s"], (1, 0, 2)) for i in range(8)], axis=0)).reshape(1, 128, 128, 4, 64)
    f_p = stack("o_ffn_p")[None]
    f_s = cat("o_ffn_s")[None]
    return (y_prompt, y_sample, gS_p, gS_s, gc_p, gc_s, k_p, k_s, v_p, v_s, f_p, f_s)
```
